# Optimizing a Trainium2 kernel written in Bass

```python
import math
import jax
import jax.numpy as jnp
from jax import lax
import numpy as np

D_MODEL = 1024
BATCH = 16
SEQ = 2048
DEPTH = 4

CHUNK = 64
CONV_W = 4
D_FF = 4 * D_MODEL
NORM_EPS = 1e-6
L2_EPS = 1e-6
MIX_W = D_MODEL // 2
MIX_OUT = 2 * MIX_W

GDN_HEADS = 4
GDN_DK = MIX_W // GDN_HEADS
GDN_DV = MIX_W // GDN_HEADS
GLA_HEADS = 4
GLA_DK = MIX_W // (2 * GLA_HEADS)
GLA_DV = MIX_W // GLA_HEADS
GLA_GATE_RANK = 16
GLA_TAU = 16.0
GLA_LOG_GATE_MIN = -1.0
SSD_HEADS = 8
SSD_P = MIX_W // SSD_HEADS
SSD_GROUPS = 2
SSD_STATE = 128
RWKV_HEADS = 8
RWKV_DK = MIX_W // RWKV_HEADS
RWKV_W_RANK = 64
RWKV_A_RANK = 64
RWKV_G_RANK = 128
RWKV_GN_EPS = 64e-5

N_EVEN = (DEPTH + 1) // 2
N_ODD = DEPTH // 2

A_QK = GDN_HEADS * GDN_DK
A_V = GDN_HEADS * GDN_DV
A_QKV = 2 * A_QK + A_V
B_QK = GLA_HEADS * GLA_DK
B_V = GLA_HEADS * GLA_DV
EVEN_SPLITS = (A_QKV, A_V, GDN_HEADS, GDN_HEADS, B_QK, B_QK, B_V, B_V, GLA_GATE_RANK)
EVEN_IN = sum(EVEN_SPLITS)

C_X = SSD_HEADS * SSD_P
C_BC = SSD_GROUPS * SSD_STATE
C_XBC = C_X + 2 * C_BC
D_HK = RWKV_HEADS * RWKV_DK
D_SPLITS = (D_HK, D_HK, D_HK, RWKV_W_RANK, RWKV_A_RANK, RWKV_G_RANK)
D_IN = sum(D_SPLITS)
ODD_SPLITS = (C_X, C_XBC, SSD_HEADS, D_IN)
ODD_IN = sum(ODD_SPLITS)

kernel_name = 'hybrid_gdn_gla_ssd_rwkv7_trunk'


def _rmsnorm(x, w):
    xf = x.astype(jnp.float32)
    y = xf * lax.rsqrt(jnp.mean(xf * xf, axis=-1, keepdims=True) + NORM_EPS)
    return (y * w.astype(jnp.float32)).astype(x.dtype)


def _l2norm(x):
    xf = x.astype(jnp.float32)
    return xf * lax.rsqrt(jnp.sum(xf * xf, axis=-1, keepdims=True) + L2_EPS)


def _split(t, sizes):
    parts, off = [], 0
    for s in sizes:
        parts.append(t[..., off:off + s])
        off += s
    return parts


def _causal_dwconv(x, w):
    ch = x.shape[-1]
    return lax.conv_general_dilated(
        x, w[:, None, :].astype(x.dtype), window_strides=(1,),
        padding=[(w.shape[0] - 1, 0)], dimension_numbers=('NWC', 'WIO', 'NWC'),
        feature_group_count=ch)


def _shift(x):
    return jnp.pad(x, ((0, 0), (1, 0), (0, 0)))[:, :-1]


def _to_chunks(t):
    t = t.reshape(t.shape[0], t.shape[1] // CHUNK, CHUNK, *t.shape[2:])
    return jnp.moveaxis(t, 2, 3)


def _from_chunks(t):
    t = jnp.moveaxis(t, 3, 2)
    return t.reshape(t.shape[0], t.shape[1] * t.shape[2], *t.shape[3:])


def _causal_mask(strict=False):
    return jnp.tril(jnp.ones((CHUNK, CHUNK), dtype=bool), -1 if strict else 0)


def _gated_delta_chunked(q, k, v, g, beta):
    bsz, _, nh, dk = q.shape
    dv = v.shape[-1]
    q = _to_chunks(q * dk ** -0.5)
    k = _to_chunks(k)
    v = _to_chunks(v)
    beta = _to_chunks(beta)
    gc = jnp.cumsum(_to_chunks(g), axis=-1)
    dmask = jnp.exp(jnp.where(_causal_mask(), gc[..., :, None] - gc[..., None, :], -jnp.inf))
    kb = k * beta[..., None]
    l_strict = jnp.where(_causal_mask(True), jnp.einsum('bnhik,bnhjk->bnhij', kb, k) * dmask, 0.0)
    eye = jnp.eye(CHUNK, dtype=l_strict.dtype)
    t_inv = lax.linalg.triangular_solve(eye + l_strict, jnp.broadcast_to(eye, l_strict.shape),
                                        left_side=True, lower=True, unit_diagonal=True)
    u = jnp.einsum('bnhij,bnhjv->bnhiv', t_inv, v * beta[..., None])
    w = jnp.einsum('bnhij,bnhjk->bnhik', t_inv, kb * jnp.exp(gc)[..., None])
    a_qk = jnp.einsum('bnhik,bnhjk->bnhij', q, k) * dmask
    qg = q * jnp.exp(gc)[..., None]
    kd = k * jnp.exp(gc[..., -1:] - gc)[..., None]
    g_last = jnp.exp(gc[..., -1])

    def step(state, inp):
        u_n, w_n, qg_n, a_n, kd_n, gl_n = inp
        v_new = u_n - jnp.einsum('bhik,bhkv->bhiv', w_n, state)
        o_n = jnp.einsum('bhik,bhkv->bhiv', qg_n, state) + jnp.einsum('bhij,bhjv->bhiv', a_n, v_new)
        state = state * gl_n[..., None, None] + jnp.einsum('bhik,bhiv->bhkv', kd_n, v_new)
        return state, o_n

    xs = tuple(jnp.moveaxis(t, 1, 0) for t in (u, w, qg, a_qk, kd, g_last))
    _, o = lax.scan(step, jnp.zeros((bsz, nh, dk, dv), q.dtype), xs)
    return _from_chunks(jnp.moveaxis(o, 0, 1))


def _gla_chunked(q, k, v, log_a):
    bsz, _, nh, dk = q.shape
    dv = v.shape[-1]
    gc = jnp.cumsum(_to_chunks(log_a), axis=3)
    q = _to_chunks(q * dk ** -0.5)
    k = _to_chunks(k)
    v = _to_chunks(v)
    qg = q * jnp.exp(gc)
    kg = k * jnp.exp(-gc)
    att = jnp.where(_causal_mask(), jnp.einsum('bnhik,bnhjk->bnhij', qg, kg), 0.0)
    o_intra = jnp.einsum('bnhij,bnhjv->bnhiv', att, v)
    kd = k * jnp.exp(gc[..., -1:, :] - gc)
    g_last = jnp.exp(gc[..., -1, :])

    def step(state, inp):
        kd_n, v_n, gl_n = inp
        new = state * gl_n[..., None] + jnp.einsum('bhik,bhiv->bhkv', kd_n, v_n)
        return new, state

    xs = tuple(jnp.moveaxis(t, 1, 0) for t in (kd, v, g_last))
    _, s_prev = lax.scan(step, jnp.zeros((bsz, nh, dk, dv), q.dtype), xs)
    o_inter = jnp.einsum('bnhik,bnhkv->bnhiv', qg, jnp.moveaxis(s_prev, 0, 1))
    return _from_chunks(o_intra + o_inter)


def _ssd_chunked(x, dt, a_head, b_in, c_in):
    bsz, seq, nh, hp = x.shape
    ng, ns = b_in.shape[2:]
    nr = nh // ng
    nc = seq // CHUNK
    ac = jnp.cumsum((dt * a_head).reshape(bsz, nc, CHUNK, ng, nr), axis=2)
    xdt = (x * dt[..., None]).reshape(bsz, nc, CHUNK, ng, nr, hp)
    bc = b_in.reshape(bsz, nc, CHUNK, ng, ns)
    cc = c_in.reshape(bsz, nc, CHUNK, ng, ns)
    seg = jnp.exp(jnp.where(_causal_mask()[:, :, None, None],
                            ac[:, :, :, None] - ac[:, :, None, :], -jnp.inf))
    cb = jnp.einsum('bcigs,bcjgs->bcijg', cc, bc)
    y_intra = jnp.einsum('bcijgr,bcjgrp->bcigrp', cb[..., None] * seg, xdt)
    st_local = jnp.einsum('bcjgs,bcjgrp->bcgrps', bc, xdt * jnp.exp(ac[:, :, -1:] - ac)[..., None])
    chunk_decay = jnp.exp(ac[:, :, -1])

    def step(state, inp):
        st_n, cd_n = inp
        return state * cd_n[..., None, None] + st_n, state

    _, s_prev = lax.scan(step, jnp.zeros((bsz, ng, nr, hp, ns), x.dtype),
                         (jnp.moveaxis(st_local, 1, 0), jnp.moveaxis(chunk_decay, 1, 0)))
    y_inter = jnp.einsum('bcigs,bcgrps->bcigrp', cc, jnp.moveaxis(s_prev, 0, 1)) * jnp.exp(ac)[..., None]
    return (y_intra + y_inter).reshape(bsz, seq, nh, hp)


def _rwkv7_scan(r, w, k, v, a, b):
    bsz, _, nh, dk = r.shape

    def step(state, inp):
        r_t, w_t, k_t, v_t, a_t, b_t = inp
        sa = jnp.einsum('bhvk,bhk->bhv', state, a_t)
        state = (state * w_t[:, :, None, :] + sa[..., None] * b_t[:, :, None, :]
                 + v_t[..., None] * k_t[:, :, None, :])
        return state, jnp.einsum('bhvk,bhk->bhv', state, r_t)

    xs = tuple(jnp.moveaxis(t, 1, 0) for t in (r, w, k, v, a, b))
    _, ys = lax.scan(step, jnp.zeros((bsz, nh, dk, dk), r.dtype), xs)
    return jnp.moveaxis(ys, 0, 1)


def _even_mixer(h, w_in, conv_w, a_log, dt_bias, gdn_norm_w, gla_w2, gla_b, gla_norm_w, w_out):
    bsz, seq, _ = h.shape
    f32 = jnp.float32
    qkv, z, b_raw, a_raw, gq, gk, gv, gr, g_lr = _split(h @ w_in, EVEN_SPLITS)
    qkv = jax.nn.silu(_causal_dwconv(qkv, conv_w)).astype(f32)
    aq, ak, av = _split(qkv, (A_QK, A_QK, A_V))
    aq = _l2norm(aq.reshape(bsz, seq, GDN_HEADS, GDN_DK))
    ak = _l2norm(ak.reshape(bsz, seq, GDN_HEADS, GDN_DK))
    av = av.reshape(bsz, seq, GDN_HEADS, GDN_DV)
    beta = jax.nn.sigmoid(b_raw.astype(f32))
    g = -jnp.exp(a_log.astype(f32)) * jax.nn.softplus(a_raw.astype(f32) + dt_bias.astype(f32))
    oa = _gated_delta_chunked(aq, ak, av, g, beta)
    oa = _rmsnorm(oa, gdn_norm_w) * jax.nn.silu(z.astype(f32).reshape(bsz, seq, GDN_HEADS, GDN_DV))
    log_a = jnp.maximum(jax.nn.log_sigmoid((g_lr @ gla_w2 + gla_b).astype(f32)) / GLA_TAU,
                        GLA_LOG_GATE_MIN)
    ob = _gla_chunked(gq.astype(f32).reshape(bsz, seq, GLA_HEADS, GLA_DK),
                      gk.astype(f32).reshape(bsz, seq, GLA_HEADS, GLA_DK),
                      gv.astype(f32).reshape(bsz, seq, GLA_HEADS, GLA_DV),
                      log_a.reshape(bsz, seq, GLA_HEADS, GLA_DK))
    ob = _rmsnorm(ob, gla_norm_w) * jax.nn.silu(gr.astype(f32).reshape(bsz, seq, GLA_HEADS, GLA_DV))
    o = jnp.concatenate([oa.reshape(bsz, seq, A_V), ob.reshape(bsz, seq, B_V)], axis=-1)
    return o.astype(h.dtype) @ w_out


def _odd_mixer(h, w_in, conv_w, conv_b, dt_bias, a_log, d_skip, ssd_norm_w, mu, w0, w2, a0, a2, g2,
               k_k, k_a, r_k, gn_w, gn_b, w_out):
    bsz, seq, _ = h.shape
    f32 = jnp.float32
    z, xbc, dt_raw, pd = _split(h @ w_in, ODD_SPLITS)
    xbc = jax.nn.silu(_causal_dwconv(xbc, conv_w) + conv_b).astype(f32)
    xs, b_in, c_in = _split(xbc, (C_X, C_BC, C_BC))
    xs = xs.reshape(bsz, seq, SSD_HEADS, SSD_P)
    dt = jax.nn.softplus(dt_raw.astype(f32) + dt_bias.astype(f32))
    yc = _ssd_chunked(xs, dt, -jnp.exp(a_log.astype(f32)),
                      b_in.reshape(bsz, seq, SSD_GROUPS, SSD_STATE),
                      c_in.reshape(bsz, seq, SSD_GROUPS, SSD_STATE))
    yc = (yc + xs * d_skip.astype(f32)[:, None]).reshape(bsz, seq, C_X) * jax.nn.silu(z.astype(f32))
    yc = _rmsnorm(yc.reshape(bsz, seq, SSD_GROUPS, C_X // SSD_GROUPS),
                  ssd_norm_w.reshape(SSD_GROUPS, C_X // SSD_GROUPS)).reshape(bsz, seq, C_X)
    pd = pd.astype(f32)
    pd = pd + (_shift(pd) - pd) * mu.astype(f32)
    r, k, v, xw, xa, xg = _split(pd, D_SPLITS)
    w_log = -jax.nn.softplus(-(w0 + jnp.tanh(xw) @ w2)) - 0.5
    decay = jnp.exp(-jnp.exp(w_log))
    a = jax.nn.sigmoid(a0 + xa @ a2)
    g = jax.nn.sigmoid(xg) @ g2
    hs = (bsz, seq, RWKV_HEADS, RWKV_DK)
    kk = _l2norm((k * k_k).reshape(hs))
    k = k * (1.0 + (a - 1.0) * k_a)
    r4, k4, v4, a4 = (t.reshape(hs) for t in (r, k, v, a))
    yd = _rwkv7_scan(r4, decay.reshape(hs), k4, v4, -kk, kk * a4)
    mean = jnp.mean(yd, axis=-1, keepdims=True)
    var = jnp.mean(jnp.square(yd - mean), axis=-1, keepdims=True)
    yd = ((yd - mean) * lax.rsqrt(var + RWKV_GN_EPS)).reshape(bsz, seq, D_HK) * gn_w + gn_b
    yd = (yd + (jnp.sum(r4 * k4 * r_k, axis=-1, keepdims=True) * v4).reshape(bsz, seq, D_HK)) * g
    o = jnp.concatenate([yc, yd], axis=-1)
    return o.astype(h.dtype) @ w_out


def _sqrelu_mlp(h, w1, w2):
    return jnp.square(jax.nn.relu(h @ w1)) @ w2


def setup_inputs(seed: int = 0) -> dict:
    key = jax.random.key(seed)
    keys = iter(jax.random.split(key, 48))
    f32 = jnp.float32

    def nrm(shape, scale):
        return jax.random.normal(next(keys), shape, f32) * scale

    def gain(shape):
        return 1.0 + nrm(shape, 0.02)

    def unif(shape, lo, hi):
        return jax.random.uniform(next(keys), shape, f32, lo, hi)

    def dtb(shape):
        dt = jnp.exp(unif(shape, math.log(1e-3), math.log(1e-1)))
        return dt + jnp.log(-jnp.expm1(-dt))

    return {
        'x': nrm((BATCH, SEQ, D_MODEL), 1.0),
        'norm_mix_w': gain((DEPTH, D_MODEL)),
        'norm_mlp_w': gain((DEPTH, D_MODEL)),
        'mlp_w1': nrm((DEPTH, D_MODEL, D_FF), D_MODEL ** -0.5),
        'mlp_w2': nrm((DEPTH, D_FF, D_MODEL), D_FF ** -0.5),
        'final_norm_w': gain((D_MODEL,)),
        'even_w_in': nrm((N_EVEN, D_MODEL, EVEN_IN), D_MODEL ** -0.5),
        'gdn_conv_w': nrm((N_EVEN, CONV_W, A_QKV), CONV_W ** -0.5),
        'gdn_a_log': jnp.log(unif((N_EVEN, GDN_HEADS), 1.0, 16.0)),
        'gdn_dt_bias': dtb((N_EVEN, GDN_HEADS)),
        'gdn_norm_w': gain((N_EVEN, GDN_DV)),
        'gla_gate_w2': nrm((N_EVEN, GLA_GATE_RANK, B_QK), GLA_GATE_RANK ** -0.5),
        'gla_gate_b': nrm((N_EVEN, B_QK), 0.1),
        'gla_norm_w': gain((N_EVEN, GLA_DV)),
        'even_w_out': nrm((N_EVEN, MIX_OUT, D_MODEL), MIX_OUT ** -0.5),
        'odd_w_in': nrm((N_ODD, D_MODEL, ODD_IN), D_MODEL ** -0.5),
        'ssd_conv_w': nrm((N_ODD, CONV_W, C_XBC), CONV_W ** -0.5),
        'ssd_conv_b': nrm((N_ODD, C_XBC), 0.02),
        'ssd_dt_bias': dtb((N_ODD, SSD_HEADS)),
        'ssd_a_log': jnp.log(unif((N_ODD, SSD_HEADS), 1.0, 16.0)),
        'ssd_d': gain((N_ODD, SSD_HEADS)),
        'ssd_norm_w': gain((N_ODD, C_X)),
        'rwkv_mu': unif((N_ODD, D_IN), 0.0, 1.0),
        'rwkv_w0': nrm((N_ODD, D_HK), 0.5),
        'rwkv_w2': nrm((N_ODD, RWKV_W_RANK, D_HK), RWKV_W_RANK ** -0.5),
        'rwkv_a0': nrm((N_ODD, D_HK), 0.1),
        'rwkv_a2': nrm((N_ODD, RWKV_A_RANK, D_HK), RWKV_A_RANK ** -0.5),
        'rwkv_g2': nrm((N_ODD, RWKV_G_RANK, D_HK), RWKV_G_RANK ** -0.5),
        'rwkv_k_k': 0.85 + nrm((N_ODD, D_HK), 0.02),
        'rwkv_k_a': gain((N_ODD, D_HK)),
        'rwkv_r_k': nrm((N_ODD, RWKV_HEADS, RWKV_DK), 0.1),
        'rwkv_gn_w': gain((N_ODD, D_HK)),
        'rwkv_gn_b': nrm((N_ODD, D_HK), 0.02),
        'odd_w_out': nrm((N_ODD, MIX_OUT, D_MODEL), MIX_OUT ** -0.5),
    }


def reference(x, norm_mix_w, norm_mlp_w, mlp_w1, mlp_w2, final_norm_w,
              even_w_in, gdn_conv_w, gdn_a_log, gdn_dt_bias, gdn_norm_w,
              gla_gate_w2, gla_gate_b, gla_norm_w, even_w_out,
              odd_w_in, ssd_conv_w, ssd_conv_b, ssd_dt_bias, ssd_a_log, ssd_d, ssd_norm_w,
              rwkv_mu, rwkv_w0, rwkv_w2, rwkv_a0, rwkv_a2, rwkv_g2, rwkv_k_k, rwkv_k_a,
              rwkv_r_k, rwkv_gn_w, rwkv_gn_b, odd_w_out):
    h = x
    for layer in range(DEPTH):
        i = layer // 2
        hn = _rmsnorm(h, norm_mix_w[layer])
        if layer % 2 == 0:
            mix = _even_mixer(hn, even_w_in[i], gdn_conv_w[i], gdn_a_log[i], gdn_dt_bias[i],
                              gdn_norm_w[i], gla_gate_w2[i], gla_gate_b[i], gla_norm_w[i],
                              even_w_out[i])
        else:
            mix = _odd_mixer(hn, odd_w_in[i], ssd_conv_w[i], ssd_conv_b[i], ssd_dt_bias[i],
                             ssd_a_log[i], ssd_d[i], ssd_norm_w[i], rwkv_mu[i], rwkv_w0[i],
                             rwkv_w2[i], rwkv_a0[i], rwkv_a2[i], rwkv_g2[i], rwkv_k_k[i],
                             rwkv_k_a[i], rwkv_r_k[i], rwkv_gn_w[i], rwkv_gn_b[i], odd_w_out[i])
        h = h + mix.astype(h.dtype)
        h = h + _sqrelu_mlp(_rmsnorm(h, norm_mlp_w[layer]), mlp_w1[layer], mlp_w2[layer]).astype(h.dtype)
    return _rmsnorm(h, final_norm_w)
```

```python
import contextlib
import os
import numpy as np
import ml_dtypes
import concourse.bass as bass
import concourse.mybir as mybir
from concourse.bass_utils import run_bass_kernel_spmd

F32 = mybir.dt.float32
BF16 = mybir.dt.bfloat16
AF = mybir.ActivationFunctionType
ALU = mybir.AluOpType
AX = mybir.AxisListType

EPOCH = 30000


class Tl:
    def __init__(self, h, name):
        self.h = h
        self.name = name
        self.w = None
        self.r = {}
        self.dsem = None
        self.dcnt = 0
        self.psum = False

    def __getitem__(self, idx):
        return V(self, self.h[idx])

    @property
    def a(self):
        return V(self, self.h[:])


class V:
    def __init__(self, t, ap):
        self.t = t
        self.ap = ap

    def __getitem__(self, idx):
        return V(self.t, self.ap[idx])

    def bitcast(self, dt):
        return V(self.t, self.ap.bitcast(dt))

    def bc(self, shape):
        return V(self.t, self.ap.to_broadcast(list(shape)))

    def re(self, s, **kw):
        return V(self.t, self.ap.rearrange(s, **kw))


def _v(x):
    return x.a if isinstance(x, Tl) else x


class Eng:
    def __init__(self, K, name, eng):
        self.K = K
        self.name = name
        self.eng = eng
        self.epoch = -1
        self.cnt = EPOCH
        self.sem = None
        self.known = {}

    def tick(self, ins):
        if self.cnt >= EPOCH:
            self.epoch += 1
            self.cnt = 0
            self.sem = self.K.new_sem(f"s_{self.name}_{self.epoch}")
        self.cnt += 1
        ins.then_inc(self.sem, 1)
        return (id(self.sem), self.sem, self.cnt, self.name)


class KB:
    def __init__(self, nc):
        self.nc = nc
        self.es = contextlib.ExitStack()
        self.pe = Eng(self, "pe", nc.tensor)
        self.dve = Eng(self, "dve", nc.vector)
        self.act = Eng(self, "act", nc.scalar)
        self.pool = Eng(self, "pool", nc.gpsimd)
        self.sp = Eng(self, "sp", nc.sync)
        self.engs = [self.pe, self.dve, self.act, self.pool, self.sp]
        self.nsem = 0
        self.ntile = 0
        self.all_dma = {}
        self.ninst = 0

    def new_sem(self, name):
        self.nsem += 1
        return self.es.enter_context(self.nc.semaphore(f"{name}_{self.nsem}"))

    def sb(self, shape, dt=F32, name="t", es=None):
        self.ntile += 1
        nm = f"{name}_{self.ntile}"
        h = (es or self.es).enter_context(self.nc.sbuf_tensor(nm, list(shape), dt))
        return Tl(h, nm)

    def psum(self, shape, dt=F32, name="p", es=None):
        self.ntile += 1
        nm = f"{name}_{self.ntile}"
        h = (es or self.es).enter_context(self.nc.psum_tensor(nm, list(shape), dt))
        t = Tl(h, nm)
        t.psum = True
        return t

    def dram(self, name, shape, dt=F32, kind="Internal"):
        h = self.nc.dram_tensor(name, list(shape), dt, kind=kind)
        return Tl(h.ap(), name)

    def _wait(self, E, toks):
        best = {}
        for tk in toks:
            if tk is None:
                continue
            sid, sem, val, who = tk
            if who == E.name and E is self.pe:
                continue
            if E.known.get(sid, 0) >= val:
                continue
            if best.get(sid, (None, 0))[1] < val:
                best[sid] = (sem, val)
        for sid, (sem, val) in best.items():
            E.eng.wait_ge(sem, val)
            E.known[sid] = val
            self.ninst += 1

    skip = False

    def op(self, E, fn, reads, writes):
        if self.skip:
            return None
        toks = []
        rt = [(_v(x).t) for x in reads if x is not None and not isinstance(x, (int, float))]
        wt = [(_v(x).t) for x in writes if x is not None]
        for t in rt:
            toks.append(t.w)
            if t.psum:
                for k, tk in t.r.items():
                    if k != E.name:
                        toks.append(tk)
        for t in wt:
            toks.append(t.w)
            for k, tk in t.r.items():
                if k == E.name:
                    continue
                toks.append(tk)
        self._wait(E, toks)
        ins = fn()
        tok = E.tick(ins)
        self.ninst += 1
        for t in rt:
            t.r[E.name] = tok
        for t in wt:
            t.w = tok
            t.r = {}
        return tok

    def dma(self, Q, out, in_):
        out = _v(out)
        in_ = _v(in_)
        ot, it = out.t, in_.t
        toks = [it.w, ot.w] + list(ot.r.values())
        self._wait(Q, toks)
        if ot.dsem is None:
            ot.dsem = self.new_sem("d_" + ot.name)
        ins = Q.eng.dma_start(out=out.ap, in_=in_.ap)
        ins.then_inc(ot.dsem, 16)
        ot.dcnt += 16
        tok = (id(ot.dsem), ot.dsem, ot.dcnt, "dma")
        self.ninst += 1
        ot.w = tok
        ot.r = {}
        it.r["dma%d" % id(ot.dsem)] = tok
        self.all_dma[id(ot.dsem)] = tok
        return tok

    def barrier(self):
        toks = list(self.all_dma.values())
        for E in self.engs:
            if E.sem is not None and E.cnt > 0:
                toks.append((id(E.sem), E.sem, E.cnt, E.name + "_b"))
        for E in self.engs:
            self._wait(E, toks)

    def mm(self, out, lhsT, rhs, start=True, stop=True, ser=False):
        out, lhsT, rhs = _v(out), _v(lhsT), _v(rhs)
        if ser and not self.skip and self.pe.sem is not None and self.pe.cnt > 0:
            self.pe.eng.wait_ge(self.pe.sem, self.pe.cnt)
        return self.op(self.pe, lambda: self.nc.tensor.matmul(out.ap, lhsT.ap, rhs.ap, start=start, stop=stop),
                       [lhsT, rhs], [out])

    def actf(self, out, in_, func, bias=None, scale=None, accum=None, E=None):
        out, in_ = _v(out), _v(in_)
        kw = {}
        rd = [in_]
        if bias is not None:
            if isinstance(bias, (int, float)):
                kw["bias"] = float(bias)
            else:
                bias = _v(bias)
                kw["bias"] = bias.ap
                rd.append(bias)
        if scale is not None:
            if isinstance(scale, (int, float)):
                kw["scale"] = float(scale)
            else:
                scale = _v(scale)
                kw["scale"] = scale.ap
                rd.append(scale)
        wr = [out]
        if accum is not None:
            accum = _v(accum)
            kw["accum_out"] = accum.ap
            wr.append(accum)
        return self.op(self.act, lambda: self.nc.scalar.activation(out.ap, in_.ap, func, **kw), rd, wr)

    def _ve(self, E):
        return E or self.dve

    def _sc(self, s, rd):
        if isinstance(s, (int, float)):
            return float(s)
        s = _v(s)
        rd.append(s)
        return s.ap

    def ts(self, out, in0, s1, op0, s2=None, op1=None, E=None, accum=None):
        E = self._ve(E)
        out, in0 = _v(out), _v(in0)
        rd = [in0]
        a1 = self._sc(s1, rd)
        a2 = None if s2 is None else self._sc(s2, rd)
        wr = [out]
        kw = {}
        if accum is not None:
            accum = _v(accum)
            kw["accum_out"] = accum.ap
            wr.append(accum)
        if op1 is None:
            return self.op(E, lambda: E.eng.tensor_scalar(out.ap, in0.ap, a1, None, op0, **kw), rd, wr)
        return self.op(E, lambda: E.eng.tensor_scalar(out.ap, in0.ap, a1, a2, op0, op1, **kw), rd, wr)

    def tt(self, out, in0, in1, op, E=None):
        E = self._ve(E)
        out, in0, in1 = _v(out), _v(in0), _v(in1)
        return self.op(E, lambda: E.eng.tensor_tensor(out.ap, in0.ap, in1.ap, op), [in0, in1], [out])

    def stt(self, out, in0, s, in1, op0, op1, E=None):
        E = self.dve
        out, in0, in1 = _v(out), _v(in0), _v(in1)
        rd = [in0, in1]
        a = self._sc(s, rd)
        return self.op(E, lambda: E.eng.scalar_tensor_tensor(out.ap, in0.ap, a, in1.ap, op0, op1), rd, [out])

    def cp(self, out, in_, E=None):
        E = self._ve(E)
        out, in_ = _v(out), _v(in_)
        if E is self.act:
            return self.op(E, lambda: self.nc.scalar.copy(out.ap, in_.ap), [in_], [out])
        return self.op(E, lambda: E.eng.tensor_copy(out.ap, in_.ap), [in_], [out])

    def memset(self, out, val, E=None):
        E = self._ve(E)
        out = _v(out)
        return self.op(E, lambda: E.eng.memset(out.ap, val), [], [out])

    def red(self, out, in_, op=ALU.add, E=None):
        E = self._ve(E)
        out, in_ = _v(out), _v(in_)
        return self.op(E, lambda: E.eng.tensor_reduce(out.ap, in_.ap, AX.X, op), [in_], [out])


D = 1024
NSEQ = 2
T = 256
CH = 128
EVEN_IN = 3608
ODD_IN = 3336
NEPS = 1e-6


def host_consts():
    i = np.arange(128)
    c = {}
    c["ident"] = np.eye(128, dtype=np.float32)
    c["tri"] = (i[:, None] <= i[None, :]).astype(np.float32)
    c["tris"] = (i[:, None] < i[None, :]).astype(np.float32)
    c["ustr"] = (i[:, None] > i[None, :]).astype(np.float32)
    c["ones"] = np.ones((128, 128), np.float32)
    c["bones"] = ((i[:, None] // 64) == (i[None, :] // 64)).astype(np.float32)
    return c


CONST_NAMES = ["ident", "tri", "tris", "ustr", "ones", "bones"]


class Model:
    def __init__(self, n_layers=4, seq=2048, mix=True, dbg=False):
        self.L = n_layers
        self.seq = seq
        self.NT = NSEQ * seq
        self.mix = mix
        self.dbg = dbg

    def build(self):
        nc = bass.Bass("TRN2", target_bir_lowering=False)
        self.nc = nc
        K = KB(nc)
        self.K = K
        L, NT = self.L, self.NT
        ne = (L + 1) // 2
        no = L // 2
        self.x = K.dram("x", [NT, D], F32, kind="ExternalInput")
        self.out = K.dram("out", [NT, D], F32, kind="ExternalOutput")
        self.hA = K.dram("hA", [D, NT], F32)
        self.hB = K.dram("hB", [D, NT], F32)
        self.cst_d = K.dram("consts", [128, len(CONST_NAMES) * 128], F32, kind="ExternalInput")
        self.w1 = K.dram("mlp_w1", [L, D, 4 * D], F32, kind="ExternalInput")
        self.w2 = K.dram("mlp_w2", [L, 4 * D, D], F32, kind="ExternalInput")
        self.nvec = 3 * L + 1
        self.vec_d = K.dram("vecs", [128, 8 * (2 * L + 1)], F32, kind="ExternalInput")
        if self.mix:
            self.declare_mixer_inputs(ne, no)
        self.cst = K.sb([128, len(CONST_NAMES) * 128], F32, "cst")
        K.dma(K.sp, self.cst, self.cst_d)
        self.C = {n: self.cst[:, i * 128:(i + 1) * 128] for i, n in enumerate(CONST_NAMES)}
        self.cstb = K.sb([128, len(CONST_NAMES) * 128], BF16, "cstb")
        K.cp(self.cstb, self.cst)
        self.CB = {n: self.cstb[:, i * 128:(i + 1) * 128] for i, n in enumerate(CONST_NAMES)}
        self.vec = K.sb([128, 8 * (2 * L + 1)], F32, "vec")
        K.dma(K.sp, self.vec, self.vec_d)
        self.epst = K.sb([128, 1], F32, "eps")
        K.memset(self.epst, NEPS)
        self.ps = [K.psum([128, 512], F32, "bank") for _ in range(8)]
        self.psi = 0

        self.phase0()
        src, dst = self.hA, self.hB
        for l in range(L):
            if self.mix:
                K.barrier()
                if l % 2 == 0:
                    self.even_phase(l, src, dst)
                elif os.environ.get("NOODD"):
                    src, dst = dst, src
                else:
                    self.odd_phase(l, src, dst)
                src, dst = dst, src
            K.barrier()
            self.mlp_phase(l, src, dst)
            src, dst = dst, src
        K.barrier()
        self.final_phase(src)
        K.barrier()
        return nc

    def bank(self):
        b = self.ps[self.psi % 7]
        self.psi += 1
        return b

    def hview(self, hd, t0, n):
        return hd.a.re("(c p) t -> p c t", p=128)[:, :, t0:t0 + n]

    def phase0(self):
        K = self.K
        es = contextlib.ExitStack()
        xt = [K.sb([128, 2, D], F32, "xt", es) for _ in range(2)]
        ht = [K.sb([128, 8, T], F32, "ht", es) for _ in range(2)]
        for i in range(self.NT // T):
            xs, hs = xt[i % 2], ht[i % 2]
            K.dma(K.sp, xs, self.x.a[i * T:(i + 1) * T, :].re("(s p) f -> p s f", p=128))
            for s in range(2):
                for half in range(2):
                    b = self.bank()
                    for j in range(4):
                        c = half * 4 + j
                        K.mm(b[:, j * 128:(j + 1) * 128], xs[:, s, c * 128:(c + 1) * 128], self.C["ident"])
                    dst = hs[:, half * 4:(half + 1) * 4, s * 128:(s + 1) * 128]
                    K.cp(dst, b.a.re("p (j t) -> p j t", j=4), E=(K.dve if half == 0 else K.act))
            K.dma(K.act, self.hview(self.hA, i * T, T), hs)
        K.barrier()
        es.close()

    def rmsnorm_fm(self, h, gain, hn, sq, rp, n=T):
        K = self.K
        K.actf(sq, h, AF.Square)
        b = self.bank()
        for c in range(8):
            K.mm(b[:, 0:n], self.CB["ones"], _v(sq)[:, c, :], start=(c == 0), stop=(c == 7))
        K.actf(rp, b[:, 0:n], AF.Sqrt, bias=self.epst, scale=1.0 / D)
        K.op(K.dve, lambda: self.nc.vector.reciprocal(_v(rp).ap, _v(rp).ap), [rp], [rp])
        for c in range(8):
            for o in (hn if isinstance(hn, (list, tuple)) else [hn]):
                K.stt(_v(o)[:, c, :], _v(h)[:, c, :], gain[:, c:c + 1], rp, ALU.mult, ALU.mult,
                      E=(K.dve if c % 2 == 0 else K.pool))

    def mlp_phase(self, l, src, dst):
        K = self.K
        P = 128
        gain = self.vec[:, 8 * (self.L + l):8 * (self.L + l + 1)]
        W1 = self.w1.a[l]
        W2 = self.w2.a[l]
        es = contextlib.ExitStack()
        h = K.sb([128, 8, P], F32, "h", es)
        hn32 = K.sb([128, 8, P], F32, "hn32", es)
        sq = K.sb([128, 8, P], BF16, "sq", es)
        rp = K.sb([128, P], F32, "rp", es)
        act32 = K.sb([128, 32, P], F32, "act32", es)
        tmp32 = K.sb([128, 4, P], F32, "tmp32", es)
        stg = [K.sb([128, 4096], F32, "stg", es) for _ in range(2)]
        ns = 0
        for s_ in range(NSEQ):
            t0 = s_ * self.seq
            K.dma(K.sp, h, self.hview(src, t0, P))
            self.rmsnorm_fm(h, gain, hn32, sq, rp, n=P)
            for mb in range(8):
                st = stg[ns % 2].a.re("p (k n) -> p k n", k=8)
                ns += 1
                K.dma(K.sp, st, W1[:, mb * 512:(mb + 1) * 512].re("(k p) n -> p k n", p=128))
                b = self.bank()
                for j in range(4):
                    for k in range(8):
                        K.mm(b[:, j * P:(j + 1) * P], st[:, k, j * 128:(j + 1) * 128], hn32[:, k, :],
                             start=(k == 0), stop=(k == 7))
                K.actf(tmp32, b.a.re("p (j t) -> p j t", j=4), AF.Relu)
                K.tt(act32[:, mb * 4:(mb + 1) * 4, :], tmp32, tmp32, ALU.mult, E=K.pool)
            for n in range(8):
                st = stg[ns % 2].a.re("p (m n) -> p m n", m=32)
                ns += 1
                for q4 in range(4):
                    K.dma(K.sp, st[:, q4 * 8:(q4 + 1) * 8, :],
                          W2[q4 * 1024:(q4 + 1) * 1024, n * 128:(n + 1) * 128].re("(m p) n -> p m n", p=128))
                b = self.bank()
                for m in range(32):
                    K.mm(b[:, 0:P], st[:, m, :], act32[:, m, :], start=(m == 0), stop=(m == 31))
                K.tt(h[:, n, :], h[:, n, :], b[:, 0:P], ALU.add)
            K.dma(K.act, self.hview(dst, t0, P), h)
        K.barrier()
        es.close()
        tiles = []
        for s_ in range(NSEQ):
            if self.seq > P:
                tiles.append((s_ * self.seq + P, min(T, self.seq) - P))
            for j in range(1, self.seq // T):
                tiles.append((s_ * self.seq + j * T, T))
        if not tiles:
            return
        es = contextlib.ExitStack()
        w1 = K.sb([128, 8, 4 * D], BF16, "w1", es)
        w2 = K.sb([128, 32, D], BF16, "w2", es)
        for k in range(8):
            K.dma(K.pool, w1[:, k, :], W1[k * 128:(k + 1) * 128, :])
        for g in range(4):
            K.dma(K.pool, w2[:, g * 8:(g + 1) * 8, :],
                  W2[g * 1024:(g + 1) * 1024, :].re("(m p) n -> p m n", p=128))
        hts = [K.sb([128, 8, T], F32, "h", es) for _ in range(2)]
        hn = K.sb([128, 8, T], BF16, "hn", es)
        sq = K.sb([128, 8, T], BF16, "sq", es)
        rp = K.sb([128, T], F32, "rp", es)
        act = K.sb([128, 32, T], BF16, "act", es)
        tmp = [K.sb([128, 2, T], BF16, "tmp", es) for _ in range(2)]
        K.dma(K.sp, hts[0][:, :, 0:tiles[0][1]], self.hview(src, tiles[0][0], tiles[0][1]))
        for i, (t0, n) in enumerate(tiles):
            h = hts[i % 2]
            if i + 1 < len(tiles):
                t1, n1 = tiles[i + 1]
                K.dma(K.sp, hts[(i + 1) % 2][:, :, 0:n1], self.hview(src, t1, n1))
            self.rmsnorm_fm(h[:, :, 0:n], gain, hn[:, :, 0:n], sq[:, :, 0:n], rp[:, 0:n], n=n)
            for mp in range(16):
                b = self.bank()
                for j in range(2):
                    m = mp * 2 + j
                    for k in range(8):
                        K.mm(b[:, j * T:j * T + n], w1[:, k, m * 128:(m + 1) * 128], hn[:, k, 0:n],
                             start=(k == 0), stop=(k == 7))
                tm = tmp[mp % 2]
                K.actf(tm[:, :, 0:n], b.a.re("p (j t) -> p j t", j=2)[:, :, 0:n], AF.Relu)
                K.tt(act[:, mp * 2:mp * 2 + 2, 0:n], tm[:, :, 0:n], tm[:, :, 0:n], ALU.mult, E=K.pool)
            for npair in range(4):
                b = self.bank()
                for j in range(2):
                    nn = npair * 2 + j
                    for m in range(32):
                        K.mm(b[:, j * T:j * T + n], w2[:, m, nn * 128:(nn + 1) * 128], act[:, m, 0:n],
                             start=(m == 0), stop=(m == 31))
                K.tt(h[:, npair * 2:npair * 2 + 2, 0:n], h[:, npair * 2:npair * 2 + 2, 0:n],
                     b.a.re("p (j t) -> p j t", j=2)[:, :, 0:n], ALU.add)
            K.dma(K.act, self.hview(dst, t0, n), h[:, :, 0:n])
        K.barrier()
        es.close()

    def final_phase(self, src):
        K = self.K
        es = contextlib.ExitStack()
        hts = [K.sb([128, 8, T], F32, "h", es) for _ in range(2)]
        hn = K.sb([128, 8, T], F32, "hn", es)
        sq = K.sb([128, 8, T], BF16, "sq", es)
        rp = K.sb([128, T], F32, "rp", es)
        ot = [K.sb([128, 2, D], F32, "ot", es) for _ in range(2)]
        gain = self.vec[:, 8 * (2 * self.L):8 * (2 * self.L + 1)]
        ntile = self.NT // T
        K.dma(K.sp, hts[0], self.hview(src, 0, T))
        for i in range(ntile):
            h = hts[i % 2]
            if i + 1 < ntile:
                K.dma(K.sp, hts[(i + 1) % 2], self.hview(src, (i + 1) * T, T))
            self.rmsnorm_fm(h, gain, hn, sq, rp)
            o = ot[i % 2]
            for s in range(2):
                for half in range(2):
                    b = self.bank()
                    for j in range(4):
                        c = half * 4 + j
                        K.mm(b[:, j * 128:(j + 1) * 128], hn[:, c, s * 128:(s + 1) * 128], self.C["ident"])
                    K.cp(o[:, s, half * 512:(half + 1) * 512], b, E=(K.dve if half == 0 else K.act))
            K.dma(K.act, self.out.a[i * T:(i + 1) * T, :].re("(s p) f -> p s f", p=128), o)
        K.barrier()
        es.close()


def host_vecs(inp, L):
    cols = []
    for l in range(L):
        cols.append(np.asarray(inp["norm_mix_w"][l], np.float32).reshape(8, 128).T)
    for l in range(L):
        cols.append(np.asarray(inp["norm_mlp_w"][l], np.float32).reshape(8, 128).T)
    cols.append(np.asarray(inp["final_norm_w"], np.float32).reshape(8, 128).T)
    return np.ascontiguousarray(np.concatenate(cols, axis=1))


def make_in_maps(inp, M, x, ncores):
    L = M.L
    consts = host_consts()
    cst = np.ascontiguousarray(np.concatenate([consts[n] for n in CONST_NAMES], axis=1))
    vecs = host_vecs(inp, L)
    shared = {"consts": cst, "vecs": vecs,
              "mlp_w1": np.ascontiguousarray(inp["mlp_w1"][:L]), "mlp_w2": np.ascontiguousarray(inp["mlp_w2"][:L])}
    if M.mix:
        shared.update(host_mixer_inputs(inp, L))
    maps = []
    for c in range(ncores):
        d = dict(shared)
        d["x"] = np.ascontiguousarray(x[2 * c:2 * c + 2]).reshape(M.NT, D)
        maps.append(d)
    return maps


EV_CONV = 0
EV_ALOG = 48
EV_DTB = 52
EV_GDNW = 56
EV_GLAW = 568
EV_W2B = 1080
EV_N = 1336


def host_even_vec(inp, i):
    v = np.zeros((128, EV_N), np.float32)
    cw = np.asarray(inp["gdn_conv_w"][i], np.float32)
    v[:, EV_CONV:EV_CONV + 48] = cw.T.reshape(12, 128, 4).transpose(1, 0, 2).reshape(128, 48)
    v[:, EV_ALOG:EV_ALOG + 4] = np.asarray(inp["gdn_a_log"][i])[None, :]
    v[:, EV_DTB:EV_DTB + 4] = np.asarray(inp["gdn_dt_bias"][i])[None, :]
    v[:, EV_GDNW:EV_GDNW + 512] = np.tile(np.asarray(inp["gdn_norm_w"][i]), 4)[None, :]
    v[:, EV_GLAW:EV_GLAW + 512] = np.tile(np.asarray(inp["gla_norm_w"][i]), 4)[None, :]
    v[0:16, EV_W2B:EV_W2B + 256] = np.asarray(inp["gla_gate_w2"][i])
    v[16, EV_W2B:EV_W2B + 256] = np.asarray(inp["gla_gate_b"][i])
    return v


def _even_phase(self, l, src, dst):
    K = self.K
    nc = self.nc
    i_l = l // 2
    esP = contextlib.ExitStack()
    C, CB = self.C, self.CB
    P = 128
    tps = self.seq // P
    WIN = self.even_w_in.a[i_l]
    WOUT = self.even_w_out.a[i_l]

    ev = K.sb([128, EV_N], F32, "ev", esP)
    K.dma(K.sp, ev, self.evec.a[i_l])
    nexpa = K.sb([128, 4], F32, "nexpa", esP)
    K.actf(nexpa, ev[:, EV_ALOG:EV_ALOG + 4], AF.Exp)
    K.ts(nexpa, nexpa, -1.0, ALU.mult)
    convw = ev[:, EV_CONV:EV_CONV + 48]
    gain = self.vec[:, 8 * l:8 * (l + 1)]
    SAs = [K.sb([128, 4, 128], F32, "SA", esP) for _ in range(NSEQ)]
    SBs = [K.sb([128, 2, 256], F32, "SB", esP) for _ in range(NSEQ)]
    hist = [K.sb([128, 12, 3], F32, "hist", esP) for _ in range(NSEQ)]
    for s_ in range(NSEQ):
        K.memset(SAs[s_], 0.0)
        K.memset(SBs[s_], 0.0)
        K.memset(hist[s_], 0.0)

    def bc4(v):
        return v.re("p (o t) -> p o t", o=1).bc([128, 4, 128])

    def bcl(v):
        return _v(v).re("p (h o) -> p h o", o=1).bc([128, 4, 128])

    def run_variant(hp, tiles):
        if not tiles:
            return
        es = contextlib.ExitStack()
        AD = F32 if hp else BF16
        ID = C["ident"] if hp else CB["ident"]

        def t512(name, dt=F32):
            return K.sb([128, 4, 128], dt, name, es)

        if hp:
            stg = [K.sb([128, 8, 512], F32, "stg", es) for _ in range(2)]
            cnt = [0]

            def _stream(dram2d, col0, n):
                t = stg[cnt[0] % 2]
                cnt[0] += 1
                K.dma(K.sp, t[:, :, 0:n], dram2d[:, col0:col0 + n].re("(k p) n -> p k n", p=128))
                return lambda k: t[:, k, 0:n]

            wblock = lambda col0, n: _stream(WIN, col0, n)
            wblock32 = wblock
            woblock = lambda n0: _stream(WOUT, n0 * 128, 128)
        else:
            w_in = K.sb([128, 8, EVEN_IN], BF16, "w_in", es)
            for k in range(8):
                K.dma(K.pool, w_in[:, k, :], WIN[k * 128:(k + 1) * 128, :])
            w_out = K.sb([128, 8, D], BF16, "w_out", es)
            K.dma(K.pool, w_out, WOUT.re("(c p) n -> p c n", p=128))
            wg = K.sb([128, 8, 24], F32, "wg", es)
            for k in range(8):
                K.dma(K.sp, wg[:, k, 0:8], WIN[k * 128:(k + 1) * 128, 2048:2056])
                K.dma(K.sp, wg[:, k, 8:24], WIN[k * 128:(k + 1) * 128, 3592:3608])
            wblock = lambda col0, n: (lambda k: w_in[:, k, col0:col0 + n])
            wblock32 = lambda col0, n: (lambda k: wg[:, k, 0:8]) if col0 == 2048 else (lambda k: wg[:, k, 8:24])
            woblock = lambda n0: (lambda c: w_out[:, c, n0 * 128:(n0 + 1) * 128])

        h = K.sb([128, 8, P], F32, "h", es)
        hn32 = K.sb([128, 8, P], F32, "hn32", es)
        hn = hn32 if hp else K.sb([128, 8, P], BF16, "hn", es)
        sq = K.sb([128, 8, P], BF16, "sq", es)
        rp = K.sb([128, P], F32, "rp", es)
        pre = K.sb([128, 12, P + 3], F32, "pre", es)
        cacc = K.sb([128, 12, P], F32, "cacc", es)
        qkv = K.sb([128, 12, P], F32, "qkv", es)
        sq2 = K.sb([128, 8, P], BF16, "sq2", es)
        rinv = K.sb([128, 8, P], F32, "rinv", es)
        oFM = K.sb([128, 8, P], AD, "oFM", es)
        g1 = K.sb([32, P], F32, "g1", es)
        K.memset(g1, 1.0)
        loga = K.sb([128, 256], F32, "loga", es)
        gkT = K.sb([128, 256], F32, "gkT", es)
        qkF = K.sb([128, 4, 128], F32, "qkF", es)
        gcf = K.sb([128, 2, 128], F32, "gcf", es)
        nmid = K.sb([128, 2], F32, "nmid", es)
        eq = K.sb([128, 2, 128], F32, "eq", es)
        ek = K.sb([128, 2, 128], F32, "ek", es)
        eq0 = K.sb([128, 2, 128], F32, "eq0", es)
        qgm = K.sb([128, 2, 128], F32, "qgm", es)
        kgm = K.sb([128, 2, 128], F32, "kgm", es)
        qg0 = K.sb([128, 2, 128], AD, "qg0", es)
        edlB = K.sb([128, 256], F32, "edlB", es)
        kdB = K.sb([128, 256], AD, "kdB", es)
        gv = K.sb([128, 512], AD, "gv", es)
        grs = K.sb([128, 512], F32, "grs", es)
        zs = K.sb([128, 512], F32, "zs", es)
        attm = t512("attm", AD)
        SBb = None if hp else K.sb([128, 2, 256], BF16, "SBb", es)
        junk = K.sb([128, 128], F32, "junk", es)
        ssB = K.sb([128, 4], F32, "ssB", es)
        Gt = K.sb([128, 512], F32, "Gt", es)
        obT = K.sb([128, 512], AD, "obT", es)
        gr8 = K.sb([128, 8], F32, "gr8", es)
        beta = K.sb([128, 4], F32, "beta", es)
        nbeta = K.sb([128, 4], F32, "nbeta", es)
        gA = K.sb([128, 4], F32, "gA", es)
        gcc = K.sb([128, 4], F32, "gcc", es)
        eg = K.sb([128, 4], F32, "eg", es)
        egl = K.sb([128, 4], F32, "egl", es)
        edl = K.sb([128, 4], F32, "edl", es)
        beg = K.sb([128, 4], F32, "beg", es)
        Gtri = t512("Gtri")
        Dm = t512("Dm")
        DTm = t512("DTm")
        A = [t512("A0"), t512("A1")]
        AT = [t512("AT0"), t512("AT1")]
        PT = t512("PT")
        aqkT = t512("aqkT")
        vb = t512("vb")
        kbg = t512("kbg")
        kdA = t512("kdA")
        wneg = t512("wneg")
        vn = t512("vn")
        qg = t512("qg")
        ssA = K.sb([128, 4], F32, "ssA", es)
        oaT = K.sb([128, 512], AD, "oaT", es)

        for (s_, it) in tiles:
            SA, SB_ = SAs[s_], SBs[s_]
            K.dma(K.sp, h, self.hview(src, it * P, P))
            K.cp(pre[:, :, 0:3], hist[s_], E=K.pool)
            if not hp:
                K.cp(SBb, SB_, E=K.pool)
            Sst = SB_ if hp else SBb
            self.rmsnorm_fm(h, gain, [hn32] if hp else [hn, hn32], sq, rp, n=P)
            for c4 in range(3):
                wb = wblock(c4 * 512, 512)
                b = self.bank()
                for j in range(4):
                    for k in range(8):
                        K.mm(b[:, j * P:(j + 1) * P], wb(k)[:, j * 128:(j + 1) * 128], hn[:, k, :],
                             start=(k == 0), stop=(k == 7))
                K.cp(pre[:, c4 * 4:(c4 + 1) * 4, 3:3 + P], b.a.re("p (j t) -> p j t", j=4), E=K.act)
            wb = wblock(1536, 512)
            bz = self.bank()
            for k in range(8):
                K.mm(bz, hn[:, k, :], wb(k), start=(k == 0), stop=(k == 7))
            K.actf(zs, bz, AF.Silu)
            wb = wblock(2312, 256)
            bgk = self.bank()
            for k in range(8):
                K.mm(bgk[:, 0:256], hn[:, k, :], wb(k), start=(k == 0), stop=(k == 7))
            K.cp(gkT, bgk[:, 0:256], E=K.act)
            wb = wblock(2568, 512)
            bgv = self.bank()
            for k in range(8):
                K.mm(bgv, hn[:, k, :], wb(k), start=(k == 0), stop=(k == 7))
            K.cp(gv, bgv, E=K.act)
            wb = wblock(3080, 512)
            bgr = self.bank()
            for k in range(8):
                K.mm(bgr, hn[:, k, :], wb(k), start=(k == 0), stop=(k == 7))
            K.actf(grs, bgr, AF.Silu)
            wb = wblock32(2048, 8)
            bg = self.bank()
            for k in range(8):
                K.mm(bg[:, 0:8], hn32[:, k, :], wb(k), start=(k == 0), stop=(k == 7))
            K.cp(gr8, bg[:, 0:8])
            wb = wblock32(3592, 16)
            bl = self.bank()
            for k in range(8):
                K.mm(bl[0:16, 0:P], wb(k), hn32[:, k, :], start=(k == 0), stop=(k == 7))
            K.cp(g1[0:16, :], bl[0:16, 0:P])
            wb = wblock(2056, 512)
            bqk = self.bank()
            for j in range(4):
                for k in range(8):
                    K.mm(bqk[:, j * P:(j + 1) * P], wb(k)[:, j * 128:(j + 1) * 128], hn[:, k, :],
                         start=(k == 0), stop=(k == 7))
            K.cp(qkF, bqk.a.re("p (j t) -> p j t", j=4))
            bqk3 = qkF
            for c in range(12):
                K.actf(cacc[:, c, :], pre[:, c, 0:P], AF.Copy, scale=convw[:, c * 4:c * 4 + 1])
                for j in range(1, 4):
                    K.stt(cacc[:, c, :], pre[:, c, j:j + P], convw[:, c * 4 + j:c * 4 + j + 1], cacc[:, c, :],
                          ALU.mult, ALU.add)
            K.cp(hist[s_], pre[:, :, P:P + 3], E=K.pool)
            K.actf(qkv, cacc, AF.Silu)
            K.tt(sq2, qkv[:, 0:8, :], qkv[:, 0:8, :], ALU.mult, E=K.pool)
            for half in range(2):
                b = self.bank()
                K.mm(b, CB["ones"], sq2[:, half * 4:(half + 1) * 4, :])
                K.actf(rinv[:, half * 4:(half + 1) * 4, :], b.a.re("p (j t) -> p j t", j=4), AF.Sqrt, bias=self.epst)
            K.op(K.dve, lambda: nc.vector.reciprocal(rinv.a.ap, rinv.a.ap), [rinv], [rinv])
            K.stt(qkv[:, 0:4, :], qkv[:, 0:4, :], float(128 ** -0.5), rinv[:, 0:4, :], ALU.mult, ALU.mult)
            K.tt(qkv[:, 4:8, :], qkv[:, 4:8, :], rinv[:, 4:8, :], ALU.mult, E=K.pool)

            bx = self.bank()
            K.mm(bx[:, 0:256], g1, ev[0:32, EV_W2B:EV_W2B + 256])
            K.ts(loga, bx[:, 0:256], -20.0, ALU.max)
            K.actf(loga, loga, AF.Exp, scale=-1.0)
            K.actf(loga, loga, AF.Ln, bias=1.0)
            K.ts(loga, loga, -1.0 / 16.0, ALU.mult, -1.0, ALU.max)
            bgc = self.bank()
            for cc in range(2):
                K.mm(bgc[:, cc * 128:(cc + 1) * 128], loga[:, cc * 128:(cc + 1) * 128], C["tri"])
            K.mm(bgc[:, 256:512], C["ustr"], loga)
            K.cp(gcf, bgc[:, 0:256].re("p (c t) -> p c t", c=2))
            K.actf(edlB, bgc[:, 256:512], AF.Exp)
            K.tt(kdB, gkT, edlB, ALU.mult)
            K.ts(nmid, gcf[:, :, 63], -1.0, ALU.mult)
            for cc in range(2):
                K.actf(eq[:, cc, :], gcf[:, cc, :], AF.Exp, bias=nmid[:, cc:cc + 1])
                K.actf(ek[:, cc, :], gcf[:, cc, :], AF.Exp, bias=gcf[:, cc, 63:64], scale=-1.0)
            K.actf(eq0, gcf, AF.Exp)
            K.stt(qgm, bqk3[:, 0:2, :], 0.125, eq, ALU.mult, ALU.mult)
            K.tt(kgm, bqk3[:, 2:4, :], ek, ALU.mult)
            K.stt(qg0, bqk3[:, 0:2, :], 0.125, eq0, ALU.mult, ALU.mult)
            bat = self.bank()
            for hh in range(4):
                pr = slice((hh % 2) * 64, (hh % 2) * 64 + 64)
                K.mm(bat[:, hh * 128:(hh + 1) * 128], kgm[pr, hh // 2, :], qgm[pr, hh // 2, :], ser=True)
            K.tt(attm, bat.a.re("p (h t) -> p h t", h=4), bc4(C["tri"]), ALU.mult)
            bo = self.bank()
            for hh in range(4):
                pr = slice((hh % 2) * 64, (hh % 2) * 64 + 64)
                K.mm(bo[:, hh * 128:(hh + 1) * 128], qg0[pr, hh // 2, :],
                     Sst[pr, hh // 2, (hh % 2) * 128:(hh % 2) * 128 + 128], start=True, stop=False, ser=True)
                K.mm(bo[:, hh * 128:(hh + 1) * 128], attm[:, hh, :], gv[:, hh * 128:(hh + 1) * 128],
                     start=False, stop=True, ser=True)
            bs = self.bank()
            for pp in range(2):
                K.mm(bs[:, pp * 256:(pp + 1) * 256], kdB[:, pp * 128:(pp + 1) * 128], gv[:, pp * 256:(pp + 1) * 256])
            for pp in range(2):
                K.stt(SB_[:, pp, :], SB_[:, pp, :], eq0[:, pp, 127:128], bs[:, pp * 256:(pp + 1) * 256],
                      ALU.mult, ALU.add)
            for hh in range(4):
                K.actf(junk, bo[:, hh * 128:(hh + 1) * 128], AF.Square, accum=ssB[:, hh:hh + 1])
            K.actf(ssB, ssB, AF.Sqrt, bias=self.epst, scale=1.0 / 128.0)
            K.op(K.dve, lambda: nc.vector.reciprocal(ssB.a.ap, ssB.a.ap), [ssB], [ssB])
            K.tt(Gt, grs, ev[:, EV_GLAW:EV_GLAW + 512], ALU.mult, E=K.pool)
            for hh in range(4):
                sl = slice(hh * 128, (hh + 1) * 128)
                K.stt(obT[:, sl], bo[:, sl], ssB[:, hh:hh + 1], Gt[:, sl], ALU.mult, ALU.mult)
            btr = self.bank()
            for hh in range(4):
                K.mm(btr[:, hh * 128:(hh + 1) * 128], obT[:, hh * 128:(hh + 1) * 128], ID)
            K.cp(oFM[:, 4:8, :], btr.a.re("p (h t) -> p h t", h=4), E=K.act)

            K.actf(beta, gr8[:, 0:4], AF.Sigmoid)
            K.ts(nbeta, beta, -1.0, ALU.mult)
            K.tt(gA, gr8[:, 4:8], ev[:, EV_DTB:EV_DTB + 4], ALU.add)
            K.actf(gA, gA, AF.Exp)
            K.actf(gA, gA, AF.Ln, bias=1.0)
            K.tt(gA, gA, nexpa, ALU.mult)
            K.tt(Gtri, bc4(C["tri"]), bcl(gA), ALU.mult)
            bsm = self.bank()
            K.mm(bsm[:, 0:4], C["tri"], gA)
            K.mm(bsm[:, 8:12], C["ones"], gA)
            br = self.bank()
            K.mm(br, C["ones"], Gtri)
            K.cp(gcc, bsm[:, 0:4])
            K.actf(eg, gcc, AF.Exp)
            K.actf(egl, bsm[:, 8:12], AF.Exp)
            K.tt(edl, bsm[:, 8:12], gcc, ALU.subtract)
            K.actf(edl, edl, AF.Exp)
            K.tt(beg, beta, eg, ALU.mult)
            for hh in range(4):
                sl = slice(hh * 128, (hh + 1) * 128)
                K.ts(Dm[:, hh, :], br[:, sl], gcc[:, hh:hh + 1], ALU.subtract, 0.0, ALU.max)
                K.ts(DTm[:, hh, :], br[:, sl], gcc[:, hh:hh + 1], ALU.subtract, 0.0, ALU.min)
            K.actf(Dm, Dm, AF.Exp, scale=-1.0)
            K.actf(DTm, DTm, AF.Exp)
            K.actf(qg, br.a.re("p (h t) -> p h t", h=4), AF.Exp)
            K.tt(Dm, Dm, bc4(C["ustr"]), ALU.mult, E=K.pool)
            K.tt(DTm, DTm, bc4(C["tri"]), ALU.mult, E=K.pool)
            K.tt(qg, qg, qkv[:, 0:4, :], ALU.mult, E=K.pool)
            bkk = self.bank()
            bqt = self.bank()
            for hh in range(4):
                sl = slice(hh * 128, (hh + 1) * 128)
                K.mm(bkk[:, sl], qkv[:, 4 + hh, :], qkv[:, 4 + hh, :])
                K.mm(bqt[:, sl], qkv[:, 4 + hh, :], qkv[:, hh, :])
            for hh in range(4):
                K.stt(A[0][:, hh, :], bkk[:, hh * 128:(hh + 1) * 128], nbeta[:, hh:hh + 1], Dm[:, hh, :],
                      ALU.mult, ALU.mult)
            K.tt(aqkT, bqt.a.re("p (h t) -> p h t", h=4), DTm, ALU.mult)
            bnt = self.bank()
            for hh in range(4):
                K.mm(bnt[:, hh * 128:(hh + 1) * 128], A[0][:, hh, :], C["ident"])
            K.cp(AT[0], bnt.a.re("p (h t) -> p h t", h=4), E=K.act)
            K.tt(PT, bnt.a.re("p (h t) -> p h t", h=4), bc4(C["ident"]), ALU.add)
            bkt = self.bank()
            bvt = self.bank()
            for hh in range(4):
                sl = slice(hh * 128, (hh + 1) * 128)
                K.mm(bkt[:, sl], qkv[:, 4 + hh, :], C["ident"])
                K.mm(bvt[:, sl], qkv[:, 8 + hh, :], C["ident"])
            K.tt(vb, bvt.a.re("p (h t) -> p h t", h=4), bcl(beta), ALU.mult)
            K.tt(kbg, bkt.a.re("p (h t) -> p h t", h=4), bcl(beg), ALU.mult)
            K.tt(kdA, bkt.a.re("p (h t) -> p h t", h=4), bcl(edl), ALU.mult)
            cur = 0
            for itr in range(6):
                nxt = 1 - cur
                b1 = self.bank()
                b2 = self.bank()
                for hh in range(4):
                    sl = slice(hh * 128, (hh + 1) * 128)
                    K.mm(b1[:, sl], AT[cur][:, hh, :], A[cur][:, hh, :])
                    K.mm(b2[:, sl], A[cur][:, hh, :], AT[cur][:, hh, :])
                K.cp(A[nxt], b1.a.re("p (h t) -> p h t", h=4), E=K.act)
                K.cp(AT[nxt], b2.a.re("p (h t) -> p h t", h=4), E=K.dve)
                b3 = self.bank()
                for hh in range(4):
                    K.mm(b3[:, hh * 128:(hh + 1) * 128], A[nxt][:, hh, :], PT[:, hh, :])
                K.tt(PT, PT, b3.a.re("p (h t) -> p h t", h=4), ALU.add)
                cur = nxt
            bw = self.bank()
            for hh in range(4):
                K.mm(bw[:, hh * 128:(hh + 1) * 128], kbg[:, hh, :], PT[:, hh, :])
            K.actf(wneg, bw.a.re("p (h t) -> p h t", h=4), AF.Copy, scale=-1.0)
            bvn = self.bank()
            for hh in range(4):
                sl = slice(hh * 128, (hh + 1) * 128)
                K.mm(bvn[:, sl], PT[:, hh, :], vb[:, hh, :], start=True, stop=False)
                K.mm(bvn[:, sl], wneg[:, hh, :], SA[:, hh, :], start=False, stop=True)
            K.cp(vn, bvn.a.re("p (h t) -> p h t", h=4))
            boa = self.bank()
            for hh in range(4):
                sl = slice(hh * 128, (hh + 1) * 128)
                K.mm(boa[:, sl], qg[:, hh, :], SA[:, hh, :], start=True, stop=False)
                K.mm(boa[:, sl], aqkT[:, hh, :], vn[:, hh, :], start=False, stop=True)
            bsa = self.bank()
            for hh in range(4):
                K.mm(bsa[:, hh * 128:(hh + 1) * 128], kdA[:, hh, :], vn[:, hh, :])
            for hh in range(4):
                K.stt(SA[:, hh, :], SA[:, hh, :], egl[:, hh:hh + 1], bsa[:, hh * 128:(hh + 1) * 128],
                      ALU.mult, ALU.add)
            for hh in range(4):
                K.actf(junk, boa[:, hh * 128:(hh + 1) * 128], AF.Square, accum=ssA[:, hh:hh + 1])
            K.actf(ssA, ssA, AF.Sqrt, bias=self.epst, scale=1.0 / 128.0)
            K.op(K.dve, lambda: nc.vector.reciprocal(ssA.a.ap, ssA.a.ap), [ssA], [ssA])
            K.tt(Gt, zs, ev[:, EV_GDNW:EV_GDNW + 512], ALU.mult, E=K.pool)
            for hh in range(4):
                sl = slice(hh * 128, (hh + 1) * 128)
                K.stt(oaT[:, sl], boa[:, sl], ssA[:, hh:hh + 1], Gt[:, sl], ALU.mult, ALU.mult)
            btr = self.bank()
            for hh in range(4):
                K.mm(btr[:, hh * 128:(hh + 1) * 128], oaT[:, hh * 128:(hh + 1) * 128], ID)
            K.cp(oFM[:, 0:4, :], btr.a.re("p (h t) -> p h t", h=4), E=K.act)

            for half in range(2):
                b = self.bank()
                for j in range(4):
                    wo = woblock(half * 4 + j)
                    for c in range(8):
                        K.mm(b[:, j * P:(j + 1) * P], wo(c), oFM[:, c, :], start=(c == 0), stop=(c == 7))
                K.tt(h[:, half * 4:(half + 1) * 4, :], h[:, half * 4:(half + 1) * 4, :],
                     b.a.re("p (j t) -> p j t", j=4), ALU.add)
            K.dma(K.act, self.hview(dst, it * P, P), h)
        K.barrier()
        es.close()

    run_variant(True, [(s_, s_ * tps) for s_ in range(NSEQ)])
    run_variant(False, [(s_, s_ * tps + j) for s_ in range(NSEQ) for j in range(1, tps)])
    K.barrier()
    esP.close()


Model.even_phase = _even_phase


def _declare_mixer_inputs(self, ne, no):
    K = self.K
    self.even_w_in = K.dram("even_w_in", [ne, D, EVEN_IN], F32, kind="ExternalInput")
    self.even_w_out = K.dram("even_w_out", [ne, D, D], F32, kind="ExternalInput")
    self.evec = K.dram("evec", [ne, 128, EV_N], F32, kind="ExternalInput")
    if no > 0:
        self.declare_odd_inputs(no)


Model.declare_mixer_inputs = _declare_mixer_inputs


def host_mixer_inputs(inp, L):
    ne = (L + 1) // 2
    no = L // 2
    d = {"even_w_in": np.ascontiguousarray(inp["even_w_in"][:ne]),
         "even_w_out": np.ascontiguousarray(inp["even_w_out"][:ne]),
         "evec": np.stack([host_even_vec(inp, i) for i in range(ne)])}
    if no > 0:
        d.update(host_odd_inputs(inp, no))
    return d


OV_CONV = 0
OV_CONVB = 32
OV_DTB = 40
OV_ALOG = 48
OV_D = 56
OV_SNW = 568
OV_MU = 1080
OV_W0 = 1094
OV_A0 = 1606
OV_KK = 1610
OV_KA = 1614
OV_RK = 1618
OV_GNW = 1622
OV_GNB = 2134
OV_SEL = 2646
OV_N = 2678


def host_odd_vec(inp, i):
    v = np.zeros((128, OV_N), np.float32)
    f = lambda a, n: np.asarray(a, np.float32).reshape(n, 128).T
    cw = np.asarray(inp["ssd_conv_w"][i], np.float32)
    v[:, OV_CONV:OV_CONV + 32] = cw.T.reshape(8, 128, 4).transpose(1, 0, 2).reshape(128, 32)
    v[:, OV_CONVB:OV_CONVB + 8] = f(inp["ssd_conv_b"][i], 8)
    v[:, OV_DTB:OV_DTB + 8] = np.asarray(inp["ssd_dt_bias"][i])[None, :]
    v[:, OV_ALOG:OV_ALOG + 8] = np.asarray(inp["ssd_a_log"][i])[None, :]
    v[:, OV_D:OV_D + 512] = np.repeat(np.asarray(inp["ssd_d"][i]), 64)[None, :]
    v[:, OV_SNW:OV_SNW + 512] = np.asarray(inp["ssd_norm_w"][i])[None, :]
    v[:, OV_MU:OV_MU + 14] = f(inp["rwkv_mu"][i], 14)
    v[:, OV_W0:OV_W0 + 512] = np.asarray(inp["rwkv_w0"][i])[None, :]
    v[:, OV_A0:OV_A0 + 4] = f(inp["rwkv_a0"][i], 4)
    v[:, OV_KK:OV_KK + 4] = f(inp["rwkv_k_k"][i], 4)
    v[:, OV_KA:OV_KA + 4] = f(inp["rwkv_k_a"][i], 4)
    v[:, OV_RK:OV_RK + 4] = f(np.asarray(inp["rwkv_r_k"][i]).reshape(-1), 4)
    v[:, OV_GNW:OV_GNW + 512] = np.asarray(inp["rwkv_gn_w"][i])[None, :]
    v[:, OV_GNB:OV_GNB + 512] = np.asarray(inp["rwkv_gn_b"][i])[None, :]
    p = np.arange(128)
    sel = np.zeros((128, 4, 8), np.float32)
    for c in range(4):
        sel[p, c, 2 * c + p // 64] = 1.0
    v[:, OV_SEL:OV_SEL + 32] = sel.reshape(128, 32)
    return v


def host_odd_inputs(inp, no):
    a2 = np.zeros((no, 128, 512), np.float32)
    a2[:, 64:128, :] = np.asarray(inp["rwkv_a2"][:no])
    w2 = np.zeros((no, 128, 512), np.float32)
    w2[:, 0:64, :] = np.asarray(inp["rwkv_w2"][:no])
    return {"odd_w_in": np.ascontiguousarray(inp["odd_w_in"][:no]),
            "odd_w_out": np.ascontiguousarray(inp["odd_w_out"][:no]),
            "ovec": np.stack([host_odd_vec(inp, i) for i in range(no)]),
            "rw2": w2, "ra2": a2, "rg2": np.ascontiguousarray(inp["rwkv_g2"][:no])}


def _declare_odd_inputs(self, no):
    K = self.K
    self.odd_w_in = K.dram("odd_w_in", [no, D, ODD_IN], F32, kind="ExternalInput")
    self.odd_w_out = K.dram("odd_w_out", [no, D, D], F32, kind="ExternalInput")
    self.ovec = K.dram("ovec", [no, 128, OV_N], F32, kind="ExternalInput")
    self.rw2 = K.dram("rw2", [no, 128, 512], F32, kind="ExternalInput")
    self.ra2 = K.dram("ra2", [no, 128, 512], F32, kind="ExternalInput")
    self.rg2 = K.dram("rg2", [no, 128, 512], F32, kind="ExternalInput")


Model.declare_odd_inputs = _declare_odd_inputs


def _odd_phase(self, l, src, dst):
    K = self.K
    nc = self.nc
    i_l = l // 2
    esP = contextlib.ExitStack()
    C, CB = self.C, self.CB
    P = 128
    tps = self.seq // P
    WIN = self.odd_w_in.a[i_l]
    WOUT = self.odd_w_out.a[i_l]

    def bc4(v, n=4):
        return v.re("p (o t) -> p o t", o=1).bc([128, n, 128])

    def bcl(v, n, m):
        return _v(v).re("p (h o) -> p h o", o=1).bc([128, n, m])

    def recip(t):
        K.op(K.dve, lambda: nc.vector.reciprocal(_v(t).ap, _v(t).ap), [t], [t])

    ov = K.sb([128, OV_N], F32, "ov", esP)
    K.dma(K.sp, ov, self.ovec.a[i_l])
    w2t = K.sb([128, 512], F32, "w2t", esP)
    a2t = K.sb([128, 512], F32, "a2t", esP)
    g2t = K.sb([128, 512], F32, "g2t", esP)
    K.dma(K.sp, w2t, self.rw2.a[i_l])
    K.dma(K.sp, a2t, self.ra2.a[i_l])
    K.dma(K.sp, g2t, self.rg2.a[i_l])
    nexpa = K.sb([128, 8], F32, "nexpa", esP)
    K.actf(nexpa, ov[:, OV_ALOG:OV_ALOG + 8], AF.Exp)
    K.ts(nexpa, nexpa, -1.0, ALU.mult)
    gneps = K.sb([128, 1], F32, "gneps", esP)
    K.memset(gneps, 64e-5)
    convw = ov[:, OV_CONV:OV_CONV + 32]
    gain = self.vec[:, 8 * l:8 * (l + 1)]
    STs = [K.sb([128, 512], F32, "ST", esP) for _ in range(NSEQ)]
    Hss = [K.sb([128, 4, 128], F32, "Hs", esP) for _ in range(NSEQ)]
    histC = [K.sb([128, 8, 3], F32, "histC", esP) for _ in range(NSEQ)]
    histP = [K.sb([128, 14, 1], F32, "histP", esP) for _ in range(NSEQ)]
    for s_ in range(NSEQ):
        K.memset(STs[s_], 0.0)
        K.memset(Hss[s_], 0.0)
        K.memset(histC[s_], 0.0)
        K.memset(histP[s_], 0.0)

    def run_variant(hp, tiles):
        if not tiles:
            return
        es = contextlib.ExitStack()
        AD = F32 if hp else BF16
        ID = C["ident"] if hp else CB["ident"]
        if hp:
            stg = [K.sb([128, 8, 512], F32, "stg", es) for _ in range(2)]
            cnt = [0]

            def _stream(dram2d, col0, n):
                t = stg[cnt[0] % 2]
                cnt[0] += 1
                K.dma(K.sp, t[:, :, 0:n], dram2d[:, col0:col0 + n].re("(k p) n -> p k n", p=128))
                return lambda k: t[:, k, 0:n]

            wblock = lambda col0, n: _stream(WIN, col0, n)
            wblock32 = wblock
            woblock = lambda n0: _stream(WOUT, n0 * 128, 128)
        else:
            w_in = K.sb([128, 8, ODD_IN], BF16, "w_in", es)
            for k in range(8):
                K.dma(K.pool, w_in[:, k, :], WIN[k * 128:(k + 1) * 128, :])
            w_out = K.sb([128, 8, D], BF16, "w_out", es)
            K.dma(K.pool, w_out, WOUT.re("(c p) n -> p c n", p=128))
            wg = K.sb([128, 8, 136], F32, "wg", es)
            for k in range(8):
                K.dma(K.sp, wg[:, k, 0:8], WIN[k * 128:(k + 1) * 128, 1536:1544])
                K.dma(K.sp, wg[:, k, 8:136], WIN[k * 128:(k + 1) * 128, 3080:3208])
            wblock = lambda col0, n: (lambda k: w_in[:, k, col0:col0 + n])
            wblock32 = lambda col0, n: (lambda k: wg[:, k, 0:8])
            woblock = lambda n0: (lambda c: w_out[:, c, n0 * 128:(n0 + 1) * 128])

        def sb(shape, name, dt=F32):
            return K.sb(shape, dt, name, es)

        h = sb([128, 8, P], "h"); hn32 = sb([128, 8, P], "hn32"); hn = hn32 if hp else sb([128, 8, P], "hn", BF16)
        sq = sb([128, 8, P], "sq", BF16); rp = sb([128, P], "rp")
        preC = sb([128, 8, P + 3], "preC"); cacc = sb([128, 8, P], "cacc"); xbc = sb([128, 8, P], "xbc")
        pdp = sb([128, 14, P + 1], "pdp"); pdm = sb([128, 14, P], "pdm"); pdd = pdm
        zs = sb([128, 512], "zs"); oFM = sb([128, 8, P], "oFM", AD)
        dt8 = sb([128, 8], "dt8"); a8 = sb([128, 8], "a8"); acc8 = sb([128, 8], "acc8")
        eac8 = sb([128, 8], "eac8"); edl8 = sb([128, 8], "edl8"); cd8 = sb([128, 8], "cd8"); dte8 = sb([128, 8], "dte8")
        segT = sb([128, 8, 128], "segT"); MT = sb([128, 8, 128], "MT", AD)
        xTs = sb([128, 512], "xTs"); xdt = sb([128, 8, 64], "xdt", AD); xdte = sb([128, 8, 64], "xdte", AD)
        Bb = sb([128, 256], "Bb", AD); Cb = sb([128, 2, 128], "Cb", AD)
        STb = None if hp else sb([128, 512], "STb", BF16)
        y1 = sb([128, 512], "y1"); y2 = sb([128, 512], "y2"); ss2 = sb([128, 2], "ss2"); junk = sb([128, 256], "junk")
        ycT = sb([128, 512], "ycT", AD)
        txw = sb([64, P], "txw"); ldT = sb([128, 512], "ldT"); aF = sb([128, 4, P], "aF"); sg = sb([128, P], "sg")
        kp = sb([128, 4, P], "kp"); kq = sb([128, 4, P], "kq"); kk = kq; bm = sb([128, 4, P], "bm"); gl4 = sb([128, 4], "gl4")
        sqk = sb([128, 4, P], "sqk"); lg = sb([128, 4, P], "lg"); lgx = sb([128, 4, P], "lgx"); nmid = sb([128, 4], "nmid")
        e_r = sb([128, 4, P], "e_r"); e_a = sb([128, 4, P], "e_a"); e_b = sb([128, 4, P], "e_b")
        e_r0 = sb([128, 4, P], "e_r0"); e_a0 = sb([128, 4, P], "e_a0")
        rt_ = e_r; r0_ = e_r0; at_ = e_a; a0_ = e_a0; bt_ = e_b; kt_ = xbc[:, 4:8, :]
        edlT = sb([128, 512], "edlT"); kdT = sb([128, 512], "kdT"); bdT = sb([128, 512], "bdT"); vT = sb([128, 512], "vT")
        A = [segT[:, 0:4, :], cacc[:, 0:4, :]]
        AT = [segT[:, 4:8, :], cacc[:, 4:8, :]]
        PT = xbc[:, 0:4, :]
        AakT = y1.a.re("p (h t) -> p h t", h=4); RbT = y2.a.re("p (h t) -> p h t", h=4); RkT = xTs.a.re("p (h t) -> p h t", h=4)
        Xs = sb([128, 4, 64], "Xs"); Us = sb([128, 8, 64], "Us")
        yd = lg.a.re("p c t -> p (c t)").re("p (h q) -> p h q", h=8); ym = lgx.a.re("p c t -> p (c t)").re("p (h q) -> p h q", h=8); mean8 = sb([128, 8], "mean8"); var8 = sb([128, 8], "var8")
        rkr = sb([128, 4, P], "rkr"); rks = sb([128, 8], "rks"); gT = sb([128, 512], "gT"); ydT = sb([128, 512], "ydT", AD)

        for (s_, it) in tiles:
            ST, Hs = STs[s_], Hss[s_]
            K.dma(K.sp, h, self.hview(src, it * P, P))
            K.cp(preC[:, :, 0:3], histC[s_], E=K.pool)
            K.cp(pdp[:, :, 0:1], histP[s_], E=K.pool)
            if not hp:
                K.cp(STb, ST, E=K.pool)
            Sst = ST if hp else STb
            self.rmsnorm_fm(h, gain, [hn32] if hp else [hn, hn32], sq, rp, n=P)
            for c4 in range(2):
                wb = wblock(512 + c4 * 512, 512)
                b = self.bank()
                for j in range(4):
                    for k in range(8):
                        K.mm(b[:, j * P:(j + 1) * P], wb(k)[:, j * 128:(j + 1) * 128], hn[:, k, :],
                             start=(k == 0), stop=(k == 7))
                K.cp(preC[:, c4 * 4:(c4 + 1) * 4, 3:3 + P], b.a.re("p (j t) -> p j t", j=4), E=K.act)
            for c4 in range(4):
                b = self.bank()
                nj = 4 if c4 < 3 else 2
                wb = wblock(1544 + c4 * 512, nj * 128)
                for j in range(nj):
                    c = c4 * 4 + j
                    for k in range(8):
                        if c == 12 and not hp:
                            K.mm(b[:, j * P:(j + 1) * P], wg[:, k, 8:136], hn32[:, k, :], start=(k == 0), stop=(k == 7))
                        else:
                            K.mm(b[:, j * P:(j + 1) * P], wb(k)[:, j * 128:(j + 1) * 128], hn[:, k, :],
                                 start=(k == 0), stop=(k == 7))
                K.cp(pdp[:, c4 * 4:c4 * 4 + nj, 1:1 + P], b[:, 0:nj * P].re("p (j t) -> p j t", j=nj), E=K.act)
            wb = wblock(0, 512)
            bz = self.bank()
            for k in range(8):
                K.mm(bz, hn[:, k, :], wb(k), start=(k == 0), stop=(k == 7))
            K.actf(zs, bz, AF.Silu)
            wb = wblock32(1536, 8)
            bg = self.bank()
            for k in range(8):
                K.mm(bg[:, 0:8], hn32[:, k, :], wb(k), start=(k == 0), stop=(k == 7))
            K.tt(dt8, bg[:, 0:8], ov[:, OV_DTB:OV_DTB + 8], ALU.add)
            K.ts(dt8, dt8, 60.0, ALU.min)
            K.actf(dt8, dt8, AF.Exp)
            K.actf(dt8, dt8, AF.Ln, bias=1.0)
            K.tt(a8, dt8, nexpa, ALU.mult)
            for c in range(8):
                K.actf(cacc[:, c, :], preC[:, c, 0:P], AF.Copy, scale=convw[:, c * 4:c * 4 + 1])
                for j in range(1, 4):
                    K.stt(cacc[:, c, :], preC[:, c, j:j + P], convw[:, c * 4 + j:c * 4 + j + 1], cacc[:, c, :],
                          ALU.mult, ALU.add)
                K.actf(xbc[:, c, :], cacc[:, c, :], AF.Silu, bias=ov[:, OV_CONVB + c:OV_CONVB + c + 1])
            K.cp(histC[s_], preC[:, :, P:P + 3], E=K.pool)
            Atri = cacc
            K.tt(Atri, bc4(C["tri"], 8), bcl(a8, 8, 128), ALU.mult)
            bsm = self.bank()
            K.mm(bsm[:, 0:8], C["tri"], a8)
            K.mm(bsm[:, 8:16], C["ones"], a8)
            K.cp(acc8, bsm[:, 0:8])
            K.actf(eac8, acc8, AF.Exp)
            K.actf(cd8, bsm[:, 8:16], AF.Exp)
            K.tt(edl8, bsm[:, 8:16], acc8, ALU.subtract)
            K.actf(edl8, edl8, AF.Exp)
            K.tt(dte8, dt8, edl8, ALU.mult)
            for hg in range(2):
                br = self.bank()
                K.mm(br, C["ones"], Atri[:, hg * 4:(hg + 1) * 4, :])
                for hh in range(4):
                    hd = hg * 4 + hh
                    K.ts(segT[:, hd, :], br[:, hh * 128:(hh + 1) * 128], acc8[:, hd:hd + 1], ALU.subtract, 0.0, ALU.min)
            K.actf(segT, segT, AF.Exp)
            K.tt(segT, segT, bc4(C["tri"], 8), ALU.mult, E=K.pool)
            bcb = self.bank()
            for g in range(2):
                K.mm(bcb[:, g * 128:(g + 1) * 128], xbc[:, 4 + g, :], xbc[:, 6 + g, :])
            for g in range(2):
                K.tt(MT[:, g * 4:(g + 1) * 4, :], segT[:, g * 4:(g + 1) * 4, :],
                     bcb[:, g * 128:(g + 1) * 128].re("p (o t) -> p o t", o=1).bc([128, 4, 128]), ALU.mult)
            K.cp(Cb, xbc[:, 6:8, :], E=K.pool)
            bxt = self.bank()
            for c in range(4):
                K.mm(bxt[:, c * 128:(c + 1) * 128], xbc[:, c, :], C["ident"])
            bbt = self.bank()
            for g in range(2):
                K.mm(bbt[:, g * 128:(g + 1) * 128], xbc[:, 4 + g, :], C["ident"])
            K.cp(xTs, bxt, E=K.act)
            K.cp(Bb, bbt[:, 0:256], E=K.act)
            K.tt(xdt, xTs.a.re("p (h q) -> p h q", h=8), bcl(dt8, 8, 64), ALU.mult)
            K.tt(xdte, xTs.a.re("p (h q) -> p h q", h=8), bcl(dte8, 8, 64), ALU.mult)
            by = self.bank()
            for hd in range(8):
                K.mm(by[:, hd * 64:(hd + 1) * 64], MT[:, hd, :], xdt[:, hd, :])
            bi = self.bank()
            for g in range(2):
                K.mm(bi[:, g * 256:(g + 1) * 256], Cb[:, g, :], Sst[:, g * 256:(g + 1) * 256])
            bst = self.bank()
            for g in range(2):
                K.mm(bst[:, g * 256:(g + 1) * 256], Bb[:, g * 128:(g + 1) * 128],
                     xdte[:, g * 4:(g + 1) * 4, :].re("p h q -> p (h q)"))
            K.tt(y1.a.re("p (h q) -> p h q", h=8), bi.a.re("p (h q) -> p h q", h=8), bcl(eac8, 8, 64), ALU.mult)
            K.tt(y1, y1, by, ALU.add)
            K.tt(y2, xTs, ov[:, OV_D:OV_D + 512], ALU.mult, E=K.pool)
            K.tt(y1, y1, y2, ALU.add)
            K.tt(y1, y1, zs, ALU.mult)
            K.tt(ST.a.re("p (h q) -> p h q", h=8), ST.a.re("p (h q) -> p h q", h=8), bcl(cd8, 8, 64), ALU.mult)
            K.tt(ST, ST, bst, ALU.add)
            for g in range(2):
                K.actf(junk, y1[:, g * 256:(g + 1) * 256], AF.Square, accum=ss2[:, g:g + 1])
            K.actf(ss2, ss2, AF.Sqrt, bias=self.epst, scale=1.0 / 256.0)
            recip(ss2)
            K.tt(y2, y1, ov[:, OV_SNW:OV_SNW + 512], ALU.mult, E=K.pool)
            for g in range(2):
                K.ts(ycT[:, g * 256:(g + 1) * 256], y2[:, g * 256:(g + 1) * 256], ss2[:, g:g + 1], ALU.mult)
            btr = self.bank()
            for hh in range(4):
                K.mm(btr[:, hh * 128:(hh + 1) * 128], ycT[:, hh * 128:(hh + 1) * 128], ID)
            K.cp(oFM[:, 0:4, :], btr.a.re("p (h t) -> p h t", h=4), E=K.act)

            K.tt(pdd, pdp[:, :, 0:P], pdp[:, :, 1:P + 1], ALU.subtract)
            K.tt(pdd, pdd, bcl(ov[:, OV_MU:OV_MU + 14], 14, P), ALU.mult, E=K.pool)
            K.tt(pdm, pdd, pdp[:, :, 1:P + 1], ALU.add)
            K.cp(histP[s_], pdp[:, :, P:P + 1], E=K.pool)
            R_, K_, V_ = pdm[:, 0:4, :], pdm[:, 4:8, :], pdm[:, 8:12, :]
            K.actf(txw, pdm[0:64, 12, :], AF.Tanh)
            K.actf(sg, pdm[:, 13, :], AF.Sigmoid)
            bw = self.bank()
            K.mm(bw, txw, w2t[0:64, :])
            K.tt(ldT, bw, ov[:, OV_W0:OV_W0 + 512], ALU.add)
            K.actf(ldT, ldT, AF.Sigmoid)
            K.ts(ldT, ldT, -float(np.exp(-0.5)), ALU.mult)
            ba = self.bank()
            for c in range(4):
                K.mm(ba[:, c * P:(c + 1) * P], a2t[64:128, c * 128:(c + 1) * 128], pdm[64:128, 12, :])
            for c in range(4):
                K.actf(aF[:, c, :], ba[:, c * P:(c + 1) * P], AF.Sigmoid, bias=ov[:, OV_A0 + c:OV_A0 + c + 1])
            bgm = self.bank()
            K.mm(bgm, sg, g2t)
            K.cp(gT, bgm, E=K.act)
            for c in range(4):
                K.ts(kp[:, c, :], aF[:, c, :], 1.0, ALU.subtract, ov[:, OV_KA + c:OV_KA + c + 1], ALU.mult)
                K.stt(kp[:, c, :], kp[:, c, :], 1.0, pdm[:, 4 + c, :], ALU.add, ALU.mult)
                K.ts(kq[:, c, :], pdm[:, 4 + c, :], ov[:, OV_KK + c:OV_KK + c + 1], ALU.mult, E=K.pool)
                K.stt(rkr[:, c, :], pdm[:, c, :], ov[:, OV_RK + c:OV_RK + c + 1], kp[:, c, :], ALU.mult, ALU.mult)
            K.tt(sqk, kq, kq, ALU.mult, E=K.pool)
            bss = self.bank()
            K.mm(bss, C["bones"], sqk)
            K.actf(sqk, bss.a.re("p (c t) -> p c t", c=4), AF.Sqrt, bias=self.epst)
            recip(sqk)
            K.tt(kk, kq, sqk, ALU.mult)
            K.tt(bm, kk, aF, ALU.mult, E=K.pool)
            blg = self.bank()
            blx = self.bank()
            for c in range(4):
                K.mm(blg[:, c * P:(c + 1) * P], ldT[:, c * 128:(c + 1) * 128], C["tri"])
                K.mm(blx[:, c * P:(c + 1) * P], ldT[:, c * 128:(c + 1) * 128], C["tris"])
            K.cp(lg, blg.a.re("p (c t) -> p c t", c=4))
            K.cp(lgx, blx.a.re("p (c t) -> p c t", c=4), E=K.act)
            K.ts(nmid, lg[:, :, 63], -1.0, ALU.mult)
            for c in range(4):
                K.actf(e_r[:, c, :], lg[:, c, :], AF.Exp, bias=nmid[:, c:c + 1])
                K.actf(e_a[:, c, :], lgx[:, c, :], AF.Exp, bias=nmid[:, c:c + 1])
                K.actf(e_b[:, c, :], lg[:, c, :], AF.Exp, bias=lg[:, c, 63:64], scale=-1.0)
            K.actf(e_r0, lg, AF.Exp)
            K.actf(e_a0, lgx, AF.Exp)
            K.cp(gl4, e_r0[:, :, 127])
            K.tt(rt_, R_, e_r, ALU.mult)
            K.tt(r0_, R_, e_r0, ALU.mult, E=K.pool)
            K.stt(at_, kk, -1.0, e_a, ALU.mult, ALU.mult)
            K.stt(a0_, kk, -1.0, e_a0, ALU.mult, ALU.mult)
            K.tt(kt_, kp, e_b, ALU.mult)
            K.tt(bt_, bm, e_b, ALU.mult, E=K.pool)
            bed = self.bank()
            K.mm(bed, C["ustr"], ldT)
            K.actf(edlT, bed, AF.Exp)
            for (srcT, dstT, scl) in ((kp, kdT, True), (bm, bdT, True), (V_, vT, False)):
                b = self.bank()
                for c in range(4):
                    K.mm(b[:, c * 128:(c + 1) * 128], srcT[:, c, :], C["ident"])
                if scl:
                    K.tt(dstT, b, edlT, ALU.mult)
                else:
                    K.cp(dstT, b, E=K.act)
            brk = self.bank()
            for c in range(4):
                K.mm(brk[:, 0:8], rkr[:, c, :], ov[:, OV_SEL + c * 8:OV_SEL + (c + 1) * 8], start=(c == 0), stop=(c == 3))
            K.cp(rks, brk[:, 0:8])
            byd = self.ps[7]
            for hg in range(2):
                b1, b2, b3, b4, b5 = [self.bank() for _ in range(5)]
                for hh in range(4):
                    hd = hg * 4 + hh
                    hq, pr = hd // 2, slice((hd % 2) * 64, (hd % 2) * 64 + 64)
                    sl = slice(hh * 128, (hh + 1) * 128)
                    K.mm(b1[:, sl], bt_[pr, hq, :], at_[pr, hq, :], ser=True)
                    K.mm(b2[:, sl], at_[pr, hq, :], bt_[pr, hq, :], ser=True)
                    K.mm(b3[:, sl], kt_[pr, hq, :], at_[pr, hq, :], ser=True)
                    K.mm(b4[:, sl], bt_[pr, hq, :], rt_[pr, hq, :], ser=True)
                    K.mm(b5[:, sl], kt_[pr, hq, :], rt_[pr, hq, :], ser=True)
                K.tt(AT[0], b1.a.re("p (h t) -> p h t", h=4), bc4(C["tris"]), ALU.mult)
                K.tt(A[0], b2.a.re("p (h t) -> p h t", h=4), bc4(C["ustr"]), ALU.mult)
                K.tt(AakT, b3.a.re("p (h t) -> p h t", h=4), bc4(C["tris"]), ALU.mult)
                K.tt(RbT, b4.a.re("p (h t) -> p h t", h=4), bc4(C["tri"]), ALU.mult)
                K.tt(RkT, b5.a.re("p (h t) -> p h t", h=4), bc4(C["tri"]), ALU.mult)
                K.tt(PT, AT[0], bc4(C["ident"]), ALU.add, E=K.pool)
                cur = 0
                for itr in range(6):
                    nxt = 1 - cur
                    c1 = self.bank()
                    c2 = self.bank()
                    for hh in range(4):
                        sl = slice(hh * 128, (hh + 1) * 128)
                        K.mm(c1[:, sl], AT[cur][:, hh, :], A[cur][:, hh, :])
                        K.mm(c2[:, sl], A[cur][:, hh, :], AT[cur][:, hh, :])
                    K.cp(A[nxt], c1.a.re("p (h t) -> p h t", h=4), E=K.act)
                    K.cp(AT[nxt], c2.a.re("p (h t) -> p h t", h=4), E=K.dve)
                    c3 = self.bank()
                    for hh in range(4):
                        K.mm(c3[:, hh * 128:(hh + 1) * 128], A[nxt][:, hh, :], PT[:, hh, :])
                    K.tt(PT, PT, c3.a.re("p (h t) -> p h t", h=4), ALU.add)
                    cur = nxt
                bx = self.bank()
                for hh in range(4):
                    hd = hg * 4 + hh
                    hq, pr = hd // 2, slice((hd % 2) * 64, (hd % 2) * 64 + 64)
                    vs = slice((hd % 2) * 64, (hd % 2) * 64 + 64)
                    K.mm(bx[:, hh * 64:(hh + 1) * 64], a0_[pr, hq, :], Hs[pr, hq, vs], start=True, stop=False, ser=True)
                    K.mm(bx[:, hh * 64:(hh + 1) * 64], AakT[:, hh, :], vT[:, hd * 64:(hd + 1) * 64], start=False, stop=True,
                         ser=True)
                K.cp(Xs, bx[:, 0:256].re("p (h q) -> p h q", h=4))
                bu = self.bank()
                for hh in range(4):
                    K.mm(bu[:, hh * 64:(hh + 1) * 64], PT[:, hh, :], Xs[:, hh, :])
                K.cp(Us[:, hg * 4:(hg + 1) * 4, :], bu[:, 0:256].re("p (h q) -> p h q", h=4))
                for hh in range(4):
                    hd = hg * 4 + hh
                    hq, pr = hd // 2, slice((hd % 2) * 64, (hd % 2) * 64 + 64)
                    vs = slice((hd % 2) * 64, (hd % 2) * 64 + 64)
                    o = byd[:, hd * 64:(hd + 1) * 64]
                    K.mm(o, r0_[pr, hq, :], Hs[pr, hq, vs], start=True, stop=False, ser=True)
                    K.mm(o, RbT[:, hh, :], Us[:, hd, :], start=False, stop=False, ser=True)
                    K.mm(o, RkT[:, hh, :], vT[:, hd * 64:(hd + 1) * 64], start=False, stop=True)
            bh = self.bank()
            for hq in range(4):
                o = bh[:, hq * 128:(hq + 1) * 128]
                K.mm(o, bdT[:, hq * 128:(hq + 1) * 128], Us[:, 2 * hq:2 * hq + 2, :].re("p h q -> p (h q)"), start=True, stop=False)
                K.mm(o, kdT[:, hq * 128:(hq + 1) * 128], vT[:, hq * 128:(hq + 1) * 128], start=False, stop=True)
            for hq in range(4):
                K.stt(Hs[:, hq, :], Hs[:, hq, :], gl4[:, hq:hq + 1], bh[:, hq * 128:(hq + 1) * 128], ALU.mult, ALU.add)
            K.cp(yd, byd.a.re("p (h q) -> p h q", h=8), E=K.act)
            K.red(mean8, yd)
            K.ts(mean8, mean8, 1.0 / 64.0, ALU.mult)
            K.tt(ym, yd, bcl(mean8, 8, 64), ALU.subtract)
            K.tt(yd, ym, ym, ALU.mult, E=K.pool)
            K.red(var8, yd)
            K.actf(var8, var8, AF.Sqrt, bias=gneps, scale=1.0 / 64.0)
            recip(var8)
            K.tt(ym, ym, bcl(var8, 8, 64), ALU.mult)
            ymf = _v(ym).re("p h q -> p (h q)")
            K.tt(ymf, ymf, ov[:, OV_GNW:OV_GNW + 512], ALU.mult, E=K.pool)
            K.tt(ymf, ymf, ov[:, OV_GNB:OV_GNB + 512], ALU.add)
            K.tt(yd, vT.a.re("p (h q) -> p h q", h=8), bcl(rks, 8, 64), ALU.mult)
            K.tt(ym, ym, yd, ALU.add)
            K.tt(ydT, ymf, gT, ALU.mult)
            btr = self.bank()
            for hh in range(4):
                K.mm(btr[:, hh * 128:(hh + 1) * 128], ydT[:, hh * 128:(hh + 1) * 128], ID)
            K.cp(oFM[:, 4:8, :], btr.a.re("p (h t) -> p h t", h=4), E=K.act)
            for half in range(2):
                b = self.bank()
                for j in range(4):
                    n = half * 4 + j
                    wo = woblock(n)
                    for c in range(8):
                        K.mm(b[:, j * P:(j + 1) * P], wo(c), oFM[:, c, :], start=(c == 0), stop=(c == 7))
                K.tt(h[:, half * 4:(half + 1) * 4, :], h[:, half * 4:(half + 1) * 4, :],
                     b.a.re("p (j t) -> p j t", j=4), ALU.add)
            K.dma(K.act, self.hview(dst, it * P, P), h)
        K.barrier()
        es.close()

    run_variant(True, [(s_, s_ * tps) for s_ in range(NSEQ)])
    run_variant(False, [(s_, s_ * tps + j) for s_ in range(NSEQ) for j in range(1, tps)])
    K.barrier()
    esP.close()


Model.odd_phase = _odd_phase


def kernel(**inputs):
    inp = {k: np.asarray(v) for k, v in inputs.items()}
    x = inp["x"]
    M = Model(n_layers=4, seq=x.shape[1], mix=True)
    nc = M.build()
    maps = make_in_maps(inp, M, x, 8)
    res = run_bass_kernel_spmd(nc, maps, core_ids=list(range(8)))
    out = np.concatenate([r["out"].reshape(NSEQ, x.shape[1], D) for r in res.results], axis=0)
    return out.astype(np.float32)
```

```python
import contextlib
import os
import numpy as np
import ml_dtypes
import concourse.bass as bass
import concourse.mybir as mybir
from concourse.bass_utils import run_bass_kernel_spmd

F32 = mybir.dt.float32
BF16 = mybir.dt.bfloat16
AF = mybir.ActivationFunctionType
ALU = mybir.AluOpType
AX = mybir.AxisListType

EPOCH = 30000


class Tl:
    def __init__(self, h, name):
        self.h = h
        self.name = name
        self.w = None
        self.r = {}
        self.dsem = None
        self.dcnt = 0
        self.psum = False

    def __getitem__(self, idx):
        return V(self, self.h[idx])

    @property
    def a(self):
        return V(self, self.h[:])


class V:
    def __init__(self, t, ap):
        self.t = t
        self.ap = ap

    def __getitem__(self, idx):
        return V(self.t, self.ap[idx])

    def bitcast(self, dt):
        return V(self.t, self.ap.bitcast(dt))

    def bc(self, shape):
        return V(self.t, self.ap.to_broadcast(list(shape)))

    def re(self, s, **kw):
        return V(self.t, self.ap.rearrange(s, **kw))


def _v(x):
    return x.a if isinstance(x, Tl) else x


class Eng:
    def __init__(self, K, name, eng):
        self.K = K
        self.name = name
        self.eng = eng
        self.epoch = -1
        self.cnt = EPOCH
        self.sem = None
        self.known = {}

    def tick(self, ins):
        if self.cnt >= EPOCH:
            self.epoch += 1
            self.cnt = 0
            self.sem = self.K.new_sem(f"s_{self.name}_{self.epoch}")
        self.cnt += 1
        ins.then_inc(self.sem, 1)
        return (id(self.sem), self.sem, self.cnt, self.name)


class KB:
    def __init__(self, nc):
        self.nc = nc
        self.es = contextlib.ExitStack()
        self.pe = Eng(self, "pe", nc.tensor)
        self.dve = Eng(self, "dve", nc.vector)
        self.act = Eng(self, "act", nc.scalar)
        self.pool = Eng(self, "pool", nc.gpsimd)
        self.sp = Eng(self, "sp", nc.sync)
        self.engs = [self.pe, self.dve, self.act, self.pool, self.sp]
        self.nsem = 0
        self.ntile = 0
        self.all_dma = {}
        self.ninst = 0

    def new_sem(self, name):
        self.nsem += 1
        return self.es.enter_context(self.nc.semaphore(f"{name}_{self.nsem}"))

    def sb(self, shape, dt=F32, name="t", es=None):
        self.ntile += 1
        nm = f"{name}_{self.ntile}"
        h = (es or self.es).enter_context(self.nc.sbuf_tensor(nm, list(shape), dt))
        return Tl(h, nm)

    def psum(self, shape, dt=F32, name="p", es=None):
        self.ntile += 1
        nm = f"{name}_{self.ntile}"
        h = (es or self.es).enter_context(self.nc.psum_tensor(nm, list(shape), dt))
        t = Tl(h, nm)
        t.psum = True
        return t

    def dram(self, name, shape, dt=F32, kind="Internal"):
        h = self.nc.dram_tensor(name, list(shape), dt, kind=kind)
        return Tl(h.ap(), name)

    def _wait(self, E, toks):
        best = {}
        for tk in toks:
            if tk is None:
                continue
            sid, sem, val, who = tk
            if who == E.name and E is self.pe:
                continue
            if E.known.get(sid, 0) >= val:
                continue
            if best.get(sid, (None, 0))[1] < val:
                best[sid] = (sem, val)
        for sid, (sem, val) in best.items():
            E.eng.wait_ge(sem, val)
            E.known[sid] = val
            self.ninst += 1

    skip = False

    def op(self, E, fn, reads, writes):
        if self.skip:
            return None
        toks = []
        rt = [(_v(x).t) for x in reads if x is not None and not isinstance(x, (int, float))]
        wt = [(_v(x).t) for x in writes if x is not None]
        for t in rt:
            toks.append(t.w)
            if t.psum:
                for k, tk in t.r.items():
                    if k != E.name:
                        toks.append(tk)
        for t in wt:
            toks.append(t.w)
            for k, tk in t.r.items():
                if k == E.name:
                    continue
                toks.append(tk)
        self._wait(E, toks)
        ins = fn()
        tok = E.tick(ins)
        self.ninst += 1
        for t in rt:
            t.r[E.name] = tok
        for t in wt:
            t.w = tok
            t.r = {}
        return tok

    def dma(self, Q, out, in_):
        out = _v(out)
        in_ = _v(in_)
        ot, it = out.t, in_.t
        toks = [it.w, ot.w] + list(ot.r.values())
        self._wait(Q, toks)
        if ot.dsem is None:
            ot.dsem = self.new_sem("d_" + ot.name)
        ins = Q.eng.dma_start(out=out.ap, in_=in_.ap)
        ins.then_inc(ot.dsem, 16)
        ot.dcnt += 16
        tok = (id(ot.dsem), ot.dsem, ot.dcnt, "dma")
        self.ninst += 1
        ot.w = tok
        ot.r = {}
        it.r["dma%d" % id(ot.dsem)] = tok
        self.all_dma[id(ot.dsem)] = tok
        return tok

    def barrier(self):
        toks = list(self.all_dma.values())
        for E in self.engs:
            if E.sem is not None and E.cnt > 0:
                toks.append((id(E.sem), E.sem, E.cnt, E.name + "_b"))
        for E in self.engs:
            self._wait(E, toks)

    def mm(self, out, lhsT, rhs, start=True, stop=True, ser=False):
        out, lhsT, rhs = _v(out), _v(lhsT), _v(rhs)
        if ser and not self.skip and self.pe.sem is not None and self.pe.cnt > 0:
            self.pe.eng.wait_ge(self.pe.sem, self.pe.cnt)
        return self.op(self.pe, lambda: self.nc.tensor.matmul(out.ap, lhsT.ap, rhs.ap, start=start, stop=stop),
                       [lhsT, rhs], [out])

    def actf(self, out, in_, func, bias=None, scale=None, accum=None, E=None):
        out, in_ = _v(out), _v(in_)
        kw = {}
        rd = [in_]
        if bias is not None:
            if isinstance(bias, (int, float)):
                kw["bias"] = float(bias)
            else:
                bias = _v(bias)
                kw["bias"] = bias.ap
                rd.append(bias)
        if scale is not None:
            if isinstance(scale, (int, float)):
                kw["scale"] = float(scale)
            else:
                scale = _v(scale)
                kw["scale"] = scale.ap
                rd.append(scale)
        wr = [out]
        if accum is not None:
            accum = _v(accum)
            kw["accum_out"] = accum.ap
            wr.append(accum)
        return self.op(self.act, lambda: self.nc.scalar.activation(out.ap, in_.ap, func, **kw), rd, wr)

    def _ve(self, E):
        return E or self.dve

    def _sc(self, s, rd):
        if isinstance(s, (int, float)):
            return float(s)
        s = _v(s)
        rd.append(s)
        return s.ap

    def ts(self, out, in0, s1, op0, s2=None, op1=None, E=None, accum=None):
        E = self._ve(E)
        out, in0 = _v(out), _v(in0)
        rd = [in0]
        a1 = self._sc(s1, rd)
        a2 = None if s2 is None else self._sc(s2, rd)
        wr = [out]
        kw = {}
        if accum is not None:
            accum = _v(accum)
            kw["accum_out"] = accum.ap
            wr.append(accum)
        if op1 is None:
            return self.op(E, lambda: E.eng.tensor_scalar(out.ap, in0.ap, a1, None, op0, **kw), rd, wr)
        return self.op(E, lambda: E.eng.tensor_scalar(out.ap, in0.ap, a1, a2, op0, op1, **kw), rd, wr)

    def tt(self, out, in0, in1, op, E=None):
        E = self._ve(E)
        out, in0, in1 = _v(out), _v(in0), _v(in1)
        return self.op(E, lambda: E.eng.tensor_tensor(out.ap, in0.ap, in1.ap, op), [in0, in1], [out])

    def stt(self, out, in0, s, in1, op0, op1, E=None):
        E = self.dve
        out, in0, in1 = _v(out), _v(in0), _v(in1)
        rd = [in0, in1]
        a = self._sc(s, rd)
        return self.op(E, lambda: E.eng.scalar_tensor_tensor(out.ap, in0.ap, a, in1.ap, op0, op1), rd, [out])

    def cp(self, out, in_, E=None):
        E = self._ve(E)
        out, in_ = _v(out), _v(in_)
        if E is self.act:
            return self.op(E, lambda: self.nc.scalar.copy(out.ap, in_.ap), [in_], [out])
        return self.op(E, lambda: E.eng.tensor_copy(out.ap, in_.ap), [in_], [out])

    def memset(self, out, val, E=None):
        E = self._ve(E)
        out = _v(out)
        return self.op(E, lambda: E.eng.memset(out.ap, val), [], [out])

    def red(self, out, in_, op=ALU.add, E=None):
        E = self._ve(E)
        out, in_ = _v(out), _v(in_)
        return self.op(E, lambda: E.eng.tensor_reduce(out.ap, in_.ap, AX.X, op), [in_], [out])


D = 1024
NSEQ = 2
T = 256
CH = 128
EVEN_IN = 3608
ODD_IN = 3336
NEPS = 1e-6


def host_consts():
    i = np.arange(128)
    c = {}
    c["ident"] = np.eye(128, dtype=np.float32)
    c["tri"] = (i[:, None] <= i[None, :]).astype(np.float32)
    c["tris"] = (i[:, None] < i[None, :]).astype(np.float32)
    c["ustr"] = (i[:, None] > i[None, :]).astype(np.float32)
    c["ones"] = np.ones((128, 128), np.float32)
    c["bones"] = ((i[:, None] // 64) == (i[None, :] // 64)).astype(np.float32)
    return c


CONST_NAMES = ["ident", "tri", "tris", "ustr", "ones", "bones"]


class Model:
    def __init__(self, n_layers=4, seq=2048, mix=True, dbg=False):
        self.L = n_layers
        self.seq = seq
        self.NT = NSEQ * seq
        self.mix = mix
        self.dbg = dbg

    def build(self):
        nc = bass.Bass("TRN2", target_bir_lowering=False)
        self.nc = nc
        K = KB(nc)
        self.K = K
        L, NT = self.L, self.NT
        ne = (L + 1) // 2
        no = L // 2
        self.x = K.dram("x", [NT, D], F32, kind="ExternalInput")
        self.out = K.dram("out", [NT, D], F32, kind="ExternalOutput")
        self.hA = K.dram("hA", [D, NT], F32)
        self.hB = K.dram("hB", [D, NT], F32)
        self.cst_d = K.dram("consts", [128, len(CONST_NAMES) * 128], F32, kind="ExternalInput")
        self.w1 = K.dram("mlp_w1", [L, D, 4 * D], F32, kind="ExternalInput")
        self.w2 = K.dram("mlp_w2", [L, 4 * D, D], F32, kind="ExternalInput")
        self.nvec = 3 * L + 1
        self.vec_d = K.dram("vecs", [128, 8 * (2 * L + 1)], F32, kind="ExternalInput")
        if self.mix:
            self.declare_mixer_inputs(ne, no)
        self.cst = K.sb([128, len(CONST_NAMES) * 128], F32, "cst")
        K.dma(K.sp, self.cst, self.cst_d)
        self.C = {n: self.cst[:, i * 128:(i + 1) * 128] for i, n in enumerate(CONST_NAMES)}
        self.cstb = K.sb([128, len(CONST_NAMES) * 128], BF16, "cstb")
        K.cp(self.cstb, self.cst)
        self.CB = {n: self.cstb[:, i * 128:(i + 1) * 128] for i, n in enumerate(CONST_NAMES)}
        self.vec = K.sb([128, 8 * (2 * L + 1)], F32, "vec")
        K.dma(K.sp, self.vec, self.vec_d)
        self.epst = K.sb([128, 1], F32, "eps")
        K.memset(self.epst, NEPS)
        self.ps = [K.psum([128, 512], F32, "bank") for _ in range(8)]
        self.psi = 0

        self.phase0()
        src, dst = self.hA, self.hB
        for l in range(L):
            if self.mix:
                K.barrier()
                if l % 2 == 0:
                    self.even_phase(l, src, dst)
                elif os.environ.get("NOODD"):
                    src, dst = dst, src
                else:
                    self.odd_phase(l, src, dst)
                src, dst = dst, src
            K.barrier()
            self.mlp_phase(l, src, dst)
            src, dst = dst, src
        K.barrier()
        self.final_phase(src)
        K.barrier()
        return nc

    def bank(self):
        b = self.ps[self.psi % 7]
        self.psi += 1
        return b

    def hview(self, hd, t0, n):
        return hd.a.re("(c p) t -> p c t", p=128)[:, :, t0:t0 + n]

    def phase0(self):
        K = self.K
        es = contextlib.ExitStack()
        xt = [K.sb([128, 2, D], F32, "xt", es) for _ in range(2)]
        ht = [K.sb([128, 8, T], F32, "ht", es) for _ in range(2)]
        for i in range(self.NT // T):
            xs, hs = xt[i % 2], ht[i % 2]
            K.dma(K.sp, xs, self.x.a[i * T:(i + 1) * T, :].re("(s p) f -> p s f", p=128))
            for s in range(2):
                for half in range(2):
                    b = self.bank()
                    for j in range(4):
                        c = half * 4 + j
                        K.mm(b[:, j * 128:(j + 1) * 128], xs[:, s, c * 128:(c + 1) * 128], self.C["ident"])
                    dst = hs[:, half * 4:(half + 1) * 4, s * 128:(s + 1) * 128]
                    K.cp(dst, b.a.re("p (j t) -> p j t", j=4), E=(K.dve if half == 0 else K.act))
            K.dma(K.act, self.hview(self.hA, i * T, T), hs)
        K.barrier()
        es.close()

    def rmsnorm_fm(self, h, gain, hn, sq, rp, n=T):
        K = self.K
        K.actf(sq, h, AF.Square)
        b = self.bank()
        for c in range(8):
            K.mm(b[:, 0:n], self.CB["ones"], _v(sq)[:, c, :], start=(c == 0), stop=(c == 7))
        K.actf(rp, b[:, 0:n], AF.Sqrt, bias=self.epst, scale=1.0 / D)
        K.op(K.dve, lambda: self.nc.vector.reciprocal(_v(rp).ap, _v(rp).ap), [rp], [rp])
        for c in range(8):
            for o in (hn if isinstance(hn, (list, tuple)) else [hn]):
                K.stt(_v(o)[:, c, :], _v(h)[:, c, :], gain[:, c:c + 1], rp, ALU.mult, ALU.mult,
                      E=(K.dve if c % 2 == 0 else K.pool))

    def mlp_phase(self, l, src, dst):
        K = self.K
        P = 128
        gain = self.vec[:, 8 * (self.L + l):8 * (self.L + l + 1)]
        W1 = self.w1.a[l]
        W2 = self.w2.a[l]
        es = contextlib.ExitStack()
        NB = NSEQ * P
        h = K.sb([128, 8, NB], F32, "h", es)
        hn32 = K.sb([128, 8, NB], F32, "hn32", es)
        sq = K.sb([128, 8, NB], BF16, "sq", es)
        rp = K.sb([128, NB], F32, "rp", es)
        act32 = K.sb([128, 32, NB], F32, "act32", es)
        tmp32 = K.sb([128, 2, NB], F32, "tmp32", es)
        stg = [K.sb([128, 4096], F32, "stg", es) for _ in range(2)]
        ns = 0
        for s_ in range(NSEQ):
            K.dma(K.sp, h[:, :, s_ * P:(s_ + 1) * P], self.hview(src, s_ * self.seq, P))
        self.rmsnorm_fm(h, gain, hn32, sq, rp, n=NB)
        for mb in range(8):
            st = stg[ns % 2].a.re("p (k n) -> p k n", k=8)
            ns += 1
            K.dma(K.sp, st, W1[:, mb * 512:(mb + 1) * 512].re("(k p) n -> p k n", p=128))
            for j2 in range(2):
                b = self.bank()
                for j in range(2):
                    mloc = j2 * 2 + j
                    for k in range(8):
                        K.mm(b[:, j * NB:(j + 1) * NB], st[:, k, mloc * 128:(mloc + 1) * 128], hn32[:, k, :],
                             start=(k == 0), stop=(k == 7))
                K.actf(tmp32, b.a.re("p (j t) -> p j t", j=2), AF.Relu)
                m0 = mb * 4 + j2 * 2
                K.tt(act32[:, m0:m0 + 2, :], tmp32, tmp32, ALU.mult, E=K.pool)
        for n in range(8):
            st = stg[ns % 2].a.re("p (m n) -> p m n", m=32)
            ns += 1
            for q4 in range(4):
                K.dma(K.sp, st[:, q4 * 8:(q4 + 1) * 8, :],
                      W2[q4 * 1024:(q4 + 1) * 1024, n * 128:(n + 1) * 128].re("(m p) n -> p m n", p=128))
            b = self.bank()
            for m in range(32):
                K.mm(b[:, 0:NB], st[:, m, :], act32[:, m, :], start=(m == 0), stop=(m == 31))
            K.tt(h[:, n, :], h[:, n, :], b[:, 0:NB], ALU.add)
        for s_ in range(NSEQ):
            K.dma(K.act, self.hview(dst, s_ * self.seq, P), h[:, :, s_ * P:(s_ + 1) * P])
        K.barrier()
        es.close()
        tiles = []
        for s_ in range(NSEQ):
            if self.seq > P:
                tiles.append((s_ * self.seq + P, min(T, self.seq) - P))
            for j in range(1, self.seq // T):
                tiles.append((s_ * self.seq + j * T, T))
        if not tiles:
            return
        es = contextlib.ExitStack()
        w1 = K.sb([128, 8, 4 * D], BF16, "w1", es)
        w2 = K.sb([128, 32, D], BF16, "w2", es)
        for k in range(8):
            K.dma(K.pool, w1[:, k, :], W1[k * 128:(k + 1) * 128, :])
        for g in range(4):
            K.dma(K.pool, w2[:, g * 8:(g + 1) * 8, :],
                  W2[g * 1024:(g + 1) * 1024, :].re("(m p) n -> p m n", p=128))
        hts = [K.sb([128, 8, T], F32, "h", es) for _ in range(2)]
        hn = K.sb([128, 8, T], BF16, "hn", es)
        sq = K.sb([128, 8, T], BF16, "sq", es)
        rp = K.sb([128, T], F32, "rp", es)
        act = K.sb([128, 32, T], BF16, "act", es)
        tmp = [K.sb([128, 2, T], BF16, "tmp", es) for _ in range(2)]
        K.dma(K.sp, hts[0][:, :, 0:tiles[0][1]], self.hview(src, tiles[0][0], tiles[0][1]))
        for i, (t0, n) in enumerate(tiles):
            h = hts[i % 2]
            if i + 1 < len(tiles):
                t1, n1 = tiles[i + 1]
                K.dma(K.sp, hts[(i + 1) % 2][:, :, 0:n1], self.hview(src, t1, n1))
            self.rmsnorm_fm(h[:, :, 0:n], gain, hn[:, :, 0:n], sq[:, :, 0:n], rp[:, 0:n], n=n)
            for mp in range(16):
                b = self.bank()
                for j in range(2):
                    m = mp * 2 + j
                    for k in range(8):
                        K.mm(b[:, j * T:j * T + n], w1[:, k, m * 128:(m + 1) * 128], hn[:, k, 0:n],
                             start=(k == 0), stop=(k == 7))
                tm = tmp[mp % 2]
                K.actf(tm[:, :, 0:n], b.a.re("p (j t) -> p j t", j=2)[:, :, 0:n], AF.Relu)
                K.tt(act[:, mp * 2:mp * 2 + 2, 0:n], tm[:, :, 0:n], tm[:, :, 0:n], ALU.mult, E=K.pool)
            for npair in range(4):
                b = self.bank()
                for j in range(2):
                    nn = npair * 2 + j
                    for m in range(32):
                        K.mm(b[:, j * T:j * T + n], w2[:, m, nn * 128:(nn + 1) * 128], act[:, m, 0:n],
                             start=(m == 0), stop=(m == 31))
                K.tt(h[:, npair * 2:npair * 2 + 2, 0:n], h[:, npair * 2:npair * 2 + 2, 0:n],
                     b.a.re("p (j t) -> p j t", j=2)[:, :, 0:n], ALU.add)
            K.dma(K.act, self.hview(dst, t0, n), h[:, :, 0:n])
        K.barrier()
        es.close()

    def final_phase(self, src):
        K = self.K
        es = contextlib.ExitStack()
        hts = [K.sb([128, 8, T], F32, "h", es) for _ in range(2)]
        hn = K.sb([128, 8, T], F32, "hn", es)
        sq = K.sb([128, 8, T], BF16, "sq", es)
        rp = K.sb([128, T], F32, "rp", es)
        ot = [K.sb([128, 2, D], F32, "ot", es) for _ in range(2)]
        gain = self.vec[:, 8 * (2 * self.L):8 * (2 * self.L + 1)]
        ntile = self.NT // T
        K.dma(K.sp, hts[0], self.hview(src, 0, T))
        for i in range(ntile):
            h = hts[i % 2]
            if i + 1 < ntile:
                K.dma(K.sp, hts[(i + 1) % 2], self.hview(src, (i + 1) * T, T))
            self.rmsnorm_fm(h, gain, hn, sq, rp)
            o = ot[i % 2]
            for s in range(2):
                for half in range(2):
                    b = self.bank()
                    for j in range(4):
                        c = half * 4 + j
                        K.mm(b[:, j * 128:(j + 1) * 128], hn[:, c, s * 128:(s + 1) * 128], self.C["ident"])
                    K.cp(o[:, s, half * 512:(half + 1) * 512], b, E=(K.dve if half == 0 else K.act))
            K.dma(K.act, self.out.a[i * T:(i + 1) * T, :].re("(s p) f -> p s f", p=128), o)
        K.barrier()
        es.close()


def host_vecs(inp, L):
    cols = []
    for l in range(L):
        cols.append(np.asarray(inp["norm_mix_w"][l], np.float32).reshape(8, 128).T)
    for l in range(L):
        cols.append(np.asarray(inp["norm_mlp_w"][l], np.float32).reshape(8, 128).T)
    cols.append(np.asarray(inp["final_norm_w"], np.float32).reshape(8, 128).T)
    return np.ascontiguousarray(np.concatenate(cols, axis=1))


def make_in_maps(inp, M, x, ncores):
    L = M.L
    consts = host_consts()
    cst = np.ascontiguousarray(np.concatenate([consts[n] for n in CONST_NAMES], axis=1))
    vecs = host_vecs(inp, L)
    shared = {"consts": cst, "vecs": vecs,
              "mlp_w1": np.ascontiguousarray(inp["mlp_w1"][:L]), "mlp_w2": np.ascontiguousarray(inp["mlp_w2"][:L])}
    if M.mix:
        shared.update(host_mixer_inputs(inp, L))
    maps = []
    for c in range(ncores):
        d = dict(shared)
        d["x"] = np.ascontiguousarray(x[2 * c:2 * c + 2]).reshape(M.NT, D)
        maps.append(d)
    return maps


EV_CONV = 0
EV_ALOG = 48
EV_DTB = 52
EV_GDNW = 56
EV_GLAW = 568
EV_W2B = 1080
EV_N = 1336


def host_even_vec(inp, i):
    v = np.zeros((128, EV_N), np.float32)
    cw = np.asarray(inp["gdn_conv_w"][i], np.float32)
    v[:, EV_CONV:EV_CONV + 48] = cw.T.reshape(12, 128, 4).transpose(1, 0, 2).reshape(128, 48)
    v[:, EV_ALOG:EV_ALOG + 4] = np.asarray(inp["gdn_a_log"][i])[None, :]
    v[:, EV_DTB:EV_DTB + 4] = np.asarray(inp["gdn_dt_bias"][i])[None, :]
    v[:, EV_GDNW:EV_GDNW + 512] = np.tile(np.asarray(inp["gdn_norm_w"][i]), 4)[None, :]
    v[:, EV_GLAW:EV_GLAW + 512] = np.tile(np.asarray(inp["gla_norm_w"][i]), 4)[None, :]
    v[0:16, EV_W2B:EV_W2B + 256] = np.asarray(inp["gla_gate_w2"][i])
    v[16, EV_W2B:EV_W2B + 256] = np.asarray(inp["gla_gate_b"][i])
    return v


def _even_phase(self, l, src, dst):
    K = self.K
    nc = self.nc
    i_l = l // 2
    esP = contextlib.ExitStack()
    C, CB = self.C, self.CB
    P = 128
    tps = self.seq // P
    WIN = self.even_w_in.a[i_l]
    WOUT = self.even_w_out.a[i_l]

    ev = K.sb([128, EV_N], F32, "ev", esP)
    K.dma(K.sp, ev, self.evec.a[i_l])
    nexpa = K.sb([128, 4], F32, "nexpa", esP)
    K.actf(nexpa, ev[:, EV_ALOG:EV_ALOG + 4], AF.Exp)
    K.ts(nexpa, nexpa, -1.0, ALU.mult)
    convw = ev[:, EV_CONV:EV_CONV + 48]
    gain = self.vec[:, 8 * l:8 * (l + 1)]
    SAs = [K.sb([128, 4, 128], F32, "SA", esP) for _ in range(NSEQ)]
    SBs = [K.sb([128, 2, 256], F32, "SB", esP) for _ in range(NSEQ)]
    hist = [K.sb([128, 12, 3], F32, "hist", esP) for _ in range(NSEQ)]
    for s_ in range(NSEQ):
        K.memset(SAs[s_], 0.0)
        K.memset(SBs[s_], 0.0)
        K.memset(hist[s_], 0.0)

    def bc4(v):
        return v.re("p (o t) -> p o t", o=1).bc([128, 4, 128])

    def bcl(v):
        return _v(v).re("p (h o) -> p h o", o=1).bc([128, 4, 128])

    def run_variant(hp, tiles):
        if not tiles:
            return
        es = contextlib.ExitStack()
        AD = F32 if hp else BF16
        ID = C["ident"] if hp else CB["ident"]

        def t512(name, dt=F32):
            return K.sb([128, 4, 128], dt, name, es)

        if hp:
            stg = [K.sb([128, 8, 512], F32, "stg", es) for _ in range(2)]
            cnt = [0]

            def _stream(dram2d, col0, n):
                t = stg[cnt[0] % 2]
                cnt[0] += 1
                K.dma(K.sp, t[:, :, 0:n], dram2d[:, col0:col0 + n].re("(k p) n -> p k n", p=128))
                return lambda k: t[:, k, 0:n]

            wblock = lambda col0, n: _stream(WIN, col0, n)
            wblock32 = wblock
            woblock = lambda n0: _stream(WOUT, n0 * 128, 128)
        else:
            w_in = K.sb([128, 8, EVEN_IN], BF16, "w_in", es)
            for k in range(8):
                K.dma(K.pool, w_in[:, k, :], WIN[k * 128:(k + 1) * 128, :])
            w_out = K.sb([128, 8, D], BF16, "w_out", es)
            K.dma(K.pool, w_out, WOUT.re("(c p) n -> p c n", p=128))
            wg = K.sb([128, 8, 24], F32, "wg", es)
            for k in range(8):
                K.dma(K.sp, wg[:, k, 0:8], WIN[k * 128:(k + 1) * 128, 2048:2056])
                K.dma(K.sp, wg[:, k, 8:24], WIN[k * 128:(k + 1) * 128, 3592:3608])
            wblock = lambda col0, n: (lambda k: w_in[:, k, col0:col0 + n])
            wblock32 = lambda col0, n: (lambda k: wg[:, k, 0:8]) if col0 == 2048 else (lambda k: wg[:, k, 8:24])
            woblock = lambda n0: (lambda c: w_out[:, c, n0 * 128:(n0 + 1) * 128])

        h = K.sb([128, 8, P], F32, "h", es)
        hn32 = K.sb([128, 8, P], F32, "hn32", es)
        hn = hn32 if hp else K.sb([128, 8, P], BF16, "hn", es)
        sq = K.sb([128, 8, P], BF16, "sq", es)
        rp = K.sb([128, P], F32, "rp", es)
        pre = K.sb([128, 12, P + 3], F32, "pre", es)
        cacc = K.sb([128, 12, P], F32, "cacc", es)
        qkv = K.sb([128, 12, P], F32, "qkv", es)
        sq2 = K.sb([128, 8, P], BF16, "sq2", es)
        rinv = K.sb([128, 8, P], F32, "rinv", es)
        oFM = K.sb([128, 8, P], AD, "oFM", es)
        g1 = K.sb([32, P], F32, "g1", es)
        K.memset(g1, 1.0)
        loga = K.sb([128, 256], F32, "loga", es)
        gkT = K.sb([128, 256], F32, "gkT", es)
        qkF = K.sb([128, 4, 128], F32, "qkF", es)
        gcf = K.sb([128, 2, 128], F32, "gcf", es)
        nmid = K.sb([128, 2], F32, "nmid", es)
        eq = K.sb([128, 2, 128], F32, "eq", es)
        ek = K.sb([128, 2, 128], F32, "ek", es)
        eq0 = K.sb([128, 2, 128], F32, "eq0", es)
        qgm = K.sb([128, 2, 128], F32, "qgm", es)
        kgm = K.sb([128, 2, 128], F32, "kgm", es)
        qg0 = K.sb([128, 2, 128], AD, "qg0", es)
        edlB = K.sb([128, 256], F32, "edlB", es)
        kdB = K.sb([128, 256], AD, "kdB", es)
        gv = K.sb([128, 512], AD, "gv", es)
        grs = K.sb([128, 512], F32, "grs", es)
        zs = K.sb([128, 512], F32, "zs", es)
        attm = t512("attm", AD)
        SBb = None if hp else K.sb([128, 2, 256], BF16, "SBb", es)
        junk = K.sb([128, 128], F32, "junk", es)
        ssB = K.sb([128, 4], F32, "ssB", es)
        Gt = K.sb([128, 512], F32, "Gt", es)
        obT = K.sb([128, 512], AD, "obT", es)
        gr8 = K.sb([128, 8], F32, "gr8", es)
        beta = K.sb([128, 4], F32, "beta", es)
        nbeta = K.sb([128, 4], F32, "nbeta", es)
        gA = K.sb([128, 4], F32, "gA", es)
        gcc = K.sb([128, 4], F32, "gcc", es)
        eg = K.sb([128, 4], F32, "eg", es)
        egl = K.sb([128, 4], F32, "egl", es)
        edl = K.sb([128, 4], F32, "edl", es)
        beg = K.sb([128, 4], F32, "beg", es)
        Gtri = t512("Gtri")
        Dm = t512("Dm")
        DTm = t512("DTm")
        A = [t512("A0"), t512("A1")]
        AT = [t512("AT0"), t512("AT1")]
        PT = t512("PT")
        aqkT = t512("aqkT")
        vb = t512("vb")
        kbg = t512("kbg")
        kdA = t512("kdA")
        wneg = t512("wneg")
        vn = t512("vn")
        qg = t512("qg")
        ssA = K.sb([128, 4], F32, "ssA", es)
        oaT = K.sb([128, 512], AD, "oaT", es)

        for (s_, it) in tiles:
            SA, SB_ = SAs[s_], SBs[s_]
            K.dma(K.sp, h, self.hview(src, it * P, P))
            K.cp(pre[:, :, 0:3], hist[s_], E=K.pool)
            if not hp:
                K.cp(SBb, SB_, E=K.pool)
            Sst = SB_ if hp else SBb
            self.rmsnorm_fm(h, gain, [hn32] if hp else [hn, hn32], sq, rp, n=P)
            for c4 in range(3):
                wb = wblock(c4 * 512, 512)
                b = self.bank()
                for j in range(4):
                    for k in range(8):
                        K.mm(b[:, j * P:(j + 1) * P], wb(k)[:, j * 128:(j + 1) * 128], hn[:, k, :],
                             start=(k == 0), stop=(k == 7))
                K.cp(pre[:, c4 * 4:(c4 + 1) * 4, 3:3 + P], b.a.re("p (j t) -> p j t", j=4), E=K.act)
            shared = {}
            wb = wblock32(2048, 8)
            bg = self.bank()
            for k in range(8):
                K.mm(bg[:, 0:8], hn32[:, k, :], wb(k), start=(k == 0), stop=(k == 7))
            K.cp(gr8, bg[:, 0:8])
            for c in range(12):
                K.actf(cacc[:, c, :], pre[:, c, 0:P], AF.Copy, scale=convw[:, c * 4:c * 4 + 1])
                for j in range(1, 4):
                    K.stt(cacc[:, c, :], pre[:, c, j:j + P], convw[:, c * 4 + j:c * 4 + j + 1], cacc[:, c, :],
                          ALU.mult, ALU.add)
            K.cp(hist[s_], pre[:, :, P:P + 3], E=K.pool)
            K.actf(qkv, cacc, AF.Silu)
            K.tt(sq2, qkv[:, 0:8, :], qkv[:, 0:8, :], ALU.mult, E=K.pool)
            for half in range(2):
                b = self.bank()
                K.mm(b, CB["ones"], sq2[:, half * 4:(half + 1) * 4, :])
                K.actf(rinv[:, half * 4:(half + 1) * 4, :], b.a.re("p (j t) -> p j t", j=4), AF.Sqrt, bias=self.epst)
            K.op(K.dve, lambda: nc.vector.reciprocal(rinv.a.ap, rinv.a.ap), [rinv], [rinv])
            K.stt(qkv[:, 0:4, :], qkv[:, 0:4, :], float(128 ** -0.5), rinv[:, 0:4, :], ALU.mult, ALU.mult)
            K.tt(qkv[:, 4:8, :], qkv[:, 4:8, :], rinv[:, 4:8, :], ALU.mult, E=K.pool)

            def seg_tm1():
                wb = wblock(1536, 512)
                bz = self.bank()
                for k in range(8):
                    K.mm(bz, hn[:, k, :], wb(k), start=(k == 0), stop=(k == 7))
                K.actf(zs, bz, AF.Silu)
                wb = wblock(2312, 256)
                bgk = self.bank()
                for k in range(8):
                    K.mm(bgk[:, 0:256], hn[:, k, :], wb(k), start=(k == 0), stop=(k == 7))
                K.cp(gkT, bgk[:, 0:256], E=K.act)
            def seg_tm2():
                wb = wblock(2568, 512)
                bgv = self.bank()
                for k in range(8):
                    K.mm(bgv, hn[:, k, :], wb(k), start=(k == 0), stop=(k == 7))
                K.cp(gv, bgv, E=K.act)
                wb = wblock(3080, 512)
                bgr = self.bank()
                for k in range(8):
                    K.mm(bgr, hn[:, k, :], wb(k), start=(k == 0), stop=(k == 7))
                K.actf(grs, bgr, AF.Silu)
            def seg_tm3():
                wb = wblock32(3592, 16)
                bl = self.bank()
                for k in range(8):
                    K.mm(bl[0:16, 0:P], wb(k), hn32[:, k, :], start=(k == 0), stop=(k == 7))
                K.cp(g1[0:16, :], bl[0:16, 0:P])
                wb = wblock(2056, 512)
                bqk = self.bank()
                for j in range(4):
                    for k in range(8):
                        K.mm(bqk[:, j * P:(j + 1) * P], wb(k)[:, j * 128:(j + 1) * 128], hn[:, k, :],
                             start=(k == 0), stop=(k == 7))
                K.cp(qkF, bqk.a.re("p (j t) -> p j t", j=4))
            def seg_gla1():
                bx = self.bank()
                K.mm(bx[:, 0:256], g1, ev[0:32, EV_W2B:EV_W2B + 256])
                K.ts(loga, bx[:, 0:256], -20.0, ALU.max)
                K.actf(loga, loga, AF.Exp, scale=-1.0)
                K.actf(loga, loga, AF.Ln, bias=1.0)
                K.ts(loga, loga, -1.0 / 16.0, ALU.mult, -1.0, ALU.max)
                bgc = self.bank()
                for cc in range(2):
                    K.mm(bgc[:, cc * 128:(cc + 1) * 128], loga[:, cc * 128:(cc + 1) * 128], C["tri"])
                K.mm(bgc[:, 256:512], C["ustr"], loga)
                K.cp(gcf, bgc[:, 0:256].re("p (c t) -> p c t", c=2))
                K.actf(edlB, bgc[:, 256:512], AF.Exp)
                K.tt(kdB, gkT, edlB, ALU.mult)
                K.ts(nmid, gcf[:, :, 63], -1.0, ALU.mult)
                for cc in range(2):
                    K.actf(eq[:, cc, :], gcf[:, cc, :], AF.Exp, bias=nmid[:, cc:cc + 1])
                    K.actf(ek[:, cc, :], gcf[:, cc, :], AF.Exp, bias=gcf[:, cc, 63:64], scale=-1.0)
                K.actf(eq0, gcf, AF.Exp)
                K.stt(qgm, qkF[:, 0:2, :], 0.125, eq, ALU.mult, ALU.mult)
                K.tt(kgm, qkF[:, 2:4, :], ek, ALU.mult)
                K.stt(qg0, qkF[:, 0:2, :], 0.125, eq0, ALU.mult, ALU.mult)
            def seg_gla2():
                bat = self.bank()
                for hh in range(4):
                    pr = slice((hh % 2) * 64, (hh % 2) * 64 + 64)
                    K.mm(bat[:, hh * 128:(hh + 1) * 128], kgm[pr, hh // 2, :], qgm[pr, hh // 2, :], ser=True)
                K.tt(attm, bat.a.re("p (h t) -> p h t", h=4), bc4(C["tri"]), ALU.mult)
                bo = self.bank()
                shared['bo'] = bo
                for hh in range(4):
                    pr = slice((hh % 2) * 64, (hh % 2) * 64 + 64)
                    K.mm(bo[:, hh * 128:(hh + 1) * 128], qg0[pr, hh // 2, :],
                         Sst[pr, hh // 2, (hh % 2) * 128:(hh % 2) * 128 + 128], start=True, stop=False, ser=True)
                    K.mm(bo[:, hh * 128:(hh + 1) * 128], attm[:, hh, :], gv[:, hh * 128:(hh + 1) * 128],
                         start=False, stop=True, ser=True)
                bs = self.bank()
                for pp in range(2):
                    K.mm(bs[:, pp * 256:(pp + 1) * 256], kdB[:, pp * 128:(pp + 1) * 128], gv[:, pp * 256:(pp + 1) * 256])
                for pp in range(2):
                    K.stt(SB_[:, pp, :], SB_[:, pp, :], eq0[:, pp, 127:128], bs[:, pp * 256:(pp + 1) * 256],
                          ALU.mult, ALU.add)
            def seg_gla3():
                bo = shared['bo']
                for hh in range(4):
                    K.actf(junk, bo[:, hh * 128:(hh + 1) * 128], AF.Square, accum=ssB[:, hh:hh + 1])
                K.actf(ssB, ssB, AF.Sqrt, bias=self.epst, scale=1.0 / 128.0)
                K.op(K.dve, lambda: nc.vector.reciprocal(ssB.a.ap, ssB.a.ap), [ssB], [ssB])
                K.tt(Gt, grs, ev[:, EV_GLAW:EV_GLAW + 512], ALU.mult, E=K.pool)
                for hh in range(4):
                    sl = slice(hh * 128, (hh + 1) * 128)
                    K.stt(obT[:, sl], bo[:, sl], ssB[:, hh:hh + 1], Gt[:, sl], ALU.mult, ALU.mult)
                btr = self.bank()
                for hh in range(4):
                    K.mm(btr[:, hh * 128:(hh + 1) * 128], obT[:, hh * 128:(hh + 1) * 128], ID)
                K.cp(oFM[:, 4:8, :], btr.a.re("p (h t) -> p h t", h=4), E=K.act)
            segs = [seg_tm1, seg_tm2, seg_tm3, seg_gla1, seg_gla2, seg_gla3]
            K.actf(beta, gr8[:, 0:4], AF.Sigmoid)
            K.ts(nbeta, beta, -1.0, ALU.mult)
            K.tt(gA, gr8[:, 4:8], ev[:, EV_DTB:EV_DTB + 4], ALU.add)
            K.actf(gA, gA, AF.Exp)
            K.actf(gA, gA, AF.Ln, bias=1.0)
            K.tt(gA, gA, nexpa, ALU.mult)
            K.tt(Gtri, bc4(C["tri"]), bcl(gA), ALU.mult)
            bsm = self.bank()
            K.mm(bsm[:, 0:4], C["tri"], gA)
            K.mm(bsm[:, 8:12], C["ones"], gA)
            br = self.bank()
            K.mm(br, C["ones"], Gtri)
            K.cp(gcc, bsm[:, 0:4])
            K.actf(eg, gcc, AF.Exp)
            K.actf(egl, bsm[:, 8:12], AF.Exp)
            K.tt(edl, bsm[:, 8:12], gcc, ALU.subtract)
            K.actf(edl, edl, AF.Exp)
            K.tt(beg, beta, eg, ALU.mult)
            for hh in range(4):
                sl = slice(hh * 128, (hh + 1) * 128)
                K.ts(Dm[:, hh, :], br[:, sl], gcc[:, hh:hh + 1], ALU.subtract, 0.0, ALU.max)
                K.ts(DTm[:, hh, :], br[:, sl], gcc[:, hh:hh + 1], ALU.subtract, 0.0, ALU.min)
            K.actf(Dm, Dm, AF.Exp, scale=-1.0)
            K.actf(DTm, DTm, AF.Exp)
            K.actf(qg, br.a.re("p (h t) -> p h t", h=4), AF.Exp)
            K.tt(Dm, Dm, bc4(C["ustr"]), ALU.mult, E=K.pool)
            K.tt(DTm, DTm, bc4(C["tri"]), ALU.mult, E=K.pool)
            K.tt(qg, qg, qkv[:, 0:4, :], ALU.mult, E=K.pool)
            bkk = self.bank()
            bqt = self.bank()
            for hh in range(4):
                sl = slice(hh * 128, (hh + 1) * 128)
                K.mm(bkk[:, sl], qkv[:, 4 + hh, :], qkv[:, 4 + hh, :])
                K.mm(bqt[:, sl], qkv[:, 4 + hh, :], qkv[:, hh, :])
            for hh in range(4):
                K.stt(A[0][:, hh, :], bkk[:, hh * 128:(hh + 1) * 128], nbeta[:, hh:hh + 1], Dm[:, hh, :],
                      ALU.mult, ALU.mult)
            K.tt(aqkT, bqt.a.re("p (h t) -> p h t", h=4), DTm, ALU.mult)
            bnt = self.bank()
            for hh in range(4):
                K.mm(bnt[:, hh * 128:(hh + 1) * 128], A[0][:, hh, :], C["ident"])
            K.cp(AT[0], bnt.a.re("p (h t) -> p h t", h=4), E=K.act)
            K.tt(PT, bnt.a.re("p (h t) -> p h t", h=4), bc4(C["ident"]), ALU.add)
            bkt = self.bank()
            bvt = self.bank()
            for hh in range(4):
                sl = slice(hh * 128, (hh + 1) * 128)
                K.mm(bkt[:, sl], qkv[:, 4 + hh, :], C["ident"])
                K.mm(bvt[:, sl], qkv[:, 8 + hh, :], C["ident"])
            K.tt(vb, bvt.a.re("p (h t) -> p h t", h=4), bcl(beta), ALU.mult)
            K.tt(kbg, bkt.a.re("p (h t) -> p h t", h=4), bcl(beg), ALU.mult)
            K.tt(kdA, bkt.a.re("p (h t) -> p h t", h=4), bcl(edl), ALU.mult)
            cur = 0
            for itr in range(6):
                nxt = 1 - cur
                b1 = self.bank()
                b2 = self.bank()
                for hh in range(4):
                    sl = slice(hh * 128, (hh + 1) * 128)
                    K.mm(b1[:, sl], AT[cur][:, hh, :], A[cur][:, hh, :])
                    K.mm(b2[:, sl], A[cur][:, hh, :], AT[cur][:, hh, :])
                K.cp(A[nxt], b1.a.re("p (h t) -> p h t", h=4), E=K.act)
                K.cp(AT[nxt], b2.a.re("p (h t) -> p h t", h=4), E=K.dve)
                b3 = self.bank()
                for hh in range(4):
                    K.mm(b3[:, hh * 128:(hh + 1) * 128], A[nxt][:, hh, :], PT[:, hh, :])
                K.tt(PT, PT, b3.a.re("p (h t) -> p h t", h=4), ALU.add)
                cur = nxt
                segs[itr]()
            bw = self.bank()
            for hh in range(4):
                K.mm(bw[:, hh * 128:(hh + 1) * 128], kbg[:, hh, :], PT[:, hh, :])
            K.actf(wneg, bw.a.re("p (h t) -> p h t", h=4), AF.Copy, scale=-1.0)
            bvn = self.bank()
            for hh in range(4):
                sl = slice(hh * 128, (hh + 1) * 128)
                K.mm(bvn[:, sl], PT[:, hh, :], vb[:, hh, :], start=True, stop=False)
                K.mm(bvn[:, sl], wneg[:, hh, :], SA[:, hh, :], start=False, stop=True)
            K.cp(vn, bvn.a.re("p (h t) -> p h t", h=4))
            boa = self.bank()
            for hh in range(4):
                sl = slice(hh * 128, (hh + 1) * 128)
                K.mm(boa[:, sl], qg[:, hh, :], SA[:, hh, :], start=True, stop=False)
                K.mm(boa[:, sl], aqkT[:, hh, :], vn[:, hh, :], start=False, stop=True)
            bsa = self.bank()
            for hh in range(4):
                K.mm(bsa[:, hh * 128:(hh + 1) * 128], kdA[:, hh, :], vn[:, hh, :])
            for hh in range(4):
                K.stt(SA[:, hh, :], SA[:, hh, :], egl[:, hh:hh + 1], bsa[:, hh * 128:(hh + 1) * 128],
                      ALU.mult, ALU.add)
            for hh in range(4):
                K.actf(junk, boa[:, hh * 128:(hh + 1) * 128], AF.Square, accum=ssA[:, hh:hh + 1])
            K.actf(ssA, ssA, AF.Sqrt, bias=self.epst, scale=1.0 / 128.0)
            K.op(K.dve, lambda: nc.vector.reciprocal(ssA.a.ap, ssA.a.ap), [ssA], [ssA])
            K.tt(Gt, zs, ev[:, EV_GDNW:EV_GDNW + 512], ALU.mult, E=K.pool)
            for hh in range(4):
                sl = slice(hh * 128, (hh + 1) * 128)
                K.stt(oaT[:, sl], boa[:, sl], ssA[:, hh:hh + 1], Gt[:, sl], ALU.mult, ALU.mult)
            btr = self.bank()
            for hh in range(4):
                K.mm(btr[:, hh * 128:(hh + 1) * 128], oaT[:, hh * 128:(hh + 1) * 128], ID)
            K.cp(oFM[:, 0:4, :], btr.a.re("p (h t) -> p h t", h=4), E=K.act)

            for half in range(2):
                b = self.bank()
                for j in range(4):
                    wo = woblock(half * 4 + j)
                    for c in range(8):
                        K.mm(b[:, j * P:(j + 1) * P], wo(c), oFM[:, c, :], start=(c == 0), stop=(c == 7))
                K.tt(h[:, half * 4:(half + 1) * 4, :], h[:, half * 4:(half + 1) * 4, :],
                     b.a.re("p (j t) -> p j t", j=4), ALU.add)
            K.dma(K.act, self.hview(dst, it * P, P), h)
        K.barrier()
        es.close()

    run_variant(True, [(s_, s_ * tps) for s_ in range(NSEQ)])
    run_variant(False, [(s_, s_ * tps + j) for s_ in range(NSEQ) for j in range(1, tps)])
    K.barrier()
    esP.close()


Model.even_phase = _even_phase


def _declare_mixer_inputs(self, ne, no):
    K = self.K
    self.even_w_in = K.dram("even_w_in", [ne, D, EVEN_IN], F32, kind="ExternalInput")
    self.even_w_out = K.dram("even_w_out", [ne, D, D], F32, kind="ExternalInput")
    self.evec = K.dram("evec", [ne, 128, EV_N], F32, kind="ExternalInput")
    if no > 0:
        self.declare_odd_inputs(no)


Model.declare_mixer_inputs = _declare_mixer_inputs


def host_mixer_inputs(inp, L):
    ne = (L + 1) // 2
    no = L // 2
    d = {"even_w_in": np.ascontiguousarray(inp["even_w_in"][:ne]),
         "even_w_out": np.ascontiguousarray(inp["even_w_out"][:ne]),
         "evec": np.stack([host_even_vec(inp, i) for i in range(ne)])}
    if no > 0:
        d.update(host_odd_inputs(inp, no))
    return d


OV_CONV = 0
OV_CONVB = 32
OV_DTB = 40
OV_ALOG = 48
OV_D = 56
OV_SNW = 568
OV_MU = 1080
OV_W0 = 1094
OV_A0 = 1606
OV_KK = 1610
OV_KA = 1614
OV_RK = 1618
OV_GNW = 1622
OV_GNB = 2134
OV_SEL = 2646
OV_N = 2678


def host_odd_vec(inp, i):
    v = np.zeros((128, OV_N), np.float32)
    f = lambda a, n: np.asarray(a, np.float32).reshape(n, 128).T
    cw = np.asarray(inp["ssd_conv_w"][i], np.float32)
    v[:, OV_CONV:OV_CONV + 32] = cw.T.reshape(8, 128, 4).transpose(1, 0, 2).reshape(128, 32)
    v[:, OV_CONVB:OV_CONVB + 8] = f(inp["ssd_conv_b"][i], 8)
    v[:, OV_DTB:OV_DTB + 8] = np.asarray(inp["ssd_dt_bias"][i])[None, :]
    v[:, OV_ALOG:OV_ALOG + 8] = np.asarray(inp["ssd_a_log"][i])[None, :]
    v[:, OV_D:OV_D + 512] = np.repeat(np.asarray(inp["ssd_d"][i]), 64)[None, :]
    v[:, OV_SNW:OV_SNW + 512] = np.asarray(inp["ssd_norm_w"][i])[None, :]
    v[:, OV_MU:OV_MU + 14] = f(inp["rwkv_mu"][i], 14)
    v[:, OV_W0:OV_W0 + 512] = np.asarray(inp["rwkv_w0"][i])[None, :]
    v[:, OV_A0:OV_A0 + 4] = f(inp["rwkv_a0"][i], 4)
    v[:, OV_KK:OV_KK + 4] = f(inp["rwkv_k_k"][i], 4)
    v[:, OV_KA:OV_KA + 4] = f(inp["rwkv_k_a"][i], 4)
    v[:, OV_RK:OV_RK + 4] = f(np.asarray(inp["rwkv_r_k"][i]).reshape(-1), 4)
    v[:, OV_GNW:OV_GNW + 512] = np.asarray(inp["rwkv_gn_w"][i])[None, :]
    v[:, OV_GNB:OV_GNB + 512] = np.asarray(inp["rwkv_gn_b"][i])[None, :]
    p = np.arange(128)
    sel = np.zeros((128, 4, 8), np.float32)
    for c in range(4):
        sel[p, c, 2 * c + p // 64] = 1.0
    v[:, OV_SEL:OV_SEL + 32] = sel.reshape(128, 32)
    return v


def host_odd_inputs(inp, no):
    a2 = np.zeros((no, 128, 512), np.float32)
    a2[:, 64:128, :] = np.asarray(inp["rwkv_a2"][:no])
    w2 = np.zeros((no, 128, 512), np.float32)
    w2[:, 0:64, :] = np.asarray(inp["rwkv_w2"][:no])
    return {"odd_w_in": np.ascontiguousarray(inp["odd_w_in"][:no]),
            "odd_w_out": np.ascontiguousarray(inp["odd_w_out"][:no]),
            "ovec": np.stack([host_odd_vec(inp, i) for i in range(no)]),
            "rw2": w2, "ra2": a2, "rg2": np.ascontiguousarray(inp["rwkv_g2"][:no])}


def _declare_odd_inputs(self, no):
    K = self.K
    self.odd_w_in = K.dram("odd_w_in", [no, D, ODD_IN], F32, kind="ExternalInput")
    self.odd_w_out = K.dram("odd_w_out", [no, D, D], F32, kind="ExternalInput")
    self.ovec = K.dram("ovec", [no, 128, OV_N], F32, kind="ExternalInput")
    self.rw2 = K.dram("rw2", [no, 128, 512], F32, kind="ExternalInput")
    self.ra2 = K.dram("ra2", [no, 128, 512], F32, kind="ExternalInput")
    self.rg2 = K.dram("rg2", [no, 128, 512], F32, kind="ExternalInput")


Model.declare_odd_inputs = _declare_odd_inputs


def _odd_phase(self, l, src, dst):
    K = self.K
    nc = self.nc
    i_l = l // 2
    esP = contextlib.ExitStack()
    C, CB = self.C, self.CB
    P = 128
    tps = self.seq // P
    WIN = self.odd_w_in.a[i_l]
    WOUT = self.odd_w_out.a[i_l]

    def bc4(v, n=4):
        return v.re("p (o t) -> p o t", o=1).bc([128, n, 128])

    def bcl(v, n, m):
        return _v(v).re("p (h o) -> p h o", o=1).bc([128, n, m])

    def recip(t):
        K.op(K.dve, lambda: nc.vector.reciprocal(_v(t).ap, _v(t).ap), [t], [t])

    ov = K.sb([128, OV_N], F32, "ov", esP)
    K.dma(K.sp, ov, self.ovec.a[i_l])
    w2t = K.sb([128, 512], F32, "w2t", esP)
    a2t = K.sb([128, 512], F32, "a2t", esP)
    g2t = K.sb([128, 512], F32, "g2t", esP)
    K.dma(K.sp, w2t, self.rw2.a[i_l])
    K.dma(K.sp, a2t, self.ra2.a[i_l])
    K.dma(K.sp, g2t, self.rg2.a[i_l])
    nexpa = K.sb([128, 8], F32, "nexpa", esP)
    K.actf(nexpa, ov[:, OV_ALOG:OV_ALOG + 8], AF.Exp)
    K.ts(nexpa, nexpa, -1.0, ALU.mult)
    gneps = K.sb([128, 1], F32, "gneps", esP)
    K.memset(gneps, 64e-5)
    convw = ov[:, OV_CONV:OV_CONV + 32]
    gain = self.vec[:, 8 * l:8 * (l + 1)]
    STs = [K.sb([128, 512], F32, "ST", esP) for _ in range(NSEQ)]
    Hss = [K.sb([128, 4, 128], F32, "Hs", esP) for _ in range(NSEQ)]
    histC = [K.sb([128, 8, 3], F32, "histC", esP) for _ in range(NSEQ)]
    histP = [K.sb([128, 14, 1], F32, "histP", esP) for _ in range(NSEQ)]
    for s_ in range(NSEQ):
        K.memset(STs[s_], 0.0)
        K.memset(Hss[s_], 0.0)
        K.memset(histC[s_], 0.0)
        K.memset(histP[s_], 0.0)

    def run_variant(hp, tiles):
        if not tiles:
            return
        es = contextlib.ExitStack()
        AD = F32 if hp else BF16
        ID = C["ident"] if hp else CB["ident"]
        if hp:
            stg = [K.sb([128, 8, 512], F32, "stg", es) for _ in range(2)]
            cnt = [0]

            def _stream(dram2d, col0, n):
                t = stg[cnt[0] % 2]
                cnt[0] += 1
                K.dma(K.sp, t[:, :, 0:n], dram2d[:, col0:col0 + n].re("(k p) n -> p k n", p=128))
                return lambda k: t[:, k, 0:n]

            wblock = lambda col0, n: _stream(WIN, col0, n)
            wblock32 = wblock
            woblock = lambda n0: _stream(WOUT, n0 * 128, 128)
        else:
            w_in = K.sb([128, 8, ODD_IN], BF16, "w_in", es)
            for k in range(8):
                K.dma(K.pool, w_in[:, k, :], WIN[k * 128:(k + 1) * 128, :])
            w_out = K.sb([128, 8, D], BF16, "w_out", es)
            K.dma(K.pool, w_out, WOUT.re("(c p) n -> p c n", p=128))
            wg = K.sb([128, 8, 136], F32, "wg", es)
            for k in range(8):
                K.dma(K.sp, wg[:, k, 0:8], WIN[k * 128:(k + 1) * 128, 1536:1544])
                K.dma(K.sp, wg[:, k, 8:136], WIN[k * 128:(k + 1) * 128, 3080:3208])
            wblock = lambda col0, n: (lambda k: w_in[:, k, col0:col0 + n])
            wblock32 = lambda col0, n: (lambda k: wg[:, k, 0:8])
            woblock = lambda n0: (lambda c: w_out[:, c, n0 * 128:(n0 + 1) * 128])

        def sb(shape, name, dt=F32):
            return K.sb(shape, dt, name, es)

        h = sb([128, 8, P], "h"); hn32 = sb([128, 8, P], "hn32"); hn = hn32 if hp else sb([128, 8, P], "hn", BF16)
        sq = sb([128, 8, P], "sq", BF16); rp = sb([128, P], "rp")
        preC = sb([128, 8, P + 3], "preC"); cacc = sb([128, 8, P], "cacc"); xbc = sb([128, 8, P], "xbc")
        pdp = sb([128, 14, P + 1], "pdp"); pdm = sb([128, 14, P], "pdm"); pdd = pdm
        zs = sb([128, 512], "zs"); oFM = sb([128, 8, P], "oFM", AD)
        dt8 = sb([128, 8], "dt8"); a8 = sb([128, 8], "a8"); acc8 = sb([128, 8], "acc8")
        eac8 = sb([128, 8], "eac8"); edl8 = sb([128, 8], "edl8"); cd8 = sb([128, 8], "cd8"); dte8 = sb([128, 8], "dte8")
        segT = sb([128, 8, 128], "segT"); MT = sb([128, 8, 128], "MT", AD)
        xTs = sb([128, 512], "xTs"); xdt = sb([128, 8, 64], "xdt", AD); xdte = sb([128, 8, 64], "xdte", AD)
        Bb = sb([128, 256], "Bb", AD); Cb = sb([128, 2, 128], "Cb", AD)
        STb = None if hp else sb([128, 512], "STb", BF16)
        y1 = sb([128, 512], "y1"); y2 = sb([128, 512], "y2"); ss2 = sb([128, 2], "ss2"); junk = sb([128, 256], "junk")
        ycT = sb([128, 512], "ycT", AD)
        txw = sb([64, P], "txw"); ldT = sb([128, 512], "ldT"); aF = sb([128, 4, P], "aF"); sg = sb([128, P], "sg")
        kp = sb([128, 4, P], "kp"); kq = sb([128, 4, P], "kq"); kk = kq; bm = sb([128, 4, P], "bm"); gl4 = sb([128, 4], "gl4")
        sqk = sb([128, 4, P], "sqk"); lg = sb([128, 4, P], "lg"); lgx = sb([128, 4, P], "lgx"); nmid = sb([128, 4], "nmid")
        e_r = sb([128, 4, P], "e_r"); e_a = sb([128, 4, P], "e_a"); e_b = sb([128, 4, P], "e_b")
        e_r0 = sb([128, 4, P], "e_r0"); e_a0 = sb([128, 4, P], "e_a0")
        rt_ = e_r; r0_ = e_r0; at_ = e_a; a0_ = e_a0; bt_ = e_b; kt_ = xbc[:, 4:8, :]
        edlT = sb([128, 512], "edlT"); kdT = sb([128, 512], "kdT"); bdT = sb([128, 512], "bdT"); vT = sb([128, 512], "vT")
        A = [segT[:, 0:4, :], cacc[:, 0:4, :]]
        AT = [segT[:, 4:8, :], cacc[:, 4:8, :]]
        PT = xbc[:, 0:4, :]
        AakT = y1.a.re("p (h t) -> p h t", h=4); RbT = y2.a.re("p (h t) -> p h t", h=4); RkT = xTs.a.re("p (h t) -> p h t", h=4)
        Xs = sb([128, 4, 64], "Xs"); Us = sb([128, 8, 64], "Us")
        yd = lg.a.re("p c t -> p (c t)").re("p (h q) -> p h q", h=8); ym = lgx.a.re("p c t -> p (c t)").re("p (h q) -> p h q", h=8); mean8 = sb([128, 8], "mean8"); var8 = sb([128, 8], "var8")
        rkr = sb([128, 4, P], "rkr"); rks = sb([128, 8], "rks"); gT = sb([128, 512], "gT"); ydT = sb([128, 512], "ydT", AD)

        for (s_, it) in tiles:
            ST, Hs = STs[s_], Hss[s_]
            K.dma(K.sp, h, self.hview(src, it * P, P))
            K.cp(preC[:, :, 0:3], histC[s_], E=K.pool)
            K.cp(pdp[:, :, 0:1], histP[s_], E=K.pool)
            if not hp:
                K.cp(STb, ST, E=K.pool)
            Sst = ST if hp else STb
            self.rmsnorm_fm(h, gain, [hn32] if hp else [hn, hn32], sq, rp, n=P)
            for c4 in range(2):
                wb = wblock(512 + c4 * 512, 512)
                b = self.bank()
                for j in range(4):
                    for k in range(8):
                        K.mm(b[:, j * P:(j + 1) * P], wb(k)[:, j * 128:(j + 1) * 128], hn[:, k, :],
                             start=(k == 0), stop=(k == 7))
                K.cp(preC[:, c4 * 4:(c4 + 1) * 4, 3:3 + P], b.a.re("p (j t) -> p j t", j=4), E=K.act)
            for c4 in range(4):
                b = self.bank()
                nj = 4 if c4 < 3 else 2
                wb = wblock(1544 + c4 * 512, nj * 128)
                for j in range(nj):
                    c = c4 * 4 + j
                    for k in range(8):
                        if c == 12 and not hp:
                            K.mm(b[:, j * P:(j + 1) * P], wg[:, k, 8:136], hn32[:, k, :], start=(k == 0), stop=(k == 7))
                        else:
                            K.mm(b[:, j * P:(j + 1) * P], wb(k)[:, j * 128:(j + 1) * 128], hn[:, k, :],
                                 start=(k == 0), stop=(k == 7))
                K.cp(pdp[:, c4 * 4:c4 * 4 + nj, 1:1 + P], b[:, 0:nj * P].re("p (j t) -> p j t", j=nj), E=K.act)
            wb = wblock(0, 512)
            bz = self.bank()
            for k in range(8):
                K.mm(bz, hn[:, k, :], wb(k), start=(k == 0), stop=(k == 7))
            K.actf(zs, bz, AF.Silu)
            wb = wblock32(1536, 8)
            bg = self.bank()
            for k in range(8):
                K.mm(bg[:, 0:8], hn32[:, k, :], wb(k), start=(k == 0), stop=(k == 7))
            K.tt(dt8, bg[:, 0:8], ov[:, OV_DTB:OV_DTB + 8], ALU.add)
            K.ts(dt8, dt8, 60.0, ALU.min)
            K.actf(dt8, dt8, AF.Exp)
            K.actf(dt8, dt8, AF.Ln, bias=1.0)
            K.tt(a8, dt8, nexpa, ALU.mult)
            for c in range(8):
                K.actf(cacc[:, c, :], preC[:, c, 0:P], AF.Copy, scale=convw[:, c * 4:c * 4 + 1])
                for j in range(1, 4):
                    K.stt(cacc[:, c, :], preC[:, c, j:j + P], convw[:, c * 4 + j:c * 4 + j + 1], cacc[:, c, :],
                          ALU.mult, ALU.add)
                K.actf(xbc[:, c, :], cacc[:, c, :], AF.Silu, bias=ov[:, OV_CONVB + c:OV_CONVB + c + 1])
            K.cp(histC[s_], preC[:, :, P:P + 3], E=K.pool)
            Atri = cacc
            K.tt(Atri, bc4(C["tri"], 8), bcl(a8, 8, 128), ALU.mult)
            bsm = self.bank()
            K.mm(bsm[:, 0:8], C["tri"], a8)
            K.mm(bsm[:, 8:16], C["ones"], a8)
            K.cp(acc8, bsm[:, 0:8])
            K.actf(eac8, acc8, AF.Exp)
            K.actf(cd8, bsm[:, 8:16], AF.Exp)
            K.tt(edl8, bsm[:, 8:16], acc8, ALU.subtract)
            K.actf(edl8, edl8, AF.Exp)
            K.tt(dte8, dt8, edl8, ALU.mult)
            for hg in range(2):
                br = self.bank()
                K.mm(br, C["ones"], Atri[:, hg * 4:(hg + 1) * 4, :])
                for hh in range(4):
                    hd = hg * 4 + hh
                    K.ts(segT[:, hd, :], br[:, hh * 128:(hh + 1) * 128], acc8[:, hd:hd + 1], ALU.subtract, 0.0, ALU.min)
            K.actf(segT, segT, AF.Exp)
            K.tt(segT, segT, bc4(C["tri"], 8), ALU.mult, E=K.pool)
            bcb = self.bank()
            for g in range(2):
                K.mm(bcb[:, g * 128:(g + 1) * 128], xbc[:, 4 + g, :], xbc[:, 6 + g, :])
            for g in range(2):
                K.tt(MT[:, g * 4:(g + 1) * 4, :], segT[:, g * 4:(g + 1) * 4, :],
                     bcb[:, g * 128:(g + 1) * 128].re("p (o t) -> p o t", o=1).bc([128, 4, 128]), ALU.mult)
            K.cp(Cb, xbc[:, 6:8, :], E=K.pool)
            bxt = self.bank()
            for c in range(4):
                K.mm(bxt[:, c * 128:(c + 1) * 128], xbc[:, c, :], C["ident"])
            bbt = self.bank()
            for g in range(2):
                K.mm(bbt[:, g * 128:(g + 1) * 128], xbc[:, 4 + g, :], C["ident"])
            K.cp(xTs, bxt, E=K.act)
            K.cp(Bb, bbt[:, 0:256], E=K.act)
            K.tt(xdt, xTs.a.re("p (h q) -> p h q", h=8), bcl(dt8, 8, 64), ALU.mult)
            K.tt(xdte, xTs.a.re("p (h q) -> p h q", h=8), bcl(dte8, 8, 64), ALU.mult)
            by = self.bank()
            for hd in range(8):
                K.mm(by[:, hd * 64:(hd + 1) * 64], MT[:, hd, :], xdt[:, hd, :])
            bi = self.bank()
            for g in range(2):
                K.mm(bi[:, g * 256:(g + 1) * 256], Cb[:, g, :], Sst[:, g * 256:(g + 1) * 256])
            bst = self.bank()
            for g in range(2):
                K.mm(bst[:, g * 256:(g + 1) * 256], Bb[:, g * 128:(g + 1) * 128],
                     xdte[:, g * 4:(g + 1) * 4, :].re("p h q -> p (h q)"))
            K.tt(y1.a.re("p (h q) -> p h q", h=8), bi.a.re("p (h q) -> p h q", h=8), bcl(eac8, 8, 64), ALU.mult)
            K.tt(y1, y1, by, ALU.add)
            K.tt(y2, xTs, ov[:, OV_D:OV_D + 512], ALU.mult, E=K.pool)
            K.tt(y1, y1, y2, ALU.add)
            K.tt(y1, y1, zs, ALU.mult)
            K.tt(ST.a.re("p (h q) -> p h q", h=8), ST.a.re("p (h q) -> p h q", h=8), bcl(cd8, 8, 64), ALU.mult)
            K.tt(ST, ST, bst, ALU.add)
            for g in range(2):
                K.actf(junk, y1[:, g * 256:(g + 1) * 256], AF.Square, accum=ss2[:, g:g + 1])
            K.actf(ss2, ss2, AF.Sqrt, bias=self.epst, scale=1.0 / 256.0)
            recip(ss2)
            K.tt(y2, y1, ov[:, OV_SNW:OV_SNW + 512], ALU.mult, E=K.pool)
            for g in range(2):
                K.ts(ycT[:, g * 256:(g + 1) * 256], y2[:, g * 256:(g + 1) * 256], ss2[:, g:g + 1], ALU.mult)
            btr = self.bank()
            for hh in range(4):
                K.mm(btr[:, hh * 128:(hh + 1) * 128], ycT[:, hh * 128:(hh + 1) * 128], ID)
            K.cp(oFM[:, 0:4, :], btr.a.re("p (h t) -> p h t", h=4), E=K.act)

            K.tt(pdd, pdp[:, :, 0:P], pdp[:, :, 1:P + 1], ALU.subtract)
            K.tt(pdd, pdd, bcl(ov[:, OV_MU:OV_MU + 14], 14, P), ALU.mult, E=K.pool)
            K.tt(pdm, pdd, pdp[:, :, 1:P + 1], ALU.add)
            K.cp(histP[s_], pdp[:, :, P:P + 1], E=K.pool)
            R_, K_, V_ = pdm[:, 0:4, :], pdm[:, 4:8, :], pdm[:, 8:12, :]
            K.actf(txw, pdm[0:64, 12, :], AF.Tanh)
            K.actf(sg, pdm[:, 13, :], AF.Sigmoid)
            bw = self.bank()
            K.mm(bw, txw, w2t[0:64, :])
            K.tt(ldT, bw, ov[:, OV_W0:OV_W0 + 512], ALU.add)
            K.actf(ldT, ldT, AF.Sigmoid)
            K.ts(ldT, ldT, -float(np.exp(-0.5)), ALU.mult)
            ba = self.bank()
            for c in range(4):
                K.mm(ba[:, c * P:(c + 1) * P], a2t[64:128, c * 128:(c + 1) * 128], pdm[64:128, 12, :])
            for c in range(4):
                K.actf(aF[:, c, :], ba[:, c * P:(c + 1) * P], AF.Sigmoid, bias=ov[:, OV_A0 + c:OV_A0 + c + 1])
            bgm = self.bank()
            K.mm(bgm, sg, g2t)
            K.cp(gT, bgm, E=K.act)
            for c in range(4):
                K.ts(kp[:, c, :], aF[:, c, :], 1.0, ALU.subtract, ov[:, OV_KA + c:OV_KA + c + 1], ALU.mult)
                K.stt(kp[:, c, :], kp[:, c, :], 1.0, pdm[:, 4 + c, :], ALU.add, ALU.mult)
                K.ts(kq[:, c, :], pdm[:, 4 + c, :], ov[:, OV_KK + c:OV_KK + c + 1], ALU.mult, E=K.pool)
                K.stt(rkr[:, c, :], pdm[:, c, :], ov[:, OV_RK + c:OV_RK + c + 1], kp[:, c, :], ALU.mult, ALU.mult)
            K.tt(sqk, kq, kq, ALU.mult, E=K.pool)
            bss = self.bank()
            K.mm(bss, C["bones"], sqk)
            K.actf(sqk, bss.a.re("p (c t) -> p c t", c=4), AF.Sqrt, bias=self.epst)
            recip(sqk)
            K.tt(kk, kq, sqk, ALU.mult)
            K.tt(bm, kk, aF, ALU.mult, E=K.pool)
            blg = self.bank()
            blx = self.bank()
            for c in range(4):
                K.mm(blg[:, c * P:(c + 1) * P], ldT[:, c * 128:(c + 1) * 128], C["tri"])
                K.mm(blx[:, c * P:(c + 1) * P], ldT[:, c * 128:(c + 1) * 128], C["tris"])
            K.cp(lg, blg.a.re("p (c t) -> p c t", c=4))
            K.cp(lgx, blx.a.re("p (c t) -> p c t", c=4), E=K.act)
            K.ts(nmid, lg[:, :, 63], -1.0, ALU.mult)
            for c in range(4):
                K.actf(e_r[:, c, :], lg[:, c, :], AF.Exp, bias=nmid[:, c:c + 1])
                K.actf(e_a[:, c, :], lgx[:, c, :], AF.Exp, bias=nmid[:, c:c + 1])
                K.actf(e_b[:, c, :], lg[:, c, :], AF.Exp, bias=lg[:, c, 63:64], scale=-1.0)
            K.actf(e_r0, lg, AF.Exp)
            K.actf(e_a0, lgx, AF.Exp)
            K.cp(gl4, e_r0[:, :, 127])
            K.tt(rt_, R_, e_r, ALU.mult)
            K.tt(r0_, R_, e_r0, ALU.mult, E=K.pool)
            K.stt(at_, kk, -1.0, e_a, ALU.mult, ALU.mult)
            K.stt(a0_, kk, -1.0, e_a0, ALU.mult, ALU.mult)
            K.tt(kt_, kp, e_b, ALU.mult)
            K.tt(bt_, bm, e_b, ALU.mult, E=K.pool)
            bed = self.bank()
            K.mm(bed, C["ustr"], ldT)
            K.actf(edlT, bed, AF.Exp)
            for (srcT, dstT, scl) in ((kp, kdT, True), (bm, bdT, True), (V_, vT, False)):
                b = self.bank()
                for c in range(4):
                    K.mm(b[:, c * 128:(c + 1) * 128], srcT[:, c, :], C["ident"])
                if scl:
                    K.tt(dstT, b, edlT, ALU.mult)
                else:
                    K.cp(dstT, b, E=K.act)
            brk = self.bank()
            for c in range(4):
                K.mm(brk[:, 0:8], rkr[:, c, :], ov[:, OV_SEL + c * 8:OV_SEL + (c + 1) * 8], start=(c == 0), stop=(c == 3))
            K.cp(rks, brk[:, 0:8])
            byd = self.ps[7]
            for hg in range(2):
                b1, b2, b3, b4, b5 = [self.bank() for _ in range(5)]
                for hh in range(4):
                    hd = hg * 4 + hh
                    hq, pr = hd // 2, slice((hd % 2) * 64, (hd % 2) * 64 + 64)
                    sl = slice(hh * 128, (hh + 1) * 128)
                    K.mm(b1[:, sl], bt_[pr, hq, :], at_[pr, hq, :])
                    K.mm(b2[:, sl], at_[pr, hq, :], bt_[pr, hq, :])
                    K.mm(b3[:, sl], kt_[pr, hq, :], at_[pr, hq, :])
                    K.mm(b4[:, sl], bt_[pr, hq, :], rt_[pr, hq, :])
                    K.mm(b5[:, sl], kt_[pr, hq, :], rt_[pr, hq, :])
                K.tt(AT[0], b1.a.re("p (h t) -> p h t", h=4), bc4(C["tris"]), ALU.mult)
                K.tt(A[0], b2.a.re("p (h t) -> p h t", h=4), bc4(C["ustr"]), ALU.mult)
                K.tt(AakT, b3.a.re("p (h t) -> p h t", h=4), bc4(C["tris"]), ALU.mult)
                K.tt(RbT, b4.a.re("p (h t) -> p h t", h=4), bc4(C["tri"]), ALU.mult)
                K.tt(RkT, b5.a.re("p (h t) -> p h t", h=4), bc4(C["tri"]), ALU.mult)
                K.tt(PT, AT[0], bc4(C["ident"]), ALU.add, E=K.pool)
                cur = 0
                for itr in range(6):
                    nxt = 1 - cur
                    c1 = self.bank()
                    c2 = self.bank()
                    for hh in range(4):
                        sl = slice(hh * 128, (hh + 1) * 128)
                        K.mm(c1[:, sl], AT[cur][:, hh, :], A[cur][:, hh, :])
                        K.mm(c2[:, sl], A[cur][:, hh, :], AT[cur][:, hh, :])
                    K.cp(A[nxt], c1.a.re("p (h t) -> p h t", h=4), E=K.act)
                    K.cp(AT[nxt], c2.a.re("p (h t) -> p h t", h=4), E=K.dve)
                    c3 = self.bank()
                    for hh in range(4):
                        K.mm(c3[:, hh * 128:(hh + 1) * 128], A[nxt][:, hh, :], PT[:, hh, :])
                    K.tt(PT, PT, c3.a.re("p (h t) -> p h t", h=4), ALU.add)
                    cur = nxt
                bx = self.bank()
                for hh in range(4):
                    hd = hg * 4 + hh
                    hq, pr = hd // 2, slice((hd % 2) * 64, (hd % 2) * 64 + 64)
                    vs = slice((hd % 2) * 64, (hd % 2) * 64 + 64)
                    K.mm(bx[:, hh * 64:(hh + 1) * 64], a0_[pr, hq, :], Hs[pr, hq, vs], start=True, stop=False)
                    K.mm(bx[:, hh * 64:(hh + 1) * 64], AakT[:, hh, :], vT[:, hd * 64:(hd + 1) * 64], start=False, stop=True)
                K.cp(Xs, bx[:, 0:256].re("p (h q) -> p h q", h=4))
                bu = self.bank()
                for hh in range(4):
                    K.mm(bu[:, hh * 64:(hh + 1) * 64], PT[:, hh, :], Xs[:, hh, :])
                K.cp(Us[:, hg * 4:(hg + 1) * 4, :], bu[:, 0:256].re("p (h q) -> p h q", h=4))
                for hh in range(4):
                    hd = hg * 4 + hh
                    hq, pr = hd // 2, slice((hd % 2) * 64, (hd % 2) * 64 + 64)
                    vs = slice((hd % 2) * 64, (hd % 2) * 64 + 64)
                    o = byd[:, hd * 64:(hd + 1) * 64]
                    K.mm(o, r0_[pr, hq, :], Hs[pr, hq, vs], start=True, stop=False)
                    K.mm(o, RbT[:, hh, :], Us[:, hd, :], start=False, stop=False)
                    K.mm(o, RkT[:, hh, :], vT[:, hd * 64:(hd + 1) * 64], start=False, stop=True)
            bh = self.bank()
            for hq in range(4):
                o = bh[:, hq * 128:(hq + 1) * 128]
                K.mm(o, bdT[:, hq * 128:(hq + 1) * 128], Us[:, 2 * hq:2 * hq + 2, :].re("p h q -> p (h q)"), start=True, stop=False)
                K.mm(o, kdT[:, hq * 128:(hq + 1) * 128], vT[:, hq * 128:(hq + 1) * 128], start=False, stop=True)
            for hq in range(4):
                K.stt(Hs[:, hq, :], Hs[:, hq, :], gl4[:, hq:hq + 1], bh[:, hq * 128:(hq + 1) * 128], ALU.mult, ALU.add)
            K.cp(yd, byd.a.re("p (h q) -> p h q", h=8), E=K.act)
            K.red(mean8, yd)
            K.ts(mean8, mean8, 1.0 / 64.0, ALU.mult)
            K.tt(ym, yd, bcl(mean8, 8, 64), ALU.subtract)
            K.tt(yd, ym, ym, ALU.mult, E=K.pool)
            K.red(var8, yd)
            K.actf(var8, var8, AF.Sqrt, bias=gneps, scale=1.0 / 64.0)
            recip(var8)
            K.tt(ym, ym, bcl(var8, 8, 64), ALU.mult)
            ymf = _v(ym).re("p h q -> p (h q)")
            K.tt(ymf, ymf, ov[:, OV_GNW:OV_GNW + 512], ALU.mult, E=K.pool)
            K.tt(ymf, ymf, ov[:, OV_GNB:OV_GNB + 512], ALU.add)
            K.tt(yd, vT.a.re("p (h q) -> p h q", h=8), bcl(rks, 8, 64), ALU.mult)
            K.tt(ym, ym, yd, ALU.add)
            K.tt(ydT, ymf, gT, ALU.mult)
            btr = self.bank()
            for hh in range(4):
                K.mm(btr[:, hh * 128:(hh + 1) * 128], ydT[:, hh * 128:(hh + 1) * 128], ID)
            K.cp(oFM[:, 4:8, :], btr.a.re("p (h t) -> p h t", h=4), E=K.act)
            for half in range(2):
                b = self.bank()
                for j in range(4):
                    n = half * 4 + j
                    wo = woblock(n)
                    for c in range(8):
                        K.mm(b[:, j * P:(j + 1) * P], wo(c), oFM[:, c, :], start=(c == 0), stop=(c == 7))
                K.tt(h[:, half * 4:(half + 1) * 4, :], h[:, half * 4:(half + 1) * 4, :],
                     b.a.re("p (j t) -> p j t", j=4), ALU.add)
            K.dma(K.act, self.hview(dst, it * P, P), h)
        K.barrier()
        es.close()

    run_variant(True, [(s_, s_ * tps) for s_ in range(NSEQ)])
    run_variant(False, [(s_, s_ * tps + j) for s_ in range(NSEQ) for j in range(1, tps)])
    K.barrier()
    esP.close()


Model.odd_phase = _odd_phase


def kernel(**inputs):
    inp = {k: np.asarray(v) for k, v in inputs.items()}
    x = inp["x"]
    M = Model(n_layers=4, seq=x.shape[1], mix=True)
    nc = M.build()
    maps = make_in_maps(inp, M, x, 8)
    res = run_bass_kernel_spmd(nc, maps, core_ids=list(range(8)))
    out = np.concatenate([r["out"].reshape(NSEQ, x.shape[1], D) for r in res.results], axis=0)
    return out.astype(np.float32)
```

```python
import contextlib
import os
import numpy as np
import ml_dtypes
import concourse.bass as bass
import concourse.mybir as mybir
from concourse.bass_utils import run_bass_kernel_spmd

F32 = mybir.dt.float32
BF16 = mybir.dt.bfloat16
AF = mybir.ActivationFunctionType
ALU = mybir.AluOpType
AX = mybir.AxisListType

EPOCH = 30000


class Tl:
    def __init__(self, h, name):
        self.h = h
        self.name = name
        self.w = None
        self.r = {}
        self.dsem = None
        self.dcnt = 0
        self.psum = False

    def __getitem__(self, idx):
        return V(self, self.h[idx])

    @property
    def a(self):
        return V(self, self.h[:])


class V:
    def __init__(self, t, ap):
        self.t = t
        self.ap = ap

    def __getitem__(self, idx):
        return V(self.t, self.ap[idx])

    def bitcast(self, dt):
        return V(self.t, self.ap.bitcast(dt))

    def bc(self, shape):
        return V(self.t, self.ap.to_broadcast(list(shape)))

    def re(self, s, **kw):
        return V(self.t, self.ap.rearrange(s, **kw))


def _v(x):
    return x.a if isinstance(x, Tl) else x


class Eng:
    def __init__(self, K, name, eng):
        self.K = K
        self.name = name
        self.eng = eng
        self.epoch = -1
        self.cnt = EPOCH
        self.sem = None
        self.known = {}

    def tick(self, ins):
        if self.cnt >= EPOCH:
            self.epoch += 1
            self.cnt = 0
            self.sem = self.K.new_sem(f"s_{self.name}_{self.epoch}")
        self.cnt += 1
        ins.then_inc(self.sem, 1)
        return (id(self.sem), self.sem, self.cnt, self.name)


class KB:
    def __init__(self, nc):
        self.nc = nc
        self.es = contextlib.ExitStack()
        self.pe = Eng(self, "pe", nc.tensor)
        self.dve = Eng(self, "dve", nc.vector)
        self.act = Eng(self, "act", nc.scalar)
        self.pool = Eng(self, "pool", nc.gpsimd)
        self.sp = Eng(self, "sp", nc.sync)
        self.engs = [self.pe, self.dve, self.act, self.pool, self.sp]
        self.nsem = 0
        self.ntile = 0
        self.all_dma = {}
        self.ninst = 0

    def new_sem(self, name):
        self.nsem += 1
        return self.es.enter_context(self.nc.semaphore(f"{name}_{self.nsem}"))

    def sb(self, shape, dt=F32, name="t", es=None):
        self.ntile += 1
        nm = f"{name}_{self.ntile}"
        h = (es or self.es).enter_context(self.nc.sbuf_tensor(nm, list(shape), dt))
        return Tl(h, nm)

    def psum(self, shape, dt=F32, name="p", es=None):
        self.ntile += 1
        nm = f"{name}_{self.ntile}"
        h = (es or self.es).enter_context(self.nc.psum_tensor(nm, list(shape), dt))
        t = Tl(h, nm)
        t.psum = True
        return t

    def dram(self, name, shape, dt=F32, kind="Internal"):
        h = self.nc.dram_tensor(name, list(shape), dt, kind=kind)
        return Tl(h.ap(), name)

    def _wait(self, E, toks):
        best = {}
        for tk in toks:
            if tk is None:
                continue
            sid, sem, val, who = tk
            if who == E.name and E is self.pe:
                continue
            if E.known.get(sid, 0) >= val:
                continue
            if best.get(sid, (None, 0))[1] < val:
                best[sid] = (sem, val)
        for sid, (sem, val) in best.items():
            E.eng.wait_ge(sem, val)
            E.known[sid] = val
            self.ninst += 1

    skip = False

    def op(self, E, fn, reads, writes):
        if self.skip:
            return None
        toks = []
        rt = [(_v(x).t) for x in reads if x is not None and not isinstance(x, (int, float))]
        wt = [(_v(x).t) for x in writes if x is not None]
        for t in rt:
            toks.append(t.w)
            if t.psum:
                for k, tk in t.r.items():
                    if k != E.name:
                        toks.append(tk)
        for t in wt:
            toks.append(t.w)
            for k, tk in t.r.items():
                if k == E.name:
                    continue
                toks.append(tk)
        self._wait(E, toks)
        ins = fn()
        tok = E.tick(ins)
        self.ninst += 1
        for t in rt:
            t.r[E.name] = tok
        for t in wt:
            t.w = tok
            t.r = {}
        return tok

    def dma(self, Q, out, in_):
        out = _v(out)
        in_ = _v(in_)
        ot, it = out.t, in_.t
        toks = [it.w, ot.w] + list(ot.r.values())
        self._wait(Q, toks)
        if ot.dsem is None:
            ot.dsem = self.new_sem("d_" + ot.name)
        ins = Q.eng.dma_start(out=out.ap, in_=in_.ap)
        ins.then_inc(ot.dsem, 16)
        ot.dcnt += 16
        tok = (id(ot.dsem), ot.dsem, ot.dcnt, "dma")
        self.ninst += 1
        ot.w = tok
        ot.r = {}
        it.r["dma%d" % id(ot.dsem)] = tok
        self.all_dma[id(ot.dsem)] = tok
        return tok

    def barrier(self):
        toks = list(self.all_dma.values())
        for E in self.engs:
            if E.sem is not None and E.cnt > 0:
                toks.append((id(E.sem), E.sem, E.cnt, E.name + "_b"))
        for E in self.engs:
            self._wait(E, toks)

    def mm(self, out, lhsT, rhs, start=True, stop=True, ser=False):
        out, lhsT, rhs = _v(out), _v(lhsT), _v(rhs)
        if ser and not self.skip and self.pe.sem is not None and self.pe.cnt > 0:
            self.pe.eng.wait_ge(self.pe.sem, self.pe.cnt)
        return self.op(self.pe, lambda: self.nc.tensor.matmul(out.ap, lhsT.ap, rhs.ap, start=start, stop=stop),
                       [lhsT, rhs], [out])

    def actf(self, out, in_, func, bias=None, scale=None, accum=None, E=None):
        out, in_ = _v(out), _v(in_)
        kw = {}
        rd = [in_]
        if bias is not None:
            if isinstance(bias, (int, float)):
                kw["bias"] = float(bias)
            else:
                bias = _v(bias)
                kw["bias"] = bias.ap
                rd.append(bias)
        if scale is not None:
            if isinstance(scale, (int, float)):
                kw["scale"] = float(scale)
            else:
                scale = _v(scale)
                kw["scale"] = scale.ap
                rd.append(scale)
        wr = [out]
        if accum is not None:
            accum = _v(accum)
            kw["accum_out"] = accum.ap
            wr.append(accum)
        return self.op(self.act, lambda: self.nc.scalar.activation(out.ap, in_.ap, func, **kw), rd, wr)

    def _ve(self, E):
        return E or self.dve

    def _sc(self, s, rd):
        if isinstance(s, (int, float)):
            return float(s)
        s = _v(s)
        rd.append(s)
        return s.ap

    def ts(self, out, in0, s1, op0, s2=None, op1=None, E=None, accum=None):
        E = self._ve(E)
        out, in0 = _v(out), _v(in0)
        rd = [in0]
        a1 = self._sc(s1, rd)
        a2 = None if s2 is None else self._sc(s2, rd)
        wr = [out]
        kw = {}
        if accum is not None:
            accum = _v(accum)
            kw["accum_out"] = accum.ap
            wr.append(accum)
        if op1 is None:
            return self.op(E, lambda: E.eng.tensor_scalar(out.ap, in0.ap, a1, None, op0, **kw), rd, wr)
        return self.op(E, lambda: E.eng.tensor_scalar(out.ap, in0.ap, a1, a2, op0, op1, **kw), rd, wr)

    def tt(self, out, in0, in1, op, E=None):
        E = self._ve(E)
        out, in0, in1 = _v(out), _v(in0), _v(in1)
        return self.op(E, lambda: E.eng.tensor_tensor(out.ap, in0.ap, in1.ap, op), [in0, in1], [out])

    def stt(self, out, in0, s, in1, op0, op1, E=None):
        E = self.dve
        out, in0, in1 = _v(out), _v(in0), _v(in1)
        rd = [in0, in1]
        a = self._sc(s, rd)
        return self.op(E, lambda: E.eng.scalar_tensor_tensor(out.ap, in0.ap, a, in1.ap, op0, op1), rd, [out])

    def cp(self, out, in_, E=None):
        E = self._ve(E)
        out, in_ = _v(out), _v(in_)
        if E is self.act:
            return self.op(E, lambda: self.nc.scalar.copy(out.ap, in_.ap), [in_], [out])
        return self.op(E, lambda: E.eng.tensor_copy(out.ap, in_.ap), [in_], [out])

    def memset(self, out, val, E=None):
        E = self._ve(E)
        out = _v(out)
        return self.op(E, lambda: E.eng.memset(out.ap, val), [], [out])

    def red(self, out, in_, op=ALU.add, E=None):
        E = self._ve(E)
        out, in_ = _v(out), _v(in_)
        return self.op(E, lambda: E.eng.tensor_reduce(out.ap, in_.ap, AX.X, op), [in_], [out])


D = 1024
NSEQ = 2
T = 256
CH = 128
EVEN_IN = 3608
ODD_IN = 3336
NEPS = 1e-6


def host_consts():
    i = np.arange(128)
    c = {}
    c["ident"] = np.eye(128, dtype=np.float32)
    c["tri"] = (i[:, None] <= i[None, :]).astype(np.float32)
    c["tris"] = (i[:, None] < i[None, :]).astype(np.float32)
    c["ustr"] = (i[:, None] > i[None, :]).astype(np.float32)
    c["ones"] = np.ones((128, 128), np.float32)
    c["bones"] = ((i[:, None] // 64) == (i[None, :] // 64)).astype(np.float32)
    return c


CONST_NAMES = ["ident", "tri", "tris", "ustr", "ones", "bones"]


class Model:
    def __init__(self, n_layers=4, seq=2048, mix=True, dbg=False):
        self.L = n_layers
        self.seq = seq
        self.NT = NSEQ * seq
        self.mix = mix
        self.dbg = dbg

    def build(self):
        nc = bass.Bass("TRN2", target_bir_lowering=False)
        self.nc = nc
        K = KB(nc)
        self.K = K
        L, NT = self.L, self.NT
        ne = (L + 1) // 2
        no = L // 2
        self.x = K.dram("x", [NT, D], F32, kind="ExternalInput")
        self.out = K.dram("out", [NT, D], F32, kind="ExternalOutput")
        self.hA = K.dram("hA", [D, NT], F32)
        self.hB = K.dram("hB", [D, NT], F32)
        self.cst_d = K.dram("consts", [128, len(CONST_NAMES) * 128], F32, kind="ExternalInput")
        self.w1 = K.dram("mlp_w1", [L, D, 4 * D], F32, kind="ExternalInput")
        self.w2 = K.dram("mlp_w2", [L, 4 * D, D], F32, kind="ExternalInput")
        self.nvec = 3 * L + 1
        self.vec_d = K.dram("vecs", [128, 8 * (2 * L + 1)], F32, kind="ExternalInput")
        if self.mix:
            self.declare_mixer_inputs(ne, no)
        self.cst = K.sb([128, len(CONST_NAMES) * 128], F32, "cst")
        K.dma(K.sp, self.cst, self.cst_d)
        self.C = {n: self.cst[:, i * 128:(i + 1) * 128] for i, n in enumerate(CONST_NAMES)}
        self.cstb = K.sb([128, len(CONST_NAMES) * 128], BF16, "cstb")
        K.cp(self.cstb, self.cst)
        self.CB = {n: self.cstb[:, i * 128:(i + 1) * 128] for i, n in enumerate(CONST_NAMES)}
        self.vec = K.sb([128, 8 * (2 * L + 1)], F32, "vec")
        K.dma(K.sp, self.vec, self.vec_d)
        self.epst = K.sb([128, 1], F32, "eps")
        K.memset(self.epst, NEPS)
        self.ps = [K.psum([128, 512], F32, "bank") for _ in range(8)]
        self.psi = 0

        self.phase0()
        src, dst = self.hA, self.hB
        for l in range(L):
            if self.mix:
                K.barrier()
                if l % 2 == 0:
                    self.even_phase(l, src, dst)
                elif os.environ.get("NOODD"):
                    src, dst = dst, src
                else:
                    self.odd_phase(l, src, dst)
                src, dst = dst, src
            K.barrier()
            self.mlp_phase(l, src, dst)
            src, dst = dst, src
        K.barrier()
        self.final_phase(src)
        K.barrier()
        return nc

    def bank(self):
        b = self.ps[self.psi % 7]
        self.psi += 1
        return b

    def hview(self, hd, t0, n):
        return hd.a.re("(c p) t -> p c t", p=128)[:, :, t0:t0 + n]

    def phase0(self):
        K = self.K
        es = contextlib.ExitStack()
        xt = [K.sb([128, 2, D], F32, "xt", es) for _ in range(2)]
        ht = [K.sb([128, 8, T], F32, "ht", es) for _ in range(2)]
        for i in range(self.NT // T):
            xs, hs = xt[i % 2], ht[i % 2]
            K.dma(K.sp, xs, self.x.a[i * T:(i + 1) * T, :].re("(s p) f -> p s f", p=128))
            for s in range(2):
                for half in range(2):
                    b = self.bank()
                    for j in range(4):
                        c = half * 4 + j
                        K.mm(b[:, j * 128:(j + 1) * 128], xs[:, s, c * 128:(c + 1) * 128], self.C["ident"])
                    dst = hs[:, half * 4:(half + 1) * 4, s * 128:(s + 1) * 128]
                    K.cp(dst, b.a.re("p (j t) -> p j t", j=4), E=(K.dve if half == 0 else K.act))
            K.dma(K.act, self.hview(self.hA, i * T, T), hs)
        K.barrier()
        es.close()

    def rmsnorm_fm(self, h, gain, hn, sq, rp, n=T):
        K = self.K
        K.actf(sq, h, AF.Square)
        b = self.bank()
        for c in range(8):
            K.mm(b[:, 0:n], self.CB["ones"], _v(sq)[:, c, :], start=(c == 0), stop=(c == 7))
        K.actf(rp, b[:, 0:n], AF.Sqrt, bias=self.epst, scale=1.0 / D)
        K.op(K.dve, lambda: self.nc.vector.reciprocal(_v(rp).ap, _v(rp).ap), [rp], [rp])
        for c in range(8):
            for o in (hn if isinstance(hn, (list, tuple)) else [hn]):
                K.stt(_v(o)[:, c, :], _v(h)[:, c, :], gain[:, c:c + 1], rp, ALU.mult, ALU.mult,
                      E=(K.dve if c % 2 == 0 else K.pool))

    def mlp_phase(self, l, src, dst):
        K = self.K
        P = 128
        gain = self.vec[:, 8 * (self.L + l):8 * (self.L + l + 1)]
        W1 = self.w1.a[l]
        W2 = self.w2.a[l]
        tiles = []
        for s_ in range(NSEQ):
            if self.seq > P:
                tiles.append((s_ * self.seq + P, min(T, self.seq) - P))
            for j in range(1, self.seq // T):
                tiles.append((s_ * self.seq + j * T, T))
        esW = contextlib.ExitStack()
        if tiles:
            w1 = K.sb([128, 8, 4 * D], BF16, "w1", esW)
            w2 = K.sb([128, 32, D], BF16, "w2", esW)
            for k in range(8):
                K.dma(K.pool, w1[:, k, :], W1[k * 128:(k + 1) * 128, :])
            for g in range(4):
                K.dma(K.pool, w2[:, g * 8:(g + 1) * 8, :],
                      W2[g * 1024:(g + 1) * 1024, :].re("(m p) n -> p m n", p=128))
        es = contextlib.ExitStack()
        NB = NSEQ * P
        h = K.sb([128, 8, NB], F32, "h", es)
        hn32 = K.sb([128, 8, NB], F32, "hn32", es)
        sq = K.sb([128, 8, NB], BF16, "sq", es)
        rp = K.sb([128, NB], F32, "rp", es)
        act32 = K.sb([128, 32, NB], F32, "act32", es)
        tmp32 = K.sb([128, 2, NB], F32, "tmp32", es)
        stg = [K.sb([128, 2048], F32, "stg", es) for _ in range(2)]
        ns = 0
        for s_ in range(NSEQ):
            K.dma(K.sp, h[:, :, s_ * P:(s_ + 1) * P], self.hview(src, s_ * self.seq, P))
        self.rmsnorm_fm(h, gain, hn32, sq, rp, n=NB)
        for mb in range(16):
            st = stg[ns % 2].a.re("p (k n) -> p k n", k=8)
            ns += 1
            K.dma(K.sp, st, W1[:, mb * 256:(mb + 1) * 256].re("(k p) n -> p k n", p=128))
            b = self.bank()
            for j in range(2):
                for k in range(8):
                    K.mm(b[:, j * NB:(j + 1) * NB], st[:, k, j * 128:(j + 1) * 128], hn32[:, k, :],
                         start=(k == 0), stop=(k == 7))
            K.actf(tmp32, b.a.re("p (j t) -> p j t", j=2), AF.Relu)
            K.tt(act32[:, mb * 2:mb * 2 + 2, :], tmp32, tmp32, ALU.mult, E=K.pool)
        for n in range(8):
            b = self.bank()
            for hf in range(2):
                st = stg[ns % 2].a.re("p (m n) -> p m n", m=16)
                ns += 1
                K.dma(K.sp, st, W2[hf * 2048:(hf + 1) * 2048, n * 128:(n + 1) * 128].re("(m p) n -> p m n", p=128))
                for m in range(16):
                    mm_ = hf * 16 + m
                    K.mm(b[:, 0:NB], st[:, m, :], act32[:, mm_, :], start=(mm_ == 0), stop=(mm_ == 31))
            K.tt(h[:, n, :], h[:, n, :], b[:, 0:NB], ALU.add)
        for s_ in range(NSEQ):
            K.dma(K.act, self.hview(dst, s_ * self.seq, P), h[:, :, s_ * P:(s_ + 1) * P])
        K.barrier()
        es.close()
        if not tiles:
            esW.close()
            return
        es = esW
        hts = [K.sb([128, 8, T], F32, "h", es) for _ in range(2)]
        hn = K.sb([128, 8, T], BF16, "hn", es)
        sq = K.sb([128, 8, T], BF16, "sq", es)
        rp = K.sb([128, T], F32, "rp", es)
        act = K.sb([128, 32, T], BF16, "act", es)
        tmp = [K.sb([128, 2, T], BF16, "tmp", es) for _ in range(2)]
        K.dma(K.sp, hts[0][:, :, 0:tiles[0][1]], self.hview(src, tiles[0][0], tiles[0][1]))
        for i, (t0, n) in enumerate(tiles):
            h = hts[i % 2]
            if i + 1 < len(tiles):
                t1, n1 = tiles[i + 1]
                K.dma(K.sp, hts[(i + 1) % 2][:, :, 0:n1], self.hview(src, t1, n1))
            self.rmsnorm_fm(h[:, :, 0:n], gain, hn[:, :, 0:n], sq[:, :, 0:n], rp[:, 0:n], n=n)
            for mp in range(16):
                b = self.bank()
                for j in range(2):
                    m = mp * 2 + j
                    for k in range(8):
                        K.mm(b[:, j * T:j * T + n], w1[:, k, m * 128:(m + 1) * 128], hn[:, k, 0:n],
                             start=(k == 0), stop=(k == 7))
                tm = tmp[mp % 2]
                K.actf(tm[:, :, 0:n], b.a.re("p (j t) -> p j t", j=2)[:, :, 0:n], AF.Relu)
                K.tt(act[:, mp * 2:mp * 2 + 2, 0:n], tm[:, :, 0:n], tm[:, :, 0:n], ALU.mult, E=K.pool)
            for npair in range(4):
                b = self.bank()
                for j in range(2):
                    nn = npair * 2 + j
                    for m in range(32):
                        K.mm(b[:, j * T:j * T + n], w2[:, m, nn * 128:(nn + 1) * 128], act[:, m, 0:n],
                             start=(m == 0), stop=(m == 31))
                K.tt(h[:, npair * 2:npair * 2 + 2, 0:n], h[:, npair * 2:npair * 2 + 2, 0:n],
                     b.a.re("p (j t) -> p j t", j=2)[:, :, 0:n], ALU.add)
            K.dma(K.act, self.hview(dst, t0, n), h[:, :, 0:n])
        K.barrier()
        es.close()

    def final_phase(self, src):
        K = self.K
        es = contextlib.ExitStack()
        hts = [K.sb([128, 8, T], F32, "h", es) for _ in range(2)]
        hn = K.sb([128, 8, T], F32, "hn", es)
        sq = K.sb([128, 8, T], BF16, "sq", es)
        rp = K.sb([128, T], F32, "rp", es)
        ot = [K.sb([128, 2, D], F32, "ot", es) for _ in range(2)]
        gain = self.vec[:, 8 * (2 * self.L):8 * (2 * self.L + 1)]
        ntile = self.NT // T
        K.dma(K.sp, hts[0], self.hview(src, 0, T))
        for i in range(ntile):
            h = hts[i % 2]
            if i + 1 < ntile:
                K.dma(K.sp, hts[(i + 1) % 2], self.hview(src, (i + 1) * T, T))
            self.rmsnorm_fm(h, gain, hn, sq, rp)
            o = ot[i % 2]
            for s in range(2):
                for half in range(2):
                    b = self.bank()
                    for j in range(4):
                        c = half * 4 + j
                        K.mm(b[:, j * 128:(j + 1) * 128], hn[:, c, s * 128:(s + 1) * 128], self.C["ident"])
                    K.cp(o[:, s, half * 512:(half + 1) * 512], b, E=(K.dve if half == 0 else K.act))
            K.dma(K.act, self.out.a[i * T:(i + 1) * T, :].re("(s p) f -> p s f", p=128), o)
        K.barrier()
        es.close()


def host_vecs(inp, L):
    cols = []
    for l in range(L):
        cols.append(np.asarray(inp["norm_mix_w"][l], np.float32).reshape(8, 128).T)
    for l in range(L):
        cols.append(np.asarray(inp["norm_mlp_w"][l], np.float32).reshape(8, 128).T)
    cols.append(np.asarray(inp["final_norm_w"], np.float32).reshape(8, 128).T)
    return np.ascontiguousarray(np.concatenate(cols, axis=1))


def make_in_maps(inp, M, x, ncores):
    L = M.L
    consts = host_consts()
    cst = np.ascontiguousarray(np.concatenate([consts[n] for n in CONST_NAMES], axis=1))
    vecs = host_vecs(inp, L)
    shared = {"consts": cst, "vecs": vecs,
              "mlp_w1": np.ascontiguousarray(inp["mlp_w1"][:L]), "mlp_w2": np.ascontiguousarray(inp["mlp_w2"][:L])}
    if M.mix:
        shared.update(host_mixer_inputs(inp, L))
    maps = []
    for c in range(ncores):
        d = dict(shared)
        d["x"] = np.ascontiguousarray(x[2 * c:2 * c + 2]).reshape(M.NT, D)
        maps.append(d)
    return maps


EV_CONV = 0
EV_ALOG = 48
EV_DTB = 52
EV_GDNW = 56
EV_GLAW = 568
EV_W2B = 1080
EV_N = 1336


def host_even_vec(inp, i):
    v = np.zeros((128, EV_N), np.float32)
    cw = np.asarray(inp["gdn_conv_w"][i], np.float32)
    v[:, EV_CONV:EV_CONV + 48] = cw.T.reshape(12, 128, 4).transpose(1, 0, 2).reshape(128, 48)
    v[:, EV_ALOG:EV_ALOG + 4] = np.asarray(inp["gdn_a_log"][i])[None, :]
    v[:, EV_DTB:EV_DTB + 4] = np.asarray(inp["gdn_dt_bias"][i])[None, :]
    v[:, EV_GDNW:EV_GDNW + 512] = np.tile(np.asarray(inp["gdn_norm_w"][i]), 4)[None, :]
    v[:, EV_GLAW:EV_GLAW + 512] = np.tile(np.asarray(inp["gla_norm_w"][i]), 4)[None, :]
    v[0:16, EV_W2B:EV_W2B + 256] = np.asarray(inp["gla_gate_w2"][i])
    v[16, EV_W2B:EV_W2B + 256] = np.asarray(inp["gla_gate_b"][i])
    return v


def _even_phase(self, l, src, dst):
    K = self.K
    nc = self.nc
    i_l = l // 2
    esP = contextlib.ExitStack()
    C, CB = self.C, self.CB
    P = 128
    tps = self.seq // P
    WIN = self.even_w_in.a[i_l]
    WOUT = self.even_w_out.a[i_l]

    ev = K.sb([128, EV_N], F32, "ev", esP)
    K.dma(K.sp, ev, self.evec.a[i_l])
    nexpa = K.sb([128, 4], F32, "nexpa", esP)
    K.actf(nexpa, ev[:, EV_ALOG:EV_ALOG + 4], AF.Exp)
    K.ts(nexpa, nexpa, -1.0, ALU.mult)
    convw = ev[:, EV_CONV:EV_CONV + 48]
    gain = self.vec[:, 8 * l:8 * (l + 1)]
    SAs = [K.sb([128, 4, 128], F32, "SA", esP) for _ in range(NSEQ)]
    SBs = [K.sb([128, 2, 256], F32, "SB", esP) for _ in range(NSEQ)]
    hist = [K.sb([128, 12, 3], F32, "hist", esP) for _ in range(NSEQ)]
    for s_ in range(NSEQ):
        K.memset(SAs[s_], 0.0)
        K.memset(SBs[s_], 0.0)
        K.memset(hist[s_], 0.0)

    def bc4(v):
        return v.re("p (o t) -> p o t", o=1).bc([128, 4, 128])

    def bcl(v):
        return _v(v).re("p (h o) -> p h o", o=1).bc([128, 4, 128])

    def run_variant(hp, tiles):
        if not tiles:
            return
        es = contextlib.ExitStack()
        AD = F32 if hp else BF16
        ID = C["ident"] if hp else CB["ident"]

        def t512(name, dt=F32):
            return K.sb([128, 4, 128], dt, name, es)

        if hp:
            stg = [K.sb([128, 8, 512], F32, "stg", es) for _ in range(2)]
            cnt = [0]

            def _stream(dram2d, col0, n):
                t = stg[cnt[0] % 2]
                cnt[0] += 1
                K.dma(K.sp, t[:, :, 0:n], dram2d[:, col0:col0 + n].re("(k p) n -> p k n", p=128))
                return lambda k: t[:, k, 0:n]

            wblock = lambda col0, n: _stream(WIN, col0, n)
            wblock32 = wblock
            woblock = lambda n0: _stream(WOUT, n0 * 128, 128)
        else:
            w_in = K.sb([128, 8, EVEN_IN], BF16, "w_in", es)
            for k in range(8):
                K.dma(K.pool, w_in[:, k, :], WIN[k * 128:(k + 1) * 128, :])
            w_out = K.sb([128, 8, D], BF16, "w_out", es)
            K.dma(K.pool, w_out, WOUT.re("(c p) n -> p c n", p=128))
            wg = K.sb([128, 8, 24], F32, "wg", es)
            for k in range(8):
                K.dma(K.sp, wg[:, k, 0:8], WIN[k * 128:(k + 1) * 128, 2048:2056])
                K.dma(K.sp, wg[:, k, 8:24], WIN[k * 128:(k + 1) * 128, 3592:3608])
            wblock = lambda col0, n: (lambda k: w_in[:, k, col0:col0 + n])
            wblock32 = lambda col0, n: (lambda k: wg[:, k, 0:8]) if col0 == 2048 else (lambda k: wg[:, k, 8:24])
            woblock = lambda n0: (lambda c: w_out[:, c, n0 * 128:(n0 + 1) * 128])

        h = K.sb([128, 8, P], F32, "h", es)
        hn32 = K.sb([128, 8, P], F32, "hn32", es)
        hn = hn32 if hp else K.sb([128, 8, P], BF16, "hn", es)
        sq = K.sb([128, 8, P], BF16, "sq", es)
        rp = K.sb([128, P], F32, "rp", es)
        pre = K.sb([128, 12, P + 3], F32, "pre", es)
        cacc = K.sb([128, 12, P], F32, "cacc", es)
        qkv = K.sb([128, 12, P], F32, "qkv", es)
        sq2 = K.sb([128, 8, P], BF16, "sq2", es)
        rinv = K.sb([128, 8, P], F32, "rinv", es)
        oFM = K.sb([128, 8, P], AD, "oFM", es)
        g1 = K.sb([32, P], F32, "g1", es)
        K.memset(g1, 1.0)
        loga = K.sb([128, 256], F32, "loga", es)
        gkT = K.sb([128, 256], F32, "gkT", es)
        qkF = K.sb([128, 4, 128], F32, "qkF", es)
        gcf = K.sb([128, 2, 128], F32, "gcf", es)
        nmid = K.sb([128, 2], F32, "nmid", es)
        eq = K.sb([128, 2, 128], F32, "eq", es)
        ek = K.sb([128, 2, 128], F32, "ek", es)
        eq0 = K.sb([128, 2, 128], F32, "eq0", es)
        qgm = K.sb([128, 2, 128], F32, "qgm", es)
        kgm = K.sb([128, 2, 128], F32, "kgm", es)
        qg0 = K.sb([128, 2, 128], AD, "qg0", es)
        edlB = K.sb([128, 256], F32, "edlB", es)
        kdB = K.sb([128, 256], AD, "kdB", es)
        gv = K.sb([128, 512], AD, "gv", es)
        grs = K.sb([128, 512], F32, "grs", es)
        zs = K.sb([128, 512], F32, "zs", es)
        attm = t512("attm", AD)
        SBb = None if hp else K.sb([128, 2, 256], BF16, "SBb", es)
        junk = K.sb([128, 128], F32, "junk", es)
        ssB = K.sb([128, 4], F32, "ssB", es)
        Gt = K.sb([128, 512], F32, "Gt", es)
        obT = K.sb([128, 512], AD, "obT", es)
        gr8 = K.sb([128, 8], F32, "gr8", es)
        beta = K.sb([128, 4], F32, "beta", es)
        nbeta = K.sb([128, 4], F32, "nbeta", es)
        gA = K.sb([128, 4], F32, "gA", es)
        gcc = K.sb([128, 4], F32, "gcc", es)
        eg = K.sb([128, 4], F32, "eg", es)
        egl = K.sb([128, 4], F32, "egl", es)
        edl = K.sb([128, 4], F32, "edl", es)
        beg = K.sb([128, 4], F32, "beg", es)
        Gtri = t512("Gtri")
        Dm = t512("Dm")
        DTm = t512("DTm")
        A = [t512("A0"), t512("A1")]
        AT = [t512("AT0"), t512("AT1")]
        PT = t512("PT")
        aqkT = t512("aqkT")
        vb = t512("vb")
        kbg = t512("kbg")
        kdA = t512("kdA")
        wneg = t512("wneg")
        vn = t512("vn")
        qg = t512("qg")
        ssA = K.sb([128, 4], F32, "ssA", es)
        oaT = K.sb([128, 512], AD, "oaT", es)

        for (s_, it) in tiles:
            SA, SB_ = SAs[s_], SBs[s_]
            K.dma(K.sp, h, self.hview(src, it * P, P))
            K.cp(pre[:, :, 0:3], hist[s_], E=K.pool)
            if not hp:
                K.cp(SBb, SB_, E=K.pool)
            Sst = SB_ if hp else SBb
            self.rmsnorm_fm(h, gain, [hn32] if hp else [hn, hn32], sq, rp, n=P)
            for c4 in range(3):
                wb = wblock(c4 * 512, 512)
                b = self.bank()
                for j in range(4):
                    for k in range(8):
                        K.mm(b[:, j * P:(j + 1) * P], wb(k)[:, j * 128:(j + 1) * 128], hn[:, k, :],
                             start=(k == 0), stop=(k == 7))
                K.cp(pre[:, c4 * 4:(c4 + 1) * 4, 3:3 + P], b.a.re("p (j t) -> p j t", j=4), E=K.act)
            shared = {}
            wb = wblock32(2048, 8)
            bg = self.bank()
            for k in range(8):
                K.mm(bg[:, 0:8], hn32[:, k, :], wb(k), start=(k == 0), stop=(k == 7))
            K.cp(gr8, bg[:, 0:8])
            for c in range(12):
                K.actf(cacc[:, c, :], pre[:, c, 0:P], AF.Copy, scale=convw[:, c * 4:c * 4 + 1])
                for j in range(1, 4):
                    K.stt(cacc[:, c, :], pre[:, c, j:j + P], convw[:, c * 4 + j:c * 4 + j + 1], cacc[:, c, :],
                          ALU.mult, ALU.add)
            K.cp(hist[s_], pre[:, :, P:P + 3], E=K.pool)
            K.actf(qkv, cacc, AF.Silu)
            K.tt(sq2, qkv[:, 0:8, :], qkv[:, 0:8, :], ALU.mult, E=K.pool)
            for half in range(2):
                b = self.bank()
                K.mm(b, CB["ones"], sq2[:, half * 4:(half + 1) * 4, :])
                K.actf(rinv[:, half * 4:(half + 1) * 4, :], b.a.re("p (j t) -> p j t", j=4), AF.Sqrt, bias=self.epst)
            K.op(K.dve, lambda: nc.vector.reciprocal(rinv.a.ap, rinv.a.ap), [rinv], [rinv])
            K.stt(qkv[:, 0:4, :], qkv[:, 0:4, :], float(128 ** -0.5), rinv[:, 0:4, :], ALU.mult, ALU.mult)
            K.tt(qkv[:, 4:8, :], qkv[:, 4:8, :], rinv[:, 4:8, :], ALU.mult, E=K.pool)

            def seg_tm1():
                wb = wblock(1536, 512)
                bz = self.bank()
                for k in range(8):
                    K.mm(bz, hn[:, k, :], wb(k), start=(k == 0), stop=(k == 7))
                K.actf(zs, bz, AF.Silu)
                wb = wblock(2312, 256)
                bgk = self.bank()
                for k in range(8):
                    K.mm(bgk[:, 0:256], hn[:, k, :], wb(k), start=(k == 0), stop=(k == 7))
                K.cp(gkT, bgk[:, 0:256], E=K.act)
            def seg_tm2():
                wb = wblock(2568, 512)
                bgv = self.bank()
                for k in range(8):
                    K.mm(bgv, hn[:, k, :], wb(k), start=(k == 0), stop=(k == 7))
                K.cp(gv, bgv, E=K.act)
                wb = wblock(3080, 512)
                bgr = self.bank()
                for k in range(8):
                    K.mm(bgr, hn[:, k, :], wb(k), start=(k == 0), stop=(k == 7))
                K.actf(grs, bgr, AF.Silu)
            def seg_tm3():
                wb = wblock32(3592, 16)
                bl = self.bank()
                for k in range(8):
                    K.mm(bl[0:16, 0:P], wb(k), hn32[:, k, :], start=(k == 0), stop=(k == 7))
                K.cp(g1[0:16, :], bl[0:16, 0:P])
                wb = wblock(2056, 512)
                bqk = self.bank()
                for j in range(4):
                    for k in range(8):
                        K.mm(bqk[:, j * P:(j + 1) * P], wb(k)[:, j * 128:(j + 1) * 128], hn[:, k, :],
                             start=(k == 0), stop=(k == 7))
                K.cp(qkF, bqk.a.re("p (j t) -> p j t", j=4))
            def seg_gla1():
                bx = self.bank()
                K.mm(bx[:, 0:256], g1, ev[0:32, EV_W2B:EV_W2B + 256])
                K.ts(loga, bx[:, 0:256], -20.0, ALU.max)
                K.actf(loga, loga, AF.Exp, scale=-1.0)
                K.actf(loga, loga, AF.Ln, bias=1.0)
                K.ts(loga, loga, -1.0 / 16.0, ALU.mult, -1.0, ALU.max)
                bgc = self.bank()
                for cc in range(2):
                    K.mm(bgc[:, cc * 128:(cc + 1) * 128], loga[:, cc * 128:(cc + 1) * 128], C["tri"])
                K.mm(bgc[:, 256:512], C["ustr"], loga)
                K.cp(gcf, bgc[:, 0:256].re("p (c t) -> p c t", c=2))
                K.actf(edlB, bgc[:, 256:512], AF.Exp)
                K.tt(kdB, gkT, edlB, ALU.mult)
                K.ts(nmid, gcf[:, :, 63], -1.0, ALU.mult)
                for cc in range(2):
                    K.actf(eq[:, cc, :], gcf[:, cc, :], AF.Exp, bias=nmid[:, cc:cc + 1])
                    K.actf(ek[:, cc, :], gcf[:, cc, :], AF.Exp, bias=gcf[:, cc, 63:64], scale=-1.0)
                K.actf(eq0, gcf, AF.Exp)
                K.stt(qgm, qkF[:, 0:2, :], 0.125, eq, ALU.mult, ALU.mult)
                K.tt(kgm, qkF[:, 2:4, :], ek, ALU.mult)
                K.stt(qg0, qkF[:, 0:2, :], 0.125, eq0, ALU.mult, ALU.mult)
            def seg_gla2():
                bat = self.bank()
                for hh in range(4):
                    pr = slice((hh % 2) * 64, (hh % 2) * 64 + 64)
                    K.mm(bat[:, hh * 128:(hh + 1) * 128], kgm[pr, hh // 2, :], qgm[pr, hh // 2, :], ser=True)
                K.tt(attm, bat.a.re("p (h t) -> p h t", h=4), bc4(C["tri"]), ALU.mult)
                bo = self.bank()
                shared['bo'] = bo
                for hh in range(4):
                    pr = slice((hh % 2) * 64, (hh % 2) * 64 + 64)
                    K.mm(bo[:, hh * 128:(hh + 1) * 128], qg0[pr, hh // 2, :],
                         Sst[pr, hh // 2, (hh % 2) * 128:(hh % 2) * 128 + 128], start=True, stop=False, ser=True)
                    K.mm(bo[:, hh * 128:(hh + 1) * 128], attm[:, hh, :], gv[:, hh * 128:(hh + 1) * 128],
                         start=False, stop=True, ser=True)
                bs = self.bank()
                for pp in range(2):
                    K.mm(bs[:, pp * 256:(pp + 1) * 256], kdB[:, pp * 128:(pp + 1) * 128], gv[:, pp * 256:(pp + 1) * 256])
                for pp in range(2):
                    K.stt(SB_[:, pp, :], SB_[:, pp, :], eq0[:, pp, 127:128], bs[:, pp * 256:(pp + 1) * 256],
                          ALU.mult, ALU.add)
            def seg_gla3():
                bo = shared['bo']
                for hh in range(4):
                    K.actf(junk, bo[:, hh * 128:(hh + 1) * 128], AF.Square, accum=ssB[:, hh:hh + 1])
                K.actf(ssB, ssB, AF.Sqrt, bias=self.epst, scale=1.0 / 128.0)
                K.op(K.dve, lambda: nc.vector.reciprocal(ssB.a.ap, ssB.a.ap), [ssB], [ssB])
                K.tt(Gt, grs, ev[:, EV_GLAW:EV_GLAW + 512], ALU.mult, E=K.pool)
                for hh in range(4):
                    sl = slice(hh * 128, (hh + 1) * 128)
                    K.stt(obT[:, sl], bo[:, sl], ssB[:, hh:hh + 1], Gt[:, sl], ALU.mult, ALU.mult)
                btr = self.bank()
                for hh in range(4):
                    K.mm(btr[:, hh * 128:(hh + 1) * 128], obT[:, hh * 128:(hh + 1) * 128], ID)
                K.cp(oFM[:, 4:8, :], btr.a.re("p (h t) -> p h t", h=4), E=K.act)
            segs = [seg_tm1, seg_tm2, seg_tm3, seg_gla1, seg_gla2, seg_gla3]
            K.actf(beta, gr8[:, 0:4], AF.Sigmoid)
            K.ts(nbeta, beta, -1.0, ALU.mult)
            K.tt(gA, gr8[:, 4:8], ev[:, EV_DTB:EV_DTB + 4], ALU.add)
            K.actf(gA, gA, AF.Exp)
            K.actf(gA, gA, AF.Ln, bias=1.0)
            K.tt(gA, gA, nexpa, ALU.mult)
            K.tt(Gtri, bc4(C["tri"]), bcl(gA), ALU.mult)
            bsm = self.bank()
            K.mm(bsm[:, 0:4], C["tri"], gA)
            K.mm(bsm[:, 8:12], C["ones"], gA)
            br = self.bank()
            K.mm(br, C["ones"], Gtri)
            K.cp(gcc, bsm[:, 0:4])
            K.actf(eg, gcc, AF.Exp)
            K.actf(egl, bsm[:, 8:12], AF.Exp)
            K.tt(edl, bsm[:, 8:12], gcc, ALU.subtract)
            K.actf(edl, edl, AF.Exp)
            K.tt(beg, beta, eg, ALU.mult)
            for hh in range(4):
                sl = slice(hh * 128, (hh + 1) * 128)
                K.ts(Dm[:, hh, :], br[:, sl], gcc[:, hh:hh + 1], ALU.subtract, 0.0, ALU.max)
                K.ts(DTm[:, hh, :], br[:, sl], gcc[:, hh:hh + 1], ALU.subtract, 0.0, ALU.min)
            K.actf(Dm, Dm, AF.Exp, scale=-1.0)
            K.actf(DTm, DTm, AF.Exp)
            K.actf(qg, br.a.re("p (h t) -> p h t", h=4), AF.Exp)
            K.tt(Dm, Dm, bc4(C["ustr"]), ALU.mult, E=K.pool)
            K.tt(DTm, DTm, bc4(C["tri"]), ALU.mult, E=K.pool)
            K.tt(qg, qg, qkv[:, 0:4, :], ALU.mult, E=K.pool)
            bkk = self.bank()
            bqt = self.bank()
            for hh in range(4):
                sl = slice(hh * 128, (hh + 1) * 128)
                K.mm(bkk[:, sl], qkv[:, 4 + hh, :], qkv[:, 4 + hh, :])
                K.mm(bqt[:, sl], qkv[:, 4 + hh, :], qkv[:, hh, :])
            for hh in range(4):
                K.stt(A[0][:, hh, :], bkk[:, hh * 128:(hh + 1) * 128], nbeta[:, hh:hh + 1], Dm[:, hh, :],
                      ALU.mult, ALU.mult)
            K.tt(aqkT, bqt.a.re("p (h t) -> p h t", h=4), DTm, ALU.mult)
            bnt = self.bank()
            for hh in range(4):
                K.mm(bnt[:, hh * 128:(hh + 1) * 128], A[0][:, hh, :], C["ident"])
            K.cp(AT[0], bnt.a.re("p (h t) -> p h t", h=4), E=K.act)
            K.tt(PT, bnt.a.re("p (h t) -> p h t", h=4), bc4(C["ident"]), ALU.add)
            bkt = self.bank()
            bvt = self.bank()
            for hh in range(4):
                sl = slice(hh * 128, (hh + 1) * 128)
                K.mm(bkt[:, sl], qkv[:, 4 + hh, :], C["ident"])
                K.mm(bvt[:, sl], qkv[:, 8 + hh, :], C["ident"])
            K.tt(vb, bvt.a.re("p (h t) -> p h t", h=4), bcl(beta), ALU.mult)
            K.tt(kbg, bkt.a.re("p (h t) -> p h t", h=4), bcl(beg), ALU.mult)
            K.tt(kdA, bkt.a.re("p (h t) -> p h t", h=4), bcl(edl), ALU.mult)
            cur = 0
            for itr in range(6):
                nxt = 1 - cur
                b1 = self.bank()
                b2 = self.bank()
                for hh in range(4):
                    sl = slice(hh * 128, (hh + 1) * 128)
                    K.mm(b1[:, sl], AT[cur][:, hh, :], A[cur][:, hh, :])
                    K.mm(b2[:, sl], A[cur][:, hh, :], AT[cur][:, hh, :])
                K.cp(A[nxt], b1.a.re("p (h t) -> p h t", h=4), E=K.act)
                K.cp(AT[nxt], b2.a.re("p (h t) -> p h t", h=4), E=K.dve)
                b3 = self.bank()
                for hh in range(4):
                    K.mm(b3[:, hh * 128:(hh + 1) * 128], A[nxt][:, hh, :], PT[:, hh, :])
                K.tt(PT, PT, b3.a.re("p (h t) -> p h t", h=4), ALU.add)
                cur = nxt
                segs[itr]()
            bw = self.bank()
            for hh in range(4):
                K.mm(bw[:, hh * 128:(hh + 1) * 128], kbg[:, hh, :], PT[:, hh, :])
            K.actf(wneg, bw.a.re("p (h t) -> p h t", h=4), AF.Copy, scale=-1.0)
            bvn = self.bank()
            for hh in range(4):
                sl = slice(hh * 128, (hh + 1) * 128)
                K.mm(bvn[:, sl], PT[:, hh, :], vb[:, hh, :], start=True, stop=False)
                K.mm(bvn[:, sl], wneg[:, hh, :], SA[:, hh, :], start=False, stop=True)
            K.cp(vn, bvn.a.re("p (h t) -> p h t", h=4))
            boa = self.bank()
            for hh in range(4):
                sl = slice(hh * 128, (hh + 1) * 128)
                K.mm(boa[:, sl], qg[:, hh, :], SA[:, hh, :], start=True, stop=False)
                K.mm(boa[:, sl], aqkT[:, hh, :], vn[:, hh, :], start=False, stop=True)
            bsa = self.bank()
            for hh in range(4):
                K.mm(bsa[:, hh * 128:(hh + 1) * 128], kdA[:, hh, :], vn[:, hh, :])
            for hh in range(4):
                K.stt(SA[:, hh, :], SA[:, hh, :], egl[:, hh:hh + 1], bsa[:, hh * 128:(hh + 1) * 128],
                      ALU.mult, ALU.add)
            for hh in range(4):
                K.actf(junk, boa[:, hh * 128:(hh + 1) * 128], AF.Square, accum=ssA[:, hh:hh + 1])
            K.actf(ssA, ssA, AF.Sqrt, bias=self.epst, scale=1.0 / 128.0)
            K.op(K.dve, lambda: nc.vector.reciprocal(ssA.a.ap, ssA.a.ap), [ssA], [ssA])
            K.tt(Gt, zs, ev[:, EV_GDNW:EV_GDNW + 512], ALU.mult, E=K.pool)
            for hh in range(4):
                sl = slice(hh * 128, (hh + 1) * 128)
                K.stt(oaT[:, sl], boa[:, sl], ssA[:, hh:hh + 1], Gt[:, sl], ALU.mult, ALU.mult)
            btr = self.bank()
            for hh in range(4):
                K.mm(btr[:, hh * 128:(hh + 1) * 128], oaT[:, hh * 128:(hh + 1) * 128], ID)
            K.cp(oFM[:, 0:4, :], btr.a.re("p (h t) -> p h t", h=4), E=K.act)

            for half in range(2):
                b = self.bank()
                for j in range(4):
                    wo = woblock(half * 4 + j)
                    for c in range(8):
                        K.mm(b[:, j * P:(j + 1) * P], wo(c), oFM[:, c, :], start=(c == 0), stop=(c == 7))
                K.tt(h[:, half * 4:(half + 1) * 4, :], h[:, half * 4:(half + 1) * 4, :],
                     b.a.re("p (j t) -> p j t", j=4), ALU.add)
            K.dma(K.act, self.hview(dst, it * P, P), h)
        K.barrier()
        es.close()

    run_variant(True, [(s_, s_ * tps) for s_ in range(NSEQ)])
    run_variant(False, [(s_, s_ * tps + j) for s_ in range(NSEQ) for j in range(1, tps)])
    K.barrier()
    esP.close()


Model.even_phase = _even_phase


def _declare_mixer_inputs(self, ne, no):
    K = self.K
    self.even_w_in = K.dram("even_w_in", [ne, D, EVEN_IN], F32, kind="ExternalInput")
    self.even_w_out = K.dram("even_w_out", [ne, D, D], F32, kind="ExternalInput")
    self.evec = K.dram("evec", [ne, 128, EV_N], F32, kind="ExternalInput")
    if no > 0:
        self.declare_odd_inputs(no)


Model.declare_mixer_inputs = _declare_mixer_inputs


def host_mixer_inputs(inp, L):
    ne = (L + 1) // 2
    no = L // 2
    d = {"even_w_in": np.ascontiguousarray(inp["even_w_in"][:ne]),
         "even_w_out": np.ascontiguousarray(inp["even_w_out"][:ne]),
         "evec": np.stack([host_even_vec(inp, i) for i in range(ne)])}
    if no > 0:
        d.update(host_odd_inputs(inp, no))
    return d


OV_CONV = 0
OV_CONVB = 32
OV_DTB = 40
OV_ALOG = 48
OV_D = 56
OV_SNW = 568
OV_MU = 1080
OV_W0 = 1094
OV_A0 = 1606
OV_KK = 1610
OV_KA = 1614
OV_RK = 1618
OV_GNW = 1622
OV_GNB = 2134
OV_SEL = 2646
OV_N = 2678


def host_odd_vec(inp, i):
    v = np.zeros((128, OV_N), np.float32)
    f = lambda a, n: np.asarray(a, np.float32).reshape(n, 128).T
    cw = np.asarray(inp["ssd_conv_w"][i], np.float32)
    v[:, OV_CONV:OV_CONV + 32] = cw.T.reshape(8, 128, 4).transpose(1, 0, 2).reshape(128, 32)
    v[:, OV_CONVB:OV_CONVB + 8] = f(inp["ssd_conv_b"][i], 8)
    v[:, OV_DTB:OV_DTB + 8] = np.asarray(inp["ssd_dt_bias"][i])[None, :]
    v[:, OV_ALOG:OV_ALOG + 8] = np.asarray(inp["ssd_a_log"][i])[None, :]
    v[:, OV_D:OV_D + 512] = np.repeat(np.asarray(inp["ssd_d"][i]), 64)[None, :]
    v[:, OV_SNW:OV_SNW + 512] = np.asarray(inp["ssd_norm_w"][i])[None, :]
    v[:, OV_MU:OV_MU + 14] = f(inp["rwkv_mu"][i], 14)
    v[:, OV_W0:OV_W0 + 512] = np.asarray(inp["rwkv_w0"][i])[None, :]
    v[:, OV_A0:OV_A0 + 4] = f(inp["rwkv_a0"][i], 4)
    v[:, OV_KK:OV_KK + 4] = f(inp["rwkv_k_k"][i], 4)
    v[:, OV_KA:OV_KA + 4] = f(inp["rwkv_k_a"][i], 4)
    v[:, OV_RK:OV_RK + 4] = f(np.asarray(inp["rwkv_r_k"][i]).reshape(-1), 4)
    v[:, OV_GNW:OV_GNW + 512] = np.asarray(inp["rwkv_gn_w"][i])[None, :]
    v[:, OV_GNB:OV_GNB + 512] = np.asarray(inp["rwkv_gn_b"][i])[None, :]
    p = np.arange(128)
    sel = np.zeros((128, 4, 8), np.float32)
    for c in range(4):
        sel[p, c, 2 * c + p // 64] = 1.0
    v[:, OV_SEL:OV_SEL + 32] = sel.reshape(128, 32)
    return v


def host_odd_inputs(inp, no):
    a2 = np.zeros((no, 128, 512), np.float32)
    a2[:, 64:128, :] = np.asarray(inp["rwkv_a2"][:no])
    w2 = np.zeros((no, 128, 512), np.float32)
    w2[:, 0:64, :] = np.asarray(inp["rwkv_w2"][:no])
    return {"odd_w_in": np.ascontiguousarray(inp["odd_w_in"][:no]),
            "odd_w_out": np.ascontiguousarray(inp["odd_w_out"][:no]),
            "ovec": np.stack([host_odd_vec(inp, i) for i in range(no)]),
            "rw2": w2, "ra2": a2, "rg2": np.ascontiguousarray(inp["rwkv_g2"][:no])}


def _declare_odd_inputs(self, no):
    K = self.K
    self.odd_w_in = K.dram("odd_w_in", [no, D, ODD_IN], F32, kind="ExternalInput")
    self.odd_w_out = K.dram("odd_w_out", [no, D, D], F32, kind="ExternalInput")
    self.ovec = K.dram("ovec", [no, 128, OV_N], F32, kind="ExternalInput")
    self.rw2 = K.dram("rw2", [no, 128, 512], F32, kind="ExternalInput")
    self.ra2 = K.dram("ra2", [no, 128, 512], F32, kind="ExternalInput")
    self.rg2 = K.dram("rg2", [no, 128, 512], F32, kind="ExternalInput")


Model.declare_odd_inputs = _declare_odd_inputs


def _odd_phase(self, l, src, dst):
    K = self.K
    nc = self.nc
    i_l = l // 2
    esP = contextlib.ExitStack()
    C, CB = self.C, self.CB
    P = 128
    tps = self.seq // P
    WIN = self.odd_w_in.a[i_l]
    WOUT = self.odd_w_out.a[i_l]

    def bc4(v, n=4):
        return v.re("p (o t) -> p o t", o=1).bc([128, n, 128])

    def bcl(v, n, m):
        return _v(v).re("p (h o) -> p h o", o=1).bc([128, n, m])

    def recip(t):
        K.op(K.dve, lambda: nc.vector.reciprocal(_v(t).ap, _v(t).ap), [t], [t])

    ov = K.sb([128, OV_N], F32, "ov", esP)
    K.dma(K.sp, ov, self.ovec.a[i_l])
    w2t = K.sb([128, 512], F32, "w2t", esP)
    a2t = K.sb([128, 512], F32, "a2t", esP)
    g2t = K.sb([128, 512], F32, "g2t", esP)
    K.dma(K.sp, w2t, self.rw2.a[i_l])
    K.dma(K.sp, a2t, self.ra2.a[i_l])
    K.dma(K.sp, g2t, self.rg2.a[i_l])
    nexpa = K.sb([128, 8], F32, "nexpa", esP)
    K.actf(nexpa, ov[:, OV_ALOG:OV_ALOG + 8], AF.Exp)
    K.ts(nexpa, nexpa, -1.0, ALU.mult)
    gneps = K.sb([128, 1], F32, "gneps", esP)
    K.memset(gneps, 64e-5)
    convw = ov[:, OV_CONV:OV_CONV + 32]
    gain = self.vec[:, 8 * l:8 * (l + 1)]
    STs = [K.sb([128, 512], F32, "ST", esP) for _ in range(NSEQ)]
    Hss = [K.sb([128, 4, 128], F32, "Hs", esP) for _ in range(NSEQ)]
    histC = [K.sb([128, 8, 3], F32, "histC", esP) for _ in range(NSEQ)]
    histP = [K.sb([128, 14, 1], F32, "histP", esP) for _ in range(NSEQ)]
    for s_ in range(NSEQ):
        K.memset(STs[s_], 0.0)
        K.memset(Hss[s_], 0.0)
        K.memset(histC[s_], 0.0)
        K.memset(histP[s_], 0.0)

    def run_variant(hp, tiles):
        if not tiles:
            return
        es = contextlib.ExitStack()
        AD = F32 if hp else BF16
        ID = C["ident"] if hp else CB["ident"]
        if hp:
            stg = [K.sb([128, 8, 512], F32, "stg", es) for _ in range(2)]
            cnt = [0]

            def _stream(dram2d, col0, n):
                t = stg[cnt[0] % 2]
                cnt[0] += 1
                K.dma(K.sp, t[:, :, 0:n], dram2d[:, col0:col0 + n].re("(k p) n -> p k n", p=128))
                return lambda k: t[:, k, 0:n]

            wblock = lambda col0, n: _stream(WIN, col0, n)
            wblock32 = wblock
            woblock = lambda n0: _stream(WOUT, n0 * 128, 128)
        else:
            w_in = K.sb([128, 8, ODD_IN], BF16, "w_in", es)
            for k in range(8):
                K.dma(K.pool, w_in[:, k, :], WIN[k * 128:(k + 1) * 128, :])
            w_out = K.sb([128, 8, D], BF16, "w_out", es)
            K.dma(K.pool, w_out, WOUT.re("(c p) n -> p c n", p=128))
            wg = K.sb([128, 8, 136], F32, "wg", es)
            for k in range(8):
                K.dma(K.sp, wg[:, k, 0:8], WIN[k * 128:(k + 1) * 128, 1536:1544])
                K.dma(K.sp, wg[:, k, 8:136], WIN[k * 128:(k + 1) * 128, 3080:3208])
            wblock = lambda col0, n: (lambda k: w_in[:, k, col0:col0 + n])
            wblock32 = lambda col0, n: (lambda k: wg[:, k, 0:8])
            woblock = lambda n0: (lambda c: w_out[:, c, n0 * 128:(n0 + 1) * 128])

        def sb(shape, name, dt=F32):
            return K.sb(shape, dt, name, es)

        h = sb([128, 8, P], "h"); hn32 = sb([128, 8, P], "hn32"); hn = hn32 if hp else sb([128, 8, P], "hn", BF16)
        sq = sb([128, 8, P], "sq", BF16); rp = sb([128, P], "rp")
        preC = sb([128, 8, P + 3], "preC"); cacc = sb([128, 8, P], "cacc"); xbc = sb([128, 8, P], "xbc")
        pdp = sb([128, 14, P + 1], "pdp"); pdm = sb([128, 14, P], "pdm"); pdd = pdm
        zs = sb([128, 512], "zs"); oFM = sb([128, 8, P], "oFM", AD)
        dt8 = sb([128, 8], "dt8"); a8 = sb([128, 8], "a8"); acc8 = sb([128, 8], "acc8")
        eac8 = sb([128, 8], "eac8"); edl8 = sb([128, 8], "edl8"); cd8 = sb([128, 8], "cd8"); dte8 = sb([128, 8], "dte8")
        segT = sb([128, 8, 128], "segT"); MT = sb([128, 8, 128], "MT", AD)
        xTs = sb([128, 512], "xTs"); xdt = sb([128, 8, 64], "xdt", AD); xdte = sb([128, 8, 64], "xdte", AD)
        Bb = sb([128, 256], "Bb", AD); Cb = sb([128, 2, 128], "Cb", AD)
        STb = None if hp else sb([128, 512], "STb", BF16)
        y1 = sb([128, 512], "y1"); y2 = sb([128, 512], "y2"); ss2 = sb([128, 2], "ss2"); junk = sb([128, 256], "junk")
        ycT = sb([128, 512], "ycT", AD)
        txw = sb([64, P], "txw"); ldT = sb([128, 512], "ldT"); aF = sb([128, 4, P], "aF"); sg = sb([128, P], "sg")
        kp = sb([128, 4, P], "kp"); kq = sb([128, 4, P], "kq"); kk = kq; bm = sb([128, 4, P], "bm"); gl4 = sb([128, 4], "gl4")
        sqk = sb([128, 4, P], "sqk"); lg = sb([128, 4, P], "lg"); lgx = sb([128, 4, P], "lgx"); nmid = sb([128, 4], "nmid")
        e_r = sb([128, 4, P], "e_r"); e_a = sb([128, 4, P], "e_a"); e_b = sb([128, 4, P], "e_b")
        e_r0 = sb([128, 4, P], "e_r0"); e_a0 = sb([128, 4, P], "e_a0")
        rt_ = e_r; r0_ = e_r0; at_ = e_a; a0_ = e_a0; bt_ = e_b; kt_ = xbc[:, 4:8, :]
        edlT = sb([128, 512], "edlT"); kdT = sb([128, 512], "kdT"); bdT = sb([128, 512], "bdT"); vT = sb([128, 512], "vT")
        A = [segT[:, 0:4, :], cacc[:, 0:4, :]]
        AT = [segT[:, 4:8, :], cacc[:, 4:8, :]]
        PT = xbc[:, 0:4, :]
        AakT = y1.a.re("p (h t) -> p h t", h=4); RbT = y2.a.re("p (h t) -> p h t", h=4); RkT = xTs.a.re("p (h t) -> p h t", h=4)
        Xs = sb([128, 4, 64], "Xs"); Us = sb([128, 8, 64], "Us")
        yd = lg.a.re("p c t -> p (c t)").re("p (h q) -> p h q", h=8); ym = lgx.a.re("p c t -> p (c t)").re("p (h q) -> p h q", h=8); mean8 = sb([128, 8], "mean8"); var8 = sb([128, 8], "var8")
        rkr = sb([128, 4, P], "rkr"); rks = sb([128, 8], "rks"); gT = sb([128, 512], "gT"); ydT = sb([128, 512], "ydT", AD)

        for (s_, it) in tiles:
            ST, Hs = STs[s_], Hss[s_]
            K.dma(K.sp, h, self.hview(src, it * P, P))
            K.cp(preC[:, :, 0:3], histC[s_], E=K.pool)
            K.cp(pdp[:, :, 0:1], histP[s_], E=K.pool)
            if not hp:
                K.cp(STb, ST, E=K.pool)
            Sst = ST if hp else STb
            self.rmsnorm_fm(h, gain, [hn32] if hp else [hn, hn32], sq, rp, n=P)
            for c4 in range(2):
                wb = wblock(512 + c4 * 512, 512)
                b = self.bank()
                for j in range(4):
                    for k in range(8):
                        K.mm(b[:, j * P:(j + 1) * P], wb(k)[:, j * 128:(j + 1) * 128], hn[:, k, :],
                             start=(k == 0), stop=(k == 7))
                K.cp(preC[:, c4 * 4:(c4 + 1) * 4, 3:3 + P], b.a.re("p (j t) -> p j t", j=4), E=K.act)
            for c4 in range(4):
                b = self.bank()
                nj = 4 if c4 < 3 else 2
                wb = wblock(1544 + c4 * 512, nj * 128)
                for j in range(nj):
                    c = c4 * 4 + j
                    for k in range(8):
                        if c == 12 and not hp:
                            K.mm(b[:, j * P:(j + 1) * P], wg[:, k, 8:136], hn32[:, k, :], start=(k == 0), stop=(k == 7))
                        else:
                            K.mm(b[:, j * P:(j + 1) * P], wb(k)[:, j * 128:(j + 1) * 128], hn[:, k, :],
                                 start=(k == 0), stop=(k == 7))
                K.cp(pdp[:, c4 * 4:c4 * 4 + nj, 1:1 + P], b[:, 0:nj * P].re("p (j t) -> p j t", j=nj), E=K.act)
            wb = wblock(0, 512)
            bz = self.bank()
            for k in range(8):
                K.mm(bz, hn[:, k, :], wb(k), start=(k == 0), stop=(k == 7))
            K.actf(zs, bz, AF.Silu)
            wb = wblock32(1536, 8)
            bg = self.bank()
            for k in range(8):
                K.mm(bg[:, 0:8], hn32[:, k, :], wb(k), start=(k == 0), stop=(k == 7))
            K.tt(dt8, bg[:, 0:8], ov[:, OV_DTB:OV_DTB + 8], ALU.add)
            K.ts(dt8, dt8, 60.0, ALU.min)
            K.actf(dt8, dt8, AF.Exp)
            K.actf(dt8, dt8, AF.Ln, bias=1.0)
            K.tt(a8, dt8, nexpa, ALU.mult)
            for c in range(8):
                K.actf(cacc[:, c, :], preC[:, c, 0:P], AF.Copy, scale=convw[:, c * 4:c * 4 + 1])
                for j in range(1, 4):
                    K.stt(cacc[:, c, :], preC[:, c, j:j + P], convw[:, c * 4 + j:c * 4 + j + 1], cacc[:, c, :],
                          ALU.mult, ALU.add)
                K.actf(xbc[:, c, :], cacc[:, c, :], AF.Silu, bias=ov[:, OV_CONVB + c:OV_CONVB + c + 1])
            K.cp(histC[s_], preC[:, :, P:P + 3], E=K.pool)
            Atri = cacc
            K.tt(Atri, bc4(C["tri"], 8), bcl(a8, 8, 128), ALU.mult)
            bsm = self.bank()
            K.mm(bsm[:, 0:8], C["tri"], a8)
            K.mm(bsm[:, 8:16], C["ones"], a8)
            K.cp(acc8, bsm[:, 0:8])
            K.actf(eac8, acc8, AF.Exp)
            K.actf(cd8, bsm[:, 8:16], AF.Exp)
            K.tt(edl8, bsm[:, 8:16], acc8, ALU.subtract)
            K.actf(edl8, edl8, AF.Exp)
            K.tt(dte8, dt8, edl8, ALU.mult)
            for hg in range(2):
                br = self.bank()
                K.mm(br, C["ones"], Atri[:, hg * 4:(hg + 1) * 4, :])
                for hh in range(4):
                    hd = hg * 4 + hh
                    K.ts(segT[:, hd, :], br[:, hh * 128:(hh + 1) * 128], acc8[:, hd:hd + 1], ALU.subtract, 0.0, ALU.min)
            K.actf(segT, segT, AF.Exp)
            K.tt(segT, segT, bc4(C["tri"], 8), ALU.mult, E=K.pool)
            bcb = self.bank()
            for g in range(2):
                K.mm(bcb[:, g * 128:(g + 1) * 128], xbc[:, 4 + g, :], xbc[:, 6 + g, :])
            for g in range(2):
                K.tt(MT[:, g * 4:(g + 1) * 4, :], segT[:, g * 4:(g + 1) * 4, :],
                     bcb[:, g * 128:(g + 1) * 128].re("p (o t) -> p o t", o=1).bc([128, 4, 128]), ALU.mult)
            K.cp(Cb, xbc[:, 6:8, :], E=K.pool)
            bxt = self.bank()
            for c in range(4):
                K.mm(bxt[:, c * 128:(c + 1) * 128], xbc[:, c, :], C["ident"])
            bbt = self.bank()
            for g in range(2):
                K.mm(bbt[:, g * 128:(g + 1) * 128], xbc[:, 4 + g, :], C["ident"])
            K.cp(xTs, bxt, E=K.act)
            K.cp(Bb, bbt[:, 0:256], E=K.act)
            K.tt(xdt, xTs.a.re("p (h q) -> p h q", h=8), bcl(dt8, 8, 64), ALU.mult)
            K.tt(xdte, xTs.a.re("p (h q) -> p h q", h=8), bcl(dte8, 8, 64), ALU.mult)
            by = self.bank()
            for hd in range(8):
                K.mm(by[:, hd * 64:(hd + 1) * 64], MT[:, hd, :], xdt[:, hd, :])
            bi = self.bank()
            for g in range(2):
                K.mm(bi[:, g * 256:(g + 1) * 256], Cb[:, g, :], Sst[:, g * 256:(g + 1) * 256])
            bst = self.bank()
            for g in range(2):
                K.mm(bst[:, g * 256:(g + 1) * 256], Bb[:, g * 128:(g + 1) * 128],
                     xdte[:, g * 4:(g + 1) * 4, :].re("p h q -> p (h q)"))
            K.tt(y1.a.re("p (h q) -> p h q", h=8), bi.a.re("p (h q) -> p h q", h=8), bcl(eac8, 8, 64), ALU.mult)
            K.tt(y1, y1, by, ALU.add)
            K.tt(y2, xTs, ov[:, OV_D:OV_D + 512], ALU.mult, E=K.pool)
            K.tt(y1, y1, y2, ALU.add)
            K.tt(y1, y1, zs, ALU.mult)
            K.tt(ST.a.re("p (h q) -> p h q", h=8), ST.a.re("p (h q) -> p h q", h=8), bcl(cd8, 8, 64), ALU.mult)
            K.tt(ST, ST, bst, ALU.add)
            for g in range(2):
                K.actf(junk, y1[:, g * 256:(g + 1) * 256], AF.Square, accum=ss2[:, g:g + 1])
            K.actf(ss2, ss2, AF.Sqrt, bias=self.epst, scale=1.0 / 256.0)
            recip(ss2)
            K.tt(y2, y1, ov[:, OV_SNW:OV_SNW + 512], ALU.mult, E=K.pool)
            for g in range(2):
                K.ts(ycT[:, g * 256:(g + 1) * 256], y2[:, g * 256:(g + 1) * 256], ss2[:, g:g + 1], ALU.mult)
            btr = self.bank()
            for hh in range(4):
                K.mm(btr[:, hh * 128:(hh + 1) * 128], ycT[:, hh * 128:(hh + 1) * 128], ID)
            K.cp(oFM[:, 0:4, :], btr.a.re("p (h t) -> p h t", h=4), E=K.act)

            K.tt(pdd, pdp[:, :, 0:P], pdp[:, :, 1:P + 1], ALU.subtract)
            K.tt(pdd, pdd, bcl(ov[:, OV_MU:OV_MU + 14], 14, P), ALU.mult, E=K.pool)
            K.tt(pdm, pdd, pdp[:, :, 1:P + 1], ALU.add)
            K.cp(histP[s_], pdp[:, :, P:P + 1], E=K.pool)
            R_, K_, V_ = pdm[:, 0:4, :], pdm[:, 4:8, :], pdm[:, 8:12, :]
            K.actf(txw, pdm[0:64, 12, :], AF.Tanh)
            K.actf(sg, pdm[:, 13, :], AF.Sigmoid)
            bw = self.bank()
            K.mm(bw, txw, w2t[0:64, :])
            K.tt(ldT, bw, ov[:, OV_W0:OV_W0 + 512], ALU.add)
            K.actf(ldT, ldT, AF.Sigmoid)
            K.ts(ldT, ldT, -float(np.exp(-0.5)), ALU.mult)
            ba = self.bank()
            for c in range(4):
                K.mm(ba[:, c * P:(c + 1) * P], a2t[64:128, c * 128:(c + 1) * 128], pdm[64:128, 12, :])
            for c in range(4):
                K.actf(aF[:, c, :], ba[:, c * P:(c + 1) * P], AF.Sigmoid, bias=ov[:, OV_A0 + c:OV_A0 + c + 1])
            bgm = self.bank()
            K.mm(bgm, sg, g2t)
            K.cp(gT, bgm, E=K.act)
            for c in range(4):
                K.ts(kp[:, c, :], aF[:, c, :], 1.0, ALU.subtract, ov[:, OV_KA + c:OV_KA + c + 1], ALU.mult)
                K.stt(kp[:, c, :], kp[:, c, :], 1.0, pdm[:, 4 + c, :], ALU.add, ALU.mult)
                K.ts(kq[:, c, :], pdm[:, 4 + c, :], ov[:, OV_KK + c:OV_KK + c + 1], ALU.mult, E=K.pool)
                K.stt(rkr[:, c, :], pdm[:, c, :], ov[:, OV_RK + c:OV_RK + c + 1], kp[:, c, :], ALU.mult, ALU.mult)
            K.tt(sqk, kq, kq, ALU.mult, E=K.pool)
            bss = self.bank()
            K.mm(bss, C["bones"], sqk)
            K.actf(sqk, bss.a.re("p (c t) -> p c t", c=4), AF.Sqrt, bias=self.epst)
            recip(sqk)
            K.tt(kk, kq, sqk, ALU.mult)
            K.tt(bm, kk, aF, ALU.mult, E=K.pool)
            blg = self.bank()
            blx = self.bank()
            for c in range(4):
                K.mm(blg[:, c * P:(c + 1) * P], ldT[:, c * 128:(c + 1) * 128], C["tri"])
                K.mm(blx[:, c * P:(c + 1) * P], ldT[:, c * 128:(c + 1) * 128], C["tris"])
            K.cp(lg, blg.a.re("p (c t) -> p c t", c=4))
            K.cp(lgx, blx.a.re("p (c t) -> p c t", c=4), E=K.act)
            K.ts(nmid, lg[:, :, 63], -1.0, ALU.mult)
            for c in range(4):
                K.actf(e_r[:, c, :], lg[:, c, :], AF.Exp, bias=nmid[:, c:c + 1])
                K.actf(e_a[:, c, :], lgx[:, c, :], AF.Exp, bias=nmid[:, c:c + 1])
                K.actf(e_b[:, c, :], lg[:, c, :], AF.Exp, bias=lg[:, c, 63:64], scale=-1.0)
            K.actf(e_r0, lg, AF.Exp)
            K.actf(e_a0, lgx, AF.Exp)
            K.cp(gl4, e_r0[:, :, 127])
            K.tt(rt_, R_, e_r, ALU.mult)
            K.tt(r0_, R_, e_r0, ALU.mult, E=K.pool)
            K.stt(at_, kk, -1.0, e_a, ALU.mult, ALU.mult)
            K.stt(a0_, kk, -1.0, e_a0, ALU.mult, ALU.mult)
            K.tt(kt_, kp, e_b, ALU.mult)
            K.tt(bt_, bm, e_b, ALU.mult, E=K.pool)
            bed = self.bank()
            K.mm(bed, C["ustr"], ldT)
            K.actf(edlT, bed, AF.Exp)
            for (srcT, dstT, scl) in ((kp, kdT, True), (bm, bdT, True), (V_, vT, False)):
                b = self.bank()
                for c in range(4):
                    K.mm(b[:, c * 128:(c + 1) * 128], srcT[:, c, :], C["ident"])
                if scl:
                    K.tt(dstT, b, edlT, ALU.mult)
                else:
                    K.cp(dstT, b, E=K.act)
            brk = self.bank()
            for c in range(4):
                K.mm(brk[:, 0:8], rkr[:, c, :], ov[:, OV_SEL + c * 8:OV_SEL + (c + 1) * 8], start=(c == 0), stop=(c == 3))
            K.cp(rks, brk[:, 0:8])
            byd = self.ps[7]
            for hg in range(2):
                b1, b2, b3, b4, b5 = [self.bank() for _ in range(5)]
                for hh in range(4):
                    hd = hg * 4 + hh
                    hq, pr = hd // 2, slice((hd % 2) * 64, (hd % 2) * 64 + 64)
                    sl = slice(hh * 128, (hh + 1) * 128)
                    K.mm(b1[:, sl], bt_[pr, hq, :], at_[pr, hq, :])
                    K.mm(b2[:, sl], at_[pr, hq, :], bt_[pr, hq, :])
                    K.mm(b3[:, sl], kt_[pr, hq, :], at_[pr, hq, :])
                    K.mm(b4[:, sl], bt_[pr, hq, :], rt_[pr, hq, :])
                    K.mm(b5[:, sl], kt_[pr, hq, :], rt_[pr, hq, :])
                K.tt(AT[0], b1.a.re("p (h t) -> p h t", h=4), bc4(C["tris"]), ALU.mult)
                K.tt(A[0], b2.a.re("p (h t) -> p h t", h=4), bc4(C["ustr"]), ALU.mult)
                K.tt(AakT, b3.a.re("p (h t) -> p h t", h=4), bc4(C["tris"]), ALU.mult)
                K.tt(RbT, b4.a.re("p (h t) -> p h t", h=4), bc4(C["tri"]), ALU.mult)
                K.tt(RkT, b5.a.re("p (h t) -> p h t", h=4), bc4(C["tri"]), ALU.mult)
                K.tt(PT, AT[0], bc4(C["ident"]), ALU.add, E=K.pool)
                cur = 0
                for itr in range(6):
                    nxt = 1 - cur
                    c1 = self.bank()
                    c2 = self.bank()
                    for hh in range(4):
                        sl = slice(hh * 128, (hh + 1) * 128)
                        K.mm(c1[:, sl], AT[cur][:, hh, :], A[cur][:, hh, :])
                        K.mm(c2[:, sl], A[cur][:, hh, :], AT[cur][:, hh, :])
                    K.cp(A[nxt], c1.a.re("p (h t) -> p h t", h=4), E=K.act)
                    K.cp(AT[nxt], c2.a.re("p (h t) -> p h t", h=4), E=K.dve)
                    c3 = self.bank()
                    for hh in range(4):
                        K.mm(c3[:, hh * 128:(hh + 1) * 128], A[nxt][:, hh, :], PT[:, hh, :])
                    K.tt(PT, PT, c3.a.re("p (h t) -> p h t", h=4), ALU.add)
                    cur = nxt
                bx = self.bank()
                for hh in range(4):
                    hd = hg * 4 + hh
                    hq, pr = hd // 2, slice((hd % 2) * 64, (hd % 2) * 64 + 64)
                    vs = slice((hd % 2) * 64, (hd % 2) * 64 + 64)
                    K.mm(bx[:, hh * 64:(hh + 1) * 64], a0_[pr, hq, :], Hs[pr, hq, vs], start=True, stop=False)
                    K.mm(bx[:, hh * 64:(hh + 1) * 64], AakT[:, hh, :], vT[:, hd * 64:(hd + 1) * 64], start=False, stop=True)
                K.cp(Xs, bx[:, 0:256].re("p (h q) -> p h q", h=4))
                bu = self.bank()
                for hh in range(4):
                    K.mm(bu[:, hh * 64:(hh + 1) * 64], PT[:, hh, :], Xs[:, hh, :])
                K.cp(Us[:, hg * 4:(hg + 1) * 4, :], bu[:, 0:256].re("p (h q) -> p h q", h=4))
                for hh in range(4):
                    hd = hg * 4 + hh
                    hq, pr = hd // 2, slice((hd % 2) * 64, (hd % 2) * 64 + 64)
                    vs = slice((hd % 2) * 64, (hd % 2) * 64 + 64)
                    o = byd[:, hd * 64:(hd + 1) * 64]
                    K.mm(o, r0_[pr, hq, :], Hs[pr, hq, vs], start=True, stop=False)
                    K.mm(o, RbT[:, hh, :], Us[:, hd, :], start=False, stop=False)
                    K.mm(o, RkT[:, hh, :], vT[:, hd * 64:(hd + 1) * 64], start=False, stop=True)
            bh = self.bank()
            for hq in range(4):
                o = bh[:, hq * 128:(hq + 1) * 128]
                K.mm(o, bdT[:, hq * 128:(hq + 1) * 128], Us[:, 2 * hq:2 * hq + 2, :].re("p h q -> p (h q)"), start=True, stop=False)
                K.mm(o, kdT[:, hq * 128:(hq + 1) * 128], vT[:, hq * 128:(hq + 1) * 128], start=False, stop=True)
            for hq in range(4):
                K.stt(Hs[:, hq, :], Hs[:, hq, :], gl4[:, hq:hq + 1], bh[:, hq * 128:(hq + 1) * 128], ALU.mult, ALU.add)
            K.cp(yd, byd.a.re("p (h q) -> p h q", h=8), E=K.act)
            K.red(mean8, yd)
            K.ts(mean8, mean8, 1.0 / 64.0, ALU.mult)
            K.tt(ym, yd, bcl(mean8, 8, 64), ALU.subtract)
            K.tt(yd, ym, ym, ALU.mult, E=K.pool)
            K.red(var8, yd)
            K.actf(var8, var8, AF.Sqrt, bias=gneps, scale=1.0 / 64.0)
            recip(var8)
            K.tt(ym, ym, bcl(var8, 8, 64), ALU.mult)
            ymf = _v(ym).re("p h q -> p (h q)")
            K.tt(ymf, ymf, ov[:, OV_GNW:OV_GNW + 512], ALU.mult, E=K.pool)
            K.tt(ymf, ymf, ov[:, OV_GNB:OV_GNB + 512], ALU.add)
            K.tt(yd, vT.a.re("p (h q) -> p h q", h=8), bcl(rks, 8, 64), ALU.mult)
            K.tt(ym, ym, yd, ALU.add)
            K.tt(ydT, ymf, gT, ALU.mult)
            btr = self.bank()
            for hh in range(4):
                K.mm(btr[:, hh * 128:(hh + 1) * 128], ydT[:, hh * 128:(hh + 1) * 128], ID)
            K.cp(oFM[:, 4:8, :], btr.a.re("p (h t) -> p h t", h=4), E=K.act)
            for half in range(2):
                b = self.bank()
                for j in range(4):
                    n = half * 4 + j
                    wo = woblock(n)
                    for c in range(8):
                        K.mm(b[:, j * P:(j + 1) * P], wo(c), oFM[:, c, :], start=(c == 0), stop=(c == 7))
                K.tt(h[:, half * 4:(half + 1) * 4, :], h[:, half * 4:(half + 1) * 4, :],
                     b.a.re("p (j t) -> p j t", j=4), ALU.add)
            K.dma(K.act, self.hview(dst, it * P, P), h)
        K.barrier()
        es.close()

    run_variant(True, [(s_, s_ * tps) for s_ in range(NSEQ)])
    run_variant(False, [(s_, s_ * tps + j) for s_ in range(NSEQ) for j in range(1, tps)])
    K.barrier()
    esP.close()


Model.odd_phase = _odd_phase


def kernel(**inputs):
    inp = {k: np.asarray(v) for k, v in inputs.items()}
    x = inp["x"]
    M = Model(n_layers=4, seq=x.shape[1], mix=True)
    nc = M.build()
    maps = make_in_maps(inp, M, x, 8)
    res = run_bass_kernel_spmd(nc, maps, core_ids=list(range(8)))
    out = np.concatenate([r["out"].reshape(NSEQ, x.shape[1], D) for r in res.results], axis=0)
    return out.astype(np.float32)
```

```python
import contextlib
import os
import numpy as np
import ml_dtypes
import concourse.bass as bass
import concourse.mybir as mybir
from concourse.bass_utils import run_bass_kernel_spmd

F32 = mybir.dt.float32
BF16 = mybir.dt.bfloat16
AF = mybir.ActivationFunctionType
ALU = mybir.AluOpType
AX = mybir.AxisListType

EPOCH = 30000


class Tl:
    def __init__(self, h, name):
        self.h = h
        self.name = name
        self.w = None
        self.r = {}
        self.dsem = None
        self.dcnt = 0
        self.psum = False

    def __getitem__(self, idx):
        return V(self, self.h[idx])

    @property
    def a(self):
        return V(self, self.h[:])


class V:
    def __init__(self, t, ap):
        self.t = t
        self.ap = ap

    def __getitem__(self, idx):
        return V(self.t, self.ap[idx])

    def bitcast(self, dt):
        return V(self.t, self.ap.bitcast(dt))

    def bc(self, shape):
        return V(self.t, self.ap.to_broadcast(list(shape)))

    def re(self, s, **kw):
        return V(self.t, self.ap.rearrange(s, **kw))


def _v(x):
    return x.a if isinstance(x, Tl) else x


class Eng:
    def __init__(self, K, name, eng):
        self.K = K
        self.name = name
        self.eng = eng
        self.epoch = -1
        self.cnt = EPOCH
        self.sem = None
        self.known = {}

    def tick(self, ins):
        if self.cnt >= EPOCH:
            self.epoch += 1
            self.cnt = 0
            self.sem = self.K.new_sem(f"s_{self.name}_{self.epoch}")
        self.cnt += 1
        ins.then_inc(self.sem, 1)
        return (id(self.sem), self.sem, self.cnt, self.name)


class KB:
    def __init__(self, nc):
        self.nc = nc
        self.es = contextlib.ExitStack()
        self.pe = Eng(self, "pe", nc.tensor)
        self.dve = Eng(self, "dve", nc.vector)
        self.act = Eng(self, "act", nc.scalar)
        self.pool = Eng(self, "pool", nc.gpsimd)
        self.sp = Eng(self, "sp", nc.sync)
        self.engs = [self.pe, self.dve, self.act, self.pool, self.sp]
        self.nsem = 0
        self.ntile = 0
        self.all_dma = {}
        self.ninst = 0

    def new_sem(self, name):
        self.nsem += 1
        return self.es.enter_context(self.nc.semaphore(f"{name}_{self.nsem}"))

    def sb(self, shape, dt=F32, name="t", es=None):
        self.ntile += 1
        nm = f"{name}_{self.ntile}"
        h = (es or self.es).enter_context(self.nc.sbuf_tensor(nm, list(shape), dt))
        return Tl(h, nm)

    def psum(self, shape, dt=F32, name="p", es=None):
        self.ntile += 1
        nm = f"{name}_{self.ntile}"
        h = (es or self.es).enter_context(self.nc.psum_tensor(nm, list(shape), dt))
        t = Tl(h, nm)
        t.psum = True
        return t

    def dram(self, name, shape, dt=F32, kind="Internal"):
        h = self.nc.dram_tensor(name, list(shape), dt, kind=kind)
        return Tl(h.ap(), name)

    def _wait(self, E, toks):
        best = {}
        for tk in toks:
            if tk is None:
                continue
            sid, sem, val, who = tk
            if who == E.name and E is self.pe:
                continue
            if E.known.get(sid, 0) >= val:
                continue
            if best.get(sid, (None, 0))[1] < val:
                best[sid] = (sem, val)
        for sid, (sem, val) in best.items():
            E.eng.wait_ge(sem, val)
            E.known[sid] = val
            self.ninst += 1

    skip = False

    def op(self, E, fn, reads, writes):
        if self.skip:
            return None
        toks = []
        rt = [(_v(x).t) for x in reads if x is not None and not isinstance(x, (int, float))]
        wt = [(_v(x).t) for x in writes if x is not None]
        for t in rt:
            toks.append(t.w)
            if t.psum:
                for k, tk in t.r.items():
                    if k != E.name:
                        toks.append(tk)
        for t in wt:
            toks.append(t.w)
            for k, tk in t.r.items():
                if k == E.name:
                    continue
                toks.append(tk)
        self._wait(E, toks)
        ins = fn()
        tok = E.tick(ins)
        self.ninst += 1
        for t in rt:
            t.r[E.name] = tok
        for t in wt:
            t.w = tok
            t.r = {}
        return tok

    def dma(self, Q, out, in_):
        out = _v(out)
        in_ = _v(in_)
        ot, it = out.t, in_.t
        toks = [it.w, ot.w] + list(ot.r.values())
        self._wait(Q, toks)
        if ot.dsem is None:
            ot.dsem = self.new_sem("d_" + ot.name)
        ins = Q.eng.dma_start(out=out.ap, in_=in_.ap)
        ins.then_inc(ot.dsem, 16)
        ot.dcnt += 16
        tok = (id(ot.dsem), ot.dsem, ot.dcnt, "dma")
        self.ninst += 1
        ot.w = tok
        ot.r = {}
        it.r["dma%d" % id(ot.dsem)] = tok
        self.all_dma[id(ot.dsem)] = tok
        return tok

    def barrier(self):
        toks = list(self.all_dma.values())
        for E in self.engs:
            if E.sem is not None and E.cnt > 0:
                toks.append((id(E.sem), E.sem, E.cnt, E.name + "_b"))
        for E in self.engs:
            self._wait(E, toks)

    def mm(self, out, lhsT, rhs, start=True, stop=True, ser=False):
        out, lhsT, rhs = _v(out), _v(lhsT), _v(rhs)
        if ser and not self.skip and self.pe.sem is not None and self.pe.cnt > 0:
            self.pe.eng.wait_ge(self.pe.sem, self.pe.cnt)
        return self.op(self.pe, lambda: self.nc.tensor.matmul(out.ap, lhsT.ap, rhs.ap, start=start, stop=stop),
                       [lhsT, rhs], [out])

    def actf(self, out, in_, func, bias=None, scale=None, accum=None, E=None):
        out, in_ = _v(out), _v(in_)
        kw = {}
        rd = [in_]
        if bias is not None:
            if isinstance(bias, (int, float)):
                kw["bias"] = float(bias)
            else:
                bias = _v(bias)
                kw["bias"] = bias.ap
                rd.append(bias)
        if scale is not None:
            if isinstance(scale, (int, float)):
                kw["scale"] = float(scale)
            else:
                scale = _v(scale)
                kw["scale"] = scale.ap
                rd.append(scale)
        wr = [out]
        if accum is not None:
            accum = _v(accum)
            kw["accum_out"] = accum.ap
            wr.append(accum)
        return self.op(self.act, lambda: self.nc.scalar.activation(out.ap, in_.ap, func, **kw), rd, wr)

    def _ve(self, E):
        return E or self.dve

    def _sc(self, s, rd):
        if isinstance(s, (int, float)):
            return float(s)
        s = _v(s)
        rd.append(s)
        return s.ap

    def ts(self, out, in0, s1, op0, s2=None, op1=None, E=None, accum=None):
        E = self._ve(E)
        out, in0 = _v(out), _v(in0)
        rd = [in0]
        a1 = self._sc(s1, rd)
        a2 = None if s2 is None else self._sc(s2, rd)
        wr = [out]
        kw = {}
        if accum is not None:
            accum = _v(accum)
            kw["accum_out"] = accum.ap
            wr.append(accum)
        if op1 is None:
            return self.op(E, lambda: E.eng.tensor_scalar(out.ap, in0.ap, a1, None, op0, **kw), rd, wr)
        return self.op(E, lambda: E.eng.tensor_scalar(out.ap, in0.ap, a1, a2, op0, op1, **kw), rd, wr)

    def tt(self, out, in0, in1, op, E=None):
        E = self._ve(E)
        out, in0, in1 = _v(out), _v(in0), _v(in1)
        return self.op(E, lambda: E.eng.tensor_tensor(out.ap, in0.ap, in1.ap, op), [in0, in1], [out])

    def stt(self, out, in0, s, in1, op0, op1, E=None):
        E = self.dve
        out, in0, in1 = _v(out), _v(in0), _v(in1)
        rd = [in0, in1]
        a = self._sc(s, rd)
        return self.op(E, lambda: E.eng.scalar_tensor_tensor(out.ap, in0.ap, a, in1.ap, op0, op1), rd, [out])

    def cp(self, out, in_, E=None):
        E = self._ve(E)
        out, in_ = _v(out), _v(in_)
        if E is self.act:
            return self.op(E, lambda: self.nc.scalar.copy(out.ap, in_.ap), [in_], [out])
        return self.op(E, lambda: E.eng.tensor_copy(out.ap, in_.ap), [in_], [out])

    def memset(self, out, val, E=None):
        E = self._ve(E)
        out = _v(out)
        return self.op(E, lambda: E.eng.memset(out.ap, val), [], [out])

    def red(self, out, in_, op=ALU.add, E=None):
        E = self._ve(E)
        out, in_ = _v(out), _v(in_)
        return self.op(E, lambda: E.eng.tensor_reduce(out.ap, in_.ap, AX.X, op), [in_], [out])


D = 1024
NSEQ = 2
T = 256
CH = 128
EVEN_IN = 3608
ODD_IN = 3336
NEPS = 1e-6


def host_consts():
    i = np.arange(128)
    c = {}
    c["ident"] = np.eye(128, dtype=np.float32)
    c["tri"] = (i[:, None] <= i[None, :]).astype(np.float32)
    c["tris"] = (i[:, None] < i[None, :]).astype(np.float32)
    c["ustr"] = (i[:, None] > i[None, :]).astype(np.float32)
    c["ones"] = np.ones((128, 128), np.float32)
    c["bones"] = ((i[:, None] // 64) == (i[None, :] // 64)).astype(np.float32)
    return c


CONST_NAMES = ["ident", "tri", "tris", "ustr", "ones", "bones"]


class Model:
    def __init__(self, n_layers=4, seq=2048, mix=True, dbg=False):
        self.L = n_layers
        self.seq = seq
        self.NT = NSEQ * seq
        self.mix = mix
        self.dbg = dbg

    def build(self):
        nc = bass.Bass("TRN2", target_bir_lowering=False)
        self.nc = nc
        K = KB(nc)
        self.K = K
        L, NT = self.L, self.NT
        ne = (L + 1) // 2
        no = L // 2
        self.x = K.dram("x", [NT, D], F32, kind="ExternalInput")
        self.out = K.dram("out", [NT, D], F32, kind="ExternalOutput")
        self.hA = K.dram("hA", [D, NT], F32)
        self.hB = K.dram("hB", [D, NT], F32)
        self.cst_d = K.dram("consts", [128, len(CONST_NAMES) * 128], F32, kind="ExternalInput")
        self.w1 = K.dram("mlp_w1", [L, D, 4 * D], F32, kind="ExternalInput")
        self.w2 = K.dram("mlp_w2", [L, 4 * D, D], F32, kind="ExternalInput")
        self.nvec = 3 * L + 1
        self.vec_d = K.dram("vecs", [128, 8 * (2 * L + 1)], F32, kind="ExternalInput")
        if self.mix:
            self.declare_mixer_inputs(ne, no)
        self.cst = K.sb([128, len(CONST_NAMES) * 128], F32, "cst")
        K.dma(K.sp, self.cst, self.cst_d)
        self.C = {n: self.cst[:, i * 128:(i + 1) * 128] for i, n in enumerate(CONST_NAMES)}
        self.cstb = K.sb([128, len(CONST_NAMES) * 128], BF16, "cstb")
        K.cp(self.cstb, self.cst)
        self.CB = {n: self.cstb[:, i * 128:(i + 1) * 128] for i, n in enumerate(CONST_NAMES)}
        self.vec = K.sb([128, 8 * (2 * L + 1)], F32, "vec")
        K.dma(K.sp, self.vec, self.vec_d)
        self.epst = K.sb([128, 1], F32, "eps")
        K.memset(self.epst, NEPS)
        self.ps = [K.psum([128, 512], F32, "bank") for _ in range(8)]
        self.psi = 0

        self.phase0()
        src, dst = self.hA, self.hB
        for l in range(L):
            if self.mix:
                K.barrier()
                if l % 2 == 0:
                    self.even_phase(l, src, dst)
                elif os.environ.get("NOODD"):
                    src, dst = dst, src
                else:
                    self.odd_phase(l, src, dst)
                src, dst = dst, src
            K.barrier()
            self.mlp_phase(l, src, dst)
            src, dst = dst, src
        K.barrier()
        self.final_phase(src)
        K.barrier()
        return nc

    def bank(self):
        b = self.ps[self.psi % 7]
        self.psi += 1
        return b

    def hview(self, hd, t0, n):
        return hd.a.re("(c p) t -> p c t", p=128)[:, :, t0:t0 + n]

    def phase0(self):
        K = self.K
        es = contextlib.ExitStack()
        xt = [K.sb([128, 2, D], F32, "xt", es) for _ in range(2)]
        ht = [K.sb([128, 8, T], F32, "ht", es) for _ in range(2)]
        for i in range(self.NT // T):
            xs, hs = xt[i % 2], ht[i % 2]
            K.dma(K.sp, xs, self.x.a[i * T:(i + 1) * T, :].re("(s p) f -> p s f", p=128))
            for s in range(2):
                for half in range(2):
                    b = self.bank()
                    for j in range(4):
                        c = half * 4 + j
                        K.mm(b[:, j * 128:(j + 1) * 128], xs[:, s, c * 128:(c + 1) * 128], self.C["ident"])
                    dst = hs[:, half * 4:(half + 1) * 4, s * 128:(s + 1) * 128]
                    K.cp(dst, b.a.re("p (j t) -> p j t", j=4), E=(K.dve if half == 0 else K.act))
            K.dma(K.act, self.hview(self.hA, i * T, T), hs)
        K.barrier()
        es.close()

    def rmsnorm_fm(self, h, gain, hn, sq, rp, n=T):
        K = self.K
        K.actf(sq, h, AF.Square)
        b = self.bank()
        for c in range(8):
            K.mm(b[:, 0:n], self.CB["ones"], _v(sq)[:, c, :], start=(c == 0), stop=(c == 7))
        K.actf(rp, b[:, 0:n], AF.Sqrt, bias=self.epst, scale=1.0 / D)
        K.op(K.dve, lambda: self.nc.vector.reciprocal(_v(rp).ap, _v(rp).ap), [rp], [rp])
        for c in range(8):
            for o in (hn if isinstance(hn, (list, tuple)) else [hn]):
                K.stt(_v(o)[:, c, :], _v(h)[:, c, :], gain[:, c:c + 1], rp, ALU.mult, ALU.mult,
                      E=(K.dve if c % 2 == 0 else K.pool))

    def mlp_phase(self, l, src, dst):
        K = self.K
        P = 128
        gain = self.vec[:, 8 * (self.L + l):8 * (self.L + l + 1)]
        W1 = self.w1.a[l]
        W2 = self.w2.a[l]
        tiles = []
        for s_ in range(NSEQ):
            if self.seq > P:
                tiles.append((s_ * self.seq + P, min(T, self.seq) - P))
            for j in range(1, self.seq // T):
                tiles.append((s_ * self.seq + j * T, T))
        esW = contextlib.ExitStack()
        if tiles:
            w1 = K.sb([128, 8, 4 * D], BF16, "w1", esW)
            w2 = K.sb([128, 32, D], BF16, "w2", esW)
            for k in range(8):
                K.dma(K.pool, w1[:, k, :], W1[k * 128:(k + 1) * 128, :])
            for g in range(4):
                K.dma(K.pool, w2[:, g * 8:(g + 1) * 8, :],
                      W2[g * 1024:(g + 1) * 1024, :].re("(m p) n -> p m n", p=128))
        es = contextlib.ExitStack()
        NB = NSEQ * P
        h = K.sb([128, 8, NB], F32, "h", es)
        hn32 = K.sb([128, 8, NB], F32, "hn32", es)
        sq = K.sb([128, 8, NB], BF16, "sq", es)
        rp = K.sb([128, NB], F32, "rp", es)
        act32 = K.sb([128, 32, NB], F32, "act32", es)
        tmp32 = K.sb([128, 2, NB], F32, "tmp32", es)
        stg = [K.sb([128, 2048], F32, "stg", es) for _ in range(2)]
        ns = 0
        for s_ in range(NSEQ):
            K.dma(K.sp, h[:, :, s_ * P:(s_ + 1) * P], self.hview(src, s_ * self.seq, P))
        self.rmsnorm_fm(h, gain, hn32, sq, rp, n=NB)
        for mb in range(16):
            st = stg[ns % 2].a.re("p (k n) -> p k n", k=8)
            ns += 1
            K.dma(K.sp, st, W1[:, mb * 256:(mb + 1) * 256].re("(k p) n -> p k n", p=128))
            b = self.bank()
            for j in range(2):
                for k in range(8):
                    K.mm(b[:, j * NB:(j + 1) * NB], st[:, k, j * 128:(j + 1) * 128], hn32[:, k, :],
                         start=(k == 0), stop=(k == 7))
            K.actf(tmp32, b.a.re("p (j t) -> p j t", j=2), AF.Relu)
            K.tt(act32[:, mb * 2:mb * 2 + 2, :], tmp32, tmp32, ALU.mult, E=K.pool)
        for n in range(8):
            b = self.bank()
            for hf in range(2):
                st = stg[ns % 2].a.re("p (m n) -> p m n", m=16)
                ns += 1
                K.dma(K.sp, st, W2[hf * 2048:(hf + 1) * 2048, n * 128:(n + 1) * 128].re("(m p) n -> p m n", p=128))
                for m in range(16):
                    mm_ = hf * 16 + m
                    K.mm(b[:, 0:NB], st[:, m, :], act32[:, mm_, :], start=(mm_ == 0), stop=(mm_ == 31))
            K.tt(h[:, n, :], h[:, n, :], b[:, 0:NB], ALU.add)
        for s_ in range(NSEQ):
            K.dma(K.act, self.hview(dst, s_ * self.seq, P), h[:, :, s_ * P:(s_ + 1) * P])
        K.barrier()
        es.close()
        if not tiles:
            esW.close()
            return
        es = esW
        hts = [K.sb([128, 8, T], F32, "h", es) for _ in range(2)]
        hns = [K.sb([128, 8, T], BF16, "hn", es) for _ in range(2)]
        sqs = [K.sb([128, 8, T], BF16, "sq", es) for _ in range(2)]
        rps = [K.sb([128, T], F32, "rp", es) for _ in range(2)]
        act = K.sb([128, 32, T], BF16, "act", es)
        tmp = [K.sb([128, 2, T], BF16, "tmp", es) for _ in range(2)]
        nb_ = {}

        def norm_a(i):
            n = tiles[i][1]
            K.actf(sqs[i % 2][:, :, 0:n], hts[i % 2][:, :, 0:n], AF.Square)

        def norm_b(i):
            n = tiles[i][1]
            b = self.bank()
            nb_[i] = b
            for c in range(8):
                K.mm(b[:, 0:n], self.CB["ones"], sqs[i % 2][:, c, 0:n], start=(c == 0), stop=(c == 7))

        def norm_c(i):
            n = tiles[i][1]
            rp = rps[i % 2][:, 0:n]
            K.actf(rp, nb_[i][:, 0:n], AF.Sqrt, bias=self.epst, scale=1.0 / D)
            K.op(K.dve, lambda: self.nc.vector.reciprocal(rp.ap, rp.ap), [rp], [rp])
            for c in range(8):
                K.stt(hns[i % 2][:, c, 0:n], hts[i % 2][:, c, 0:n], gain[:, c:c + 1], rp, ALU.mult, ALU.mult)

        K.dma(K.sp, hts[0][:, :, 0:tiles[0][1]], self.hview(src, tiles[0][0], tiles[0][1]))
        norm_a(0)
        norm_b(0)
        norm_c(0)
        for i, (t0, n) in enumerate(tiles):
            h = hts[i % 2]
            hn = hns[i % 2]
            nxt = i + 1 < len(tiles)
            if nxt:
                t1, n1 = tiles[i + 1]
                K.dma(K.sp, hts[(i + 1) % 2][:, :, 0:n1], self.hview(src, t1, n1))
            for mp in range(16):
                b = self.bank()
                for j in range(2):
                    m = mp * 2 + j
                    for k in range(8):
                        K.mm(b[:, j * T:j * T + n], w1[:, k, m * 128:(m + 1) * 128], hn[:, k, 0:n],
                             start=(k == 0), stop=(k == 7))
                tm = tmp[mp % 2]
                K.actf(tm[:, :, 0:n], b.a.re("p (j t) -> p j t", j=2)[:, :, 0:n], AF.Relu)
                K.tt(act[:, mp * 2:mp * 2 + 2, 0:n], tm[:, :, 0:n], tm[:, :, 0:n], ALU.mult, E=K.pool)
            if nxt:
                norm_a(i + 1)
            for npair in range(4):
                b = self.bank()
                for j in range(2):
                    nn = npair * 2 + j
                    for m in range(32):
                        K.mm(b[:, j * T:j * T + n], w2[:, m, nn * 128:(nn + 1) * 128], act[:, m, 0:n],
                             start=(m == 0), stop=(m == 31))
                K.tt(h[:, npair * 2:npair * 2 + 2, 0:n], h[:, npair * 2:npair * 2 + 2, 0:n],
                     b.a.re("p (j t) -> p j t", j=2)[:, :, 0:n], ALU.add)
                if nxt and npair == 1:
                    norm_b(i + 1)
                if nxt and npair == 2:
                    norm_c(i + 1)
            K.dma(K.act, self.hview(dst, t0, n), h[:, :, 0:n])
        K.barrier()
        es.close()

    def final_phase(self, src):
        K = self.K
        es = contextlib.ExitStack()
        hts = [K.sb([128, 8, T], F32, "h", es) for _ in range(2)]
        hn = K.sb([128, 8, T], F32, "hn", es)
        sq = K.sb([128, 8, T], BF16, "sq", es)
        rp = K.sb([128, T], F32, "rp", es)
        ot = [K.sb([128, 2, D], F32, "ot", es) for _ in range(2)]
        gain = self.vec[:, 8 * (2 * self.L):8 * (2 * self.L + 1)]
        ntile = self.NT // T
        K.dma(K.sp, hts[0], self.hview(src, 0, T))
        for i in range(ntile):
            h = hts[i % 2]
            if i + 1 < ntile:
                K.dma(K.sp, hts[(i + 1) % 2], self.hview(src, (i + 1) * T, T))
            self.rmsnorm_fm(h, gain, hn, sq, rp)
            o = ot[i % 2]
            for s in range(2):
                for half in range(2):
                    b = self.bank()
                    for j in range(4):
                        c = half * 4 + j
                        K.mm(b[:, j * 128:(j + 1) * 128], hn[:, c, s * 128:(s + 1) * 128], self.C["ident"])
                    K.cp(o[:, s, half * 512:(half + 1) * 512], b, E=(K.dve if half == 0 else K.act))
            K.dma(K.act, self.out.a[i * T:(i + 1) * T, :].re("(s p) f -> p s f", p=128), o)
        K.barrier()
        es.close()


def host_vecs(inp, L):
    cols = []
    for l in range(L):
        cols.append(np.asarray(inp["norm_mix_w"][l], np.float32).reshape(8, 128).T)
    for l in range(L):
        cols.append(np.asarray(inp["norm_mlp_w"][l], np.float32).reshape(8, 128).T)
    cols.append(np.asarray(inp["final_norm_w"], np.float32).reshape(8, 128).T)
    return np.ascontiguousarray(np.concatenate(cols, axis=1))


def make_in_maps(inp, M, x, ncores):
    L = M.L
    consts = host_consts()
    cst = np.ascontiguousarray(np.concatenate([consts[n] for n in CONST_NAMES], axis=1))
    vecs = host_vecs(inp, L)
    shared = {"consts": cst, "vecs": vecs,
              "mlp_w1": np.ascontiguousarray(inp["mlp_w1"][:L]), "mlp_w2": np.ascontiguousarray(inp["mlp_w2"][:L])}
    if M.mix:
        shared.update(host_mixer_inputs(inp, L))
    maps = []
    for c in range(ncores):
        d = dict(shared)
        d["x"] = np.ascontiguousarray(x[2 * c:2 * c + 2]).reshape(M.NT, D)
        maps.append(d)
    return maps


EV_CONV = 0
EV_ALOG = 48
EV_DTB = 52
EV_GDNW = 56
EV_GLAW = 568
EV_W2B = 1080
EV_N = 1336


def host_even_vec(inp, i):
    v = np.zeros((128, EV_N), np.float32)
    cw = np.asarray(inp["gdn_conv_w"][i], np.float32)
    v[:, EV_CONV:EV_CONV + 48] = cw.T.reshape(12, 128, 4).transpose(1, 0, 2).reshape(128, 48)
    v[:, EV_ALOG:EV_ALOG + 4] = np.asarray(inp["gdn_a_log"][i])[None, :]
    v[:, EV_DTB:EV_DTB + 4] = np.asarray(inp["gdn_dt_bias"][i])[None, :]
    v[:, EV_GDNW:EV_GDNW + 512] = np.tile(np.asarray(inp["gdn_norm_w"][i]), 4)[None, :]
    v[:, EV_GLAW:EV_GLAW + 512] = np.tile(np.asarray(inp["gla_norm_w"][i]), 4)[None, :]
    v[0:16, EV_W2B:EV_W2B + 256] = np.asarray(inp["gla_gate_w2"][i])
    v[16, EV_W2B:EV_W2B + 256] = np.asarray(inp["gla_gate_b"][i])
    return v


def _even_phase(self, l, src, dst):
    K = self.K
    nc = self.nc
    i_l = l // 2
    esP = contextlib.ExitStack()
    C, CB = self.C, self.CB
    P = 128
    tps = self.seq // P
    WIN = self.even_w_in.a[i_l]
    WOUT = self.even_w_out.a[i_l]

    ev = K.sb([128, EV_N], F32, "ev", esP)
    K.dma(K.sp, ev, self.evec.a[i_l])
    nexpa = K.sb([128, 4], F32, "nexpa", esP)
    K.actf(nexpa, ev[:, EV_ALOG:EV_ALOG + 4], AF.Exp)
    K.ts(nexpa, nexpa, -1.0, ALU.mult)
    convw = ev[:, EV_CONV:EV_CONV + 48]
    gain = self.vec[:, 8 * l:8 * (l + 1)]
    SAs = [K.sb([128, 4, 128], F32, "SA", esP) for _ in range(NSEQ)]
    SBs = [K.sb([128, 2, 256], F32, "SB", esP) for _ in range(NSEQ)]
    hist = [K.sb([128, 12, 3], F32, "hist", esP) for _ in range(NSEQ)]
    for s_ in range(NSEQ):
        K.memset(SAs[s_], 0.0)
        K.memset(SBs[s_], 0.0)
        K.memset(hist[s_], 0.0)

    def bc4(v):
        return v.re("p (o t) -> p o t", o=1).bc([128, 4, 128])

    def bcl(v):
        return _v(v).re("p (h o) -> p h o", o=1).bc([128, 4, 128])

    def run_variant(hp, tiles):
        if not tiles:
            return
        es = contextlib.ExitStack()
        AD = F32 if hp else BF16
        ID = C["ident"] if hp else CB["ident"]

        def t512(name, dt=F32):
            return K.sb([128, 4, 128], dt, name, es)

        if hp:
            stg = [K.sb([128, 8, 512], F32, "stg", es) for _ in range(2)]
            cnt = [0]

            def _stream(dram2d, col0, n):
                t = stg[cnt[0] % 2]
                cnt[0] += 1
                K.dma(K.sp, t[:, :, 0:n], dram2d[:, col0:col0 + n].re("(k p) n -> p k n", p=128))
                return lambda k: t[:, k, 0:n]

            wblock = lambda col0, n: _stream(WIN, col0, n)
            wblock32 = wblock
            woblock = lambda n0: _stream(WOUT, n0 * 128, 128)
        else:
            w_in = K.sb([128, 8, EVEN_IN], BF16, "w_in", es)
            for k in range(8):
                K.dma(K.pool, w_in[:, k, :], WIN[k * 128:(k + 1) * 128, :])
            w_out = K.sb([128, 8, D], BF16, "w_out", es)
            K.dma(K.pool, w_out, WOUT.re("(c p) n -> p c n", p=128))
            wg = K.sb([128, 8, 24], F32, "wg", es)
            for k in range(8):
                K.dma(K.sp, wg[:, k, 0:8], WIN[k * 128:(k + 1) * 128, 2048:2056])
                K.dma(K.sp, wg[:, k, 8:24], WIN[k * 128:(k + 1) * 128, 3592:3608])
            wblock = lambda col0, n: (lambda k: w_in[:, k, col0:col0 + n])
            wblock32 = lambda col0, n: (lambda k: wg[:, k, 0:8]) if col0 == 2048 else (lambda k: wg[:, k, 8:24])
            woblock = lambda n0: (lambda c: w_out[:, c, n0 * 128:(n0 + 1) * 128])

        h = K.sb([128, 8, P], F32, "h", es)
        hn32 = K.sb([128, 8, P], F32, "hn32", es)
        hn = hn32 if hp else K.sb([128, 8, P], BF16, "hn", es)
        sq = K.sb([128, 8, P], BF16, "sq", es)
        rp = K.sb([128, P], F32, "rp", es)
        pre = K.sb([128, 12, P + 3], F32, "pre", es)
        cacc = K.sb([128, 12, P], F32, "cacc", es)
        qkv = K.sb([128, 12, P], F32, "qkv", es)
        sq2 = K.sb([128, 8, P], BF16, "sq2", es)
        rinv = K.sb([128, 8, P], F32, "rinv", es)
        oFM = K.sb([128, 8, P], AD, "oFM", es)
        g1 = K.sb([32, P], F32, "g1", es)
        K.memset(g1, 1.0)
        loga = K.sb([128, 256], F32, "loga", es)
        gkT = K.sb([128, 256], F32, "gkT", es)
        qkF = K.sb([128, 4, 128], F32, "qkF", es)
        gcf = K.sb([128, 2, 128], F32, "gcf", es)
        nmid = K.sb([128, 2], F32, "nmid", es)
        eq = K.sb([128, 2, 128], F32, "eq", es)
        ek = K.sb([128, 2, 128], F32, "ek", es)
        eq0 = K.sb([128, 2, 128], F32, "eq0", es)
        qgm = K.sb([128, 2, 128], F32, "qgm", es)
        kgm = K.sb([128, 2, 128], F32, "kgm", es)
        qg0 = K.sb([128, 2, 128], AD, "qg0", es)
        edlB = K.sb([128, 256], F32, "edlB", es)
        kdB = K.sb([128, 256], AD, "kdB", es)
        gv = K.sb([128, 512], AD, "gv", es)
        grs = K.sb([128, 512], F32, "grs", es)
        zs = K.sb([128, 512], F32, "zs", es)
        attm = t512("attm", AD)
        SBb = None if hp else K.sb([128, 2, 256], BF16, "SBb", es)
        junk = K.sb([128, 128], F32, "junk", es)
        ssB = K.sb([128, 4], F32, "ssB", es)
        Gt = K.sb([128, 512], F32, "Gt", es)
        obT = K.sb([128, 512], AD, "obT", es)
        gr8 = K.sb([128, 8], F32, "gr8", es)
        beta = K.sb([128, 4], F32, "beta", es)
        nbeta = K.sb([128, 4], F32, "nbeta", es)
        gA = K.sb([128, 4], F32, "gA", es)
        gcc = K.sb([128, 4], F32, "gcc", es)
        eg = K.sb([128, 4], F32, "eg", es)
        egl = K.sb([128, 4], F32, "egl", es)
        edl = K.sb([128, 4], F32, "edl", es)
        beg = K.sb([128, 4], F32, "beg", es)
        Gtri = t512("Gtri")
        Dm = t512("Dm")
        DTm = t512("DTm")
        A = [t512("A0"), t512("A1")]
        AT = [t512("AT0"), t512("AT1")]
        PT = t512("PT")
        aqkT = t512("aqkT")
        vb = t512("vb")
        kbg = t512("kbg")
        kdA = t512("kdA")
        wneg = t512("wneg")
        vn = t512("vn")
        qg = t512("qg")
        ssA = K.sb([128, 4], F32, "ssA", es)
        oaT = K.sb([128, 512], AD, "oaT", es)

        for (s_, it) in tiles:
            SA, SB_ = SAs[s_], SBs[s_]
            K.dma(K.sp, h, self.hview(src, it * P, P))
            K.cp(pre[:, :, 0:3], hist[s_], E=K.pool)
            if not hp:
                K.cp(SBb, SB_, E=K.pool)
            Sst = SB_ if hp else SBb
            self.rmsnorm_fm(h, gain, [hn32] if hp else [hn, hn32], sq, rp, n=P)
            for c4 in range(3):
                wb = wblock(c4 * 512, 512)
                b = self.bank()
                for j in range(4):
                    for k in range(8):
                        K.mm(b[:, j * P:(j + 1) * P], wb(k)[:, j * 128:(j + 1) * 128], hn[:, k, :],
                             start=(k == 0), stop=(k == 7))
                K.cp(pre[:, c4 * 4:(c4 + 1) * 4, 3:3 + P], b.a.re("p (j t) -> p j t", j=4), E=K.act)
            shared = {}
            wb = wblock32(2048, 8)
            bg = self.bank()
            for k in range(8):
                K.mm(bg[:, 0:8], hn32[:, k, :], wb(k), start=(k == 0), stop=(k == 7))
            K.cp(gr8, bg[:, 0:8])
            for c in range(12):
                K.actf(cacc[:, c, :], pre[:, c, 0:P], AF.Copy, scale=convw[:, c * 4:c * 4 + 1])
                for j in range(1, 4):
                    K.stt(cacc[:, c, :], pre[:, c, j:j + P], convw[:, c * 4 + j:c * 4 + j + 1], cacc[:, c, :],
                          ALU.mult, ALU.add)
            K.cp(hist[s_], pre[:, :, P:P + 3], E=K.pool)
            K.actf(qkv, cacc, AF.Silu)
            K.tt(sq2, qkv[:, 0:8, :], qkv[:, 0:8, :], ALU.mult, E=K.pool)
            for half in range(2):
                b = self.bank()
                K.mm(b, CB["ones"], sq2[:, half * 4:(half + 1) * 4, :])
                K.actf(rinv[:, half * 4:(half + 1) * 4, :], b.a.re("p (j t) -> p j t", j=4), AF.Sqrt, bias=self.epst)
            K.op(K.dve, lambda: nc.vector.reciprocal(rinv.a.ap, rinv.a.ap), [rinv], [rinv])
            K.stt(qkv[:, 0:4, :], qkv[:, 0:4, :], float(128 ** -0.5), rinv[:, 0:4, :], ALU.mult, ALU.mult)
            K.tt(qkv[:, 4:8, :], qkv[:, 4:8, :], rinv[:, 4:8, :], ALU.mult, E=K.pool)

            def seg_tm1():
                wb = wblock(1536, 512)
                bz = self.bank()
                for k in range(8):
                    K.mm(bz, hn[:, k, :], wb(k), start=(k == 0), stop=(k == 7))
                K.actf(zs, bz, AF.Silu)
                wb = wblock(2312, 256)
                bgk = self.bank()
                for k in range(8):
                    K.mm(bgk[:, 0:256], hn[:, k, :], wb(k), start=(k == 0), stop=(k == 7))
                K.cp(gkT, bgk[:, 0:256], E=K.act)
            def seg_tm2():
                wb = wblock(2568, 512)
                bgv = self.bank()
                for k in range(8):
                    K.mm(bgv, hn[:, k, :], wb(k), start=(k == 0), stop=(k == 7))
                K.cp(gv, bgv, E=K.act)
                wb = wblock(3080, 512)
                bgr = self.bank()
                for k in range(8):
                    K.mm(bgr, hn[:, k, :], wb(k), start=(k == 0), stop=(k == 7))
                K.actf(grs, bgr, AF.Silu)
            def seg_tm3():
                wb = wblock32(3592, 16)
                bl = self.bank()
                for k in range(8):
                    K.mm(bl[0:16, 0:P], wb(k), hn32[:, k, :], start=(k == 0), stop=(k == 7))
                K.cp(g1[0:16, :], bl[0:16, 0:P])
                wb = wblock(2056, 512)
                bqk = self.bank()
                for j in range(4):
                    for k in range(8):
                        K.mm(bqk[:, j * P:(j + 1) * P], wb(k)[:, j * 128:(j + 1) * 128], hn[:, k, :],
                             start=(k == 0), stop=(k == 7))
                K.cp(qkF, bqk.a.re("p (j t) -> p j t", j=4))
            def seg_gla1():
                bx = self.bank()
                K.mm(bx[:, 0:256], g1, ev[0:32, EV_W2B:EV_W2B + 256])
                K.ts(loga, bx[:, 0:256], -20.0, ALU.max)
                K.actf(loga, loga, AF.Exp, scale=-1.0)
                K.actf(loga, loga, AF.Ln, bias=1.0)
                K.ts(loga, loga, -1.0 / 16.0, ALU.mult, -1.0, ALU.max)
                bgc = self.bank()
                for cc in range(2):
                    K.mm(bgc[:, cc * 128:(cc + 1) * 128], loga[:, cc * 128:(cc + 1) * 128], C["tri"])
                K.mm(bgc[:, 256:512], C["ustr"], loga)
                K.cp(gcf, bgc[:, 0:256].re("p (c t) -> p c t", c=2))
                K.actf(edlB, bgc[:, 256:512], AF.Exp)
                K.tt(kdB, gkT, edlB, ALU.mult)
                K.ts(nmid, gcf[:, :, 63], -1.0, ALU.mult)
                for cc in range(2):
                    K.actf(eq[:, cc, :], gcf[:, cc, :], AF.Exp, bias=nmid[:, cc:cc + 1])
                    K.actf(ek[:, cc, :], gcf[:, cc, :], AF.Exp, bias=gcf[:, cc, 63:64], scale=-1.0)
                K.actf(eq0, gcf, AF.Exp)
                K.stt(qgm, qkF[:, 0:2, :], 0.125, eq, ALU.mult, ALU.mult)
                K.tt(kgm, qkF[:, 2:4, :], ek, ALU.mult)
                K.stt(qg0, qkF[:, 0:2, :], 0.125, eq0, ALU.mult, ALU.mult)
            def seg_gla2():
                bat = self.bank()
                for hh in range(4):
                    pr = slice((hh % 2) * 64, (hh % 2) * 64 + 64)
                    K.mm(bat[:, hh * 128:(hh + 1) * 128], kgm[pr, hh // 2, :], qgm[pr, hh // 2, :], ser=True)
                K.tt(attm, bat.a.re("p (h t) -> p h t", h=4), bc4(C["tri"]), ALU.mult)
                bo = self.bank()
                shared['bo'] = bo
                for hh in range(4):
                    pr = slice((hh % 2) * 64, (hh % 2) * 64 + 64)
                    K.mm(bo[:, hh * 128:(hh + 1) * 128], qg0[pr, hh // 2, :],
                         Sst[pr, hh // 2, (hh % 2) * 128:(hh % 2) * 128 + 128], start=True, stop=False, ser=True)
                    K.mm(bo[:, hh * 128:(hh + 1) * 128], attm[:, hh, :], gv[:, hh * 128:(hh + 1) * 128],
                         start=False, stop=True, ser=True)
                bs = self.bank()
                for pp in range(2):
                    K.mm(bs[:, pp * 256:(pp + 1) * 256], kdB[:, pp * 128:(pp + 1) * 128], gv[:, pp * 256:(pp + 1) * 256])
                for pp in range(2):
                    K.stt(SB_[:, pp, :], SB_[:, pp, :], eq0[:, pp, 127:128], bs[:, pp * 256:(pp + 1) * 256],
                          ALU.mult, ALU.add)
            def seg_gla3():
                bo = shared['bo']
                for hh in range(4):
                    K.actf(junk, bo[:, hh * 128:(hh + 1) * 128], AF.Square, accum=ssB[:, hh:hh + 1])
                K.actf(ssB, ssB, AF.Sqrt, bias=self.epst, scale=1.0 / 128.0)
                K.op(K.dve, lambda: nc.vector.reciprocal(ssB.a.ap, ssB.a.ap), [ssB], [ssB])
                K.tt(Gt, grs, ev[:, EV_GLAW:EV_GLAW + 512], ALU.mult, E=K.pool)
                for hh in range(4):
                    sl = slice(hh * 128, (hh + 1) * 128)
                    K.stt(obT[:, sl], bo[:, sl], ssB[:, hh:hh + 1], Gt[:, sl], ALU.mult, ALU.mult)
                btr = self.bank()
                for hh in range(4):
                    K.mm(btr[:, hh * 128:(hh + 1) * 128], obT[:, hh * 128:(hh + 1) * 128], ID)
                K.cp(oFM[:, 4:8, :], btr.a.re("p (h t) -> p h t", h=4), E=K.act)
            segs = [seg_tm1, seg_tm2, seg_tm3, seg_gla1, seg_gla2, seg_gla3]
            K.actf(beta, gr8[:, 0:4], AF.Sigmoid)
            K.ts(nbeta, beta, -1.0, ALU.mult)
            K.tt(gA, gr8[:, 4:8], ev[:, EV_DTB:EV_DTB + 4], ALU.add)
            K.actf(gA, gA, AF.Exp)
            K.actf(gA, gA, AF.Ln, bias=1.0)
            K.tt(gA, gA, nexpa, ALU.mult)
            K.tt(Gtri, bc4(C["tri"]), bcl(gA), ALU.mult)
            bsm = self.bank()
            K.mm(bsm[:, 0:4], C["tri"], gA)
            K.mm(bsm[:, 8:12], C["ones"], gA)
            br = self.bank()
            K.mm(br, C["ones"], Gtri)
            K.cp(gcc, bsm[:, 0:4])
            K.actf(eg, gcc, AF.Exp)
            K.actf(egl, bsm[:, 8:12], AF.Exp)
            K.tt(edl, bsm[:, 8:12], gcc, ALU.subtract)
            K.actf(edl, edl, AF.Exp)
            K.tt(beg, beta, eg, ALU.mult)
            for hh in range(4):
                sl = slice(hh * 128, (hh + 1) * 128)
                K.ts(Dm[:, hh, :], br[:, sl], gcc[:, hh:hh + 1], ALU.subtract, 0.0, ALU.max)
                K.ts(DTm[:, hh, :], br[:, sl], gcc[:, hh:hh + 1], ALU.subtract, 0.0, ALU.min)
            K.actf(Dm, Dm, AF.Exp, scale=-1.0)
            K.actf(DTm, DTm, AF.Exp)
            K.actf(qg, br.a.re("p (h t) -> p h t", h=4), AF.Exp)
            K.tt(Dm, Dm, bc4(C["ustr"]), ALU.mult, E=K.pool)
            K.tt(DTm, DTm, bc4(C["tri"]), ALU.mult, E=K.pool)
            K.tt(qg, qg, qkv[:, 0:4, :], ALU.mult, E=K.pool)
            bkk = self.bank()
            bqt = self.bank()
            for hh in range(4):
                sl = slice(hh * 128, (hh + 1) * 128)
                K.mm(bkk[:, sl], qkv[:, 4 + hh, :], qkv[:, 4 + hh, :])
                K.mm(bqt[:, sl], qkv[:, 4 + hh, :], qkv[:, hh, :])
            for hh in range(4):
                K.stt(A[0][:, hh, :], bkk[:, hh * 128:(hh + 1) * 128], nbeta[:, hh:hh + 1], Dm[:, hh, :],
                      ALU.mult, ALU.mult)
            K.tt(aqkT, bqt.a.re("p (h t) -> p h t", h=4), DTm, ALU.mult)
            bnt = self.bank()
            for hh in range(4):
                K.mm(bnt[:, hh * 128:(hh + 1) * 128], A[0][:, hh, :], C["ident"])
            K.cp(AT[0], bnt.a.re("p (h t) -> p h t", h=4), E=K.act)
            K.tt(PT, bnt.a.re("p (h t) -> p h t", h=4), bc4(C["ident"]), ALU.add)
            bkt = self.bank()
            bvt = self.bank()
            for hh in range(4):
                sl = slice(hh * 128, (hh + 1) * 128)
                K.mm(bkt[:, sl], qkv[:, 4 + hh, :], C["ident"])
                K.mm(bvt[:, sl], qkv[:, 8 + hh, :], C["ident"])
            K.tt(vb, bvt.a.re("p (h t) -> p h t", h=4), bcl(beta), ALU.mult)
            K.tt(kbg, bkt.a.re("p (h t) -> p h t", h=4), bcl(beg), ALU.mult)
            K.tt(kdA, bkt.a.re("p (h t) -> p h t", h=4), bcl(edl), ALU.mult)
            cur = 0
            for itr in range(6):
                nxt = 1 - cur
                b1 = self.bank()
                b2 = self.bank()
                for hh in range(4):
                    sl = slice(hh * 128, (hh + 1) * 128)
                    K.mm(b1[:, sl], AT[cur][:, hh, :], A[cur][:, hh, :])
                    K.mm(b2[:, sl], A[cur][:, hh, :], AT[cur][:, hh, :])
                K.cp(A[nxt], b1.a.re("p (h t) -> p h t", h=4), E=K.act)
                K.cp(AT[nxt], b2.a.re("p (h t) -> p h t", h=4), E=K.dve)
                b3 = self.bank()
                for hh in range(4):
                    K.mm(b3[:, hh * 128:(hh + 1) * 128], A[nxt][:, hh, :], PT[:, hh, :])
                K.tt(PT, PT, b3.a.re("p (h t) -> p h t", h=4), ALU.add)
                cur = nxt
                segs[itr]()
            bw = self.bank()
            for hh in range(4):
                K.mm(bw[:, hh * 128:(hh + 1) * 128], kbg[:, hh, :], PT[:, hh, :])
            K.actf(wneg, bw.a.re("p (h t) -> p h t", h=4), AF.Copy, scale=-1.0)
            bvn = self.bank()
            for hh in range(4):
                sl = slice(hh * 128, (hh + 1) * 128)
                K.mm(bvn[:, sl], PT[:, hh, :], vb[:, hh, :], start=True, stop=False)
                K.mm(bvn[:, sl], wneg[:, hh, :], SA[:, hh, :], start=False, stop=True)
            K.cp(vn, bvn.a.re("p (h t) -> p h t", h=4))
            boa = self.bank()
            for hh in range(4):
                sl = slice(hh * 128, (hh + 1) * 128)
                K.mm(boa[:, sl], qg[:, hh, :], SA[:, hh, :], start=True, stop=False)
                K.mm(boa[:, sl], aqkT[:, hh, :], vn[:, hh, :], start=False, stop=True)
            bsa = self.bank()
            for hh in range(4):
                K.mm(bsa[:, hh * 128:(hh + 1) * 128], kdA[:, hh, :], vn[:, hh, :])
            for hh in range(4):
                K.stt(SA[:, hh, :], SA[:, hh, :], egl[:, hh:hh + 1], bsa[:, hh * 128:(hh + 1) * 128],
                      ALU.mult, ALU.add)
            for hh in range(4):
                K.actf(junk, boa[:, hh * 128:(hh + 1) * 128], AF.Square, accum=ssA[:, hh:hh + 1])
            K.actf(ssA, ssA, AF.Sqrt, bias=self.epst, scale=1.0 / 128.0)
            K.op(K.dve, lambda: nc.vector.reciprocal(ssA.a.ap, ssA.a.ap), [ssA], [ssA])
            K.tt(Gt, zs, ev[:, EV_GDNW:EV_GDNW + 512], ALU.mult, E=K.pool)
            for hh in range(4):
                sl = slice(hh * 128, (hh + 1) * 128)
                K.stt(oaT[:, sl], boa[:, sl], ssA[:, hh:hh + 1], Gt[:, sl], ALU.mult, ALU.mult)
            btr = self.bank()
            for hh in range(4):
                K.mm(btr[:, hh * 128:(hh + 1) * 128], oaT[:, hh * 128:(hh + 1) * 128], ID)
            K.cp(oFM[:, 0:4, :], btr.a.re("p (h t) -> p h t", h=4), E=K.act)

            for half in range(2):
                b = self.bank()
                for j in range(4):
                    wo = woblock(half * 4 + j)
                    for c in range(8):
                        K.mm(b[:, j * P:(j + 1) * P], wo(c), oFM[:, c, :], start=(c == 0), stop=(c == 7))
                K.tt(h[:, half * 4:(half + 1) * 4, :], h[:, half * 4:(half + 1) * 4, :],
                     b.a.re("p (j t) -> p j t", j=4), ALU.add)
            K.dma(K.act, self.hview(dst, it * P, P), h)
        K.barrier()
        es.close()

    run_variant(True, [(s_, s_ * tps) for s_ in range(NSEQ)])
    run_variant(False, [(s_, s_ * tps + j) for s_ in range(NSEQ) for j in range(1, tps)])
    K.barrier()
    esP.close()


Model.even_phase = _even_phase


def _declare_mixer_inputs(self, ne, no):
    K = self.K
    self.even_w_in = K.dram("even_w_in", [ne, D, EVEN_IN], F32, kind="ExternalInput")
    self.even_w_out = K.dram("even_w_out", [ne, D, D], F32, kind="ExternalInput")
    self.evec = K.dram("evec", [ne, 128, EV_N], F32, kind="ExternalInput")
    if no > 0:
        self.declare_odd_inputs(no)


Model.declare_mixer_inputs = _declare_mixer_inputs


def host_mixer_inputs(inp, L):
    ne = (L + 1) // 2
    no = L // 2
    d = {"even_w_in": np.ascontiguousarray(inp["even_w_in"][:ne]),
         "even_w_out": np.ascontiguousarray(inp["even_w_out"][:ne]),
         "evec": np.stack([host_even_vec(inp, i) for i in range(ne)])}
    if no > 0:
        d.update(host_odd_inputs(inp, no))
    return d


OV_CONV = 0
OV_CONVB = 32
OV_DTB = 40
OV_ALOG = 48
OV_D = 56
OV_SNW = 568
OV_MU = 1080
OV_W0 = 1094
OV_A0 = 1606
OV_KK = 1610
OV_KA = 1614
OV_RK = 1618
OV_GNW = 1622
OV_GNB = 2134
OV_SEL = 2646
OV_N = 2678


def host_odd_vec(inp, i):
    v = np.zeros((128, OV_N), np.float32)
    f = lambda a, n: np.asarray(a, np.float32).reshape(n, 128).T
    cw = np.asarray(inp["ssd_conv_w"][i], np.float32)
    v[:, OV_CONV:OV_CONV + 32] = cw.T.reshape(8, 128, 4).transpose(1, 0, 2).reshape(128, 32)
    v[:, OV_CONVB:OV_CONVB + 8] = f(inp["ssd_conv_b"][i], 8)
    v[:, OV_DTB:OV_DTB + 8] = np.asarray(inp["ssd_dt_bias"][i])[None, :]
    v[:, OV_ALOG:OV_ALOG + 8] = np.asarray(inp["ssd_a_log"][i])[None, :]
    v[:, OV_D:OV_D + 512] = np.repeat(np.asarray(inp["ssd_d"][i]), 64)[None, :]
    v[:, OV_SNW:OV_SNW + 512] = np.asarray(inp["ssd_norm_w"][i])[None, :]
    v[:, OV_MU:OV_MU + 14] = f(inp["rwkv_mu"][i], 14)
    v[:, OV_W0:OV_W0 + 512] = np.asarray(inp["rwkv_w0"][i])[None, :]
    v[:, OV_A0:OV_A0 + 4] = f(inp["rwkv_a0"][i], 4)
    v[:, OV_KK:OV_KK + 4] = f(inp["rwkv_k_k"][i], 4)
    v[:, OV_KA:OV_KA + 4] = f(inp["rwkv_k_a"][i], 4)
    v[:, OV_RK:OV_RK + 4] = f(np.asarray(inp["rwkv_r_k"][i]).reshape(-1), 4)
    v[:, OV_GNW:OV_GNW + 512] = np.asarray(inp["rwkv_gn_w"][i])[None, :]
    v[:, OV_GNB:OV_GNB + 512] = np.asarray(inp["rwkv_gn_b"][i])[None, :]
    p = np.arange(128)
    sel = np.zeros((128, 4, 8), np.float32)
    for c in range(4):
        sel[p, c, 2 * c + p // 64] = 1.0
    v[:, OV_SEL:OV_SEL + 32] = sel.reshape(128, 32)
    return v


def host_odd_inputs(inp, no):
    a2 = np.zeros((no, 128, 512), np.float32)
    a2[:, 64:128, :] = np.asarray(inp["rwkv_a2"][:no])
    w2 = np.zeros((no, 128, 512), np.float32)
    w2[:, 0:64, :] = np.asarray(inp["rwkv_w2"][:no])
    return {"odd_w_in": np.ascontiguousarray(inp["odd_w_in"][:no]),
            "odd_w_out": np.ascontiguousarray(inp["odd_w_out"][:no]),
            "ovec": np.stack([host_odd_vec(inp, i) for i in range(no)]),
            "rw2": w2, "ra2": a2, "rg2": np.ascontiguousarray(inp["rwkv_g2"][:no])}


def _declare_odd_inputs(self, no):
    K = self.K
    self.odd_w_in = K.dram("odd_w_in", [no, D, ODD_IN], F32, kind="ExternalInput")
    self.odd_w_out = K.dram("odd_w_out", [no, D, D], F32, kind="ExternalInput")
    self.ovec = K.dram("ovec", [no, 128, OV_N], F32, kind="ExternalInput")
    self.rw2 = K.dram("rw2", [no, 128, 512], F32, kind="ExternalInput")
    self.ra2 = K.dram("ra2", [no, 128, 512], F32, kind="ExternalInput")
    self.rg2 = K.dram("rg2", [no, 128, 512], F32, kind="ExternalInput")


Model.declare_odd_inputs = _declare_odd_inputs


def _odd_phase(self, l, src, dst):
    K = self.K
    nc = self.nc
    i_l = l // 2
    esP = contextlib.ExitStack()
    C, CB = self.C, self.CB
    P = 128
    tps = self.seq // P
    WIN = self.odd_w_in.a[i_l]
    WOUT = self.odd_w_out.a[i_l]

    def bc4(v, n=4):
        return v.re("p (o t) -> p o t", o=1).bc([128, n, 128])

    def bcl(v, n, m):
        return _v(v).re("p (h o) -> p h o", o=1).bc([128, n, m])

    def recip(t):
        K.op(K.dve, lambda: nc.vector.reciprocal(_v(t).ap, _v(t).ap), [t], [t])

    ov = K.sb([128, OV_N], F32, "ov", esP)
    K.dma(K.sp, ov, self.ovec.a[i_l])
    w2t = K.sb([128, 512], F32, "w2t", esP)
    a2t = K.sb([128, 512], F32, "a2t", esP)
    g2t = K.sb([128, 512], F32, "g2t", esP)
    K.dma(K.sp, w2t, self.rw2.a[i_l])
    K.dma(K.sp, a2t, self.ra2.a[i_l])
    K.dma(K.sp, g2t, self.rg2.a[i_l])
    nexpa = K.sb([128, 8], F32, "nexpa", esP)
    K.actf(nexpa, ov[:, OV_ALOG:OV_ALOG + 8], AF.Exp)
    K.ts(nexpa, nexpa, -1.0, ALU.mult)
    gneps = K.sb([128, 1], F32, "gneps", esP)
    K.memset(gneps, 64e-5)
    convw = ov[:, OV_CONV:OV_CONV + 32]
    gain = self.vec[:, 8 * l:8 * (l + 1)]
    STs = [K.sb([128, 512], F32, "ST", esP) for _ in range(NSEQ)]
    Hss = [K.sb([128, 4, 128], F32, "Hs", esP) for _ in range(NSEQ)]
    histC = [K.sb([128, 8, 3], F32, "histC", esP) for _ in range(NSEQ)]
    histP = [K.sb([128, 14, 1], F32, "histP", esP) for _ in range(NSEQ)]
    for s_ in range(NSEQ):
        K.memset(STs[s_], 0.0)
        K.memset(Hss[s_], 0.0)
        K.memset(histC[s_], 0.0)
        K.memset(histP[s_], 0.0)

    def run_variant(hp, tiles):
        if not tiles:
            return
        es = contextlib.ExitStack()
        AD = F32 if hp else BF16
        ID = C["ident"] if hp else CB["ident"]
        if hp:
            stg = [K.sb([128, 8, 512], F32, "stg", es) for _ in range(2)]
            cnt = [0]

            def _stream(dram2d, col0, n):
                t = stg[cnt[0] % 2]
                cnt[0] += 1
                K.dma(K.sp, t[:, :, 0:n], dram2d[:, col0:col0 + n].re("(k p) n -> p k n", p=128))
                return lambda k: t[:, k, 0:n]

            wblock = lambda col0, n: _stream(WIN, col0, n)
            wblock32 = wblock
            woblock = lambda n0: _stream(WOUT, n0 * 128, 128)
        else:
            w_in = K.sb([128, 8, ODD_IN], BF16, "w_in", es)
            for k in range(8):
                K.dma(K.pool, w_in[:, k, :], WIN[k * 128:(k + 1) * 128, :])
            w_out = K.sb([128, 8, D], BF16, "w_out", es)
            K.dma(K.pool, w_out, WOUT.re("(c p) n -> p c n", p=128))
            wg = K.sb([128, 8, 136], F32, "wg", es)
            for k in range(8):
                K.dma(K.sp, wg[:, k, 0:8], WIN[k * 128:(k + 1) * 128, 1536:1544])
                K.dma(K.sp, wg[:, k, 8:136], WIN[k * 128:(k + 1) * 128, 3080:3208])
            wblock = lambda col0, n: (lambda k: w_in[:, k, col0:col0 + n])
            wblock32 = lambda col0, n: (lambda k: wg[:, k, 0:8])
            woblock = lambda n0: (lambda c: w_out[:, c, n0 * 128:(n0 + 1) * 128])

        def sb(shape, name, dt=F32):
            return K.sb(shape, dt, name, es)

        h = sb([128, 8, P], "h"); hn32 = sb([128, 8, P], "hn32"); hn = hn32 if hp else sb([128, 8, P], "hn", BF16)
        sq = sb([128, 8, P], "sq", BF16); rp = sb([128, P], "rp")
        preC = sb([128, 8, P + 3], "preC"); cacc = sb([128, 8, P], "cacc"); xbc = sb([128, 8, P], "xbc")
        pdp = sb([128, 14, P + 1], "pdp"); pdm = sb([128, 14, P], "pdm"); pdd = pdm
        zs = sb([128, 512], "zs"); oFM = sb([128, 8, P], "oFM", AD)
        dt8 = sb([128, 8], "dt8"); a8 = sb([128, 8], "a8"); acc8 = sb([128, 8], "acc8")
        eac8 = sb([128, 8], "eac8"); edl8 = sb([128, 8], "edl8"); cd8 = sb([128, 8], "cd8"); dte8 = sb([128, 8], "dte8")
        segT = sb([128, 8, 128], "segT"); MT = sb([128, 8, 128], "MT", AD)
        xTs = sb([128, 512], "xTs"); xdt = sb([128, 8, 64], "xdt", AD); xdte = sb([128, 8, 64], "xdte", AD)
        Bb = sb([128, 256], "Bb", AD); Cb = sb([128, 2, 128], "Cb", AD)
        STb = None if hp else sb([128, 512], "STb", BF16)
        y1 = sb([128, 512], "y1"); y2 = sb([128, 512], "y2"); ss2 = sb([128, 2], "ss2"); junk = sb([128, 256], "junk")
        ycT = sb([128, 512], "ycT", AD)
        txw = sb([64, P], "txw"); ldT = sb([128, 512], "ldT"); aF = sb([128, 4, P], "aF"); sg = sb([128, P], "sg")
        kp = sb([128, 4, P], "kp"); kq = sb([128, 4, P], "kq"); kk = kq; bm = sb([128, 4, P], "bm"); gl4 = sb([128, 4], "gl4")
        sqk = sb([128, 4, P], "sqk"); lg = sb([128, 4, P], "lg"); lgx = sb([128, 4, P], "lgx"); nmid = sb([128, 4], "nmid")
        e_r = sb([128, 4, P], "e_r"); e_a = sb([128, 4, P], "e_a"); e_b = sb([128, 4, P], "e_b")
        e_r0 = sb([128, 4, P], "e_r0"); e_a0 = sb([128, 4, P], "e_a0")
        rt_ = e_r; r0_ = e_r0; at_ = e_a; a0_ = e_a0; bt_ = e_b; kt_ = xbc[:, 4:8, :]
        edlT = sb([128, 512], "edlT"); kdT = sb([128, 512], "kdT"); bdT = sb([128, 512], "bdT"); vT = sb([128, 512], "vT")
        A = [segT[:, 0:4, :], cacc[:, 0:4, :]]
        AT = [segT[:, 4:8, :], cacc[:, 4:8, :]]
        PT = xbc[:, 0:4, :]
        AakT = y1.a.re("p (h t) -> p h t", h=4); RbT = y2.a.re("p (h t) -> p h t", h=4); RkT = xTs.a.re("p (h t) -> p h t", h=4)
        Xs = sb([128, 4, 64], "Xs"); Us = sb([128, 8, 64], "Us")
        yd = lg.a.re("p c t -> p (c t)").re("p (h q) -> p h q", h=8); ym = lgx.a.re("p c t -> p (c t)").re("p (h q) -> p h q", h=8); mean8 = sb([128, 8], "mean8"); var8 = sb([128, 8], "var8")
        rkr = sb([128, 4, P], "rkr"); rks = sb([128, 8], "rks"); gT = sb([128, 512], "gT"); ydT = sb([128, 512], "ydT", AD)

        for (s_, it) in tiles:
            ST, Hs = STs[s_], Hss[s_]
            K.dma(K.sp, h, self.hview(src, it * P, P))
            K.cp(preC[:, :, 0:3], histC[s_], E=K.pool)
            K.cp(pdp[:, :, 0:1], histP[s_], E=K.pool)
            if not hp:
                K.cp(STb, ST, E=K.pool)
            Sst = ST if hp else STb
            self.rmsnorm_fm(h, gain, [hn32] if hp else [hn, hn32], sq, rp, n=P)
            for c4 in range(2):
                wb = wblock(512 + c4 * 512, 512)
                b = self.bank()
                for j in range(4):
                    for k in range(8):
                        K.mm(b[:, j * P:(j + 1) * P], wb(k)[:, j * 128:(j + 1) * 128], hn[:, k, :],
                             start=(k == 0), stop=(k == 7))
                K.cp(preC[:, c4 * 4:(c4 + 1) * 4, 3:3 + P], b.a.re("p (j t) -> p j t", j=4), E=K.act)
            for c4 in range(4):
                b = self.bank()
                nj = 4 if c4 < 3 else 2
                wb = wblock(1544 + c4 * 512, nj * 128)
                for j in range(nj):
                    c = c4 * 4 + j
                    for k in range(8):
                        if c == 12 and not hp:
                            K.mm(b[:, j * P:(j + 1) * P], wg[:, k, 8:136], hn32[:, k, :], start=(k == 0), stop=(k == 7))
                        else:
                            K.mm(b[:, j * P:(j + 1) * P], wb(k)[:, j * 128:(j + 1) * 128], hn[:, k, :],
                                 start=(k == 0), stop=(k == 7))
                K.cp(pdp[:, c4 * 4:c4 * 4 + nj, 1:1 + P], b[:, 0:nj * P].re("p (j t) -> p j t", j=nj), E=K.act)
            wb = wblock(0, 512)
            bz = self.bank()
            for k in range(8):
                K.mm(bz, hn[:, k, :], wb(k), start=(k == 0), stop=(k == 7))
            K.actf(zs, bz, AF.Silu)
            wb = wblock32(1536, 8)
            bg = self.bank()
            for k in range(8):
                K.mm(bg[:, 0:8], hn32[:, k, :], wb(k), start=(k == 0), stop=(k == 7))
            K.tt(dt8, bg[:, 0:8], ov[:, OV_DTB:OV_DTB + 8], ALU.add)
            K.ts(dt8, dt8, 60.0, ALU.min)
            K.actf(dt8, dt8, AF.Exp)
            K.actf(dt8, dt8, AF.Ln, bias=1.0)
            K.tt(a8, dt8, nexpa, ALU.mult)
            for c in range(8):
                K.actf(cacc[:, c, :], preC[:, c, 0:P], AF.Copy, scale=convw[:, c * 4:c * 4 + 1])
                for j in range(1, 4):
                    K.stt(cacc[:, c, :], preC[:, c, j:j + P], convw[:, c * 4 + j:c * 4 + j + 1], cacc[:, c, :],
                          ALU.mult, ALU.add)
                K.actf(xbc[:, c, :], cacc[:, c, :], AF.Silu, bias=ov[:, OV_CONVB + c:OV_CONVB + c + 1])
            K.cp(histC[s_], preC[:, :, P:P + 3], E=K.pool)
            Atri = cacc
            K.tt(Atri, bc4(C["tri"], 8), bcl(a8, 8, 128), ALU.mult)
            bsm = self.bank()
            K.mm(bsm[:, 0:8], C["tri"], a8)
            K.mm(bsm[:, 8:16], C["ones"], a8)
            K.cp(acc8, bsm[:, 0:8])
            K.actf(eac8, acc8, AF.Exp)
            K.actf(cd8, bsm[:, 8:16], AF.Exp)
            K.tt(edl8, bsm[:, 8:16], acc8, ALU.subtract)
            K.actf(edl8, edl8, AF.Exp)
            K.tt(dte8, dt8, edl8, ALU.mult)
            for hg in range(2):
                br = self.bank()
                K.mm(br, C["ones"], Atri[:, hg * 4:(hg + 1) * 4, :])
                for hh in range(4):
                    hd = hg * 4 + hh
                    K.ts(segT[:, hd, :], br[:, hh * 128:(hh + 1) * 128], acc8[:, hd:hd + 1], ALU.subtract, 0.0, ALU.min)
            K.actf(segT, segT, AF.Exp)
            K.tt(segT, segT, bc4(C["tri"], 8), ALU.mult, E=K.pool)
            bcb = self.bank()
            for g in range(2):
                K.mm(bcb[:, g * 128:(g + 1) * 128], xbc[:, 4 + g, :], xbc[:, 6 + g, :])
            for g in range(2):
                K.tt(MT[:, g * 4:(g + 1) * 4, :], segT[:, g * 4:(g + 1) * 4, :],
                     bcb[:, g * 128:(g + 1) * 128].re("p (o t) -> p o t", o=1).bc([128, 4, 128]), ALU.mult)
            K.cp(Cb, xbc[:, 6:8, :], E=K.pool)
            bxt = self.bank()
            for c in range(4):
                K.mm(bxt[:, c * 128:(c + 1) * 128], xbc[:, c, :], C["ident"])
            bbt = self.bank()
            for g in range(2):
                K.mm(bbt[:, g * 128:(g + 1) * 128], xbc[:, 4 + g, :], C["ident"])
            K.cp(xTs, bxt, E=K.act)
            K.cp(Bb, bbt[:, 0:256], E=K.act)
            K.tt(xdt, xTs.a.re("p (h q) -> p h q", h=8), bcl(dt8, 8, 64), ALU.mult)
            K.tt(xdte, xTs.a.re("p (h q) -> p h q", h=8), bcl(dte8, 8, 64), ALU.mult)
            by = self.bank()
            for hd in range(8):
                K.mm(by[:, hd * 64:(hd + 1) * 64], MT[:, hd, :], xdt[:, hd, :])
            bi = self.bank()
            for g in range(2):
                K.mm(bi[:, g * 256:(g + 1) * 256], Cb[:, g, :], Sst[:, g * 256:(g + 1) * 256])
            bst = self.bank()
            for g in range(2):
                K.mm(bst[:, g * 256:(g + 1) * 256], Bb[:, g * 128:(g + 1) * 128],
                     xdte[:, g * 4:(g + 1) * 4, :].re("p h q -> p (h q)"))
            K.tt(y1.a.re("p (h q) -> p h q", h=8), bi.a.re("p (h q) -> p h q", h=8), bcl(eac8, 8, 64), ALU.mult)
            K.tt(y1, y1, by, ALU.add)
            K.tt(y2, xTs, ov[:, OV_D:OV_D + 512], ALU.mult, E=K.pool)
            K.tt(y1, y1, y2, ALU.add)
            K.tt(y1, y1, zs, ALU.mult)
            K.tt(ST.a.re("p (h q) -> p h q", h=8), ST.a.re("p (h q) -> p h q", h=8), bcl(cd8, 8, 64), ALU.mult)
            K.tt(ST, ST, bst, ALU.add)
            for g in range(2):
                K.actf(junk, y1[:, g * 256:(g + 1) * 256], AF.Square, accum=ss2[:, g:g + 1])
            K.actf(ss2, ss2, AF.Sqrt, bias=self.epst, scale=1.0 / 256.0)
            recip(ss2)
            K.tt(y2, y1, ov[:, OV_SNW:OV_SNW + 512], ALU.mult, E=K.pool)
            for g in range(2):
                K.ts(ycT[:, g * 256:(g + 1) * 256], y2[:, g * 256:(g + 1) * 256], ss2[:, g:g + 1], ALU.mult)
            btr = self.bank()
            for hh in range(4):
                K.mm(btr[:, hh * 128:(hh + 1) * 128], ycT[:, hh * 128:(hh + 1) * 128], ID)
            K.cp(oFM[:, 0:4, :], btr.a.re("p (h t) -> p h t", h=4), E=K.act)

            K.tt(pdd, pdp[:, :, 0:P], pdp[:, :, 1:P + 1], ALU.subtract)
            K.tt(pdd, pdd, bcl(ov[:, OV_MU:OV_MU + 14], 14, P), ALU.mult, E=K.pool)
            K.tt(pdm, pdd, pdp[:, :, 1:P + 1], ALU.add)
            K.cp(histP[s_], pdp[:, :, P:P + 1], E=K.pool)
            R_, K_, V_ = pdm[:, 0:4, :], pdm[:, 4:8, :], pdm[:, 8:12, :]
            K.actf(txw, pdm[0:64, 12, :], AF.Tanh)
            K.actf(sg, pdm[:, 13, :], AF.Sigmoid)
            bw = self.bank()
            K.mm(bw, txw, w2t[0:64, :])
            K.tt(ldT, bw, ov[:, OV_W0:OV_W0 + 512], ALU.add)
            K.actf(ldT, ldT, AF.Sigmoid)
            K.ts(ldT, ldT, -float(np.exp(-0.5)), ALU.mult)
            ba = self.bank()
            for c in range(4):
                K.mm(ba[:, c * P:(c + 1) * P], a2t[64:128, c * 128:(c + 1) * 128], pdm[64:128, 12, :])
            for c in range(4):
                K.actf(aF[:, c, :], ba[:, c * P:(c + 1) * P], AF.Sigmoid, bias=ov[:, OV_A0 + c:OV_A0 + c + 1])
            bgm = self.bank()
            K.mm(bgm, sg, g2t)
            K.cp(gT, bgm, E=K.act)
            for c in range(4):
                K.ts(kp[:, c, :], aF[:, c, :], 1.0, ALU.subtract, ov[:, OV_KA + c:OV_KA + c + 1], ALU.mult)
                K.stt(kp[:, c, :], kp[:, c, :], 1.0, pdm[:, 4 + c, :], ALU.add, ALU.mult)
                K.ts(kq[:, c, :], pdm[:, 4 + c, :], ov[:, OV_KK + c:OV_KK + c + 1], ALU.mult, E=K.pool)
                K.stt(rkr[:, c, :], pdm[:, c, :], ov[:, OV_RK + c:OV_RK + c + 1], kp[:, c, :], ALU.mult, ALU.mult)
            K.tt(sqk, kq, kq, ALU.mult, E=K.pool)
            bss = self.bank()
            K.mm(bss, C["bones"], sqk)
            K.actf(sqk, bss.a.re("p (c t) -> p c t", c=4), AF.Sqrt, bias=self.epst)
            recip(sqk)
            K.tt(kk, kq, sqk, ALU.mult)
            K.tt(bm, kk, aF, ALU.mult, E=K.pool)
            blg = self.bank()
            blx = self.bank()
            for c in range(4):
                K.mm(blg[:, c * P:(c + 1) * P], ldT[:, c * 128:(c + 1) * 128], C["tri"])
                K.mm(blx[:, c * P:(c + 1) * P], ldT[:, c * 128:(c + 1) * 128], C["tris"])
            K.cp(lg, blg.a.re("p (c t) -> p c t", c=4))
            K.cp(lgx, blx.a.re("p (c t) -> p c t", c=4), E=K.act)
            K.ts(nmid, lg[:, :, 63], -1.0, ALU.mult)
            for c in range(4):
                K.actf(e_r[:, c, :], lg[:, c, :], AF.Exp, bias=nmid[:, c:c + 1])
                K.actf(e_a[:, c, :], lgx[:, c, :], AF.Exp, bias=nmid[:, c:c + 1])
                K.actf(e_b[:, c, :], lg[:, c, :], AF.Exp, bias=lg[:, c, 63:64], scale=-1.0)
            K.actf(e_r0, lg, AF.Exp)
            K.actf(e_a0, lgx, AF.Exp)
            K.cp(gl4, e_r0[:, :, 127])
            K.tt(rt_, R_, e_r, ALU.mult)
            K.tt(r0_, R_, e_r0, ALU.mult, E=K.pool)
            K.stt(at_, kk, -1.0, e_a, ALU.mult, ALU.mult)
            K.stt(a0_, kk, -1.0, e_a0, ALU.mult, ALU.mult)
            K.tt(kt_, kp, e_b, ALU.mult)
            K.tt(bt_, bm, e_b, ALU.mult, E=K.pool)
            bed = self.bank()
            K.mm(bed, C["ustr"], ldT)
            K.actf(edlT, bed, AF.Exp)
            for (srcT, dstT, scl) in ((kp, kdT, True), (bm, bdT, True), (V_, vT, False)):
                b = self.bank()
                for c in range(4):
                    K.mm(b[:, c * 128:(c + 1) * 128], srcT[:, c, :], C["ident"])
                if scl:
                    K.tt(dstT, b, edlT, ALU.mult)
                else:
                    K.cp(dstT, b, E=K.act)
            brk = self.bank()
            for c in range(4):
                K.mm(brk[:, 0:8], rkr[:, c, :], ov[:, OV_SEL + c * 8:OV_SEL + (c + 1) * 8], start=(c == 0), stop=(c == 3))
            K.cp(rks, brk[:, 0:8])
            byd = self.ps[7]
            for hg in range(2):
                b1, b2, b3, b4, b5 = [self.bank() for _ in range(5)]
                for hh in range(4):
                    hd = hg * 4 + hh
                    hq, pr = hd // 2, slice((hd % 2) * 64, (hd % 2) * 64 + 64)
                    sl = slice(hh * 128, (hh + 1) * 128)
                    K.mm(b1[:, sl], bt_[pr, hq, :], at_[pr, hq, :])
                    K.mm(b2[:, sl], at_[pr, hq, :], bt_[pr, hq, :])
                    K.mm(b3[:, sl], kt_[pr, hq, :], at_[pr, hq, :])
                    K.mm(b4[:, sl], bt_[pr, hq, :], rt_[pr, hq, :])
                    K.mm(b5[:, sl], kt_[pr, hq, :], rt_[pr, hq, :])
                K.tt(AT[0], b1.a.re("p (h t) -> p h t", h=4), bc4(C["tris"]), ALU.mult)
                K.tt(A[0], b2.a.re("p (h t) -> p h t", h=4), bc4(C["ustr"]), ALU.mult)
                K.tt(AakT, b3.a.re("p (h t) -> p h t", h=4), bc4(C["tris"]), ALU.mult)
                K.tt(RbT, b4.a.re("p (h t) -> p h t", h=4), bc4(C["tri"]), ALU.mult)
                K.tt(RkT, b5.a.re("p (h t) -> p h t", h=4), bc4(C["tri"]), ALU.mult)
                K.tt(PT, AT[0], bc4(C["ident"]), ALU.add, E=K.pool)
                cur = 0
                for itr in range(6):
                    nxt = 1 - cur
                    c1 = self.bank()
                    c2 = self.bank()
                    for hh in range(4):
                        sl = slice(hh * 128, (hh + 1) * 128)
                        K.mm(c1[:, sl], AT[cur][:, hh, :], A[cur][:, hh, :])
                        K.mm(c2[:, sl], A[cur][:, hh, :], AT[cur][:, hh, :])
                    K.cp(A[nxt], c1.a.re("p (h t) -> p h t", h=4), E=K.act)
                    K.cp(AT[nxt], c2.a.re("p (h t) -> p h t", h=4), E=K.dve)
                    c3 = self.bank()
                    for hh in range(4):
                        K.mm(c3[:, hh * 128:(hh + 1) * 128], A[nxt][:, hh, :], PT[:, hh, :])
                    K.tt(PT, PT, c3.a.re("p (h t) -> p h t", h=4), ALU.add)
                    cur = nxt
                bx = self.bank()
                for hh in range(4):
                    hd = hg * 4 + hh
                    hq, pr = hd // 2, slice((hd % 2) * 64, (hd % 2) * 64 + 64)
                    vs = slice((hd % 2) * 64, (hd % 2) * 64 + 64)
                    K.mm(bx[:, hh * 64:(hh + 1) * 64], a0_[pr, hq, :], Hs[pr, hq, vs], start=True, stop=False)
                    K.mm(bx[:, hh * 64:(hh + 1) * 64], AakT[:, hh, :], vT[:, hd * 64:(hd + 1) * 64], start=False, stop=True)
                K.cp(Xs, bx[:, 0:256].re("p (h q) -> p h q", h=4))
                bu = self.bank()
                for hh in range(4):
                    K.mm(bu[:, hh * 64:(hh + 1) * 64], PT[:, hh, :], Xs[:, hh, :])
                K.cp(Us[:, hg * 4:(hg + 1) * 4, :], bu[:, 0:256].re("p (h q) -> p h q", h=4))
                for hh in range(4):
                    hd = hg * 4 + hh
                    hq, pr = hd // 2, slice((hd % 2) * 64, (hd % 2) * 64 + 64)
                    vs = slice((hd % 2) * 64, (hd % 2) * 64 + 64)
                    o = byd[:, hd * 64:(hd + 1) * 64]
                    K.mm(o, r0_[pr, hq, :], Hs[pr, hq, vs], start=True, stop=False)
                    K.mm(o, RbT[:, hh, :], Us[:, hd, :], start=False, stop=False)
                    K.mm(o, RkT[:, hh, :], vT[:, hd * 64:(hd + 1) * 64], start=False, stop=True)
            bh = self.bank()
            for hq in range(4):
                o = bh[:, hq * 128:(hq + 1) * 128]
                K.mm(o, bdT[:, hq * 128:(hq + 1) * 128], Us[:, 2 * hq:2 * hq + 2, :].re("p h q -> p (h q)"), start=True, stop=False)
                K.mm(o, kdT[:, hq * 128:(hq + 1) * 128], vT[:, hq * 128:(hq + 1) * 128], start=False, stop=True)
            for hq in range(4):
                K.stt(Hs[:, hq, :], Hs[:, hq, :], gl4[:, hq:hq + 1], bh[:, hq * 128:(hq + 1) * 128], ALU.mult, ALU.add)
            K.cp(yd, byd.a.re("p (h q) -> p h q", h=8), E=K.act)
            K.red(mean8, yd)
            K.ts(mean8, mean8, 1.0 / 64.0, ALU.mult)
            K.tt(ym, yd, bcl(mean8, 8, 64), ALU.subtract)
            K.tt(yd, ym, ym, ALU.mult, E=K.pool)
            K.red(var8, yd)
            K.actf(var8, var8, AF.Sqrt, bias=gneps, scale=1.0 / 64.0)
            recip(var8)
            K.tt(ym, ym, bcl(var8, 8, 64), ALU.mult)
            ymf = _v(ym).re("p h q -> p (h q)")
            K.tt(ymf, ymf, ov[:, OV_GNW:OV_GNW + 512], ALU.mult, E=K.pool)
            K.tt(ymf, ymf, ov[:, OV_GNB:OV_GNB + 512], ALU.add)
            K.tt(yd, vT.a.re("p (h q) -> p h q", h=8), bcl(rks, 8, 64), ALU.mult)
            K.tt(ym, ym, yd, ALU.add)
            K.tt(ydT, ymf, gT, ALU.mult)
            btr = self.bank()
            for hh in range(4):
                K.mm(btr[:, hh * 128:(hh + 1) * 128], ydT[:, hh * 128:(hh + 1) * 128], ID)
            K.cp(oFM[:, 4:8, :], btr.a.re("p (h t) -> p h t", h=4), E=K.act)
            for half in range(2):
                b = self.bank()
                for j in range(4):
                    n = half * 4 + j
                    wo = woblock(n)
                    for c in range(8):
                        K.mm(b[:, j * P:(j + 1) * P], wo(c), oFM[:, c, :], start=(c == 0), stop=(c == 7))
                K.tt(h[:, half * 4:(half + 1) * 4, :], h[:, half * 4:(half + 1) * 4, :],
                     b.a.re("p (j t) -> p j t", j=4), ALU.add)
            K.dma(K.act, self.hview(dst, it * P, P), h)
        K.barrier()
        es.close()

    run_variant(True, [(s_, s_ * tps) for s_ in range(NSEQ)])
    run_variant(False, [(s_, s_ * tps + j) for s_ in range(NSEQ) for j in range(1, tps)])
    K.barrier()
    esP.close()


Model.odd_phase = _odd_phase


def kernel(**inputs):
    inp = {k: np.asarray(v) for k, v in inputs.items()}
    x = inp["x"]
    M = Model(n_layers=4, seq=x.shape[1], mix=True)
    nc = M.build()
    maps = make_in_maps(inp, M, x, 8)
    res = run_bass_kernel_spmd(nc, maps, core_ids=list(range(8)))
    out = np.concatenate([r["out"].reshape(NSEQ, x.shape[1], D) for r in res.results], axis=0)
    return out.astype(np.float32)
```

```python
import contextlib
import os
import numpy as np
import ml_dtypes
import concourse.bass as bass
import concourse.mybir as mybir
from concourse.bass_utils import run_bass_kernel_spmd

F32 = mybir.dt.float32
BF16 = mybir.dt.bfloat16
AF = mybir.ActivationFunctionType
ALU = mybir.AluOpType
AX = mybir.AxisListType

EPOCH = 30000


class Tl:
    def __init__(self, h, name):
        self.h = h
        self.name = name
        self.w = None
        self.r = {}
        self.dsem = None
        self.dcnt = 0
        self.psum = False

    def __getitem__(self, idx):
        return V(self, self.h[idx])

    @property
    def a(self):
        return V(self, self.h[:])


class V:
    def __init__(self, t, ap):
        self.t = t
        self.ap = ap

    def __getitem__(self, idx):
        return V(self.t, self.ap[idx])

    def bitcast(self, dt):
        return V(self.t, self.ap.bitcast(dt))

    def bc(self, shape):
        return V(self.t, self.ap.to_broadcast(list(shape)))

    def re(self, s, **kw):
        return V(self.t, self.ap.rearrange(s, **kw))


def _v(x):
    return x.a if isinstance(x, Tl) else x


class Eng:
    def __init__(self, K, name, eng):
        self.K = K
        self.name = name
        self.eng = eng
        self.epoch = -1
        self.cnt = EPOCH
        self.sem = None
        self.known = {}

    def tick(self, ins):
        if self.cnt >= EPOCH:
            self.epoch += 1
            self.cnt = 0
            self.sem = self.K.new_sem(f"s_{self.name}_{self.epoch}")
        self.cnt += 1
        ins.then_inc(self.sem, 1)
        return (id(self.sem), self.sem, self.cnt, self.name)


class KB:
    def __init__(self, nc):
        self.nc = nc
        self.es = contextlib.ExitStack()
        self.pe = Eng(self, "pe", nc.tensor)
        self.dve = Eng(self, "dve", nc.vector)
        self.act = Eng(self, "act", nc.scalar)
        self.pool = Eng(self, "pool", nc.gpsimd)
        self.sp = Eng(self, "sp", nc.sync)
        self.engs = [self.pe, self.dve, self.act, self.pool, self.sp]
        self.nsem = 0
        self.ntile = 0
        self.all_dma = {}
        self.ninst = 0

    def new_sem(self, name):
        self.nsem += 1
        return self.es.enter_context(self.nc.semaphore(f"{name}_{self.nsem}"))

    def sb(self, shape, dt=F32, name="t", es=None):
        self.ntile += 1
        nm = f"{name}_{self.ntile}"
        h = (es or self.es).enter_context(self.nc.sbuf_tensor(nm, list(shape), dt))
        return Tl(h, nm)

    def psum(self, shape, dt=F32, name="p", es=None):
        self.ntile += 1
        nm = f"{name}_{self.ntile}"
        h = (es or self.es).enter_context(self.nc.psum_tensor(nm, list(shape), dt))
        t = Tl(h, nm)
        t.psum = True
        return t

    def dram(self, name, shape, dt=F32, kind="Internal"):
        h = self.nc.dram_tensor(name, list(shape), dt, kind=kind)
        return Tl(h.ap(), name)

    def _wait(self, E, toks):
        best = {}
        for tk in toks:
            if tk is None:
                continue
            sid, sem, val, who = tk
            if who == E.name and E is self.pe:
                continue
            if E.known.get(sid, 0) >= val:
                continue
            if best.get(sid, (None, 0))[1] < val:
                best[sid] = (sem, val)
        for sid, (sem, val) in best.items():
            E.eng.wait_ge(sem, val)
            E.known[sid] = val
            self.ninst += 1

    skip = False

    def op(self, E, fn, reads, writes):
        if self.skip:
            return None
        toks = []
        rt = [(_v(x).t) for x in reads if x is not None and not isinstance(x, (int, float))]
        wt = [(_v(x).t) for x in writes if x is not None]
        for t in rt:
            toks.append(t.w)
            if t.psum:
                for k, tk in t.r.items():
                    if k != E.name:
                        toks.append(tk)
        for t in wt:
            toks.append(t.w)
            for k, tk in t.r.items():
                if k == E.name:
                    continue
                toks.append(tk)
        self._wait(E, toks)
        ins = fn()
        tok = E.tick(ins)
        self.ninst += 1
        for t in rt:
            t.r[E.name] = tok
        for t in wt:
            t.w = tok
            t.r = {}
        return tok

    def dma(self, Q, out, in_):
        out = _v(out)
        in_ = _v(in_)
        ot, it = out.t, in_.t
        toks = [it.w, ot.w] + list(ot.r.values())
        self._wait(Q, toks)
        if ot.dsem is None:
            ot.dsem = self.new_sem("d_" + ot.name)
        ins = Q.eng.dma_start(out=out.ap, in_=in_.ap)
        ins.then_inc(ot.dsem, 16)
        ot.dcnt += 16
        tok = (id(ot.dsem), ot.dsem, ot.dcnt, "dma")
        self.ninst += 1
        ot.w = tok
        ot.r = {}
        it.r["dma%d" % id(ot.dsem)] = tok
        self.all_dma[id(ot.dsem)] = tok
        return tok

    def barrier(self):
        toks = list(self.all_dma.values())
        for E in self.engs:
            if E.sem is not None and E.cnt > 0:
                toks.append((id(E.sem), E.sem, E.cnt, E.name + "_b"))
        for E in self.engs:
            self._wait(E, toks)

    def mm(self, out, lhsT, rhs, start=True, stop=True, ser=False):
        out, lhsT, rhs = _v(out), _v(lhsT), _v(rhs)
        if ser and not self.skip and self.pe.sem is not None and self.pe.cnt > 0:
            self.pe.eng.wait_ge(self.pe.sem, self.pe.cnt)
        return self.op(self.pe, lambda: self.nc.tensor.matmul(out.ap, lhsT.ap, rhs.ap, start=start, stop=stop),
                       [lhsT, rhs], [out])

    def actf(self, out, in_, func, bias=None, scale=None, accum=None, E=None):
        out, in_ = _v(out), _v(in_)
        kw = {}
        rd = [in_]
        if bias is not None:
            if isinstance(bias, (int, float)):
                kw["bias"] = float(bias)
            else:
                bias = _v(bias)
                kw["bias"] = bias.ap
                rd.append(bias)
        if scale is not None:
            if isinstance(scale, (int, float)):
                kw["scale"] = float(scale)
            else:
                scale = _v(scale)
                kw["scale"] = scale.ap
                rd.append(scale)
        wr = [out]
        if accum is not None:
            accum = _v(accum)
            kw["accum_out"] = accum.ap
            wr.append(accum)
        return self.op(self.act, lambda: self.nc.scalar.activation(out.ap, in_.ap, func, **kw), rd, wr)

    def _ve(self, E):
        return E or self.dve

    def _sc(self, s, rd):
        if isinstance(s, (int, float)):
            return float(s)
        s = _v(s)
        rd.append(s)
        return s.ap

    def ts(self, out, in0, s1, op0, s2=None, op1=None, E=None, accum=None):
        E = self._ve(E)
        out, in0 = _v(out), _v(in0)
        rd = [in0]
        a1 = self._sc(s1, rd)
        a2 = None if s2 is None else self._sc(s2, rd)
        wr = [out]
        kw = {}
        if accum is not None:
            accum = _v(accum)
            kw["accum_out"] = accum.ap
            wr.append(accum)
        if op1 is None:
            return self.op(E, lambda: E.eng.tensor_scalar(out.ap, in0.ap, a1, None, op0, **kw), rd, wr)
        return self.op(E, lambda: E.eng.tensor_scalar(out.ap, in0.ap, a1, a2, op0, op1, **kw), rd, wr)

    def tt(self, out, in0, in1, op, E=None):
        E = self._ve(E)
        out, in0, in1 = _v(out), _v(in0), _v(in1)
        return self.op(E, lambda: E.eng.tensor_tensor(out.ap, in0.ap, in1.ap, op), [in0, in1], [out])

    def stt(self, out, in0, s, in1, op0, op1, E=None):
        E = self.dve
        out, in0, in1 = _v(out), _v(in0), _v(in1)
        rd = [in0, in1]
        a = self._sc(s, rd)
        return self.op(E, lambda: E.eng.scalar_tensor_tensor(out.ap, in0.ap, a, in1.ap, op0, op1), rd, [out])

    def cp(self, out, in_, E=None):
        E = self._ve(E)
        out, in_ = _v(out), _v(in_)
        if E is self.act:
            return self.op(E, lambda: self.nc.scalar.copy(out.ap, in_.ap), [in_], [out])
        return self.op(E, lambda: E.eng.tensor_copy(out.ap, in_.ap), [in_], [out])

    def memset(self, out, val, E=None):
        E = self._ve(E)
        out = _v(out)
        return self.op(E, lambda: E.eng.memset(out.ap, val), [], [out])

    def red(self, out, in_, op=ALU.add, E=None):
        E = self._ve(E)
        out, in_ = _v(out), _v(in_)
        return self.op(E, lambda: E.eng.tensor_reduce(out.ap, in_.ap, AX.X, op), [in_], [out])


D = 1024
NSEQ = 2
T = 256
CH = 128
EVEN_IN = 3608
ODD_IN = 3336
NEPS = 1e-6


def host_consts():
    i = np.arange(128)
    c = {}
    c["ident"] = np.eye(128, dtype=np.float32)
    c["tri"] = (i[:, None] <= i[None, :]).astype(np.float32)
    c["tris"] = (i[:, None] < i[None, :]).astype(np.float32)
    c["ustr"] = (i[:, None] > i[None, :]).astype(np.float32)
    c["ones"] = np.ones((128, 128), np.float32)
    c["bones"] = ((i[:, None] // 64) == (i[None, :] // 64)).astype(np.float32)
    return c


CONST_NAMES = ["ident", "tri", "tris", "ustr", "ones", "bones"]


class Model:
    def __init__(self, n_layers=4, seq=2048, mix=True, dbg=False):
        self.L = n_layers
        self.seq = seq
        self.NT = NSEQ * seq
        self.mix = mix
        self.dbg = dbg

    def build(self):
        nc = bass.Bass("TRN2", target_bir_lowering=False)
        self.nc = nc
        K = KB(nc)
        self.K = K
        L, NT = self.L, self.NT
        ne = (L + 1) // 2
        no = L // 2
        self.x = K.dram("x", [NT, D], F32, kind="ExternalInput")
        self.out = K.dram("out", [NT, D], F32, kind="ExternalOutput")
        self.hA = K.dram("hA", [D, NT], F32)
        self.hB = K.dram("hB", [D, NT], F32)
        self.cst_d = K.dram("consts", [128, len(CONST_NAMES) * 128], F32, kind="ExternalInput")
        self.w1 = K.dram("mlp_w1", [L, D, 4 * D], F32, kind="ExternalInput")
        self.w2 = K.dram("mlp_w2", [L, 4 * D, D], F32, kind="ExternalInput")
        self.nvec = 3 * L + 1
        self.vec_d = K.dram("vecs", [128, 8 * (2 * L + 1)], F32, kind="ExternalInput")
        if self.mix:
            self.declare_mixer_inputs(ne, no)
        self.cst = K.sb([128, len(CONST_NAMES) * 128], F32, "cst")
        K.dma(K.sp, self.cst, self.cst_d)
        self.C = {n: self.cst[:, i * 128:(i + 1) * 128] for i, n in enumerate(CONST_NAMES)}
        self.cstb = K.sb([128, len(CONST_NAMES) * 128], BF16, "cstb")
        K.cp(self.cstb, self.cst)
        self.CB = {n: self.cstb[:, i * 128:(i + 1) * 128] for i, n in enumerate(CONST_NAMES)}
        self.vec = K.sb([128, 8 * (2 * L + 1)], F32, "vec")
        K.dma(K.sp, self.vec, self.vec_d)
        self.epst = K.sb([128, 1], F32, "eps")
        K.memset(self.epst, NEPS)
        self.ps = [K.psum([128, 512], F32, "bank") for _ in range(8)]
        self.psi = 0

        self.phase0()
        src, dst = self.hA, self.hB
        for l in range(L):
            if self.mix:
                K.barrier()
                if l % 2 == 0:
                    self.even_phase(l, src, dst)
                elif os.environ.get("NOODD"):
                    src, dst = dst, src
                else:
                    self.odd_phase(l, src, dst)
                src, dst = dst, src
            K.barrier()
            self.mlp_phase(l, src, dst)
            src, dst = dst, src
        K.barrier()
        self.final_phase(src)
        K.barrier()
        return nc

    def bank(self):
        b = self.ps[self.psi % 7]
        self.psi += 1
        return b

    def hview(self, hd, t0, n):
        return hd.a.re("(c p) t -> p c t", p=128)[:, :, t0:t0 + n]

    def phase0(self):
        K = self.K
        es = contextlib.ExitStack()
        xt = [K.sb([128, 2, D], F32, "xt", es) for _ in range(2)]
        ht = [K.sb([128, 8, T], F32, "ht", es) for _ in range(2)]
        for i in range(self.NT // T):
            xs, hs = xt[i % 2], ht[i % 2]
            K.dma(K.sp, xs, self.x.a[i * T:(i + 1) * T, :].re("(s p) f -> p s f", p=128))
            for s in range(2):
                for half in range(2):
                    b = self.bank()
                    for j in range(4):
                        c = half * 4 + j
                        K.mm(b[:, j * 128:(j + 1) * 128], xs[:, s, c * 128:(c + 1) * 128], self.C["ident"])
                    dst = hs[:, half * 4:(half + 1) * 4, s * 128:(s + 1) * 128]
                    K.cp(dst, b.a.re("p (j t) -> p j t", j=4), E=(K.dve if half == 0 else K.act))
            K.dma(K.act, self.hview(self.hA, i * T, T), hs)
        K.barrier()
        es.close()

    def rmsnorm_fm(self, h, gain, hn, sq, rp, n=T):
        K = self.K
        K.actf(sq, h, AF.Square)
        b = self.bank()
        for c in range(8):
            K.mm(b[:, 0:n], self.CB["ones"], _v(sq)[:, c, :], start=(c == 0), stop=(c == 7))
        K.actf(rp, b[:, 0:n], AF.Sqrt, bias=self.epst, scale=1.0 / D)
        K.op(K.dve, lambda: self.nc.vector.reciprocal(_v(rp).ap, _v(rp).ap), [rp], [rp])
        for c in range(8):
            for o in (hn if isinstance(hn, (list, tuple)) else [hn]):
                K.stt(_v(o)[:, c, :], _v(h)[:, c, :], gain[:, c:c + 1], rp, ALU.mult, ALU.mult,
                      E=(K.dve if c % 2 == 0 else K.pool))

    def mlp_phase(self, l, src, dst):
        K = self.K
        P = 128
        gain = self.vec[:, 8 * (self.L + l):8 * (self.L + l + 1)]
        W1 = self.w1.a[l]
        W2 = self.w2.a[l]
        tiles = []
        for s_ in range(NSEQ):
            if self.seq > P:
                tiles.append((s_ * self.seq + P, min(T, self.seq) - P))
            for j in range(1, self.seq // T):
                tiles.append((s_ * self.seq + j * T, T))
        esW = contextlib.ExitStack()
        if tiles:
            w1 = K.sb([128, 8, 4 * D], BF16, "w1", esW)
            w2 = K.sb([128, 32, D], BF16, "w2", esW)
            for k in range(8):
                K.dma(K.pool, w1[:, k, :], W1[k * 128:(k + 1) * 128, :])
            for g in range(4):
                K.dma(K.pool, w2[:, g * 8:(g + 1) * 8, :],
                      W2[g * 1024:(g + 1) * 1024, :].re("(m p) n -> p m n", p=128))
        es = contextlib.ExitStack()
        NB = NSEQ * P
        h = K.sb([128, 8, NB], F32, "h", es)
        hn32 = K.sb([128, 8, NB], F32, "hn32", es)
        sq = K.sb([128, 8, NB], BF16, "sq", es)
        rp = K.sb([128, NB], F32, "rp", es)
        act32 = K.sb([128, 32, NB], F32, "act32", es)
        tmp32 = K.sb([128, 2, NB], F32, "tmp32", es)
        stg = [K.sb([128, 2048], F32, "stg", es) for _ in range(2)]
        ns = 0
        for s_ in range(NSEQ):
            K.dma(K.sp, h[:, :, s_ * P:(s_ + 1) * P], self.hview(src, s_ * self.seq, P))
        self.rmsnorm_fm(h, gain, hn32, sq, rp, n=NB)
        for mb in range(16):
            st = stg[ns % 2].a.re("p (k n) -> p k n", k=8)
            ns += 1
            srcw = W1[:, mb * 256:(mb + 1) * 256].re("(k p) n -> p k n", p=128)
            K.dma(K.sp, st, srcw)
            b = self.bank()
            for j in range(2):
                for k in range(8):
                    K.mm(b[:, j * NB:(j + 1) * NB], st[:, k, j * 128:(j + 1) * 128], hn32[:, k, :],
                         start=(k == 0), stop=(k == 7))
            K.actf(tmp32, b.a.re("p (j t) -> p j t", j=2), AF.Relu)
            K.tt(act32[:, mb * 2:mb * 2 + 2, :], tmp32, tmp32, ALU.mult, E=K.pool)
        for n in range(8):
            b = self.bank()
            for hf in range(2):
                st = stg[ns % 2].a.re("p (m n) -> p m n", m=16)
                ns += 1
                srcw = W2[hf * 2048:(hf + 1) * 2048, n * 128:(n + 1) * 128].re("(m p) n -> p m n", p=128)
                K.dma(K.sp, st, srcw)
                for m in range(16):
                    mm_ = hf * 16 + m
                    K.mm(b[:, 0:NB], st[:, m, :], act32[:, mm_, :], start=(mm_ == 0), stop=(mm_ == 31))
            K.tt(h[:, n, :], h[:, n, :], b[:, 0:NB], ALU.add)
        for s_ in range(NSEQ):
            K.dma(K.act, self.hview(dst, s_ * self.seq, P), h[:, :, s_ * P:(s_ + 1) * P])
        K.barrier()
        es.close()
        if not tiles:
            esW.close()
            return
        es = esW
        hts = [K.sb([128, 8, T], F32, "h", es) for _ in range(2)]
        hns = [K.sb([128, 8, T], BF16, "hn", es) for _ in range(2)]
        sqs = [K.sb([128, 8, T], BF16, "sq", es) for _ in range(2)]
        rps = [K.sb([128, T], F32, "rp", es) for _ in range(2)]
        act = K.sb([128, 32, T], BF16, "act", es)
        tmp = [K.sb([128, 2, T], BF16, "tmp", es) for _ in range(2)]
        nb_ = {}

        def norm_a(i):
            n = tiles[i][1]
            K.actf(sqs[i % 2][:, :, 0:n], hts[i % 2][:, :, 0:n], AF.Square)

        def norm_b(i):
            n = tiles[i][1]
            b = self.bank()
            nb_[i] = b
            for c in range(8):
                K.mm(b[:, 0:n], self.CB["ones"], sqs[i % 2][:, c, 0:n], start=(c == 0), stop=(c == 7))

        def norm_c(i):
            n = tiles[i][1]
            rp = rps[i % 2][:, 0:n]
            K.actf(rp, nb_[i][:, 0:n], AF.Sqrt, bias=self.epst, scale=1.0 / D)
            K.op(K.dve, lambda: self.nc.vector.reciprocal(rp.ap, rp.ap), [rp], [rp])
            for c in range(8):
                K.stt(hns[i % 2][:, c, 0:n], hts[i % 2][:, c, 0:n], gain[:, c:c + 1], rp, ALU.mult, ALU.mult)

        K.dma(K.sp, hts[0][:, :, 0:tiles[0][1]], self.hview(src, tiles[0][0], tiles[0][1]))
        norm_a(0)
        norm_b(0)
        norm_c(0)
        for i, (t0, n) in enumerate(tiles):
            h = hts[i % 2]
            hn = hns[i % 2]
            nxt = i + 1 < len(tiles)
            if nxt:
                t1, n1 = tiles[i + 1]
                K.dma(K.sp, hts[(i + 1) % 2][:, :, 0:n1], self.hview(src, t1, n1))
            for mp in range(16):
                b = self.bank()
                for j in range(2):
                    m = mp * 2 + j
                    for k in range(8):
                        K.mm(b[:, j * T:j * T + n], w1[:, k, m * 128:(m + 1) * 128], hn[:, k, 0:n],
                             start=(k == 0), stop=(k == 7))
                tm = tmp[mp % 2]
                K.actf(tm[:, :, 0:n], b.a.re("p (j t) -> p j t", j=2)[:, :, 0:n], AF.Relu)
                K.tt(act[:, mp * 2:mp * 2 + 2, 0:n], tm[:, :, 0:n], tm[:, :, 0:n], ALU.mult, E=K.pool)
            if nxt:
                norm_a(i + 1)
            for npair in range(4):
                b = self.bank()
                for j in range(2):
                    nn = npair * 2 + j
                    for m in range(32):
                        K.mm(b[:, j * T:j * T + n], w2[:, m, nn * 128:(nn + 1) * 128], act[:, m, 0:n],
                             start=(m == 0), stop=(m == 31))
                K.tt(h[:, npair * 2:npair * 2 + 2, 0:n], h[:, npair * 2:npair * 2 + 2, 0:n],
                     b.a.re("p (j t) -> p j t", j=2)[:, :, 0:n], ALU.add)
                if nxt and npair == 1:
                    norm_b(i + 1)
                if nxt and npair == 2:
                    norm_c(i + 1)
            K.dma(K.act, self.hview(dst, t0, n), h[:, :, 0:n])
        K.barrier()
        es.close()

    def final_phase(self, src):
        K = self.K
        es = contextlib.ExitStack()
        hts = [K.sb([128, 8, T], F32, "h", es) for _ in range(2)]
        hn = K.sb([128, 8, T], F32, "hn", es)
        sq = K.sb([128, 8, T], BF16, "sq", es)
        rp = K.sb([128, T], F32, "rp", es)
        ot = [K.sb([128, 2, D], F32, "ot", es) for _ in range(2)]
        gain = self.vec[:, 8 * (2 * self.L):8 * (2 * self.L + 1)]
        ntile = self.NT // T
        K.dma(K.sp, hts[0], self.hview(src, 0, T))
        for i in range(ntile):
            h = hts[i % 2]
            if i + 1 < ntile:
                K.dma(K.sp, hts[(i + 1) % 2], self.hview(src, (i + 1) * T, T))
            self.rmsnorm_fm(h, gain, hn, sq, rp)
            o = ot[i % 2]
            for s in range(2):
                for half in range(2):
                    b = self.bank()
                    for j in range(4):
                        c = half * 4 + j
                        K.mm(b[:, j * 128:(j + 1) * 128], hn[:, c, s * 128:(s + 1) * 128], self.C["ident"])
                    K.cp(o[:, s, half * 512:(half + 1) * 512], b, E=(K.dve if half == 0 else K.act))
            K.dma(K.act, self.out.a[i * T:(i + 1) * T, :].re("(s p) f -> p s f", p=128), o)
        K.barrier()
        es.close()


def host_vecs(inp, L):
    cols = []
    for l in range(L):
        cols.append(np.asarray(inp["norm_mix_w"][l], np.float32).reshape(8, 128).T)
    for l in range(L):
        cols.append(np.asarray(inp["norm_mlp_w"][l], np.float32).reshape(8, 128).T)
    cols.append(np.asarray(inp["final_norm_w"], np.float32).reshape(8, 128).T)
    return np.ascontiguousarray(np.concatenate(cols, axis=1))


def make_in_maps(inp, M, x, ncores):
    L = M.L
    consts = host_consts()
    cst = np.ascontiguousarray(np.concatenate([consts[n] for n in CONST_NAMES], axis=1))
    vecs = host_vecs(inp, L)
    shared = {"consts": cst, "vecs": vecs,
              "mlp_w1": np.ascontiguousarray(inp["mlp_w1"][:L]), "mlp_w2": np.ascontiguousarray(inp["mlp_w2"][:L])}
    if M.mix:
        shared.update(host_mixer_inputs(inp, L))
    maps = []
    for c in range(ncores):
        d = dict(shared)
        d["x"] = np.ascontiguousarray(x[2 * c:2 * c + 2]).reshape(M.NT, D)
        maps.append(d)
    return maps


EV_CONV = 0
EV_ALOG = 48
EV_DTB = 52
EV_GDNW = 56
EV_GLAW = 568
EV_W2B = 1080
EV_N = 1336


def host_even_vec(inp, i):
    v = np.zeros((128, EV_N), np.float32)
    cw = np.asarray(inp["gdn_conv_w"][i], np.float32)
    v[:, EV_CONV:EV_CONV + 48] = cw.T.reshape(12, 128, 4).transpose(1, 0, 2).reshape(128, 48)
    v[:, EV_ALOG:EV_ALOG + 4] = np.asarray(inp["gdn_a_log"][i])[None, :]
    v[:, EV_DTB:EV_DTB + 4] = np.asarray(inp["gdn_dt_bias"][i])[None, :]
    v[:, EV_GDNW:EV_GDNW + 512] = np.tile(np.asarray(inp["gdn_norm_w"][i]), 4)[None, :]
    v[:, EV_GLAW:EV_GLAW + 512] = np.tile(np.asarray(inp["gla_norm_w"][i]), 4)[None, :]
    v[0:16, EV_W2B:EV_W2B + 256] = np.asarray(inp["gla_gate_w2"][i])
    v[16, EV_W2B:EV_W2B + 256] = np.asarray(inp["gla_gate_b"][i])
    return v


def _even_phase(self, l, src, dst):
    K = self.K
    nc = self.nc
    i_l = l // 2
    esP = contextlib.ExitStack()
    C, CB = self.C, self.CB
    P = 128
    tps = self.seq // P
    WIN = self.even_w_in.a[i_l]
    WOUT = self.even_w_out.a[i_l]

    ev = K.sb([128, EV_N], F32, "ev", esP)
    K.dma(K.sp, ev, self.evec.a[i_l])
    nexpa = K.sb([128, 4], F32, "nexpa", esP)
    K.actf(nexpa, ev[:, EV_ALOG:EV_ALOG + 4], AF.Exp)
    K.ts(nexpa, nexpa, -1.0, ALU.mult)
    convw = ev[:, EV_CONV:EV_CONV + 48]
    gain = self.vec[:, 8 * l:8 * (l + 1)]
    SAs = [K.sb([128, 4, 128], F32, "SA", esP) for _ in range(NSEQ)]
    SBs = [K.sb([128, 2, 256], F32, "SB", esP) for _ in range(NSEQ)]
    hist = [K.sb([128, 12, 3], F32, "hist", esP) for _ in range(NSEQ)]
    for s_ in range(NSEQ):
        K.memset(SAs[s_], 0.0)
        K.memset(SBs[s_], 0.0)
        K.memset(hist[s_], 0.0)

    def bc4(v):
        return v.re("p (o t) -> p o t", o=1).bc([128, 4, 128])

    def bcl(v):
        return _v(v).re("p (h o) -> p h o", o=1).bc([128, 4, 128])

    def run_variant(hp, tiles):
        if not tiles:
            return
        es = contextlib.ExitStack()
        AD = F32 if hp else BF16
        ID = C["ident"] if hp else CB["ident"]

        def t512(name, dt=F32):
            return K.sb([128, 4, 128], dt, name, es)

        if hp:
            stg = [K.sb([128, 8, 512], F32, "stg", es) for _ in range(2)]
            cnt = [0]

            def _stream(dram2d, col0, n):
                t = stg[cnt[0] % 2]
                cnt[0] += 1
                src_ = dram2d[:, col0:col0 + n].re("(k p) n -> p k n", p=128)
                K.dma(K.sp, t[:, :, 0:n], src_)
                return lambda k: t[:, k, 0:n]

            wblock = lambda col0, n: _stream(WIN, col0, n)
            wblock32 = wblock
            woblock = lambda n0: _stream(WOUT, n0 * 128, 128)
        else:
            w_in = K.sb([128, 8, EVEN_IN], BF16, "w_in", es)
            for k in range(8):
                K.dma(K.pool, w_in[:, k, :], WIN[k * 128:(k + 1) * 128, :])
            w_out = K.sb([128, 8, D], BF16, "w_out", es)
            K.dma(K.pool, w_out, WOUT.re("(c p) n -> p c n", p=128))
            wg = K.sb([128, 8, 24], F32, "wg", es)
            for k in range(8):
                K.dma(K.sp, wg[:, k, 0:8], WIN[k * 128:(k + 1) * 128, 2048:2056])
                K.dma(K.sp, wg[:, k, 8:24], WIN[k * 128:(k + 1) * 128, 3592:3608])
            wblock = lambda col0, n: (lambda k: w_in[:, k, col0:col0 + n])
            wblock32 = lambda col0, n: (lambda k: wg[:, k, 0:8]) if col0 == 2048 else (lambda k: wg[:, k, 8:24])
            woblock = lambda n0: (lambda c: w_out[:, c, n0 * 128:(n0 + 1) * 128])

        h = K.sb([128, 8, P], F32, "h", es)
        hn32 = K.sb([128, 8, P], F32, "hn32", es)
        hn = hn32 if hp else K.sb([128, 8, P], BF16, "hn", es)
        sq = K.sb([128, 8, P], BF16, "sq", es)
        rp = K.sb([128, P], F32, "rp", es)
        pre = K.sb([128, 12, P + 3], F32, "pre", es)
        cacc = K.sb([128, 12, P], F32, "cacc", es)
        qkv = K.sb([128, 12, P], F32, "qkv", es)
        sq2 = K.sb([128, 8, P], BF16, "sq2", es)
        rinv = K.sb([128, 8, P], F32, "rinv", es)
        oFM = K.sb([128, 8, P], AD, "oFM", es)
        g1 = K.sb([32, P], F32, "g1", es)
        K.memset(g1, 1.0)
        loga = K.sb([128, 256], F32, "loga", es)
        gkT = K.sb([128, 256], F32, "gkT", es)
        qkF = K.sb([128, 4, 128], F32, "qkF", es)
        gcf = K.sb([128, 2, 128], F32, "gcf", es)
        nmid = K.sb([128, 2], F32, "nmid", es)
        eq = K.sb([128, 2, 128], F32, "eq", es)
        ek = K.sb([128, 2, 128], F32, "ek", es)
        eq0 = K.sb([128, 2, 128], F32, "eq0", es)
        qgm = K.sb([128, 2, 128], F32, "qgm", es)
        kgm = K.sb([128, 2, 128], F32, "kgm", es)
        qg0 = K.sb([128, 2, 128], AD, "qg0", es)
        edlB = K.sb([128, 256], F32, "edlB", es)
        kdB = K.sb([128, 256], AD, "kdB", es)
        gv = K.sb([128, 512], AD, "gv", es)
        grs = K.sb([128, 512], F32, "grs", es)
        zs = K.sb([128, 512], F32, "zs", es)
        attm = t512("attm", AD)
        SBb = None if hp else K.sb([128, 2, 256], BF16, "SBb", es)
        junk = K.sb([128, 128], F32, "junk", es)
        ssB = K.sb([128, 4], F32, "ssB", es)
        Gt = K.sb([128, 512], F32, "Gt", es)
        obT = K.sb([128, 512], AD, "obT", es)
        gr8 = K.sb([128, 8], F32, "gr8", es)
        beta = K.sb([128, 4], F32, "beta", es)
        nbeta = K.sb([128, 4], F32, "nbeta", es)
        gA = K.sb([128, 4], F32, "gA", es)
        gcc = K.sb([128, 4], F32, "gcc", es)
        eg = K.sb([128, 4], F32, "eg", es)
        egl = K.sb([128, 4], F32, "egl", es)
        edl = K.sb([128, 4], F32, "edl", es)
        beg = K.sb([128, 4], F32, "beg", es)
        Gtri = t512("Gtri")
        Dm = t512("Dm")
        DTm = t512("DTm")
        A = [t512("A0"), t512("A1")]
        AT = [t512("AT0"), t512("AT1")]
        PT = t512("PT")
        aqkT = t512("aqkT")
        vb = t512("vb")
        kbg = t512("kbg")
        kdA = t512("kdA")
        wneg = t512("wneg")
        vn = t512("vn")
        qg = t512("qg")
        ssA = K.sb([128, 4], F32, "ssA", es)
        oaT = K.sb([128, 512], AD, "oaT", es)

        for (s_, it) in tiles:
            SA, SB_ = SAs[s_], SBs[s_]
            K.dma(K.sp, h, self.hview(src, it * P, P))
            K.cp(pre[:, :, 0:3], hist[s_], E=K.pool)
            if not hp:
                K.cp(SBb, SB_, E=K.pool)
            Sst = SB_ if hp else SBb
            self.rmsnorm_fm(h, gain, [hn32] if hp else [hn, hn32], sq, rp, n=P)
            for c4 in range(3):
                wb = wblock(c4 * 512, 512)
                b = self.bank()
                for j in range(4):
                    for k in range(8):
                        K.mm(b[:, j * P:(j + 1) * P], wb(k)[:, j * 128:(j + 1) * 128], hn[:, k, :],
                             start=(k == 0), stop=(k == 7))
                K.cp(pre[:, c4 * 4:(c4 + 1) * 4, 3:3 + P], b.a.re("p (j t) -> p j t", j=4), E=K.act)
            shared = {}
            wb = wblock32(2048, 8)
            bg = self.bank()
            for k in range(8):
                K.mm(bg[:, 0:8], hn32[:, k, :], wb(k), start=(k == 0), stop=(k == 7))
            K.cp(gr8, bg[:, 0:8])
            for c in range(12):
                K.actf(cacc[:, c, :], pre[:, c, 0:P], AF.Copy, scale=convw[:, c * 4:c * 4 + 1])
                for j in range(1, 4):
                    K.stt(cacc[:, c, :], pre[:, c, j:j + P], convw[:, c * 4 + j:c * 4 + j + 1], cacc[:, c, :],
                          ALU.mult, ALU.add)
            K.cp(hist[s_], pre[:, :, P:P + 3], E=K.pool)
            K.actf(qkv, cacc, AF.Silu)
            K.tt(sq2, qkv[:, 0:8, :], qkv[:, 0:8, :], ALU.mult, E=K.pool)
            for half in range(2):
                b = self.bank()
                K.mm(b, CB["ones"], sq2[:, half * 4:(half + 1) * 4, :])
                K.actf(rinv[:, half * 4:(half + 1) * 4, :], b.a.re("p (j t) -> p j t", j=4), AF.Sqrt, bias=self.epst)
            K.op(K.dve, lambda: nc.vector.reciprocal(rinv.a.ap, rinv.a.ap), [rinv], [rinv])
            K.stt(qkv[:, 0:4, :], qkv[:, 0:4, :], float(128 ** -0.5), rinv[:, 0:4, :], ALU.mult, ALU.mult)
            K.tt(qkv[:, 4:8, :], qkv[:, 4:8, :], rinv[:, 4:8, :], ALU.mult, E=K.pool)

            def seg_tm1():
                wb = wblock(1536, 512)
                bz = self.bank()
                for k in range(8):
                    K.mm(bz, hn[:, k, :], wb(k), start=(k == 0), stop=(k == 7))
                K.actf(zs, bz, AF.Silu)
                wb = wblock(2312, 256)
                bgk = self.bank()
                for k in range(8):
                    K.mm(bgk[:, 0:256], hn[:, k, :], wb(k), start=(k == 0), stop=(k == 7))
                K.cp(gkT, bgk[:, 0:256], E=K.act)
            def seg_tm2():
                wb = wblock(2568, 512)
                bgv = self.bank()
                for k in range(8):
                    K.mm(bgv, hn[:, k, :], wb(k), start=(k == 0), stop=(k == 7))
                K.cp(gv, bgv, E=K.act)
                wb = wblock(3080, 512)
                bgr = self.bank()
                for k in range(8):
                    K.mm(bgr, hn[:, k, :], wb(k), start=(k == 0), stop=(k == 7))
                K.actf(grs, bgr, AF.Silu)
            def seg_tm3():
                wb = wblock32(3592, 16)
                bl = self.bank()
                for k in range(8):
                    K.mm(bl[0:16, 0:P], wb(k), hn32[:, k, :], start=(k == 0), stop=(k == 7))
                K.cp(g1[0:16, :], bl[0:16, 0:P])
                wb = wblock(2056, 512)
                bqk = self.bank()
                for j in range(4):
                    for k in range(8):
                        K.mm(bqk[:, j * P:(j + 1) * P], wb(k)[:, j * 128:(j + 1) * 128], hn[:, k, :],
                             start=(k == 0), stop=(k == 7))
                K.cp(qkF, bqk.a.re("p (j t) -> p j t", j=4))
            def seg_gla1():
                bx = self.bank()
                K.mm(bx[:, 0:256], g1, ev[0:32, EV_W2B:EV_W2B + 256])
                K.ts(loga, bx[:, 0:256], -20.0, ALU.max)
                K.actf(loga, loga, AF.Exp, scale=-1.0)
                K.actf(loga, loga, AF.Ln, bias=1.0)
                K.ts(loga, loga, -1.0 / 16.0, ALU.mult, -1.0, ALU.max)
                bgc = self.bank()
                for cc in range(2):
                    K.mm(bgc[:, cc * 128:(cc + 1) * 128], loga[:, cc * 128:(cc + 1) * 128], C["tri"])
                K.mm(bgc[:, 256:512], C["ustr"], loga)
                K.cp(gcf, bgc[:, 0:256].re("p (c t) -> p c t", c=2))
                K.actf(edlB, bgc[:, 256:512], AF.Exp)
                K.tt(kdB, gkT, edlB, ALU.mult)
                K.ts(nmid, gcf[:, :, 63], -1.0, ALU.mult)
                for cc in range(2):
                    K.actf(eq[:, cc, :], gcf[:, cc, :], AF.Exp, bias=nmid[:, cc:cc + 1])
                    K.actf(ek[:, cc, :], gcf[:, cc, :], AF.Exp, bias=gcf[:, cc, 63:64], scale=-1.0)
                K.actf(eq0, gcf, AF.Exp)
                K.stt(qgm, qkF[:, 0:2, :], 0.125, eq, ALU.mult, ALU.mult)
                K.tt(kgm, qkF[:, 2:4, :], ek, ALU.mult)
                K.stt(qg0, qkF[:, 0:2, :], 0.125, eq0, ALU.mult, ALU.mult)
            def seg_gla2():
                bat = self.bank()
                for hh in range(4):
                    pr = slice((hh % 2) * 64, (hh % 2) * 64 + 64)
                    K.mm(bat[:, hh * 128:(hh + 1) * 128], kgm[pr, hh // 2, :], qgm[pr, hh // 2, :], ser=True)
                K.tt(attm, bat.a.re("p (h t) -> p h t", h=4), bc4(C["tri"]), ALU.mult)
                bo = self.bank()
                shared['bo'] = bo
                for hh in range(4):
                    pr = slice((hh % 2) * 64, (hh % 2) * 64 + 64)
                    K.mm(bo[:, hh * 128:(hh + 1) * 128], qg0[pr, hh // 2, :],
                         Sst[pr, hh // 2, (hh % 2) * 128:(hh % 2) * 128 + 128], start=True, stop=False, ser=True)
                    K.mm(bo[:, hh * 128:(hh + 1) * 128], attm[:, hh, :], gv[:, hh * 128:(hh + 1) * 128],
                         start=False, stop=True, ser=True)
                bs = self.bank()
                for pp in range(2):
                    K.mm(bs[:, pp * 256:(pp + 1) * 256], kdB[:, pp * 128:(pp + 1) * 128], gv[:, pp * 256:(pp + 1) * 256])
                for pp in range(2):
                    K.stt(SB_[:, pp, :], SB_[:, pp, :], eq0[:, pp, 127:128], bs[:, pp * 256:(pp + 1) * 256],
                          ALU.mult, ALU.add)
            def seg_gla3():
                bo = shared['bo']
                for hh in range(4):
                    K.actf(junk, bo[:, hh * 128:(hh + 1) * 128], AF.Square, accum=ssB[:, hh:hh + 1])
                K.actf(ssB, ssB, AF.Sqrt, bias=self.epst, scale=1.0 / 128.0)
                K.op(K.dve, lambda: nc.vector.reciprocal(ssB.a.ap, ssB.a.ap), [ssB], [ssB])
                K.tt(Gt, grs, ev[:, EV_GLAW:EV_GLAW + 512], ALU.mult, E=K.pool)
                for hh in range(4):
                    sl = slice(hh * 128, (hh + 1) * 128)
                    K.stt(obT[:, sl], bo[:, sl], ssB[:, hh:hh + 1], Gt[:, sl], ALU.mult, ALU.mult)
                btr = self.bank()
                for hh in range(4):
                    K.mm(btr[:, hh * 128:(hh + 1) * 128], obT[:, hh * 128:(hh + 1) * 128], ID)
                K.cp(oFM[:, 4:8, :], btr.a.re("p (h t) -> p h t", h=4), E=K.act)
            segs = [seg_tm1, seg_tm2, seg_tm3, seg_gla1, seg_gla2, seg_gla3]
            K.actf(beta, gr8[:, 0:4], AF.Sigmoid)
            K.ts(nbeta, beta, -1.0, ALU.mult)
            K.tt(gA, gr8[:, 4:8], ev[:, EV_DTB:EV_DTB + 4], ALU.add)
            K.actf(gA, gA, AF.Exp)
            K.actf(gA, gA, AF.Ln, bias=1.0)
            K.tt(gA, gA, nexpa, ALU.mult)
            K.tt(Gtri, bc4(C["tri"]), bcl(gA), ALU.mult)
            bsm = self.bank()
            K.mm(bsm[:, 0:4], C["tri"], gA)
            K.mm(bsm[:, 8:12], C["ones"], gA)
            br = self.bank()
            K.mm(br, C["ones"], Gtri)
            K.cp(gcc, bsm[:, 0:4])
            K.actf(eg, gcc, AF.Exp)
            K.actf(egl, bsm[:, 8:12], AF.Exp)
            K.tt(edl, bsm[:, 8:12], gcc, ALU.subtract)
            K.actf(edl, edl, AF.Exp)
            K.tt(beg, beta, eg, ALU.mult)
            for hh in range(4):
                sl = slice(hh * 128, (hh + 1) * 128)
                K.ts(Dm[:, hh, :], br[:, sl], gcc[:, hh:hh + 1], ALU.subtract, 0.0, ALU.max)
                K.ts(DTm[:, hh, :], br[:, sl], gcc[:, hh:hh + 1], ALU.subtract, 0.0, ALU.min)
            K.actf(Dm, Dm, AF.Exp, scale=-1.0)
            K.actf(DTm, DTm, AF.Exp)
            K.actf(qg, br.a.re("p (h t) -> p h t", h=4), AF.Exp)
            K.tt(Dm, Dm, bc4(C["ustr"]), ALU.mult, E=K.pool)
            K.tt(DTm, DTm, bc4(C["tri"]), ALU.mult, E=K.pool)
            K.tt(qg, qg, qkv[:, 0:4, :], ALU.mult, E=K.pool)
            bkk = self.bank()
            bqt = self.bank()
            for hh in range(4):
                sl = slice(hh * 128, (hh + 1) * 128)
                K.mm(bkk[:, sl], qkv[:, 4 + hh, :], qkv[:, 4 + hh, :])
                K.mm(bqt[:, sl], qkv[:, 4 + hh, :], qkv[:, hh, :])
            for hh in range(4):
                K.stt(A[0][:, hh, :], bkk[:, hh * 128:(hh + 1) * 128], nbeta[:, hh:hh + 1], Dm[:, hh, :],
                      ALU.mult, ALU.mult)
            K.tt(aqkT, bqt.a.re("p (h t) -> p h t", h=4), DTm, ALU.mult)
            bnt = self.bank()
            for hh in range(4):
                K.mm(bnt[:, hh * 128:(hh + 1) * 128], A[0][:, hh, :], C["ident"])
            K.cp(AT[0], bnt.a.re("p (h t) -> p h t", h=4), E=K.act)
            K.tt(PT, bnt.a.re("p (h t) -> p h t", h=4), bc4(C["ident"]), ALU.add)
            bkt = self.bank()
            bvt = self.bank()
            for hh in range(4):
                sl = slice(hh * 128, (hh + 1) * 128)
                K.mm(bkt[:, sl], qkv[:, 4 + hh, :], C["ident"])
                K.mm(bvt[:, sl], qkv[:, 8 + hh, :], C["ident"])
            K.tt(vb, bvt.a.re("p (h t) -> p h t", h=4), bcl(beta), ALU.mult)
            K.tt(kbg, bkt.a.re("p (h t) -> p h t", h=4), bcl(beg), ALU.mult)
            K.tt(kdA, bkt.a.re("p (h t) -> p h t", h=4), bcl(edl), ALU.mult)
            cur = 0
            for itr in range(6):
                nxt = 1 - cur
                b1 = self.bank()
                b2 = self.bank()
                for hh in range(4):
                    sl = slice(hh * 128, (hh + 1) * 128)
                    K.mm(b1[:, sl], AT[cur][:, hh, :], A[cur][:, hh, :])
                    if itr < 5:
                        K.mm(b2[:, sl], A[cur][:, hh, :], AT[cur][:, hh, :])
                K.cp(A[nxt], b1.a.re("p (h t) -> p h t", h=4), E=K.act)
                if itr < 5:
                    K.cp(AT[nxt], b2.a.re("p (h t) -> p h t", h=4), E=K.dve)
                b3 = self.bank()
                for hh in range(4):
                    K.mm(b3[:, hh * 128:(hh + 1) * 128], A[nxt][:, hh, :], PT[:, hh, :])
                K.tt(PT, PT, b3.a.re("p (h t) -> p h t", h=4), ALU.add)
                cur = nxt
                segs[itr]()
            bw = self.bank()
            for hh in range(4):
                K.mm(bw[:, hh * 128:(hh + 1) * 128], kbg[:, hh, :], PT[:, hh, :])
            K.actf(wneg, bw.a.re("p (h t) -> p h t", h=4), AF.Copy, scale=-1.0)
            bvn = self.bank()
            for hh in range(4):
                sl = slice(hh * 128, (hh + 1) * 128)
                K.mm(bvn[:, sl], PT[:, hh, :], vb[:, hh, :], start=True, stop=False)
                K.mm(bvn[:, sl], wneg[:, hh, :], SA[:, hh, :], start=False, stop=True)
            K.cp(vn, bvn.a.re("p (h t) -> p h t", h=4))
            boa = self.bank()
            for hh in range(4):
                sl = slice(hh * 128, (hh + 1) * 128)
                K.mm(boa[:, sl], qg[:, hh, :], SA[:, hh, :], start=True, stop=False)
                K.mm(boa[:, sl], aqkT[:, hh, :], vn[:, hh, :], start=False, stop=True)
            bsa = self.bank()
            for hh in range(4):
                K.mm(bsa[:, hh * 128:(hh + 1) * 128], kdA[:, hh, :], vn[:, hh, :])
            for hh in range(4):
                K.stt(SA[:, hh, :], SA[:, hh, :], egl[:, hh:hh + 1], bsa[:, hh * 128:(hh + 1) * 128],
                      ALU.mult, ALU.add)
            for hh in range(4):
                K.actf(junk, boa[:, hh * 128:(hh + 1) * 128], AF.Square, accum=ssA[:, hh:hh + 1])
            K.actf(ssA, ssA, AF.Sqrt, bias=self.epst, scale=1.0 / 128.0)
            K.op(K.dve, lambda: nc.vector.reciprocal(ssA.a.ap, ssA.a.ap), [ssA], [ssA])
            K.tt(Gt, zs, ev[:, EV_GDNW:EV_GDNW + 512], ALU.mult, E=K.pool)
            for hh in range(4):
                sl = slice(hh * 128, (hh + 1) * 128)
                K.stt(oaT[:, sl], boa[:, sl], ssA[:, hh:hh + 1], Gt[:, sl], ALU.mult, ALU.mult)
            btr = self.bank()
            for hh in range(4):
                K.mm(btr[:, hh * 128:(hh + 1) * 128], oaT[:, hh * 128:(hh + 1) * 128], ID)
            K.cp(oFM[:, 0:4, :], btr.a.re("p (h t) -> p h t", h=4), E=K.act)

            for half in range(2):
                b = self.bank()
                for j in range(4):
                    wo = woblock(half * 4 + j)
                    for c in range(8):
                        K.mm(b[:, j * P:(j + 1) * P], wo(c), oFM[:, c, :], start=(c == 0), stop=(c == 7))
                K.tt(h[:, half * 4:(half + 1) * 4, :], h[:, half * 4:(half + 1) * 4, :],
                     b.a.re("p (j t) -> p j t", j=4), ALU.add)
            K.dma(K.act, self.hview(dst, it * P, P), h)
        K.barrier()
        es.close()

    run_variant(True, [(s_, s_ * tps) for s_ in range(NSEQ)])
    run_variant(False, [(s_, s_ * tps + j) for s_ in range(NSEQ) for j in range(1, tps)])
    K.barrier()
    esP.close()


Model.even_phase = _even_phase


def _declare_mixer_inputs(self, ne, no):
    K = self.K
    self.even_w_in = K.dram("even_w_in", [ne, D, EVEN_IN], F32, kind="ExternalInput")
    self.even_w_out = K.dram("even_w_out", [ne, D, D], F32, kind="ExternalInput")
    self.evec = K.dram("evec", [ne, 128, EV_N], F32, kind="ExternalInput")
    if no > 0:
        self.declare_odd_inputs(no)


Model.declare_mixer_inputs = _declare_mixer_inputs


def host_mixer_inputs(inp, L):
    ne = (L + 1) // 2
    no = L // 2
    d = {"even_w_in": np.ascontiguousarray(inp["even_w_in"][:ne]),
         "even_w_out": np.ascontiguousarray(inp["even_w_out"][:ne]),
         "evec": np.stack([host_even_vec(inp, i) for i in range(ne)])}
    if no > 0:
        d.update(host_odd_inputs(inp, no))
    return d


OV_CONV = 0
OV_CONVB = 32
OV_DTB = 40
OV_ALOG = 48
OV_D = 56
OV_SNW = 568
OV_MU = 1080
OV_W0 = 1094
OV_A0 = 1606
OV_KK = 1610
OV_KA = 1614
OV_RK = 1618
OV_GNW = 1622
OV_GNB = 2134
OV_SEL = 2646
OV_N = 2678


def host_odd_vec(inp, i):
    v = np.zeros((128, OV_N), np.float32)
    f = lambda a, n: np.asarray(a, np.float32).reshape(n, 128).T
    cw = np.asarray(inp["ssd_conv_w"][i], np.float32)
    v[:, OV_CONV:OV_CONV + 32] = cw.T.reshape(8, 128, 4).transpose(1, 0, 2).reshape(128, 32)
    v[:, OV_CONVB:OV_CONVB + 8] = f(inp["ssd_conv_b"][i], 8)
    v[:, OV_DTB:OV_DTB + 8] = np.asarray(inp["ssd_dt_bias"][i])[None, :]
    v[:, OV_ALOG:OV_ALOG + 8] = np.asarray(inp["ssd_a_log"][i])[None, :]
    v[:, OV_D:OV_D + 512] = np.repeat(np.asarray(inp["ssd_d"][i]), 64)[None, :]
    v[:, OV_SNW:OV_SNW + 512] = np.asarray(inp["ssd_norm_w"][i])[None, :]
    v[:, OV_MU:OV_MU + 14] = f(inp["rwkv_mu"][i], 14)
    v[:, OV_W0:OV_W0 + 512] = np.asarray(inp["rwkv_w0"][i])[None, :]
    v[:, OV_A0:OV_A0 + 4] = f(inp["rwkv_a0"][i], 4)
    v[:, OV_KK:OV_KK + 4] = f(inp["rwkv_k_k"][i], 4)
    v[:, OV_KA:OV_KA + 4] = f(inp["rwkv_k_a"][i], 4)
    v[:, OV_RK:OV_RK + 4] = f(np.asarray(inp["rwkv_r_k"][i]).reshape(-1), 4)
    v[:, OV_GNW:OV_GNW + 512] = np.asarray(inp["rwkv_gn_w"][i])[None, :]
    v[:, OV_GNB:OV_GNB + 512] = np.asarray(inp["rwkv_gn_b"][i])[None, :]
    p = np.arange(128)
    sel = np.zeros((128, 4, 8), np.float32)
    for c in range(4):
        sel[p, c, 2 * c + p // 64] = 1.0
    v[:, OV_SEL:OV_SEL + 32] = sel.reshape(128, 32)
    return v


def host_odd_inputs(inp, no):
    a2 = np.zeros((no, 128, 512), np.float32)
    a2[:, 64:128, :] = np.asarray(inp["rwkv_a2"][:no])
    w2 = np.zeros((no, 128, 512), np.float32)
    w2[:, 0:64, :] = np.asarray(inp["rwkv_w2"][:no])
    return {"odd_w_in": np.ascontiguousarray(inp["odd_w_in"][:no]),
            "odd_w_out": np.ascontiguousarray(inp["odd_w_out"][:no]),
            "ovec": np.stack([host_odd_vec(inp, i) for i in range(no)]),
            "rw2": w2, "ra2": a2, "rg2": np.ascontiguousarray(inp["rwkv_g2"][:no])}


def _declare_odd_inputs(self, no):
    K = self.K
    self.odd_w_in = K.dram("odd_w_in", [no, D, ODD_IN], F32, kind="ExternalInput")
    self.odd_w_out = K.dram("odd_w_out", [no, D, D], F32, kind="ExternalInput")
    self.ovec = K.dram("ovec", [no, 128, OV_N], F32, kind="ExternalInput")
    self.rw2 = K.dram("rw2", [no, 128, 512], F32, kind="ExternalInput")
    self.ra2 = K.dram("ra2", [no, 128, 512], F32, kind="ExternalInput")
    self.rg2 = K.dram("rg2", [no, 128, 512], F32, kind="ExternalInput")


Model.declare_odd_inputs = _declare_odd_inputs


def _odd_phase(self, l, src, dst):
    K = self.K
    nc = self.nc
    i_l = l // 2
    esP = contextlib.ExitStack()
    C, CB = self.C, self.CB
    P = 128
    tps = self.seq // P
    WIN = self.odd_w_in.a[i_l]
    WOUT = self.odd_w_out.a[i_l]

    def bc4(v, n=4):
        return v.re("p (o t) -> p o t", o=1).bc([128, n, 128])

    def bcl(v, n, m):
        return _v(v).re("p (h o) -> p h o", o=1).bc([128, n, m])

    def recip(t):
        K.op(K.dve, lambda: nc.vector.reciprocal(_v(t).ap, _v(t).ap), [t], [t])

    ov = K.sb([128, OV_N], F32, "ov", esP)
    K.dma(K.sp, ov, self.ovec.a[i_l])
    w2t = K.sb([128, 512], F32, "w2t", esP)
    a2t = K.sb([128, 512], F32, "a2t", esP)
    g2t = K.sb([128, 512], F32, "g2t", esP)
    K.dma(K.sp, w2t, self.rw2.a[i_l])
    K.dma(K.sp, a2t, self.ra2.a[i_l])
    K.dma(K.sp, g2t, self.rg2.a[i_l])
    nexpa = K.sb([128, 8], F32, "nexpa", esP)
    K.actf(nexpa, ov[:, OV_ALOG:OV_ALOG + 8], AF.Exp)
    K.ts(nexpa, nexpa, -1.0, ALU.mult)
    gneps = K.sb([128, 1], F32, "gneps", esP)
    K.memset(gneps, 64e-5)
    convw = ov[:, OV_CONV:OV_CONV + 32]
    gain = self.vec[:, 8 * l:8 * (l + 1)]
    STs = [K.sb([128, 512], F32, "ST", esP) for _ in range(NSEQ)]
    Hss = [K.sb([128, 4, 128], F32, "Hs", esP) for _ in range(NSEQ)]
    histC = [K.sb([128, 8, 3], F32, "histC", esP) for _ in range(NSEQ)]
    histP = [K.sb([128, 14, 1], F32, "histP", esP) for _ in range(NSEQ)]
    for s_ in range(NSEQ):
        K.memset(STs[s_], 0.0)
        K.memset(Hss[s_], 0.0)
        K.memset(histC[s_], 0.0)
        K.memset(histP[s_], 0.0)

    def run_variant(hp, tiles):
        if not tiles:
            return
        es = contextlib.ExitStack()
        AD = F32 if hp else BF16
        ID = C["ident"] if hp else CB["ident"]
        if hp:
            stg = [K.sb([128, 8, 512], F32, "stg", es) for _ in range(2)]
            cnt = [0]

            def _stream(dram2d, col0, n):
                t = stg[cnt[0] % 2]
                cnt[0] += 1
                src_ = dram2d[:, col0:col0 + n].re("(k p) n -> p k n", p=128)
                K.dma(K.sp, t[:, :, 0:n], src_)
                return lambda k: t[:, k, 0:n]

            wblock = lambda col0, n: _stream(WIN, col0, n)
            wblock32 = wblock
            woblock = lambda n0: _stream(WOUT, n0 * 128, 128)
        else:
            w_in = K.sb([128, 8, ODD_IN], BF16, "w_in", es)
            for k in range(8):
                K.dma(K.pool, w_in[:, k, :], WIN[k * 128:(k + 1) * 128, :])
            w_out = K.sb([128, 8, D], BF16, "w_out", es)
            K.dma(K.pool, w_out, WOUT.re("(c p) n -> p c n", p=128))
            wg = K.sb([128, 8, 136], F32, "wg", es)
            for k in range(8):
                K.dma(K.sp, wg[:, k, 0:8], WIN[k * 128:(k + 1) * 128, 1536:1544])
                K.dma(K.sp, wg[:, k, 8:136], WIN[k * 128:(k + 1) * 128, 3080:3208])
            wblock = lambda col0, n: (lambda k: w_in[:, k, col0:col0 + n])
            wblock32 = lambda col0, n: (lambda k: wg[:, k, 0:8])
            woblock = lambda n0: (lambda c: w_out[:, c, n0 * 128:(n0 + 1) * 128])

        def sb(shape, name, dt=F32):
            return K.sb(shape, dt, name, es)

        h = sb([128, 8, P], "h"); hn32 = sb([128, 8, P], "hn32"); hn = hn32 if hp else sb([128, 8, P], "hn", BF16)
        sq = sb([128, 8, P], "sq", BF16); rp = sb([128, P], "rp")
        preC = sb([128, 8, P + 3], "preC"); cacc = sb([128, 8, P], "cacc"); xbc = sb([128, 8, P], "xbc")
        pdp = sb([128, 14, P + 1], "pdp"); pdm = sb([128, 14, P], "pdm"); pdd = pdm
        zs = sb([128, 512], "zs"); oFM = sb([128, 8, P], "oFM", AD)
        dt8 = sb([128, 8], "dt8"); a8 = sb([128, 8], "a8"); acc8 = sb([128, 8], "acc8")
        eac8 = sb([128, 8], "eac8"); edl8 = sb([128, 8], "edl8"); cd8 = sb([128, 8], "cd8"); dte8 = sb([128, 8], "dte8")
        segT = sb([128, 8, 128], "segT"); MT = sb([128, 8, 128], "MT", AD)
        xTs = sb([128, 512], "xTs"); xdt = sb([128, 8, 64], "xdt", AD); xdte = sb([128, 8, 64], "xdte", AD)
        Bb = sb([128, 256], "Bb", AD); Cb = sb([128, 2, 128], "Cb", AD)
        STb = None if hp else sb([128, 512], "STb", BF16)
        y1 = sb([128, 512], "y1"); y2 = sb([128, 512], "y2"); ss2 = sb([128, 2], "ss2"); junk = sb([128, 256], "junk")
        ycT = sb([128, 512], "ycT", AD)
        txw = sb([64, P], "txw"); ldT = sb([128, 512], "ldT"); aF = sb([128, 4, P], "aF"); sg = sb([128, P], "sg")
        kp = sb([128, 4, P], "kp"); kq = sb([128, 4, P], "kq"); kk = kq; bm = sb([128, 4, P], "bm"); gl4 = sb([128, 4], "gl4")
        sqk = sb([128, 4, P], "sqk"); lg = sb([128, 4, P], "lg"); lgx = sb([128, 4, P], "lgx"); nmid = sb([128, 4], "nmid")
        e_r = sb([128, 4, P], "e_r"); e_a = sb([128, 4, P], "e_a"); e_b = sb([128, 4, P], "e_b")
        e_r0 = sb([128, 4, P], "e_r0"); e_a0 = sb([128, 4, P], "e_a0")
        rt_ = e_r; r0_ = e_r0; at_ = e_a; a0_ = e_a0; bt_ = e_b; kt_ = xbc[:, 4:8, :]
        edlT = sb([128, 512], "edlT"); kdT = sb([128, 512], "kdT"); bdT = sb([128, 512], "bdT"); vT = sb([128, 512], "vT")
        A = [segT[:, 0:4, :], cacc[:, 0:4, :]]
        AT = [segT[:, 4:8, :], cacc[:, 4:8, :]]
        PT = xbc[:, 0:4, :]
        AakT = y1.a.re("p (h t) -> p h t", h=4); RbT = y2.a.re("p (h t) -> p h t", h=4); RkT = xTs.a.re("p (h t) -> p h t", h=4)
        Xs = sb([128, 4, 64], "Xs"); Us = sb([128, 8, 64], "Us")
        yd = lg.a.re("p c t -> p (c t)").re("p (h q) -> p h q", h=8); ym = lgx.a.re("p c t -> p (c t)").re("p (h q) -> p h q", h=8); mean8 = sb([128, 8], "mean8"); var8 = sb([128, 8], "var8")
        rkr = sb([128, 4, P], "rkr"); rks = sb([128, 8], "rks"); gT = sb([128, 512], "gT"); ydT = sb([128, 512], "ydT", AD)

        for (s_, it) in tiles:
            ST, Hs = STs[s_], Hss[s_]
            K.dma(K.sp, h, self.hview(src, it * P, P))
            K.cp(preC[:, :, 0:3], histC[s_], E=K.pool)
            K.cp(pdp[:, :, 0:1], histP[s_], E=K.pool)
            if not hp:
                K.cp(STb, ST, E=K.pool)
            Sst = ST if hp else STb
            self.rmsnorm_fm(h, gain, [hn32] if hp else [hn, hn32], sq, rp, n=P)
            for c4 in range(2):
                wb = wblock(512 + c4 * 512, 512)
                b = self.bank()
                for j in range(4):
                    for k in range(8):
                        K.mm(b[:, j * P:(j + 1) * P], wb(k)[:, j * 128:(j + 1) * 128], hn[:, k, :],
                             start=(k == 0), stop=(k == 7))
                K.cp(preC[:, c4 * 4:(c4 + 1) * 4, 3:3 + P], b.a.re("p (j t) -> p j t", j=4), E=K.act)
            for c4 in range(4):
                b = self.bank()
                nj = 4 if c4 < 3 else 2
                wb = wblock(1544 + c4 * 512, nj * 128)
                for j in range(nj):
                    c = c4 * 4 + j
                    for k in range(8):
                        if c == 12 and not hp:
                            K.mm(b[:, j * P:(j + 1) * P], wg[:, k, 8:136], hn32[:, k, :], start=(k == 0), stop=(k == 7))
                        else:
                            K.mm(b[:, j * P:(j + 1) * P], wb(k)[:, j * 128:(j + 1) * 128], hn[:, k, :],
                                 start=(k == 0), stop=(k == 7))
                K.cp(pdp[:, c4 * 4:c4 * 4 + nj, 1:1 + P], b[:, 0:nj * P].re("p (j t) -> p j t", j=nj), E=K.act)
            wb = wblock(0, 512)
            bz = self.bank()
            for k in range(8):
                K.mm(bz, hn[:, k, :], wb(k), start=(k == 0), stop=(k == 7))
            K.actf(zs, bz, AF.Silu)
            wb = wblock32(1536, 8)
            bg = self.bank()
            for k in range(8):
                K.mm(bg[:, 0:8], hn32[:, k, :], wb(k), start=(k == 0), stop=(k == 7))
            K.tt(dt8, bg[:, 0:8], ov[:, OV_DTB:OV_DTB + 8], ALU.add)
            K.ts(dt8, dt8, 60.0, ALU.min)
            K.actf(dt8, dt8, AF.Exp)
            K.actf(dt8, dt8, AF.Ln, bias=1.0)
            K.tt(a8, dt8, nexpa, ALU.mult)
            for c in range(8):
                K.actf(cacc[:, c, :], preC[:, c, 0:P], AF.Copy, scale=convw[:, c * 4:c * 4 + 1])
                for j in range(1, 4):
                    K.stt(cacc[:, c, :], preC[:, c, j:j + P], convw[:, c * 4 + j:c * 4 + j + 1], cacc[:, c, :],
                          ALU.mult, ALU.add)
                K.actf(xbc[:, c, :], cacc[:, c, :], AF.Silu, bias=ov[:, OV_CONVB + c:OV_CONVB + c + 1])
            K.cp(histC[s_], preC[:, :, P:P + 3], E=K.pool)
            Atri = cacc
            K.tt(Atri, bc4(C["tri"], 8), bcl(a8, 8, 128), ALU.mult)
            bsm = self.bank()
            K.mm(bsm[:, 0:8], C["tri"], a8)
            K.mm(bsm[:, 8:16], C["ones"], a8)
            K.cp(acc8, bsm[:, 0:8])
            K.actf(eac8, acc8, AF.Exp)
            K.actf(cd8, bsm[:, 8:16], AF.Exp)
            K.tt(edl8, bsm[:, 8:16], acc8, ALU.subtract)
            K.actf(edl8, edl8, AF.Exp)
            K.tt(dte8, dt8, edl8, ALU.mult)
            for hg in range(2):
                br = self.bank()
                K.mm(br, C["ones"], Atri[:, hg * 4:(hg + 1) * 4, :])
                for hh in range(4):
                    hd = hg * 4 + hh
                    K.ts(segT[:, hd, :], br[:, hh * 128:(hh + 1) * 128], acc8[:, hd:hd + 1], ALU.subtract, 0.0, ALU.min)
            K.actf(segT, segT, AF.Exp)
            K.tt(segT, segT, bc4(C["tri"], 8), ALU.mult, E=K.pool)
            bcb = self.bank()
            for g in range(2):
                K.mm(bcb[:, g * 128:(g + 1) * 128], xbc[:, 4 + g, :], xbc[:, 6 + g, :])
            for g in range(2):
                K.tt(MT[:, g * 4:(g + 1) * 4, :], segT[:, g * 4:(g + 1) * 4, :],
                     bcb[:, g * 128:(g + 1) * 128].re("p (o t) -> p o t", o=1).bc([128, 4, 128]), ALU.mult)
            K.cp(Cb, xbc[:, 6:8, :], E=K.pool)
            bxt = self.bank()
            for c in range(4):
                K.mm(bxt[:, c * 128:(c + 1) * 128], xbc[:, c, :], C["ident"])
            bbt = self.bank()
            for g in range(2):
                K.mm(bbt[:, g * 128:(g + 1) * 128], xbc[:, 4 + g, :], C["ident"])
            K.cp(xTs, bxt, E=K.act)
            K.cp(Bb, bbt[:, 0:256], E=K.act)
            K.tt(xdt, xTs.a.re("p (h q) -> p h q", h=8), bcl(dt8, 8, 64), ALU.mult)
            K.tt(xdte, xTs.a.re("p (h q) -> p h q", h=8), bcl(dte8, 8, 64), ALU.mult)
            by = self.bank()
            for hd in range(8):
                K.mm(by[:, hd * 64:(hd + 1) * 64], MT[:, hd, :], xdt[:, hd, :])
            bi = self.bank()
            for g in range(2):
                K.mm(bi[:, g * 256:(g + 1) * 256], Cb[:, g, :], Sst[:, g * 256:(g + 1) * 256])
            bst = self.bank()
            for g in range(2):
                K.mm(bst[:, g * 256:(g + 1) * 256], Bb[:, g * 128:(g + 1) * 128],
                     xdte[:, g * 4:(g + 1) * 4, :].re("p h q -> p (h q)"))
            K.tt(y1.a.re("p (h q) -> p h q", h=8), bi.a.re("p (h q) -> p h q", h=8), bcl(eac8, 8, 64), ALU.mult)
            K.tt(y1, y1, by, ALU.add)
            K.tt(y2, xTs, ov[:, OV_D:OV_D + 512], ALU.mult, E=K.pool)
            K.tt(y1, y1, y2, ALU.add)
            K.tt(y1, y1, zs, ALU.mult)
            K.tt(ST.a.re("p (h q) -> p h q", h=8), ST.a.re("p (h q) -> p h q", h=8), bcl(cd8, 8, 64), ALU.mult)
            K.tt(ST, ST, bst, ALU.add)
            for g in range(2):
                K.actf(junk, y1[:, g * 256:(g + 1) * 256], AF.Square, accum=ss2[:, g:g + 1])
            K.actf(ss2, ss2, AF.Sqrt, bias=self.epst, scale=1.0 / 256.0)
            recip(ss2)
            K.tt(y2, y1, ov[:, OV_SNW:OV_SNW + 512], ALU.mult, E=K.pool)
            for g in range(2):
                K.ts(ycT[:, g * 256:(g + 1) * 256], y2[:, g * 256:(g + 1) * 256], ss2[:, g:g + 1], ALU.mult)
            btr = self.bank()
            for hh in range(4):
                K.mm(btr[:, hh * 128:(hh + 1) * 128], ycT[:, hh * 128:(hh + 1) * 128], ID)
            K.cp(oFM[:, 0:4, :], btr.a.re("p (h t) -> p h t", h=4), E=K.act)

            K.tt(pdd, pdp[:, :, 0:P], pdp[:, :, 1:P + 1], ALU.subtract)
            K.tt(pdd, pdd, bcl(ov[:, OV_MU:OV_MU + 14], 14, P), ALU.mult, E=K.pool)
            K.tt(pdm, pdd, pdp[:, :, 1:P + 1], ALU.add)
            K.cp(histP[s_], pdp[:, :, P:P + 1], E=K.pool)
            R_, K_, V_ = pdm[:, 0:4, :], pdm[:, 4:8, :], pdm[:, 8:12, :]
            K.actf(txw, pdm[0:64, 12, :], AF.Tanh)
            K.actf(sg, pdm[:, 13, :], AF.Sigmoid)
            bw = self.bank()
            K.mm(bw, txw, w2t[0:64, :])
            K.tt(ldT, bw, ov[:, OV_W0:OV_W0 + 512], ALU.add)
            K.actf(ldT, ldT, AF.Sigmoid)
            K.ts(ldT, ldT, -float(np.exp(-0.5)), ALU.mult)
            ba = self.bank()
            for c in range(4):
                K.mm(ba[:, c * P:(c + 1) * P], a2t[64:128, c * 128:(c + 1) * 128], pdm[64:128, 12, :])
            for c in range(4):
                K.actf(aF[:, c, :], ba[:, c * P:(c + 1) * P], AF.Sigmoid, bias=ov[:, OV_A0 + c:OV_A0 + c + 1])
            bgm = self.bank()
            K.mm(bgm, sg, g2t)
            K.cp(gT, bgm, E=K.act)
            for c in range(4):
                K.ts(kp[:, c, :], aF[:, c, :], 1.0, ALU.subtract, ov[:, OV_KA + c:OV_KA + c + 1], ALU.mult)
                K.stt(kp[:, c, :], kp[:, c, :], 1.0, pdm[:, 4 + c, :], ALU.add, ALU.mult)
                K.ts(kq[:, c, :], pdm[:, 4 + c, :], ov[:, OV_KK + c:OV_KK + c + 1], ALU.mult, E=K.pool)
                K.stt(rkr[:, c, :], pdm[:, c, :], ov[:, OV_RK + c:OV_RK + c + 1], kp[:, c, :], ALU.mult, ALU.mult)
            K.tt(sqk, kq, kq, ALU.mult, E=K.pool)
            bss = self.bank()
            K.mm(bss, C["bones"], sqk)
            K.actf(sqk, bss.a.re("p (c t) -> p c t", c=4), AF.Sqrt, bias=self.epst)
            recip(sqk)
            K.tt(kk, kq, sqk, ALU.mult)
            K.tt(bm, kk, aF, ALU.mult, E=K.pool)
            blg = self.bank()
            blx = self.bank()
            for c in range(4):
                K.mm(blg[:, c * P:(c + 1) * P], ldT[:, c * 128:(c + 1) * 128], C["tri"])
                K.mm(blx[:, c * P:(c + 1) * P], ldT[:, c * 128:(c + 1) * 128], C["tris"])
            K.cp(lg, blg.a.re("p (c t) -> p c t", c=4))
            K.cp(lgx, blx.a.re("p (c t) -> p c t", c=4), E=K.act)
            K.ts(nmid, lg[:, :, 63], -1.0, ALU.mult)
            for c in range(4):
                K.actf(e_r[:, c, :], lg[:, c, :], AF.Exp, bias=nmid[:, c:c + 1])
                K.actf(e_a[:, c, :], lgx[:, c, :], AF.Exp, bias=nmid[:, c:c + 1])
                K.actf(e_b[:, c, :], lg[:, c, :], AF.Exp, bias=lg[:, c, 63:64], scale=-1.0)
            K.actf(e_r0, lg, AF.Exp)
            K.actf(e_a0, lgx, AF.Exp)
            K.cp(gl4, e_r0[:, :, 127])
            K.tt(rt_, R_, e_r, ALU.mult)
            K.tt(r0_, R_, e_r0, ALU.mult, E=K.pool)
            K.stt(at_, kk, -1.0, e_a, ALU.mult, ALU.mult)
            K.stt(a0_, kk, -1.0, e_a0, ALU.mult, ALU.mult)
            K.tt(kt_, kp, e_b, ALU.mult)
            K.tt(bt_, bm, e_b, ALU.mult, E=K.pool)
            bed = self.bank()
            K.mm(bed, C["ustr"], ldT)
            K.actf(edlT, bed, AF.Exp)
            for (srcT, dstT, scl) in ((kp, kdT, True), (bm, bdT, True), (V_, vT, False)):
                b = self.bank()
                for c in range(4):
                    K.mm(b[:, c * 128:(c + 1) * 128], srcT[:, c, :], C["ident"])
                if scl:
                    K.tt(dstT, b, edlT, ALU.mult)
                else:
                    K.cp(dstT, b, E=K.act)
            brk = self.bank()
            for c in range(4):
                K.mm(brk[:, 0:8], rkr[:, c, :], ov[:, OV_SEL + c * 8:OV_SEL + (c + 1) * 8], start=(c == 0), stop=(c == 3))
            K.cp(rks, brk[:, 0:8])
            byd = self.ps[7]
            for hg in range(2):
                b1, b2, b3, b4, b5 = [self.bank() for _ in range(5)]
                for hh in range(4):
                    hd = hg * 4 + hh
                    hq, pr = hd // 2, slice((hd % 2) * 64, (hd % 2) * 64 + 64)
                    sl = slice(hh * 128, (hh + 1) * 128)
                    K.mm(b1[:, sl], bt_[pr, hq, :], at_[pr, hq, :])
                    K.mm(b2[:, sl], at_[pr, hq, :], bt_[pr, hq, :])
                    K.mm(b3[:, sl], kt_[pr, hq, :], at_[pr, hq, :])
                    K.mm(b4[:, sl], bt_[pr, hq, :], rt_[pr, hq, :])
                    K.mm(b5[:, sl], kt_[pr, hq, :], rt_[pr, hq, :])
                K.tt(AT[0], b1.a.re("p (h t) -> p h t", h=4), bc4(C["tris"]), ALU.mult)
                K.tt(A[0], b2.a.re("p (h t) -> p h t", h=4), bc4(C["ustr"]), ALU.mult)
                K.tt(AakT, b3.a.re("p (h t) -> p h t", h=4), bc4(C["tris"]), ALU.mult)
                K.tt(RbT, b4.a.re("p (h t) -> p h t", h=4), bc4(C["tri"]), ALU.mult)
                K.tt(RkT, b5.a.re("p (h t) -> p h t", h=4), bc4(C["tri"]), ALU.mult)
                K.tt(PT, AT[0], bc4(C["ident"]), ALU.add, E=K.pool)
                cur = 0
                for itr in range(6):
                    nxt = 1 - cur
                    c1 = self.bank()
                    c2 = self.bank()
                    for hh in range(4):
                        sl = slice(hh * 128, (hh + 1) * 128)
                        K.mm(c1[:, sl], AT[cur][:, hh, :], A[cur][:, hh, :])
                        if itr < 5:
                            K.mm(c2[:, sl], A[cur][:, hh, :], AT[cur][:, hh, :])
                    K.cp(A[nxt], c1.a.re("p (h t) -> p h t", h=4), E=K.act)
                    if itr < 5:
                        K.cp(AT[nxt], c2.a.re("p (h t) -> p h t", h=4), E=K.dve)
                    c3 = self.bank()
                    for hh in range(4):
                        K.mm(c3[:, hh * 128:(hh + 1) * 128], A[nxt][:, hh, :], PT[:, hh, :])
                    K.tt(PT, PT, c3.a.re("p (h t) -> p h t", h=4), ALU.add)
                    cur = nxt
                bx = self.bank()
                for hh in range(4):
                    hd = hg * 4 + hh
                    hq, pr = hd // 2, slice((hd % 2) * 64, (hd % 2) * 64 + 64)
                    vs = slice((hd % 2) * 64, (hd % 2) * 64 + 64)
                    K.mm(bx[:, hh * 64:(hh + 1) * 64], a0_[pr, hq, :], Hs[pr, hq, vs], start=True, stop=False)
                    K.mm(bx[:, hh * 64:(hh + 1) * 64], AakT[:, hh, :], vT[:, hd * 64:(hd + 1) * 64], start=False, stop=True)
                K.cp(Xs, bx[:, 0:256].re("p (h q) -> p h q", h=4))
                bu = self.bank()
                for hh in range(4):
                    K.mm(bu[:, hh * 64:(hh + 1) * 64], PT[:, hh, :], Xs[:, hh, :])
                K.cp(Us[:, hg * 4:(hg + 1) * 4, :], bu[:, 0:256].re("p (h q) -> p h q", h=4))
                for hh in range(4):
                    hd = hg * 4 + hh
                    hq, pr = hd // 2, slice((hd % 2) * 64, (hd % 2) * 64 + 64)
                    vs = slice((hd % 2) * 64, (hd % 2) * 64 + 64)
                    o = byd[:, hd * 64:(hd + 1) * 64]
                    K.mm(o, r0_[pr, hq, :], Hs[pr, hq, vs], start=True, stop=False)
                    K.mm(o, RbT[:, hh, :], Us[:, hd, :], start=False, stop=False)
                    K.mm(o, RkT[:, hh, :], vT[:, hd * 64:(hd + 1) * 64], start=False, stop=True)
            bh = self.bank()
            for hq in range(4):
                o = bh[:, hq * 128:(hq + 1) * 128]
                K.mm(o, bdT[:, hq * 128:(hq + 1) * 128], Us[:, 2 * hq:2 * hq + 2, :].re("p h q -> p (h q)"), start=True, stop=False)
                K.mm(o, kdT[:, hq * 128:(hq + 1) * 128], vT[:, hq * 128:(hq + 1) * 128], start=False, stop=True)
            for hq in range(4):
                K.stt(Hs[:, hq, :], Hs[:, hq, :], gl4[:, hq:hq + 1], bh[:, hq * 128:(hq + 1) * 128], ALU.mult, ALU.add)
            K.cp(yd, byd.a.re("p (h q) -> p h q", h=8), E=K.act)
            K.red(mean8, yd)
            K.ts(mean8, mean8, 1.0 / 64.0, ALU.mult)
            K.tt(ym, yd, bcl(mean8, 8, 64), ALU.subtract)
            K.tt(yd, ym, ym, ALU.mult, E=K.pool)
            K.red(var8, yd)
            K.actf(var8, var8, AF.Sqrt, bias=gneps, scale=1.0 / 64.0)
            recip(var8)
            K.tt(ym, ym, bcl(var8, 8, 64), ALU.mult)
            ymf = _v(ym).re("p h q -> p (h q)")
            K.tt(ymf, ymf, ov[:, OV_GNW:OV_GNW + 512], ALU.mult, E=K.pool)
            K.tt(ymf, ymf, ov[:, OV_GNB:OV_GNB + 512], ALU.add)
            K.tt(yd, vT.a.re("p (h q) -> p h q", h=8), bcl(rks, 8, 64), ALU.mult)
            K.tt(ym, ym, yd, ALU.add)
            K.tt(ydT, ymf, gT, ALU.mult)
            btr = self.bank()
            for hh in range(4):
                K.mm(btr[:, hh * 128:(hh + 1) * 128], ydT[:, hh * 128:(hh + 1) * 128], ID)
            K.cp(oFM[:, 4:8, :], btr.a.re("p (h t) -> p h t", h=4), E=K.act)
            for half in range(2):
                b = self.bank()
                for j in range(4):
                    n = half * 4 + j
                    wo = woblock(n)
                    for c in range(8):
                        K.mm(b[:, j * P:(j + 1) * P], wo(c), oFM[:, c, :], start=(c == 0), stop=(c == 7))
                K.tt(h[:, half * 4:(half + 1) * 4, :], h[:, half * 4:(half + 1) * 4, :],
                     b.a.re("p (j t) -> p j t", j=4), ALU.add)
            K.dma(K.act, self.hview(dst, it * P, P), h)
        K.barrier()
        es.close()

    run_variant(True, [(s_, s_ * tps) for s_ in range(NSEQ)])
    run_variant(False, [(s_, s_ * tps + j) for s_ in range(NSEQ) for j in range(1, tps)])
    K.barrier()
    esP.close()


Model.odd_phase = _odd_phase


def kernel(**inputs):
    inp = {k: np.asarray(v) for k, v in inputs.items()}
    x = inp["x"]
    M = Model(n_layers=4, seq=x.shape[1], mix=True)
    nc = M.build()
    maps = make_in_maps(inp, M, x, 8)
    res = run_bass_kernel_spmd(nc, maps, core_ids=list(range(8)))
    out = np.concatenate([r["out"].reshape(NSEQ, x.shape[1], D) for r in res.results], axis=0)
    return out.astype(np.float32)
```

```python
import contextlib
import os
import numpy as np
import ml_dtypes
import concourse.bass as bass
import concourse.mybir as mybir
from concourse.bass_utils import run_bass_kernel_spmd

F32 = mybir.dt.float32
BF16 = mybir.dt.bfloat16
AF = mybir.ActivationFunctionType
ALU = mybir.AluOpType
AX = mybir.AxisListType

EPOCH = 30000


class Tl:
    def __init__(self, h, name):
        self.h = h
        self.name = name
        self.w = None
        self.r = {}
        self.dsem = None
        self.dcnt = 0
        self.psum = False

    def __getitem__(self, idx):
        return V(self, self.h[idx])

    @property
    def a(self):
        return V(self, self.h[:])


class V:
    def __init__(self, t, ap):
        self.t = t
        self.ap = ap

    def __getitem__(self, idx):
        return V(self.t, self.ap[idx])

    def bitcast(self, dt):
        return V(self.t, self.ap.bitcast(dt))

    def bc(self, shape):
        return V(self.t, self.ap.to_broadcast(list(shape)))

    def re(self, s, **kw):
        return V(self.t, self.ap.rearrange(s, **kw))


def _v(x):
    return x.a if isinstance(x, Tl) else x


class Eng:
    def __init__(self, K, name, eng):
        self.K = K
        self.name = name
        self.eng = eng
        self.epoch = -1
        self.cnt = EPOCH
        self.sem = None
        self.known = {}

    def tick(self, ins):
        if self.cnt >= EPOCH:
            self.epoch += 1
            self.cnt = 0
            self.sem = self.K.new_sem(f"s_{self.name}_{self.epoch}")
        self.cnt += 1
        ins.then_inc(self.sem, 1)
        return (id(self.sem), self.sem, self.cnt, self.name)


class KB:
    def __init__(self, nc):
        self.nc = nc
        self.es = contextlib.ExitStack()
        self.pe = Eng(self, "pe", nc.tensor)
        self.dve = Eng(self, "dve", nc.vector)
        self.act = Eng(self, "act", nc.scalar)
        self.pool = Eng(self, "pool", nc.gpsimd)
        self.sp = Eng(self, "sp", nc.sync)
        self.engs = [self.pe, self.dve, self.act, self.pool, self.sp]
        self.nsem = 0
        self.ntile = 0
        self.all_dma = {}
        self.ninst = 0

    def new_sem(self, name):
        self.nsem += 1
        return self.es.enter_context(self.nc.semaphore(f"{name}_{self.nsem}"))

    def sb(self, shape, dt=F32, name="t", es=None):
        self.ntile += 1
        nm = f"{name}_{self.ntile}"
        h = (es or self.es).enter_context(self.nc.sbuf_tensor(nm, list(shape), dt))
        return Tl(h, nm)

    def psum(self, shape, dt=F32, name="p", es=None):
        self.ntile += 1
        nm = f"{name}_{self.ntile}"
        h = (es or self.es).enter_context(self.nc.psum_tensor(nm, list(shape), dt))
        t = Tl(h, nm)
        t.psum = True
        return t

    def dram(self, name, shape, dt=F32, kind="Internal"):
        h = self.nc.dram_tensor(name, list(shape), dt, kind=kind)
        return Tl(h.ap(), name)

    def _wait(self, E, toks):
        best = {}
        for tk in toks:
            if tk is None:
                continue
            sid, sem, val, who = tk
            if who == E.name and E is self.pe:
                continue
            if E.known.get(sid, 0) >= val:
                continue
            if best.get(sid, (None, 0))[1] < val:
                best[sid] = (sem, val)
        for sid, (sem, val) in best.items():
            E.eng.wait_ge(sem, val)
            E.known[sid] = val
            self.ninst += 1

    skip = False

    def op(self, E, fn, reads, writes):
        if self.skip:
            return None
        toks = []
        rt = [(_v(x).t) for x in reads if x is not None and not isinstance(x, (int, float))]
        wt = [(_v(x).t) for x in writes if x is not None]
        for t in rt:
            toks.append(t.w)
            if t.psum:
                for k, tk in t.r.items():
                    if k != E.name:
                        toks.append(tk)
        for t in wt:
            toks.append(t.w)
            for k, tk in t.r.items():
                if k == E.name:
                    continue
                toks.append(tk)
        self._wait(E, toks)
        ins = fn()
        tok = E.tick(ins)
        self.ninst += 1
        for t in rt:
            t.r[E.name] = tok
        for t in wt:
            t.w = tok
            t.r = {}
        return tok

    def dma(self, Q, out, in_):
        out = _v(out)
        in_ = _v(in_)
        ot, it = out.t, in_.t
        toks = [it.w, ot.w] + list(ot.r.values())
        self._wait(Q, toks)
        if ot.dsem is None:
            ot.dsem = self.new_sem("d_" + ot.name)
        ins = Q.eng.dma_start(out=out.ap, in_=in_.ap)
        ins.then_inc(ot.dsem, 16)
        ot.dcnt += 16
        tok = (id(ot.dsem), ot.dsem, ot.dcnt, "dma")
        self.ninst += 1
        ot.w = tok
        ot.r = {}
        it.r["dma%d" % id(ot.dsem)] = tok
        self.all_dma[id(ot.dsem)] = tok
        return tok

    def barrier(self):
        toks = list(self.all_dma.values())
        for E in self.engs:
            if E.sem is not None and E.cnt > 0:
                toks.append((id(E.sem), E.sem, E.cnt, E.name + "_b"))
        for E in self.engs:
            self._wait(E, toks)

    def mm(self, out, lhsT, rhs, start=True, stop=True, ser=False):
        out, lhsT, rhs = _v(out), _v(lhsT), _v(rhs)
        if ser and not self.skip and self.pe.sem is not None and self.pe.cnt > 0:
            self.pe.eng.wait_ge(self.pe.sem, self.pe.cnt)
        return self.op(self.pe, lambda: self.nc.tensor.matmul(out.ap, lhsT.ap, rhs.ap, start=start, stop=stop),
                       [lhsT, rhs], [out])

    def actf(self, out, in_, func, bias=None, scale=None, accum=None, E=None):
        out, in_ = _v(out), _v(in_)
        kw = {}
        rd = [in_]
        if bias is not None:
            if isinstance(bias, (int, float)):
                kw["bias"] = float(bias)
            else:
                bias = _v(bias)
                kw["bias"] = bias.ap
                rd.append(bias)
        if scale is not None:
            if isinstance(scale, (int, float)):
                kw["scale"] = float(scale)
            else:
                scale = _v(scale)
                kw["scale"] = scale.ap
                rd.append(scale)
        wr = [out]
        if accum is not None:
            accum = _v(accum)
            kw["accum_out"] = accum.ap
            wr.append(accum)
        return self.op(self.act, lambda: self.nc.scalar.activation(out.ap, in_.ap, func, **kw), rd, wr)

    def _ve(self, E):
        return E or self.dve

    def _sc(self, s, rd):
        if isinstance(s, (int, float)):
            return float(s)
        s = _v(s)
        rd.append(s)
        return s.ap

    def ts(self, out, in0, s1, op0, s2=None, op1=None, E=None, accum=None):
        E = self._ve(E)
        out, in0 = _v(out), _v(in0)
        rd = [in0]
        a1 = self._sc(s1, rd)
        a2 = None if s2 is None else self._sc(s2, rd)
        wr = [out]
        kw = {}
        if accum is not None:
            accum = _v(accum)
            kw["accum_out"] = accum.ap
            wr.append(accum)
        if op1 is None:
            return self.op(E, lambda: E.eng.tensor_scalar(out.ap, in0.ap, a1, None, op0, **kw), rd, wr)
        return self.op(E, lambda: E.eng.tensor_scalar(out.ap, in0.ap, a1, a2, op0, op1, **kw), rd, wr)

    def tt(self, out, in0, in1, op, E=None):
        E = self._ve(E)
        out, in0, in1 = _v(out), _v(in0), _v(in1)
        return self.op(E, lambda: E.eng.tensor_tensor(out.ap, in0.ap, in1.ap, op), [in0, in1], [out])

    def stt(self, out, in0, s, in1, op0, op1, E=None):
        E = self.dve
        out, in0, in1 = _v(out), _v(in0), _v(in1)
        rd = [in0, in1]
        a = self._sc(s, rd)
        return self.op(E, lambda: E.eng.scalar_tensor_tensor(out.ap, in0.ap, a, in1.ap, op0, op1), rd, [out])

    def cp(self, out, in_, E=None):
        E = self._ve(E)
        out, in_ = _v(out), _v(in_)
        if E is self.act:
            return self.op(E, lambda: self.nc.scalar.copy(out.ap, in_.ap), [in_], [out])
        return self.op(E, lambda: E.eng.tensor_copy(out.ap, in_.ap), [in_], [out])

    def memset(self, out, val, E=None):
        E = self._ve(E)
        out = _v(out)
        return self.op(E, lambda: E.eng.memset(out.ap, val), [], [out])

    def red(self, out, in_, op=ALU.add, E=None):
        E = self._ve(E)
        out, in_ = _v(out), _v(in_)
        return self.op(E, lambda: E.eng.tensor_reduce(out.ap, in_.ap, AX.X, op), [in_], [out])


D = 1024
NSEQ = 2
T = 256
CH = 128
EVEN_IN = 3608
ODD_IN = 3336
NEPS = 1e-6


def host_consts():
    i = np.arange(128)
    c = {}
    c["ident"] = np.eye(128, dtype=np.float32)
    c["tri"] = (i[:, None] <= i[None, :]).astype(np.float32)
    c["tris"] = (i[:, None] < i[None, :]).astype(np.float32)
    c["ustr"] = (i[:, None] > i[None, :]).astype(np.float32)
    c["ones"] = np.ones((128, 128), np.float32)
    c["bones"] = ((i[:, None] // 64) == (i[None, :] // 64)).astype(np.float32)
    return c


CONST_NAMES = ["ident", "tri", "tris", "ustr", "ones", "bones"]


class Model:
    def __init__(self, n_layers=4, seq=2048, mix=True, dbg=False):
        self.L = n_layers
        self.seq = seq
        self.NT = NSEQ * seq
        self.mix = mix
        self.dbg = dbg

    def build(self):
        nc = bass.Bass("TRN2", target_bir_lowering=False)
        self.nc = nc
        K = KB(nc)
        self.K = K
        L, NT = self.L, self.NT
        ne = (L + 1) // 2
        no = L // 2
        self.x = K.dram("x", [NT, D], F32, kind="ExternalInput")
        self.out = K.dram("out", [NT, D], F32, kind="ExternalOutput")
        self.hA = K.dram("hA", [D, NT], F32)
        self.hB = K.dram("hB", [D, NT], F32)
        self.cst_d = K.dram("consts", [128, len(CONST_NAMES) * 128], F32, kind="ExternalInput")
        self.w1 = K.dram("mlp_w1", [L, D, 4 * D], F32, kind="ExternalInput")
        self.w2 = K.dram("mlp_w2", [L, 4 * D, D], F32, kind="ExternalInput")
        self.nvec = 3 * L + 1
        self.vec_d = K.dram("vecs", [128, 8 * (2 * L + 1)], F32, kind="ExternalInput")
        if self.mix:
            self.declare_mixer_inputs(ne, no)
        self.cst = K.sb([128, len(CONST_NAMES) * 128], F32, "cst")
        K.dma(K.sp, self.cst, self.cst_d)
        self.C = {n: self.cst[:, i * 128:(i + 1) * 128] for i, n in enumerate(CONST_NAMES)}
        self.cstb = K.sb([128, len(CONST_NAMES) * 128], BF16, "cstb")
        K.cp(self.cstb, self.cst)
        self.CB = {n: self.cstb[:, i * 128:(i + 1) * 128] for i, n in enumerate(CONST_NAMES)}
        self.vec = K.sb([128, 8 * (2 * L + 1)], F32, "vec")
        K.dma(K.sp, self.vec, self.vec_d)
        self.epst = K.sb([128, 1], F32, "eps")
        K.memset(self.epst, NEPS)
        self.ps = [K.psum([128, 512], F32, "bank") for _ in range(8)]
        self.psi = 0

        self.phase0()
        src, dst = self.hA, self.hB
        for l in range(L):
            if self.mix:
                K.barrier()
                if l % 2 == 0:
                    self.even_phase(l, src, dst)
                elif os.environ.get("NOODD"):
                    src, dst = dst, src
                else:
                    self.odd_phase(l, src, dst)
                src, dst = dst, src
            K.barrier()
            self.mlp_phase(l, src, dst)
            src, dst = dst, src
        K.barrier()
        self.final_phase(src)
        K.barrier()
        return nc

    def bank(self):
        b = self.ps[self.psi % 7]
        self.psi += 1
        return b

    def hview(self, hd, t0, n):
        return hd.a.re("(c p) t -> p c t", p=128)[:, :, t0:t0 + n]

    def phase0(self):
        K = self.K
        es = contextlib.ExitStack()
        xt = [K.sb([128, 2, D], F32, "xt", es) for _ in range(2)]
        ht = [K.sb([128, 8, T], F32, "ht", es) for _ in range(2)]
        for i in range(self.NT // T):
            xs, hs = xt[i % 2], ht[i % 2]
            K.dma(K.sp, xs, self.x.a[i * T:(i + 1) * T, :].re("(s p) f -> p s f", p=128))
            for s in range(2):
                for half in range(2):
                    b = self.bank()
                    for j in range(4):
                        c = half * 4 + j
                        K.mm(b[:, j * 128:(j + 1) * 128], xs[:, s, c * 128:(c + 1) * 128], self.C["ident"])
                    dst = hs[:, half * 4:(half + 1) * 4, s * 128:(s + 1) * 128]
                    K.cp(dst, b.a.re("p (j t) -> p j t", j=4), E=(K.dve if half == 0 else K.act))
            K.dma(K.act, self.hview(self.hA, i * T, T), hs)
        K.barrier()
        es.close()

    def rmsnorm_fm(self, h, gain, hn, sq, rp, n=T):
        K = self.K
        K.actf(sq, h, AF.Square)
        b = self.bank()
        for c in range(8):
            K.mm(b[:, 0:n], self.CB["ones"], _v(sq)[:, c, :], start=(c == 0), stop=(c == 7))
        K.actf(rp, b[:, 0:n], AF.Sqrt, bias=self.epst, scale=1.0 / D)
        K.op(K.dve, lambda: self.nc.vector.reciprocal(_v(rp).ap, _v(rp).ap), [rp], [rp])
        for c in range(8):
            for o in (hn if isinstance(hn, (list, tuple)) else [hn]):
                K.stt(_v(o)[:, c, :], _v(h)[:, c, :], gain[:, c:c + 1], rp, ALU.mult, ALU.mult,
                      E=(K.dve if c % 2 == 0 else K.pool))

    def mlp_phase(self, l, src, dst):
        K = self.K
        P = 128
        gain = self.vec[:, 8 * (self.L + l):8 * (self.L + l + 1)]
        W1 = self.w1.a[l]
        W2 = self.w2.a[l]
        tiles = []
        for s_ in range(NSEQ):
            if self.seq > P:
                tiles.append((s_ * self.seq + P, min(T, self.seq) - P))
            for j in range(1, self.seq // T):
                tiles.append((s_ * self.seq + j * T, T))
        esW = contextlib.ExitStack()
        if tiles:
            w1 = K.sb([128, 8, 4 * D], BF16, "w1", esW)
            w2 = K.sb([128, 32, D], BF16, "w2", esW)
            for k in range(8):
                K.dma(K.pool, w1[:, k, :], W1[k * 128:(k + 1) * 128, :])
            for g in range(4):
                K.dma(K.pool, w2[:, g * 8:(g + 1) * 8, :],
                      W2[g * 1024:(g + 1) * 1024, :].re("(m p) n -> p m n", p=128))
        es = contextlib.ExitStack()
        NB = NSEQ * P
        h = K.sb([128, 8, NB], F32, "h", es)
        hn32 = K.sb([128, 8, NB], F32, "hn32", es)
        sq = K.sb([128, 8, NB], BF16, "sq", es)
        rp = K.sb([128, NB], F32, "rp", es)
        act32 = K.sb([128, 32, NB], F32, "act32", es)
        tmp32 = K.sb([128, 2, NB], F32, "tmp32", es)
        stg = [K.sb([128, 2048], F32, "stg", es) for _ in range(2)]
        ns = 0
        for s_ in range(NSEQ):
            K.dma(K.sp, h[:, :, s_ * P:(s_ + 1) * P], self.hview(src, s_ * self.seq, P))
        self.rmsnorm_fm(h, gain, hn32, sq, rp, n=NB)
        for mb in range(16):
            st = stg[ns % 2].a.re("p (k n) -> p k n", k=8)
            ns += 1
            srcw = W1[:, mb * 256:(mb + 1) * 256].re("(k p) n -> p k n", p=128)
            K.dma(K.sp, st, srcw)
            b = self.bank()
            for j in range(2):
                for k in range(8):
                    K.mm(b[:, j * NB:(j + 1) * NB], st[:, k, j * 128:(j + 1) * 128], hn32[:, k, :],
                         start=(k == 0), stop=(k == 7))
            K.actf(tmp32, b.a.re("p (j t) -> p j t", j=2), AF.Relu)
            K.tt(act32[:, mb * 2:mb * 2 + 2, :], tmp32, tmp32, ALU.mult, E=K.pool)
        for n in range(8):
            b = self.bank()
            for hf in range(2):
                st = stg[ns % 2].a.re("p (m n) -> p m n", m=16)
                ns += 1
                srcw = W2[hf * 2048:(hf + 1) * 2048, n * 128:(n + 1) * 128].re("(m p) n -> p m n", p=128)
                K.dma(K.sp, st, srcw)
                for m in range(16):
                    mm_ = hf * 16 + m
                    K.mm(b[:, 0:NB], st[:, m, :], act32[:, mm_, :], start=(mm_ == 0), stop=(mm_ == 31))
            K.tt(h[:, n, :], h[:, n, :], b[:, 0:NB], ALU.add)
        for s_ in range(NSEQ):
            K.dma(K.act, self.hview(dst, s_ * self.seq, P), h[:, :, s_ * P:(s_ + 1) * P])
        K.barrier()
        es.close()
        if not tiles:
            esW.close()
            return
        es = esW
        hts = [K.sb([128, 8, T], F32, "h", es) for _ in range(2)]
        hns = [K.sb([128, 8, T], BF16, "hn", es) for _ in range(2)]
        sqs = [K.sb([128, 8, T], BF16, "sq", es) for _ in range(2)]
        rps = [K.sb([128, T], F32, "rp", es) for _ in range(2)]
        act = K.sb([128, 32, T], BF16, "act", es)
        tmp = [K.sb([128, 2, T], BF16, "tmp", es) for _ in range(2)]
        nb_ = {}

        def norm_a(i):
            n = tiles[i][1]
            K.actf(sqs[i % 2][:, :, 0:n], hts[i % 2][:, :, 0:n], AF.Square)

        def norm_b(i):
            n = tiles[i][1]
            b = self.bank()
            nb_[i] = b
            for c in range(8):
                K.mm(b[:, 0:n], self.CB["ones"], sqs[i % 2][:, c, 0:n], start=(c == 0), stop=(c == 7))

        def norm_c(i):
            n = tiles[i][1]
            rp = rps[i % 2][:, 0:n]
            K.actf(rp, nb_[i][:, 0:n], AF.Sqrt, bias=self.epst, scale=1.0 / D)
            K.op(K.dve, lambda: self.nc.vector.reciprocal(rp.ap, rp.ap), [rp], [rp])
            for c in range(8):
                K.stt(hns[i % 2][:, c, 0:n], hts[i % 2][:, c, 0:n], gain[:, c:c + 1], rp, ALU.mult, ALU.mult)

        K.dma(K.sp, hts[0][:, :, 0:tiles[0][1]], self.hview(src, tiles[0][0], tiles[0][1]))
        norm_a(0)
        norm_b(0)
        norm_c(0)
        for i, (t0, n) in enumerate(tiles):
            h = hts[i % 2]
            hn = hns[i % 2]
            nxt = i + 1 < len(tiles)
            if nxt:
                t1, n1 = tiles[i + 1]
                K.dma(K.sp, hts[(i + 1) % 2][:, :, 0:n1], self.hview(src, t1, n1))
            for mp in range(16):
                b = self.bank()
                for j in range(2):
                    m = mp * 2 + j
                    for k in range(8):
                        K.mm(b[:, j * T:j * T + n], w1[:, k, m * 128:(m + 1) * 128], hn[:, k, 0:n],
                             start=(k == 0), stop=(k == 7))
                tm = tmp[mp % 2]
                K.actf(tm[:, :, 0:n], b.a.re("p (j t) -> p j t", j=2)[:, :, 0:n], AF.Relu)
                K.tt(act[:, mp * 2:mp * 2 + 2, 0:n], tm[:, :, 0:n], tm[:, :, 0:n], ALU.mult, E=K.pool)
            if nxt:
                norm_a(i + 1)
            for npair in range(4):
                b = self.bank()
                for j in range(2):
                    nn = npair * 2 + j
                    for m in range(32):
                        K.mm(b[:, j * T:j * T + n], w2[:, m, nn * 128:(nn + 1) * 128], act[:, m, 0:n],
                             start=(m == 0), stop=(m == 31))
                K.tt(h[:, npair * 2:npair * 2 + 2, 0:n], h[:, npair * 2:npair * 2 + 2, 0:n],
                     b.a.re("p (j t) -> p j t", j=2)[:, :, 0:n], ALU.add)
                if nxt and npair == 1:
                    norm_b(i + 1)
                if nxt and npair == 2:
                    norm_c(i + 1)
            K.dma(K.act, self.hview(dst, t0, n), h[:, :, 0:n])
        K.barrier()
        es.close()

    def final_phase(self, src):
        K = self.K
        es = contextlib.ExitStack()
        hts = [K.sb([128, 8, T], F32, "h", es) for _ in range(2)]
        hn = K.sb([128, 8, T], F32, "hn", es)
        sq = K.sb([128, 8, T], BF16, "sq", es)
        rp = K.sb([128, T], F32, "rp", es)
        ot = [K.sb([128, 2, D], F32, "ot", es) for _ in range(2)]
        gain = self.vec[:, 8 * (2 * self.L):8 * (2 * self.L + 1)]
        ntile = self.NT // T
        K.dma(K.sp, hts[0], self.hview(src, 0, T))
        for i in range(ntile):
            h = hts[i % 2]
            if i + 1 < ntile:
                K.dma(K.sp, hts[(i + 1) % 2], self.hview(src, (i + 1) * T, T))
            self.rmsnorm_fm(h, gain, hn, sq, rp)
            o = ot[i % 2]
            for s in range(2):
                for half in range(2):
                    b = self.bank()
                    for j in range(4):
                        c = half * 4 + j
                        K.mm(b[:, j * 128:(j + 1) * 128], hn[:, c, s * 128:(s + 1) * 128], self.C["ident"])
                    K.cp(o[:, s, half * 512:(half + 1) * 512], b, E=(K.dve if half == 0 else K.act))
            K.dma(K.act, self.out.a[i * T:(i + 1) * T, :].re("(s p) f -> p s f", p=128), o)
        K.barrier()
        es.close()


def host_vecs(inp, L):
    cols = []
    for l in range(L):
        cols.append(np.asarray(inp["norm_mix_w"][l], np.float32).reshape(8, 128).T)
    for l in range(L):
        cols.append(np.asarray(inp["norm_mlp_w"][l], np.float32).reshape(8, 128).T)
    cols.append(np.asarray(inp["final_norm_w"], np.float32).reshape(8, 128).T)
    return np.ascontiguousarray(np.concatenate(cols, axis=1))


def make_in_maps(inp, M, x, ncores):
    L = M.L
    consts = host_consts()
    cst = np.ascontiguousarray(np.concatenate([consts[n] for n in CONST_NAMES], axis=1))
    vecs = host_vecs(inp, L)
    shared = {"consts": cst, "vecs": vecs,
              "mlp_w1": np.ascontiguousarray(inp["mlp_w1"][:L]), "mlp_w2": np.ascontiguousarray(inp["mlp_w2"][:L])}
    if M.mix:
        shared.update(host_mixer_inputs(inp, L))
    maps = []
    for c in range(ncores):
        d = dict(shared)
        d["x"] = np.ascontiguousarray(x[2 * c:2 * c + 2]).reshape(M.NT, D)
        maps.append(d)
    return maps


EV_CONV = 0
EV_ALOG = 48
EV_DTB = 52
EV_GDNW = 56
EV_GLAW = 568
EV_W2B = 1080
EV_N = 1336


def host_even_vec(inp, i):
    v = np.zeros((128, EV_N), np.float32)
    cw = np.asarray(inp["gdn_conv_w"][i], np.float32)
    v[:, EV_CONV:EV_CONV + 48] = cw.T.reshape(12, 128, 4).transpose(1, 0, 2).reshape(128, 48)
    v[:, EV_ALOG:EV_ALOG + 4] = np.asarray(inp["gdn_a_log"][i])[None, :]
    v[:, EV_DTB:EV_DTB + 4] = np.asarray(inp["gdn_dt_bias"][i])[None, :]
    v[:, EV_GDNW:EV_GDNW + 512] = np.tile(np.asarray(inp["gdn_norm_w"][i]), 4)[None, :]
    v[:, EV_GLAW:EV_GLAW + 512] = np.tile(np.asarray(inp["gla_norm_w"][i]), 4)[None, :]
    v[0:16, EV_W2B:EV_W2B + 256] = np.asarray(inp["gla_gate_w2"][i])
    v[16, EV_W2B:EV_W2B + 256] = np.asarray(inp["gla_gate_b"][i])
    return v


def _even_phase(self, l, src, dst):
    K = self.K
    nc = self.nc
    i_l = l // 2
    esP = contextlib.ExitStack()
    C, CB = self.C, self.CB
    P = 128
    tps = self.seq // P
    WIN = self.even_w_in.a[i_l]
    WOUT = self.even_w_out.a[i_l]

    ev = K.sb([128, EV_N], F32, "ev", esP)
    K.dma(K.sp, ev, self.evec.a[i_l])
    nexpa = K.sb([128, 4], F32, "nexpa", esP)
    K.actf(nexpa, ev[:, EV_ALOG:EV_ALOG + 4], AF.Exp)
    K.ts(nexpa, nexpa, -1.0, ALU.mult)
    convw = ev[:, EV_CONV:EV_CONV + 48]
    gain = self.vec[:, 8 * l:8 * (l + 1)]
    SAs = [K.sb([128, 4, 128], F32, "SA", esP) for _ in range(NSEQ)]
    SBs = [K.sb([128, 2, 256], F32, "SB", esP) for _ in range(NSEQ)]
    hist = [K.sb([128, 12, 3], F32, "hist", esP) for _ in range(NSEQ)]
    for s_ in range(NSEQ):
        K.memset(SAs[s_], 0.0)
        K.memset(SBs[s_], 0.0)
        K.memset(hist[s_], 0.0)

    def bc4(v):
        return v.re("p (o t) -> p o t", o=1).bc([128, 4, 128])

    def bcl(v):
        return _v(v).re("p (h o) -> p h o", o=1).bc([128, 4, 128])

    def run_variant(hp, tiles):
        if not tiles:
            return
        es = contextlib.ExitStack()
        AD = F32 if hp else BF16
        ID = C["ident"] if hp else CB["ident"]

        def t512(name, dt=F32):
            return K.sb([128, 4, 128], dt, name, es)

        if hp:
            stg = [K.sb([128, 8, 512], F32, "stg", es) for _ in range(2)]
            cnt = [0]

            def _stream(dram2d, col0, n):
                t = stg[cnt[0] % 2]
                cnt[0] += 1
                src_ = dram2d[:, col0:col0 + n].re("(k p) n -> p k n", p=128)
                K.dma(K.sp, t[:, :, 0:n], src_)
                return lambda k: t[:, k, 0:n]

            wblock = lambda col0, n: _stream(WIN, col0, n)
            wblock32 = wblock
            woblock = lambda n0: _stream(WOUT, n0 * 128, 128)
        else:
            w_in = K.sb([128, 8, EVEN_IN], BF16, "w_in", es)
            for k in range(8):
                K.dma(K.pool, w_in[:, k, :], WIN[k * 128:(k + 1) * 128, :])
            w_out = K.sb([128, 8, D], BF16, "w_out", es)
            K.dma(K.pool, w_out, WOUT.re("(c p) n -> p c n", p=128))
            wg = K.sb([128, 8, 24], F32, "wg", es)
            for k in range(8):
                K.dma(K.sp, wg[:, k, 0:8], WIN[k * 128:(k + 1) * 128, 2048:2056])
                K.dma(K.sp, wg[:, k, 8:24], WIN[k * 128:(k + 1) * 128, 3592:3608])
            wblock = lambda col0, n: (lambda k: w_in[:, k, col0:col0 + n])
            wblock32 = lambda col0, n: (lambda k: wg[:, k, 0:8]) if col0 == 2048 else (lambda k: wg[:, k, 8:24])
            woblock = lambda n0: (lambda c: w_out[:, c, n0 * 128:(n0 + 1) * 128])

        h = K.sb([128, 8, P], F32, "h", es)
        hn32 = K.sb([128, 8, P], F32, "hn32", es)
        hn = hn32 if hp else K.sb([128, 8, P], BF16, "hn", es)
        sq = K.sb([128, 8, P], BF16, "sq", es)
        rp = K.sb([128, P], F32, "rp", es)
        pre = K.sb([128, 12, P + 3], F32, "pre", es)
        cacc = K.sb([128, 12, P], F32, "cacc", es)
        qkv = K.sb([128, 12, P], F32, "qkv", es)
        sq2 = K.sb([128, 8, P], BF16, "sq2", es)
        rinv = K.sb([128, 8, P], F32, "rinv", es)
        oFM = K.sb([128, 8, P], AD, "oFM", es)
        g1 = K.sb([32, P], F32, "g1", es)
        K.memset(g1, 1.0)
        loga = K.sb([128, 256], F32, "loga", es)
        gkT = K.sb([128, 256], F32, "gkT", es)
        qkF = K.sb([128, 4, 128], F32, "qkF", es)
        gcf = K.sb([128, 2, 128], F32, "gcf", es)
        nmid = K.sb([128, 2], F32, "nmid", es)
        eq = K.sb([128, 2, 128], F32, "eq", es)
        ek = K.sb([128, 2, 128], F32, "ek", es)
        eq0 = K.sb([128, 2, 128], F32, "eq0", es)
        qgm = K.sb([128, 2, 128], F32, "qgm", es)
        kgm = K.sb([128, 2, 128], F32, "kgm", es)
        qg0 = K.sb([128, 2, 128], AD, "qg0", es)
        edlB = K.sb([128, 256], F32, "edlB", es)
        kdB = K.sb([128, 256], AD, "kdB", es)
        gv = K.sb([128, 512], AD, "gv", es)
        grs = K.sb([128, 512], F32, "grs", es)
        zs = K.sb([128, 512], F32, "zs", es)
        attm = t512("attm", AD)
        SBb = None if hp else K.sb([128, 2, 256], BF16, "SBb", es)
        junk = K.sb([128, 128], F32, "junk", es)
        ssB = K.sb([128, 4], F32, "ssB", es)
        Gt = K.sb([128, 512], F32, "Gt", es)
        obT = K.sb([128, 512], AD, "obT", es)
        gr8 = K.sb([128, 8], F32, "gr8", es)
        beta = K.sb([128, 4], F32, "beta", es)
        nbeta = K.sb([128, 4], F32, "nbeta", es)
        gA = K.sb([128, 4], F32, "gA", es)
        gcc = K.sb([128, 4], F32, "gcc", es)
        eg = K.sb([128, 4], F32, "eg", es)
        egl = K.sb([128, 4], F32, "egl", es)
        edl = K.sb([128, 4], F32, "edl", es)
        beg = K.sb([128, 4], F32, "beg", es)
        Gtri = t512("Gtri")
        Dm = t512("Dm")
        DTm = t512("DTm")
        A = [t512("A0"), t512("A1")]
        AT = [t512("AT0"), t512("AT1")]
        PT = t512("PT")
        aqkT = t512("aqkT")
        vb = t512("vb")
        kbg = t512("kbg")
        kdA = t512("kdA")
        wneg = t512("wneg")
        vn = t512("vn")
        qg = t512("qg")
        ssA = K.sb([128, 4], F32, "ssA", es)
        oaT = K.sb([128, 512], AD, "oaT", es)

        for (s_, it) in tiles:
            SA, SB_ = SAs[s_], SBs[s_]
            K.dma(K.sp, h, self.hview(src, it * P, P))
            K.cp(pre[:, :, 0:3], hist[s_], E=K.pool)
            if not hp:
                K.cp(SBb, SB_, E=K.pool)
            Sst = SB_ if hp else SBb
            self.rmsnorm_fm(h, gain, [hn32] if hp else [hn, hn32], sq, rp, n=P)
            for c4 in range(3):
                wb = wblock(c4 * 512, 512)
                b = self.bank()
                for j in range(4):
                    for k in range(8):
                        K.mm(b[:, j * P:(j + 1) * P], wb(k)[:, j * 128:(j + 1) * 128], hn[:, k, :],
                             start=(k == 0), stop=(k == 7))
                K.cp(pre[:, c4 * 4:(c4 + 1) * 4, 3:3 + P], b.a.re("p (j t) -> p j t", j=4), E=K.act)
            shared = {}
            wb = wblock32(2048, 8)
            bg = self.bank()
            for k in range(8):
                K.mm(bg[:, 0:8], hn32[:, k, :], wb(k), start=(k == 0), stop=(k == 7))
            K.cp(gr8, bg[:, 0:8])
            for c in range(12):
                K.actf(cacc[:, c, :], pre[:, c, 0:P], AF.Copy, scale=convw[:, c * 4:c * 4 + 1])
                for j in range(1, 4):
                    K.stt(cacc[:, c, :], pre[:, c, j:j + P], convw[:, c * 4 + j:c * 4 + j + 1], cacc[:, c, :],
                          ALU.mult, ALU.add)
            K.cp(hist[s_], pre[:, :, P:P + 3], E=K.pool)
            K.actf(qkv, cacc, AF.Silu)
            K.tt(sq2, qkv[:, 0:8, :], qkv[:, 0:8, :], ALU.mult, E=K.pool)
            for half in range(2):
                b = self.bank()
                K.mm(b, CB["ones"], sq2[:, half * 4:(half + 1) * 4, :])
                K.actf(rinv[:, half * 4:(half + 1) * 4, :], b.a.re("p (j t) -> p j t", j=4), AF.Ln, bias=self.epst)
            K.actf(rinv, rinv, AF.Exp, scale=-0.5)
            K.stt(qkv[:, 0:4, :], qkv[:, 0:4, :], float(128 ** -0.5), rinv[:, 0:4, :], ALU.mult, ALU.mult)
            K.tt(qkv[:, 4:8, :], qkv[:, 4:8, :], rinv[:, 4:8, :], ALU.mult, E=K.pool)

            def seg_tm1():
                wb = wblock(1536, 512)
                bz = self.bank()
                for k in range(8):
                    K.mm(bz, hn[:, k, :], wb(k), start=(k == 0), stop=(k == 7))
                K.actf(zs, bz, AF.Silu)
                wb = wblock(2312, 256)
                bgk = self.bank()
                for k in range(8):
                    K.mm(bgk[:, 0:256], hn[:, k, :], wb(k), start=(k == 0), stop=(k == 7))
                K.cp(gkT, bgk[:, 0:256], E=K.act)
            def seg_tm2():
                wb = wblock(2568, 512)
                bgv = self.bank()
                for k in range(8):
                    K.mm(bgv, hn[:, k, :], wb(k), start=(k == 0), stop=(k == 7))
                K.cp(gv, bgv, E=K.act)
                wb = wblock(3080, 512)
                bgr = self.bank()
                for k in range(8):
                    K.mm(bgr, hn[:, k, :], wb(k), start=(k == 0), stop=(k == 7))
                K.actf(grs, bgr, AF.Silu)
            def seg_tm3():
                wb = wblock32(3592, 16)
                bl = self.bank()
                for k in range(8):
                    K.mm(bl[0:16, 0:P], wb(k), hn32[:, k, :], start=(k == 0), stop=(k == 7))
                K.cp(g1[0:16, :], bl[0:16, 0:P])
                wb = wblock(2056, 512)
                bqk = self.bank()
                for j in range(4):
                    for k in range(8):
                        K.mm(bqk[:, j * P:(j + 1) * P], wb(k)[:, j * 128:(j + 1) * 128], hn[:, k, :],
                             start=(k == 0), stop=(k == 7))
                K.cp(qkF, bqk.a.re("p (j t) -> p j t", j=4))
            def seg_gla1():
                bx = self.bank()
                K.mm(bx[:, 0:256], g1, ev[0:32, EV_W2B:EV_W2B + 256])
                K.ts(loga, bx[:, 0:256], -20.0, ALU.max)
                K.actf(loga, loga, AF.Exp, scale=-1.0)
                K.actf(loga, loga, AF.Ln, bias=1.0)
                K.ts(loga, loga, -1.0 / 16.0, ALU.mult, -1.0, ALU.max)
                bgc = self.bank()
                for cc in range(2):
                    K.mm(bgc[:, cc * 128:(cc + 1) * 128], loga[:, cc * 128:(cc + 1) * 128], C["tri"])
                K.mm(bgc[:, 256:512], C["ustr"], loga)
                K.cp(gcf, bgc[:, 0:256].re("p (c t) -> p c t", c=2))
                K.actf(edlB, bgc[:, 256:512], AF.Exp)
                K.tt(kdB, gkT, edlB, ALU.mult)
                K.ts(nmid, gcf[:, :, 63], -1.0, ALU.mult)
                for cc in range(2):
                    K.actf(eq[:, cc, :], gcf[:, cc, :], AF.Exp, bias=nmid[:, cc:cc + 1])
                    K.actf(ek[:, cc, :], gcf[:, cc, :], AF.Exp, bias=gcf[:, cc, 63:64], scale=-1.0)
                K.actf(eq0, gcf, AF.Exp)
                K.stt(qgm, qkF[:, 0:2, :], 0.125, eq, ALU.mult, ALU.mult)
                K.tt(kgm, qkF[:, 2:4, :], ek, ALU.mult)
                K.stt(qg0, qkF[:, 0:2, :], 0.125, eq0, ALU.mult, ALU.mult)
            def seg_gla2():
                bat = self.bank()
                for hh in range(4):
                    pr = slice((hh % 2) * 64, (hh % 2) * 64 + 64)
                    K.mm(bat[:, hh * 128:(hh + 1) * 128], kgm[pr, hh // 2, :], qgm[pr, hh // 2, :], ser=True)
                K.tt(attm, bat.a.re("p (h t) -> p h t", h=4), bc4(C["tri"]), ALU.mult)
                bo = self.bank()
                shared['bo'] = bo
                for hh in range(4):
                    pr = slice((hh % 2) * 64, (hh % 2) * 64 + 64)
                    K.mm(bo[:, hh * 128:(hh + 1) * 128], qg0[pr, hh // 2, :],
                         Sst[pr, hh // 2, (hh % 2) * 128:(hh % 2) * 128 + 128], start=True, stop=False, ser=True)
                    K.mm(bo[:, hh * 128:(hh + 1) * 128], attm[:, hh, :], gv[:, hh * 128:(hh + 1) * 128],
                         start=False, stop=True, ser=True)
                bs = self.bank()
                for pp in range(2):
                    K.mm(bs[:, pp * 256:(pp + 1) * 256], kdB[:, pp * 128:(pp + 1) * 128], gv[:, pp * 256:(pp + 1) * 256])
                for pp in range(2):
                    K.stt(SB_[:, pp, :], SB_[:, pp, :], eq0[:, pp, 127:128], bs[:, pp * 256:(pp + 1) * 256],
                          ALU.mult, ALU.add)
            def seg_gla3():
                bo = shared['bo']
                for hh in range(4):
                    K.actf(junk, bo[:, hh * 128:(hh + 1) * 128], AF.Square, accum=ssB[:, hh:hh + 1])
                K.actf(ssB, ssB, AF.Sqrt, bias=self.epst, scale=1.0 / 128.0)
                K.op(K.dve, lambda: nc.vector.reciprocal(ssB.a.ap, ssB.a.ap), [ssB], [ssB])
                K.tt(Gt, grs, ev[:, EV_GLAW:EV_GLAW + 512], ALU.mult, E=K.pool)
                for hh in range(4):
                    sl = slice(hh * 128, (hh + 1) * 128)
                    K.stt(obT[:, sl], bo[:, sl], ssB[:, hh:hh + 1], Gt[:, sl], ALU.mult, ALU.mult)
                btr = self.bank()
                for hh in range(4):
                    K.mm(btr[:, hh * 128:(hh + 1) * 128], obT[:, hh * 128:(hh + 1) * 128], ID)
                K.cp(oFM[:, 4:8, :], btr.a.re("p (h t) -> p h t", h=4), E=K.act)
            segs = [seg_tm1, seg_tm2, seg_tm3, seg_gla1, seg_gla2, seg_gla3]
            K.actf(beta, gr8[:, 0:4], AF.Sigmoid)
            K.ts(nbeta, beta, -1.0, ALU.mult)
            K.tt(gA, gr8[:, 4:8], ev[:, EV_DTB:EV_DTB + 4], ALU.add)
            K.actf(gA, gA, AF.Exp)
            K.actf(gA, gA, AF.Ln, bias=1.0)
            K.tt(gA, gA, nexpa, ALU.mult)
            K.tt(Gtri, bc4(C["tri"]), bcl(gA), ALU.mult)
            bsm = self.bank()
            K.mm(bsm[:, 0:4], C["tri"], gA)
            K.mm(bsm[:, 8:12], C["ones"], gA)
            br = self.bank()
            K.mm(br, C["ones"], Gtri)
            K.cp(gcc, bsm[:, 0:4])
            K.actf(eg, gcc, AF.Exp)
            K.actf(egl, bsm[:, 8:12], AF.Exp)
            K.tt(edl, bsm[:, 8:12], gcc, ALU.subtract)
            K.actf(edl, edl, AF.Exp)
            K.tt(beg, beta, eg, ALU.mult)
            for hh in range(4):
                sl = slice(hh * 128, (hh + 1) * 128)
                K.ts(Dm[:, hh, :], br[:, sl], gcc[:, hh:hh + 1], ALU.subtract, 0.0, ALU.max)
                K.ts(DTm[:, hh, :], br[:, sl], gcc[:, hh:hh + 1], ALU.subtract, 0.0, ALU.min)
            K.actf(Dm, Dm, AF.Exp, scale=-1.0)
            K.actf(DTm, DTm, AF.Exp)
            K.actf(qg, br.a.re("p (h t) -> p h t", h=4), AF.Exp)
            K.tt(Dm, Dm, bc4(C["ustr"]), ALU.mult, E=K.pool)
            K.tt(DTm, DTm, bc4(C["tri"]), ALU.mult, E=K.pool)
            K.tt(qg, qg, qkv[:, 0:4, :], ALU.mult, E=K.pool)
            bkk = self.bank()
            bqt = self.bank()
            for hh in range(4):
                sl = slice(hh * 128, (hh + 1) * 128)
                K.mm(bkk[:, sl], qkv[:, 4 + hh, :], qkv[:, 4 + hh, :])
                K.mm(bqt[:, sl], qkv[:, 4 + hh, :], qkv[:, hh, :])
            for hh in range(4):
                K.stt(A[0][:, hh, :], bkk[:, hh * 128:(hh + 1) * 128], nbeta[:, hh:hh + 1], Dm[:, hh, :],
                      ALU.mult, ALU.mult)
            K.tt(aqkT, bqt.a.re("p (h t) -> p h t", h=4), DTm, ALU.mult)
            bnt = self.bank()
            for hh in range(4):
                K.mm(bnt[:, hh * 128:(hh + 1) * 128], A[0][:, hh, :], C["ident"])
            K.cp(AT[0], bnt.a.re("p (h t) -> p h t", h=4), E=K.act)
            K.tt(PT, bnt.a.re("p (h t) -> p h t", h=4), bc4(C["ident"]), ALU.add)
            bkt = self.bank()
            bvt = self.bank()
            for hh in range(4):
                sl = slice(hh * 128, (hh + 1) * 128)
                K.mm(bkt[:, sl], qkv[:, 4 + hh, :], C["ident"])
                K.mm(bvt[:, sl], qkv[:, 8 + hh, :], C["ident"])
            K.tt(vb, bvt.a.re("p (h t) -> p h t", h=4), bcl(beta), ALU.mult)
            K.tt(kbg, bkt.a.re("p (h t) -> p h t", h=4), bcl(beg), ALU.mult)
            K.tt(kdA, bkt.a.re("p (h t) -> p h t", h=4), bcl(edl), ALU.mult)
            cur = 0
            for itr in range(6):
                nxt = 1 - cur
                b1 = self.bank()
                b2 = self.bank()
                for hh in range(4):
                    sl = slice(hh * 128, (hh + 1) * 128)
                    K.mm(b1[:, sl], AT[cur][:, hh, :], A[cur][:, hh, :])
                    if itr < 5:
                        K.mm(b2[:, sl], A[cur][:, hh, :], AT[cur][:, hh, :])
                K.cp(A[nxt], b1.a.re("p (h t) -> p h t", h=4), E=K.act)
                if itr < 5:
                    K.cp(AT[nxt], b2.a.re("p (h t) -> p h t", h=4), E=K.dve)
                b3 = self.bank()
                for hh in range(4):
                    K.mm(b3[:, hh * 128:(hh + 1) * 128], A[nxt][:, hh, :], PT[:, hh, :])
                K.tt(PT, PT, b3.a.re("p (h t) -> p h t", h=4), ALU.add)
                cur = nxt
                segs[itr]()
            bw = self.bank()
            for hh in range(4):
                K.mm(bw[:, hh * 128:(hh + 1) * 128], kbg[:, hh, :], PT[:, hh, :])
            K.actf(wneg, bw.a.re("p (h t) -> p h t", h=4), AF.Copy, scale=-1.0)
            bvn = self.bank()
            for hh in range(4):
                sl = slice(hh * 128, (hh + 1) * 128)
                K.mm(bvn[:, sl], PT[:, hh, :], vb[:, hh, :], start=True, stop=False)
                K.mm(bvn[:, sl], wneg[:, hh, :], SA[:, hh, :], start=False, stop=True)
            K.cp(vn, bvn.a.re("p (h t) -> p h t", h=4))
            boa = self.bank()
            for hh in range(4):
                sl = slice(hh * 128, (hh + 1) * 128)
                K.mm(boa[:, sl], qg[:, hh, :], SA[:, hh, :], start=True, stop=False)
                K.mm(boa[:, sl], aqkT[:, hh, :], vn[:, hh, :], start=False, stop=True)
            bsa = self.bank()
            for hh in range(4):
                K.mm(bsa[:, hh * 128:(hh + 1) * 128], kdA[:, hh, :], vn[:, hh, :])
            for hh in range(4):
                K.stt(SA[:, hh, :], SA[:, hh, :], egl[:, hh:hh + 1], bsa[:, hh * 128:(hh + 1) * 128],
                      ALU.mult, ALU.add)
            for hh in range(4):
                K.actf(junk, boa[:, hh * 128:(hh + 1) * 128], AF.Square, accum=ssA[:, hh:hh + 1])
            K.actf(ssA, ssA, AF.Sqrt, bias=self.epst, scale=1.0 / 128.0)
            K.op(K.dve, lambda: nc.vector.reciprocal(ssA.a.ap, ssA.a.ap), [ssA], [ssA])
            K.tt(Gt, zs, ev[:, EV_GDNW:EV_GDNW + 512], ALU.mult, E=K.pool)
            for hh in range(4):
                sl = slice(hh * 128, (hh + 1) * 128)
                K.stt(oaT[:, sl], boa[:, sl], ssA[:, hh:hh + 1], Gt[:, sl], ALU.mult, ALU.mult)
            btr = self.bank()
            for hh in range(4):
                K.mm(btr[:, hh * 128:(hh + 1) * 128], oaT[:, hh * 128:(hh + 1) * 128], ID)
            K.cp(oFM[:, 0:4, :], btr.a.re("p (h t) -> p h t", h=4), E=K.act)

            for half in range(2):
                b = self.bank()
                for j in range(4):
                    wo = woblock(half * 4 + j)
                    for c in range(8):
                        K.mm(b[:, j * P:(j + 1) * P], wo(c), oFM[:, c, :], start=(c == 0), stop=(c == 7))
                K.tt(h[:, half * 4:(half + 1) * 4, :], h[:, half * 4:(half + 1) * 4, :],
                     b.a.re("p (j t) -> p j t", j=4), ALU.add)
            K.dma(K.act, self.hview(dst, it * P, P), h)
        K.barrier()
        es.close()

    run_variant(True, [(s_, s_ * tps) for s_ in range(NSEQ)])
    run_variant(False, [(s_, s_ * tps + j) for s_ in range(NSEQ) for j in range(1, tps)])
    K.barrier()
    esP.close()


Model.even_phase = _even_phase


def _declare_mixer_inputs(self, ne, no):
    K = self.K
    self.even_w_in = K.dram("even_w_in", [ne, D, EVEN_IN], F32, kind="ExternalInput")
    self.even_w_out = K.dram("even_w_out", [ne, D, D], F32, kind="ExternalInput")
    self.evec = K.dram("evec", [ne, 128, EV_N], F32, kind="ExternalInput")
    if no > 0:
        self.declare_odd_inputs(no)


Model.declare_mixer_inputs = _declare_mixer_inputs


def host_mixer_inputs(inp, L):
    ne = (L + 1) // 2
    no = L // 2
    d = {"even_w_in": np.ascontiguousarray(inp["even_w_in"][:ne]),
         "even_w_out": np.ascontiguousarray(inp["even_w_out"][:ne]),
         "evec": np.stack([host_even_vec(inp, i) for i in range(ne)])}
    if no > 0:
        d.update(host_odd_inputs(inp, no))
    return d


OV_CONV = 0
OV_CONVB = 32
OV_DTB = 40
OV_ALOG = 48
OV_D = 56
OV_SNW = 568
OV_MU = 1080
OV_W0 = 1094
OV_A0 = 1606
OV_KK = 1610
OV_KA = 1614
OV_RK = 1618
OV_GNW = 1622
OV_GNB = 2134
OV_SEL = 2646
OV_N = 2678


def host_odd_vec(inp, i):
    v = np.zeros((128, OV_N), np.float32)
    f = lambda a, n: np.asarray(a, np.float32).reshape(n, 128).T
    cw = np.asarray(inp["ssd_conv_w"][i], np.float32)
    v[:, OV_CONV:OV_CONV + 32] = cw.T.reshape(8, 128, 4).transpose(1, 0, 2).reshape(128, 32)
    v[:, OV_CONVB:OV_CONVB + 8] = f(inp["ssd_conv_b"][i], 8)
    v[:, OV_DTB:OV_DTB + 8] = np.asarray(inp["ssd_dt_bias"][i])[None, :]
    v[:, OV_ALOG:OV_ALOG + 8] = np.asarray(inp["ssd_a_log"][i])[None, :]
    v[:, OV_D:OV_D + 512] = np.repeat(np.asarray(inp["ssd_d"][i]), 64)[None, :]
    v[:, OV_SNW:OV_SNW + 512] = np.asarray(inp["ssd_norm_w"][i])[None, :]
    v[:, OV_MU:OV_MU + 14] = f(inp["rwkv_mu"][i], 14)
    v[:, OV_W0:OV_W0 + 512] = np.asarray(inp["rwkv_w0"][i])[None, :]
    v[:, OV_A0:OV_A0 + 4] = f(inp["rwkv_a0"][i], 4)
    v[:, OV_KK:OV_KK + 4] = f(inp["rwkv_k_k"][i], 4)
    v[:, OV_KA:OV_KA + 4] = f(inp["rwkv_k_a"][i], 4)
    v[:, OV_RK:OV_RK + 4] = f(np.asarray(inp["rwkv_r_k"][i]).reshape(-1), 4)
    v[:, OV_GNW:OV_GNW + 512] = np.asarray(inp["rwkv_gn_w"][i])[None, :]
    v[:, OV_GNB:OV_GNB + 512] = np.asarray(inp["rwkv_gn_b"][i])[None, :]
    p = np.arange(128)
    sel = np.zeros((128, 4, 8), np.float32)
    for c in range(4):
        sel[p, c, 2 * c + p // 64] = 1.0
    v[:, OV_SEL:OV_SEL + 32] = sel.reshape(128, 32)
    return v


def host_odd_inputs(inp, no):
    a2 = np.zeros((no, 128, 512), np.float32)
    a2[:, 64:128, :] = np.asarray(inp["rwkv_a2"][:no])
    w2 = np.zeros((no, 128, 512), np.float32)
    w2[:, 0:64, :] = np.asarray(inp["rwkv_w2"][:no])
    return {"odd_w_in": np.ascontiguousarray(inp["odd_w_in"][:no]),
            "odd_w_out": np.ascontiguousarray(inp["odd_w_out"][:no]),
            "ovec": np.stack([host_odd_vec(inp, i) for i in range(no)]),
            "rw2": w2, "ra2": a2, "rg2": np.ascontiguousarray(inp["rwkv_g2"][:no])}


def _declare_odd_inputs(self, no):
    K = self.K
    self.odd_w_in = K.dram("odd_w_in", [no, D, ODD_IN], F32, kind="ExternalInput")
    self.odd_w_out = K.dram("odd_w_out", [no, D, D], F32, kind="ExternalInput")
    self.ovec = K.dram("ovec", [no, 128, OV_N], F32, kind="ExternalInput")
    self.rw2 = K.dram("rw2", [no, 128, 512], F32, kind="ExternalInput")
    self.ra2 = K.dram("ra2", [no, 128, 512], F32, kind="ExternalInput")
    self.rg2 = K.dram("rg2", [no, 128, 512], F32, kind="ExternalInput")


Model.declare_odd_inputs = _declare_odd_inputs


def _odd_phase(self, l, src, dst):
    K = self.K
    nc = self.nc
    i_l = l // 2
    esP = contextlib.ExitStack()
    C, CB = self.C, self.CB
    P = 128
    tps = self.seq // P
    WIN = self.odd_w_in.a[i_l]
    WOUT = self.odd_w_out.a[i_l]

    def bc4(v, n=4):
        return v.re("p (o t) -> p o t", o=1).bc([128, n, 128])

    def bcl(v, n, m):
        return _v(v).re("p (h o) -> p h o", o=1).bc([128, n, m])

    def recip(t):
        K.op(K.dve, lambda: nc.vector.reciprocal(_v(t).ap, _v(t).ap), [t], [t])

    ov = K.sb([128, OV_N], F32, "ov", esP)
    K.dma(K.sp, ov, self.ovec.a[i_l])
    w2t = K.sb([128, 512], F32, "w2t", esP)
    a2t = K.sb([128, 512], F32, "a2t", esP)
    g2t = K.sb([128, 512], F32, "g2t", esP)
    K.dma(K.sp, w2t, self.rw2.a[i_l])
    K.dma(K.sp, a2t, self.ra2.a[i_l])
    K.dma(K.sp, g2t, self.rg2.a[i_l])
    nexpa = K.sb([128, 8], F32, "nexpa", esP)
    K.actf(nexpa, ov[:, OV_ALOG:OV_ALOG + 8], AF.Exp)
    K.ts(nexpa, nexpa, -1.0, ALU.mult)
    gneps = K.sb([128, 1], F32, "gneps", esP)
    K.memset(gneps, 64e-5)
    convw = ov[:, OV_CONV:OV_CONV + 32]
    gain = self.vec[:, 8 * l:8 * (l + 1)]
    STs = [K.sb([128, 512], F32, "ST", esP) for _ in range(NSEQ)]
    Hss = [K.sb([128, 4, 128], F32, "Hs", esP) for _ in range(NSEQ)]
    histC = [K.sb([128, 8, 3], F32, "histC", esP) for _ in range(NSEQ)]
    histP = [K.sb([128, 14, 1], F32, "histP", esP) for _ in range(NSEQ)]
    for s_ in range(NSEQ):
        K.memset(STs[s_], 0.0)
        K.memset(Hss[s_], 0.0)
        K.memset(histC[s_], 0.0)
        K.memset(histP[s_], 0.0)

    def run_variant(hp, tiles):
        if not tiles:
            return
        es = contextlib.ExitStack()
        AD = F32 if hp else BF16
        ID = C["ident"] if hp else CB["ident"]
        if hp:
            stg = [K.sb([128, 8, 512], F32, "stg", es) for _ in range(2)]
            cnt = [0]

            def _stream(dram2d, col0, n):
                t = stg[cnt[0] % 2]
                cnt[0] += 1
                src_ = dram2d[:, col0:col0 + n].re("(k p) n -> p k n", p=128)
                K.dma(K.sp, t[:, :, 0:n], src_)
                return lambda k: t[:, k, 0:n]

            wblock = lambda col0, n: _stream(WIN, col0, n)
            wblock32 = wblock
            woblock = lambda n0: _stream(WOUT, n0 * 128, 128)
        else:
            w_in = K.sb([128, 8, ODD_IN], BF16, "w_in", es)
            for k in range(8):
                K.dma(K.pool, w_in[:, k, :], WIN[k * 128:(k + 1) * 128, :])
            w_out = K.sb([128, 8, D], BF16, "w_out", es)
            K.dma(K.pool, w_out, WOUT.re("(c p) n -> p c n", p=128))
            wg = K.sb([128, 8, 136], F32, "wg", es)
            for k in range(8):
                K.dma(K.sp, wg[:, k, 0:8], WIN[k * 128:(k + 1) * 128, 1536:1544])
                K.dma(K.sp, wg[:, k, 8:136], WIN[k * 128:(k + 1) * 128, 3080:3208])
            wblock = lambda col0, n: (lambda k: w_in[:, k, col0:col0 + n])
            wblock32 = lambda col0, n: (lambda k: wg[:, k, 0:8])
            woblock = lambda n0: (lambda c: w_out[:, c, n0 * 128:(n0 + 1) * 128])

        def sb(shape, name, dt=F32):
            return K.sb(shape, dt, name, es)

        h = sb([128, 8, P], "h"); hn32 = sb([128, 8, P], "hn32"); hn = hn32 if hp else sb([128, 8, P], "hn", BF16)
        sq = sb([128, 8, P], "sq", BF16); rp = sb([128, P], "rp")
        preC = sb([128, 8, P + 3], "preC"); cacc = sb([128, 8, P], "cacc"); xbc = sb([128, 8, P], "xbc")
        pdp = sb([128, 14, P + 1], "pdp"); pdm = sb([128, 14, P], "pdm"); pdd = pdm
        zs = sb([128, 512], "zs"); oFM = sb([128, 8, P], "oFM", AD)
        dt8 = sb([128, 8], "dt8"); a8 = sb([128, 8], "a8"); acc8 = sb([128, 8], "acc8")
        eac8 = sb([128, 8], "eac8"); edl8 = sb([128, 8], "edl8"); cd8 = sb([128, 8], "cd8"); dte8 = sb([128, 8], "dte8")
        segT = sb([128, 8, 128], "segT"); MT = sb([128, 8, 128], "MT", AD)
        xTs = sb([128, 512], "xTs"); xdt = sb([128, 8, 64], "xdt", AD); xdte = sb([128, 8, 64], "xdte", AD)
        Bb = sb([128, 256], "Bb", AD); Cb = sb([128, 2, 128], "Cb", AD)
        STb = None if hp else sb([128, 512], "STb", BF16)
        y1 = sb([128, 512], "y1"); y2 = sb([128, 512], "y2"); ss2 = sb([128, 2], "ss2"); junk = sb([128, 256], "junk")
        ycT = sb([128, 512], "ycT", AD)
        txw = sb([64, P], "txw"); ldT = sb([128, 512], "ldT"); aF = sb([128, 4, P], "aF"); sg = sb([128, P], "sg")
        kp = sb([128, 4, P], "kp"); kq = sb([128, 4, P], "kq"); kk = kq; bm = sb([128, 4, P], "bm"); gl4 = sb([128, 4], "gl4")
        sqk = sb([128, 4, P], "sqk"); lg = sb([128, 4, P], "lg"); lgx = sb([128, 4, P], "lgx"); nmid = sb([128, 4], "nmid")
        e_r = sb([128, 4, P], "e_r"); e_a = sb([128, 4, P], "e_a"); e_b = sb([128, 4, P], "e_b")
        e_r0 = sb([128, 4, P], "e_r0"); e_a0 = sb([128, 4, P], "e_a0")
        rt_ = e_r; r0_ = e_r0; at_ = e_a; a0_ = e_a0; bt_ = e_b; kt_ = xbc[:, 4:8, :]
        edlT = sb([128, 512], "edlT"); kdT = sb([128, 512], "kdT"); bdT = sb([128, 512], "bdT"); vT = sb([128, 512], "vT")
        A = [segT[:, 0:4, :], cacc[:, 0:4, :]]
        AT = [segT[:, 4:8, :], cacc[:, 4:8, :]]
        PT = xbc[:, 0:4, :]
        AakT = y1.a.re("p (h t) -> p h t", h=4); RbT = y2.a.re("p (h t) -> p h t", h=4); RkT = xTs.a.re("p (h t) -> p h t", h=4)
        Xs = sb([128, 4, 64], "Xs"); Us = sb([128, 8, 64], "Us")
        yd = lg.a.re("p c t -> p (c t)").re("p (h q) -> p h q", h=8); ym = lgx.a.re("p c t -> p (c t)").re("p (h q) -> p h q", h=8); mean8 = sb([128, 8], "mean8"); var8 = sb([128, 8], "var8")
        rkr = sb([128, 4, P], "rkr"); rks = sb([128, 8], "rks"); gT = sb([128, 512], "gT"); ydT = sb([128, 512], "ydT", AD)

        for (s_, it) in tiles:
            ST, Hs = STs[s_], Hss[s_]
            K.dma(K.sp, h, self.hview(src, it * P, P))
            K.cp(preC[:, :, 0:3], histC[s_], E=K.pool)
            K.cp(pdp[:, :, 0:1], histP[s_], E=K.pool)
            if not hp:
                K.cp(STb, ST, E=K.pool)
            Sst = ST if hp else STb
            self.rmsnorm_fm(h, gain, [hn32] if hp else [hn, hn32], sq, rp, n=P)
            for c4 in range(2):
                wb = wblock(512 + c4 * 512, 512)
                b = self.bank()
                for j in range(4):
                    for k in range(8):
                        K.mm(b[:, j * P:(j + 1) * P], wb(k)[:, j * 128:(j + 1) * 128], hn[:, k, :],
                             start=(k == 0), stop=(k == 7))
                K.cp(preC[:, c4 * 4:(c4 + 1) * 4, 3:3 + P], b.a.re("p (j t) -> p j t", j=4), E=K.act)
            for c4 in range(4):
                b = self.bank()
                nj = 4 if c4 < 3 else 2
                wb = wblock(1544 + c4 * 512, nj * 128)
                for j in range(nj):
                    c = c4 * 4 + j
                    for k in range(8):
                        if c == 12 and not hp:
                            K.mm(b[:, j * P:(j + 1) * P], wg[:, k, 8:136], hn32[:, k, :], start=(k == 0), stop=(k == 7))
                        else:
                            K.mm(b[:, j * P:(j + 1) * P], wb(k)[:, j * 128:(j + 1) * 128], hn[:, k, :],
                                 start=(k == 0), stop=(k == 7))
                K.cp(pdp[:, c4 * 4:c4 * 4 + nj, 1:1 + P], b[:, 0:nj * P].re("p (j t) -> p j t", j=nj), E=K.act)
            wb = wblock(0, 512)
            bz = self.bank()
            for k in range(8):
                K.mm(bz, hn[:, k, :], wb(k), start=(k == 0), stop=(k == 7))
            K.actf(zs, bz, AF.Silu)
            wb = wblock32(1536, 8)
            bg = self.bank()
            for k in range(8):
                K.mm(bg[:, 0:8], hn32[:, k, :], wb(k), start=(k == 0), stop=(k == 7))
            K.tt(dt8, bg[:, 0:8], ov[:, OV_DTB:OV_DTB + 8], ALU.add)
            K.ts(dt8, dt8, 60.0, ALU.min)
            K.actf(dt8, dt8, AF.Exp)
            K.actf(dt8, dt8, AF.Ln, bias=1.0)
            K.tt(a8, dt8, nexpa, ALU.mult)
            for c in range(8):
                K.actf(cacc[:, c, :], preC[:, c, 0:P], AF.Copy, scale=convw[:, c * 4:c * 4 + 1])
                for j in range(1, 4):
                    K.stt(cacc[:, c, :], preC[:, c, j:j + P], convw[:, c * 4 + j:c * 4 + j + 1], cacc[:, c, :],
                          ALU.mult, ALU.add)
                K.actf(xbc[:, c, :], cacc[:, c, :], AF.Silu, bias=ov[:, OV_CONVB + c:OV_CONVB + c + 1])
            K.cp(histC[s_], preC[:, :, P:P + 3], E=K.pool)
            Atri = cacc
            K.tt(Atri, bc4(C["tri"], 8), bcl(a8, 8, 128), ALU.mult)
            bsm = self.bank()
            K.mm(bsm[:, 0:8], C["tri"], a8)
            K.mm(bsm[:, 8:16], C["ones"], a8)
            K.cp(acc8, bsm[:, 0:8])
            K.actf(eac8, acc8, AF.Exp)
            K.actf(cd8, bsm[:, 8:16], AF.Exp)
            K.tt(edl8, bsm[:, 8:16], acc8, ALU.subtract)
            K.actf(edl8, edl8, AF.Exp)
            K.tt(dte8, dt8, edl8, ALU.mult)
            for hg in range(2):
                br = self.bank()
                K.mm(br, C["ones"], Atri[:, hg * 4:(hg + 1) * 4, :])
                for hh in range(4):
                    hd = hg * 4 + hh
                    K.ts(segT[:, hd, :], br[:, hh * 128:(hh + 1) * 128], acc8[:, hd:hd + 1], ALU.subtract, 0.0, ALU.min)
            K.actf(segT, segT, AF.Exp)
            K.tt(segT, segT, bc4(C["tri"], 8), ALU.mult, E=K.pool)
            bcb = self.bank()
            for g in range(2):
                K.mm(bcb[:, g * 128:(g + 1) * 128], xbc[:, 4 + g, :], xbc[:, 6 + g, :])
            for g in range(2):
                K.tt(MT[:, g * 4:(g + 1) * 4, :], segT[:, g * 4:(g + 1) * 4, :],
                     bcb[:, g * 128:(g + 1) * 128].re("p (o t) -> p o t", o=1).bc([128, 4, 128]), ALU.mult)
            K.cp(Cb, xbc[:, 6:8, :], E=K.pool)
            bxt = self.bank()
            for c in range(4):
                K.mm(bxt[:, c * 128:(c + 1) * 128], xbc[:, c, :], C["ident"])
            bbt = self.bank()
            for g in range(2):
                K.mm(bbt[:, g * 128:(g + 1) * 128], xbc[:, 4 + g, :], C["ident"])
            K.cp(xTs, bxt, E=K.act)
            K.cp(Bb, bbt[:, 0:256], E=K.act)
            K.tt(xdt, xTs.a.re("p (h q) -> p h q", h=8), bcl(dt8, 8, 64), ALU.mult)
            K.tt(xdte, xTs.a.re("p (h q) -> p h q", h=8), bcl(dte8, 8, 64), ALU.mult)
            by = self.bank()
            for hd in range(8):
                K.mm(by[:, hd * 64:(hd + 1) * 64], MT[:, hd, :], xdt[:, hd, :])
            bi = self.bank()
            for g in range(2):
                K.mm(bi[:, g * 256:(g + 1) * 256], Cb[:, g, :], Sst[:, g * 256:(g + 1) * 256])
            bst = self.bank()
            for g in range(2):
                K.mm(bst[:, g * 256:(g + 1) * 256], Bb[:, g * 128:(g + 1) * 128],
                     xdte[:, g * 4:(g + 1) * 4, :].re("p h q -> p (h q)"))
            K.tt(y1.a.re("p (h q) -> p h q", h=8), bi.a.re("p (h q) -> p h q", h=8), bcl(eac8, 8, 64), ALU.mult)
            K.tt(y1, y1, by, ALU.add)
            K.tt(y2, xTs, ov[:, OV_D:OV_D + 512], ALU.mult, E=K.pool)
            K.tt(y1, y1, y2, ALU.add)
            K.tt(y1, y1, zs, ALU.mult)
            K.tt(ST.a.re("p (h q) -> p h q", h=8), ST.a.re("p (h q) -> p h q", h=8), bcl(cd8, 8, 64), ALU.mult)
            K.tt(ST, ST, bst, ALU.add)
            for g in range(2):
                K.actf(junk, y1[:, g * 256:(g + 1) * 256], AF.Square, accum=ss2[:, g:g + 1])
            K.actf(ss2, ss2, AF.Sqrt, bias=self.epst, scale=1.0 / 256.0)
            recip(ss2)
            K.tt(y2, y1, ov[:, OV_SNW:OV_SNW + 512], ALU.mult, E=K.pool)
            for g in range(2):
                K.ts(ycT[:, g * 256:(g + 1) * 256], y2[:, g * 256:(g + 1) * 256], ss2[:, g:g + 1], ALU.mult)
            btr = self.bank()
            for hh in range(4):
                K.mm(btr[:, hh * 128:(hh + 1) * 128], ycT[:, hh * 128:(hh + 1) * 128], ID)
            K.cp(oFM[:, 0:4, :], btr.a.re("p (h t) -> p h t", h=4), E=K.act)

            K.tt(pdd, pdp[:, :, 0:P], pdp[:, :, 1:P + 1], ALU.subtract)
            K.tt(pdd, pdd, bcl(ov[:, OV_MU:OV_MU + 14], 14, P), ALU.mult, E=K.pool)
            K.tt(pdm, pdd, pdp[:, :, 1:P + 1], ALU.add)
            K.cp(histP[s_], pdp[:, :, P:P + 1], E=K.pool)
            R_, K_, V_ = pdm[:, 0:4, :], pdm[:, 4:8, :], pdm[:, 8:12, :]
            K.actf(txw, pdm[0:64, 12, :], AF.Tanh)
            K.actf(sg, pdm[:, 13, :], AF.Sigmoid)
            bw = self.bank()
            K.mm(bw, txw, w2t[0:64, :])
            K.tt(ldT, bw, ov[:, OV_W0:OV_W0 + 512], ALU.add)
            K.actf(ldT, ldT, AF.Sigmoid)
            K.ts(ldT, ldT, -float(np.exp(-0.5)), ALU.mult)
            ba = self.bank()
            for c in range(4):
                K.mm(ba[:, c * P:(c + 1) * P], a2t[64:128, c * 128:(c + 1) * 128], pdm[64:128, 12, :])
            for c in range(4):
                K.actf(aF[:, c, :], ba[:, c * P:(c + 1) * P], AF.Sigmoid, bias=ov[:, OV_A0 + c:OV_A0 + c + 1])
            bgm = self.bank()
            K.mm(bgm, sg, g2t)
            K.cp(gT, bgm, E=K.act)
            for c in range(4):
                K.ts(kp[:, c, :], aF[:, c, :], 1.0, ALU.subtract, ov[:, OV_KA + c:OV_KA + c + 1], ALU.mult)
                K.stt(kp[:, c, :], kp[:, c, :], 1.0, pdm[:, 4 + c, :], ALU.add, ALU.mult)
                K.ts(kq[:, c, :], pdm[:, 4 + c, :], ov[:, OV_KK + c:OV_KK + c + 1], ALU.mult, E=K.pool)
                K.stt(rkr[:, c, :], pdm[:, c, :], ov[:, OV_RK + c:OV_RK + c + 1], kp[:, c, :], ALU.mult, ALU.mult)
            K.tt(sqk, kq, kq, ALU.mult, E=K.pool)
            bss = self.bank()
            K.mm(bss, C["bones"], sqk)
            K.actf(sqk, bss.a.re("p (c t) -> p c t", c=4), AF.Ln, bias=self.epst)
            K.actf(sqk, sqk, AF.Exp, scale=-0.5)
            K.tt(kk, kq, sqk, ALU.mult)
            K.tt(bm, kk, aF, ALU.mult, E=K.pool)
            blg = self.bank()
            blx = self.bank()
            for c in range(4):
                K.mm(blg[:, c * P:(c + 1) * P], ldT[:, c * 128:(c + 1) * 128], C["tri"])
                K.mm(blx[:, c * P:(c + 1) * P], ldT[:, c * 128:(c + 1) * 128], C["tris"])
            K.cp(lg, blg.a.re("p (c t) -> p c t", c=4))
            K.cp(lgx, blx.a.re("p (c t) -> p c t", c=4), E=K.act)
            K.ts(nmid, lg[:, :, 63], -1.0, ALU.mult)
            for c in range(4):
                K.actf(e_r[:, c, :], lg[:, c, :], AF.Exp, bias=nmid[:, c:c + 1])
                K.actf(e_a[:, c, :], lgx[:, c, :], AF.Exp, bias=nmid[:, c:c + 1])
                K.actf(e_b[:, c, :], lg[:, c, :], AF.Exp, bias=lg[:, c, 63:64], scale=-1.0)
            K.actf(e_r0, lg, AF.Exp)
            K.actf(e_a0, lgx, AF.Exp)
            K.cp(gl4, e_r0[:, :, 127])
            K.tt(rt_, R_, e_r, ALU.mult)
            K.tt(r0_, R_, e_r0, ALU.mult, E=K.pool)
            K.stt(at_, kk, -1.0, e_a, ALU.mult, ALU.mult)
            K.stt(a0_, kk, -1.0, e_a0, ALU.mult, ALU.mult)
            K.tt(kt_, kp, e_b, ALU.mult)
            K.tt(bt_, bm, e_b, ALU.mult, E=K.pool)
            bed = self.bank()
            K.mm(bed, C["ustr"], ldT)
            K.actf(edlT, bed, AF.Exp)
            for (srcT, dstT, scl) in ((kp, kdT, True), (bm, bdT, True), (V_, vT, False)):
                b = self.bank()
                for c in range(4):
                    K.mm(b[:, c * 128:(c + 1) * 128], srcT[:, c, :], C["ident"])
                if scl:
                    K.tt(dstT, b, edlT, ALU.mult)
                else:
                    K.cp(dstT, b, E=K.act)
            brk = self.bank()
            for c in range(4):
                K.mm(brk[:, 0:8], rkr[:, c, :], ov[:, OV_SEL + c * 8:OV_SEL + (c + 1) * 8], start=(c == 0), stop=(c == 3))
            K.cp(rks, brk[:, 0:8])
            byd = self.ps[7]
            for hg in range(2):
                b1, b2, b3, b4, b5 = [self.bank() for _ in range(5)]
                for hh in range(4):
                    hd = hg * 4 + hh
                    hq, pr = hd // 2, slice((hd % 2) * 64, (hd % 2) * 64 + 64)
                    sl = slice(hh * 128, (hh + 1) * 128)
                    K.mm(b1[:, sl], bt_[pr, hq, :], at_[pr, hq, :])
                    K.mm(b2[:, sl], at_[pr, hq, :], bt_[pr, hq, :])
                    K.mm(b3[:, sl], kt_[pr, hq, :], at_[pr, hq, :])
                    K.mm(b4[:, sl], bt_[pr, hq, :], rt_[pr, hq, :])
                    K.mm(b5[:, sl], kt_[pr, hq, :], rt_[pr, hq, :])
                K.tt(AT[0], b1.a.re("p (h t) -> p h t", h=4), bc4(C["tris"]), ALU.mult)
                K.tt(A[0], b2.a.re("p (h t) -> p h t", h=4), bc4(C["ustr"]), ALU.mult)
                K.tt(AakT, b3.a.re("p (h t) -> p h t", h=4), bc4(C["tris"]), ALU.mult)
                K.tt(RbT, b4.a.re("p (h t) -> p h t", h=4), bc4(C["tri"]), ALU.mult)
                K.tt(RkT, b5.a.re("p (h t) -> p h t", h=4), bc4(C["tri"]), ALU.mult)
                K.tt(PT, AT[0], bc4(C["ident"]), ALU.add, E=K.pool)
                cur = 0
                for itr in range(6):
                    nxt = 1 - cur
                    c1 = self.bank()
                    c2 = self.bank()
                    for hh in range(4):
                        sl = slice(hh * 128, (hh + 1) * 128)
                        K.mm(c1[:, sl], AT[cur][:, hh, :], A[cur][:, hh, :])
                        if itr < 5:
                            K.mm(c2[:, sl], A[cur][:, hh, :], AT[cur][:, hh, :])
                    K.cp(A[nxt], c1.a.re("p (h t) -> p h t", h=4), E=K.act)
                    if itr < 5:
                        K.cp(AT[nxt], c2.a.re("p (h t) -> p h t", h=4), E=K.dve)
                    c3 = self.bank()
                    for hh in range(4):
                        K.mm(c3[:, hh * 128:(hh + 1) * 128], A[nxt][:, hh, :], PT[:, hh, :])
                    K.tt(PT, PT, c3.a.re("p (h t) -> p h t", h=4), ALU.add)
                    cur = nxt
                bx = self.bank()
                for hh in range(4):
                    hd = hg * 4 + hh
                    hq, pr = hd // 2, slice((hd % 2) * 64, (hd % 2) * 64 + 64)
                    vs = slice((hd % 2) * 64, (hd % 2) * 64 + 64)
                    K.mm(bx[:, hh * 64:(hh + 1) * 64], a0_[pr, hq, :], Hs[pr, hq, vs], start=True, stop=False)
                    K.mm(bx[:, hh * 64:(hh + 1) * 64], AakT[:, hh, :], vT[:, hd * 64:(hd + 1) * 64], start=False, stop=True)
                K.cp(Xs, bx[:, 0:256].re("p (h q) -> p h q", h=4))
                bu = self.bank()
                for hh in range(4):
                    K.mm(bu[:, hh * 64:(hh + 1) * 64], PT[:, hh, :], Xs[:, hh, :])
                K.cp(Us[:, hg * 4:(hg + 1) * 4, :], bu[:, 0:256].re("p (h q) -> p h q", h=4))
                for hh in range(4):
                    hd = hg * 4 + hh
                    hq, pr = hd // 2, slice((hd % 2) * 64, (hd % 2) * 64 + 64)
                    vs = slice((hd % 2) * 64, (hd % 2) * 64 + 64)
                    o = byd[:, hd * 64:(hd + 1) * 64]
                    K.mm(o, r0_[pr, hq, :], Hs[pr, hq, vs], start=True, stop=False)
                    K.mm(o, RbT[:, hh, :], Us[:, hd, :], start=False, stop=False)
                    K.mm(o, RkT[:, hh, :], vT[:, hd * 64:(hd + 1) * 64], start=False, stop=True)
            bh = self.bank()
            for hq in range(4):
                o = bh[:, hq * 128:(hq + 1) * 128]
                K.mm(o, bdT[:, hq * 128:(hq + 1) * 128], Us[:, 2 * hq:2 * hq + 2, :].re("p h q -> p (h q)"), start=True, stop=False)
                K.mm(o, kdT[:, hq * 128:(hq + 1) * 128], vT[:, hq * 128:(hq + 1) * 128], start=False, stop=True)
            for hq in range(4):
                K.stt(Hs[:, hq, :], Hs[:, hq, :], gl4[:, hq:hq + 1], bh[:, hq * 128:(hq + 1) * 128], ALU.mult, ALU.add)
            K.cp(yd, byd.a.re("p (h q) -> p h q", h=8), E=K.act)
            K.red(mean8, yd)
            K.ts(mean8, mean8, 1.0 / 64.0, ALU.mult)
            K.tt(ym, yd, bcl(mean8, 8, 64), ALU.subtract)
            K.tt(yd, ym, ym, ALU.mult, E=K.pool)
            K.red(var8, yd)
            K.actf(var8, var8, AF.Sqrt, bias=gneps, scale=1.0 / 64.0)
            recip(var8)
            K.tt(ym, ym, bcl(var8, 8, 64), ALU.mult)
            ymf = _v(ym).re("p h q -> p (h q)")
            K.tt(ymf, ymf, ov[:, OV_GNW:OV_GNW + 512], ALU.mult, E=K.pool)
            K.tt(ymf, ymf, ov[:, OV_GNB:OV_GNB + 512], ALU.add)
            K.tt(yd, vT.a.re("p (h q) -> p h q", h=8), bcl(rks, 8, 64), ALU.mult)
            K.tt(ym, ym, yd, ALU.add)
            K.tt(ydT, ymf, gT, ALU.mult)
            btr = self.bank()
            for hh in range(4):
                K.mm(btr[:, hh * 128:(hh + 1) * 128], ydT[:, hh * 128:(hh + 1) * 128], ID)
            K.cp(oFM[:, 4:8, :], btr.a.re("p (h t) -> p h t", h=4), E=K.act)
            for half in range(2):
                b = self.bank()
                for j in range(4):
                    n = half * 4 + j
                    wo = woblock(n)
                    for c in range(8):
                        K.mm(b[:, j * P:(j + 1) * P], wo(c), oFM[:, c, :], start=(c == 0), stop=(c == 7))
                K.tt(h[:, half * 4:(half + 1) * 4, :], h[:, half * 4:(half + 1) * 4, :],
                     b.a.re("p (j t) -> p j t", j=4), ALU.add)
            K.dma(K.act, self.hview(dst, it * P, P), h)
        K.barrier()
        es.close()

    run_variant(True, [(s_, s_ * tps) for s_ in range(NSEQ)])
    run_variant(False, [(s_, s_ * tps + j) for s_ in range(NSEQ) for j in range(1, tps)])
    K.barrier()
    esP.close()


Model.odd_phase = _odd_phase


def kernel(**inputs):
    inp = {k: np.asarray(v) for k, v in inputs.items()}
    x = inp["x"]
    M = Model(n_layers=4, seq=x.shape[1], mix=True)
    nc = M.build()
    maps = make_in_maps(inp, M, x, 8)
    res = run_bass_kernel_spmd(nc, maps, core_ids=list(range(8)))
    out = np.concatenate([r["out"].reshape(NSEQ, x.shape[1], D) for r in res.results], axis=0)
    return out.astype(np.float32)
```

```python
import contextlib
import os
import numpy as np
import ml_dtypes
import concourse.bass as bass
import concourse.mybir as mybir
from concourse.bass_utils import run_bass_kernel_spmd

F32 = mybir.dt.float32
BF16 = mybir.dt.bfloat16
AF = mybir.ActivationFunctionType
ALU = mybir.AluOpType
AX = mybir.AxisListType

EPOCH = 30000


class Tl:
    def __init__(self, h, name):
        self.h = h
        self.name = name
        self.w = None
        self.r = {}
        self.dsem = None
        self.dcnt = 0
        self.psum = False

    def __getitem__(self, idx):
        return V(self, self.h[idx])

    @property
    def a(self):
        return V(self, self.h[:])


class V:
    def __init__(self, t, ap):
        self.t = t
        self.ap = ap

    def __getitem__(self, idx):
        return V(self.t, self.ap[idx])

    def bitcast(self, dt):
        return V(self.t, self.ap.bitcast(dt))

    def bc(self, shape):
        return V(self.t, self.ap.to_broadcast(list(shape)))

    def re(self, s, **kw):
        return V(self.t, self.ap.rearrange(s, **kw))


def _v(x):
    return x.a if isinstance(x, Tl) else x


class Eng:
    def __init__(self, K, name, eng):
        self.K = K
        self.name = name
        self.eng = eng
        self.epoch = -1
        self.cnt = EPOCH
        self.sem = None
        self.known = {}

    def tick(self, ins):
        if self.cnt >= EPOCH:
            self.epoch += 1
            self.cnt = 0
            self.sem = self.K.new_sem(f"s_{self.name}_{self.epoch}")
        self.cnt += 1
        ins.then_inc(self.sem, 1)
        return (id(self.sem), self.sem, self.cnt, self.name)


class KB:
    def __init__(self, nc):
        self.nc = nc
        self.es = contextlib.ExitStack()
        self.pe = Eng(self, "pe", nc.tensor)
        self.dve = Eng(self, "dve", nc.vector)
        self.act = Eng(self, "act", nc.scalar)
        self.pool = Eng(self, "pool", nc.gpsimd)
        self.sp = Eng(self, "sp", nc.sync)
        self.engs = [self.pe, self.dve, self.act, self.pool, self.sp]
        self.nsem = 0
        self.ntile = 0
        self.all_dma = {}
        self.ninst = 0

    def new_sem(self, name):
        self.nsem += 1
        return self.es.enter_context(self.nc.semaphore(f"{name}_{self.nsem}"))

    def sb(self, shape, dt=F32, name="t", es=None):
        self.ntile += 1
        nm = f"{name}_{self.ntile}"
        h = (es or self.es).enter_context(self.nc.sbuf_tensor(nm, list(shape), dt))
        return Tl(h, nm)

    def psum(self, shape, dt=F32, name="p", es=None):
        self.ntile += 1
        nm = f"{name}_{self.ntile}"
        h = (es or self.es).enter_context(self.nc.psum_tensor(nm, list(shape), dt))
        t = Tl(h, nm)
        t.psum = True
        return t

    def dram(self, name, shape, dt=F32, kind="Internal"):
        h = self.nc.dram_tensor(name, list(shape), dt, kind=kind)
        return Tl(h.ap(), name)

    def _wait(self, E, toks):
        best = {}
        for tk in toks:
            if tk is None:
                continue
            sid, sem, val, who = tk
            if who == E.name and E is self.pe:
                continue
            if E.known.get(sid, 0) >= val:
                continue
            if best.get(sid, (None, 0))[1] < val:
                best[sid] = (sem, val)
        for sid, (sem, val) in best.items():
            E.eng.wait_ge(sem, val)
            E.known[sid] = val
            self.ninst += 1

    skip = False

    def op(self, E, fn, reads, writes):
        if self.skip:
            return None
        toks = []
        rt = [(_v(x).t) for x in reads if x is not None and not isinstance(x, (int, float))]
        wt = [(_v(x).t) for x in writes if x is not None]
        for t in rt:
            toks.append(t.w)
            if t.psum:
                for k, tk in t.r.items():
                    if k != E.name:
                        toks.append(tk)
        for t in wt:
            toks.append(t.w)
            for k, tk in t.r.items():
                if k == E.name:
                    continue
                toks.append(tk)
        self._wait(E, toks)
        ins = fn()
        tok = E.tick(ins)
        self.ninst += 1
        for t in rt:
            t.r[E.name] = tok
        for t in wt:
            t.w = tok
            t.r = {}
        return tok

    def dma(self, Q, out, in_):
        out = _v(out)
        in_ = _v(in_)
        ot, it = out.t, in_.t
        toks = [it.w, ot.w] + list(ot.r.values())
        self._wait(Q, toks)
        if ot.dsem is None:
            ot.dsem = self.new_sem("d_" + ot.name)
        ins = Q.eng.dma_start(out=out.ap, in_=in_.ap)
        ins.then_inc(ot.dsem, 16)
        ot.dcnt += 16
        tok = (id(ot.dsem), ot.dsem, ot.dcnt, "dma")
        self.ninst += 1
        ot.w = tok
        ot.r = {}
        it.r["dma%d" % id(ot.dsem)] = tok
        self.all_dma[id(ot.dsem)] = tok
        return tok

    def barrier(self):
        toks = list(self.all_dma.values())
        for E in self.engs:
            if E.sem is not None and E.cnt > 0:
                toks.append((id(E.sem), E.sem, E.cnt, E.name + "_b"))
        for E in self.engs:
            self._wait(E, toks)

    def mm(self, out, lhsT, rhs, start=True, stop=True, ser=False):
        out, lhsT, rhs = _v(out), _v(lhsT), _v(rhs)
        if ser and not self.skip and self.pe.sem is not None and self.pe.cnt > 0:
            self.pe.eng.wait_ge(self.pe.sem, self.pe.cnt)
        return self.op(self.pe, lambda: self.nc.tensor.matmul(out.ap, lhsT.ap, rhs.ap, start=start, stop=stop),
                       [lhsT, rhs], [out])

    def actf(self, out, in_, func, bias=None, scale=None, accum=None, E=None):
        out, in_ = _v(out), _v(in_)
        kw = {}
        rd = [in_]
        if bias is not None:
            if isinstance(bias, (int, float)):
                kw["bias"] = float(bias)
            else:
                bias = _v(bias)
                kw["bias"] = bias.ap
                rd.append(bias)
        if scale is not None:
            if isinstance(scale, (int, float)):
                kw["scale"] = float(scale)
            else:
                scale = _v(scale)
                kw["scale"] = scale.ap
                rd.append(scale)
        wr = [out]
        if accum is not None:
            accum = _v(accum)
            kw["accum_out"] = accum.ap
            wr.append(accum)
        return self.op(self.act, lambda: self.nc.scalar.activation(out.ap, in_.ap, func, **kw), rd, wr)

    def _ve(self, E):
        return E or self.dve

    def _sc(self, s, rd):
        if isinstance(s, (int, float)):
            return float(s)
        s = _v(s)
        rd.append(s)
        return s.ap

    def ts(self, out, in0, s1, op0, s2=None, op1=None, E=None, accum=None):
        E = self._ve(E)
        out, in0 = _v(out), _v(in0)
        rd = [in0]
        a1 = self._sc(s1, rd)
        a2 = None if s2 is None else self._sc(s2, rd)
        wr = [out]
        kw = {}
        if accum is not None:
            accum = _v(accum)
            kw["accum_out"] = accum.ap
            wr.append(accum)
        if op1 is None:
            return self.op(E, lambda: E.eng.tensor_scalar(out.ap, in0.ap, a1, None, op0, **kw), rd, wr)
        return self.op(E, lambda: E.eng.tensor_scalar(out.ap, in0.ap, a1, a2, op0, op1, **kw), rd, wr)

    def tt(self, out, in0, in1, op, E=None):
        E = self._ve(E)
        out, in0, in1 = _v(out), _v(in0), _v(in1)
        return self.op(E, lambda: E.eng.tensor_tensor(out.ap, in0.ap, in1.ap, op), [in0, in1], [out])

    def stt(self, out, in0, s, in1, op0, op1, E=None):
        E = self.dve
        out, in0, in1 = _v(out), _v(in0), _v(in1)
        rd = [in0, in1]
        a = self._sc(s, rd)
        return self.op(E, lambda: E.eng.scalar_tensor_tensor(out.ap, in0.ap, a, in1.ap, op0, op1), rd, [out])

    def cp(self, out, in_, E=None):
        E = self._ve(E)
        out, in_ = _v(out), _v(in_)
        if E is self.act:
            return self.op(E, lambda: self.nc.scalar.copy(out.ap, in_.ap), [in_], [out])
        return self.op(E, lambda: E.eng.tensor_copy(out.ap, in_.ap), [in_], [out])

    def memset(self, out, val, E=None):
        E = self._ve(E)
        out = _v(out)
        return self.op(E, lambda: E.eng.memset(out.ap, val), [], [out])

    def red(self, out, in_, op=ALU.add, E=None):
        E = self._ve(E)
        out, in_ = _v(out), _v(in_)
        return self.op(E, lambda: E.eng.tensor_reduce(out.ap, in_.ap, AX.X, op), [in_], [out])


D = 1024
NSEQ = 2
T = 256
CH = 128
EVEN_IN = 3608
ODD_IN = 3336
NEPS = 1e-6


def host_consts():
    i = np.arange(128)
    c = {}
    c["ident"] = np.eye(128, dtype=np.float32)
    c["tri"] = (i[:, None] <= i[None, :]).astype(np.float32)
    c["tris"] = (i[:, None] < i[None, :]).astype(np.float32)
    c["ustr"] = (i[:, None] > i[None, :]).astype(np.float32)
    c["ones"] = np.ones((128, 128), np.float32)
    c["bones"] = ((i[:, None] // 64) == (i[None, :] // 64)).astype(np.float32)
    return c


CONST_NAMES = ["ident", "tri", "tris", "ustr", "ones", "bones"]


class Model:
    def __init__(self, n_layers=4, seq=2048, mix=True, dbg=False):
        self.L = n_layers
        self.seq = seq
        self.NT = NSEQ * seq
        self.mix = mix
        self.dbg = dbg

    def build(self):
        nc = bass.Bass("TRN2", target_bir_lowering=False)
        self.nc = nc
        K = KB(nc)
        self.K = K
        L, NT = self.L, self.NT
        ne = (L + 1) // 2
        no = L // 2
        self.x = K.dram("x", [NT, D], F32, kind="ExternalInput")
        self.out = K.dram("out", [NT, D], F32, kind="ExternalOutput")
        self.hA = K.dram("hA", [D, NT], F32)
        self.hB = K.dram("hB", [D, NT], F32)
        self.cst_d = K.dram("consts", [128, len(CONST_NAMES) * 128], F32, kind="ExternalInput")
        self.w1 = K.dram("mlp_w1", [L, D, 4 * D], F32, kind="ExternalInput")
        self.w2 = K.dram("mlp_w2", [L, 4 * D, D], F32, kind="ExternalInput")
        self.nvec = 3 * L + 1
        self.vec_d = K.dram("vecs", [128, 8 * (2 * L + 1)], F32, kind="ExternalInput")
        if self.mix:
            self.declare_mixer_inputs(ne, no)
        self.cst = K.sb([128, len(CONST_NAMES) * 128], F32, "cst")
        K.dma(K.sp, self.cst, self.cst_d)
        self.C = {n: self.cst[:, i * 128:(i + 1) * 128] for i, n in enumerate(CONST_NAMES)}
        self.cstb = K.sb([128, len(CONST_NAMES) * 128], BF16, "cstb")
        K.cp(self.cstb, self.cst)
        self.CB = {n: self.cstb[:, i * 128:(i + 1) * 128] for i, n in enumerate(CONST_NAMES)}
        self.vec = K.sb([128, 8 * (2 * L + 1)], F32, "vec")
        K.dma(K.sp, self.vec, self.vec_d)
        self.epst = K.sb([128, 1], F32, "eps")
        K.memset(self.epst, NEPS)
        self.ps = [K.psum([128, 512], F32, "bank") for _ in range(8)]
        self.psi = 0

        self.phase0()
        src, dst = self.hA, self.hB
        for l in range(L):
            if self.mix:
                K.barrier()
                if l % 2 == 0:
                    self.even_phase(l, src, dst)
                elif os.environ.get("NOODD"):
                    src, dst = dst, src
                else:
                    self.odd_phase(l, src, dst)
                src, dst = dst, src
            K.barrier()
            self.mlp_phase(l, src, dst)
            src, dst = dst, src
        K.barrier()
        self.final_phase(src)
        K.barrier()
        return nc

    def bank(self):
        b = self.ps[self.psi % 7]
        self.psi += 1
        return b

    def hview(self, hd, t0, n):
        return hd.a.re("(c p) t -> p c t", p=128)[:, :, t0:t0 + n]

    def phase0(self):
        K = self.K
        es = contextlib.ExitStack()
        xt = [K.sb([128, 2, D], F32, "xt", es) for _ in range(2)]
        ht = [K.sb([128, 8, T], F32, "ht", es) for _ in range(2)]
        for i in range(self.NT // T):
            xs, hs = xt[i % 2], ht[i % 2]
            K.dma(K.sp, xs, self.x.a[i * T:(i + 1) * T, :].re("(s p) f -> p s f", p=128))
            for s in range(2):
                for half in range(2):
                    b = self.bank()
                    for j in range(4):
                        c = half * 4 + j
                        K.mm(b[:, j * 128:(j + 1) * 128], xs[:, s, c * 128:(c + 1) * 128], self.C["ident"])
                    dst = hs[:, half * 4:(half + 1) * 4, s * 128:(s + 1) * 128]
                    K.cp(dst, b.a.re("p (j t) -> p j t", j=4), E=(K.dve if half == 0 else K.act))
            K.dma(K.act, self.hview(self.hA, i * T, T), hs)
        K.barrier()
        es.close()

    def rmsnorm_fm(self, h, gain, hn, sq, rp, n=T):
        K = self.K
        K.actf(sq, h, AF.Square)
        b = self.bank()
        for c in range(8):
            K.mm(b[:, 0:n], self.CB["ones"], _v(sq)[:, c, :], start=(c == 0), stop=(c == 7))
        K.actf(rp, b[:, 0:n], AF.Ln, bias=self.epst, scale=1.0 / D)
        K.actf(rp, rp, AF.Exp, scale=-0.5)
        for c in range(8):
            for o in (hn if isinstance(hn, (list, tuple)) else [hn]):
                K.stt(_v(o)[:, c, :], _v(h)[:, c, :], gain[:, c:c + 1], rp, ALU.mult, ALU.mult,
                      E=(K.dve if c % 2 == 0 else K.pool))

    def mlp_phase(self, l, src, dst):
        K = self.K
        P = 128
        gain = self.vec[:, 8 * (self.L + l):8 * (self.L + l + 1)]
        W1 = self.w1.a[l]
        W2 = self.w2.a[l]
        tiles = []
        for s_ in range(NSEQ):
            if self.seq > P:
                tiles.append((s_ * self.seq + P, min(T, self.seq) - P))
            for j in range(1, self.seq // T):
                tiles.append((s_ * self.seq + j * T, T))
        esW = contextlib.ExitStack()
        if tiles:
            w1 = K.sb([128, 8, 4 * D], BF16, "w1", esW)
            w2 = K.sb([128, 32, D], BF16, "w2", esW)
            for k in range(8):
                K.dma(K.pool, w1[:, k, :], W1[k * 128:(k + 1) * 128, :])
            for g in range(4):
                K.dma(K.pool, w2[:, g * 8:(g + 1) * 8, :],
                      W2[g * 1024:(g + 1) * 1024, :].re("(m p) n -> p m n", p=128))
        es = contextlib.ExitStack()
        NB = NSEQ * P
        h = K.sb([128, 8, NB], F32, "h", es)
        hn32 = K.sb([128, 8, NB], F32, "hn32", es)
        sq = K.sb([128, 8, NB], BF16, "sq", es)
        rp = K.sb([128, NB], F32, "rp", es)
        act32 = K.sb([128, 32, NB], F32, "act32", es)
        tmp32 = K.sb([128, 2, NB], F32, "tmp32", es)
        stg = [K.sb([128, 2048], F32, "stg", es) for _ in range(2)]
        ns = 0
        for s_ in range(NSEQ):
            K.dma(K.sp, h[:, :, s_ * P:(s_ + 1) * P], self.hview(src, s_ * self.seq, P))
        self.rmsnorm_fm(h, gain, hn32, sq, rp, n=NB)
        for mb in range(16):
            st = stg[ns % 2].a.re("p (k n) -> p k n", k=8)
            ns += 1
            srcw = W1[:, mb * 256:(mb + 1) * 256].re("(k p) n -> p k n", p=128)
            K.dma(K.sp, st, srcw)
            b = self.bank()
            for j in range(2):
                for k in range(8):
                    K.mm(b[:, j * NB:(j + 1) * NB], st[:, k, j * 128:(j + 1) * 128], hn32[:, k, :],
                         start=(k == 0), stop=(k == 7))
            K.actf(tmp32, b.a.re("p (j t) -> p j t", j=2), AF.Relu)
            K.tt(act32[:, mb * 2:mb * 2 + 2, :], tmp32, tmp32, ALU.mult, E=K.pool)
        for n in range(8):
            b = self.bank()
            for hf in range(2):
                st = stg[ns % 2].a.re("p (m n) -> p m n", m=16)
                ns += 1
                srcw = W2[hf * 2048:(hf + 1) * 2048, n * 128:(n + 1) * 128].re("(m p) n -> p m n", p=128)
                K.dma(K.sp, st, srcw)
                for m in range(16):
                    mm_ = hf * 16 + m
                    K.mm(b[:, 0:NB], st[:, m, :], act32[:, mm_, :], start=(mm_ == 0), stop=(mm_ == 31))
            K.tt(h[:, n, :], h[:, n, :], b[:, 0:NB], ALU.add)
        for s_ in range(NSEQ):
            K.dma(K.act, self.hview(dst, s_ * self.seq, P), h[:, :, s_ * P:(s_ + 1) * P])
        K.barrier()
        es.close()
        if not tiles:
            esW.close()
            return
        es = esW
        hts = [K.sb([128, 8, T], F32, "h", es) for _ in range(2)]
        hns = [K.sb([128, 8, T], BF16, "hn", es) for _ in range(2)]
        sqs = [K.sb([128, 8, T], BF16, "sq", es) for _ in range(2)]
        rps = [K.sb([128, T], F32, "rp", es) for _ in range(2)]
        act = K.sb([128, 32, T], BF16, "act", es)
        tmp = [K.sb([128, 2, T], BF16, "tmp", es) for _ in range(2)]
        nb_ = {}

        def norm_a(i):
            n = tiles[i][1]
            K.actf(sqs[i % 2][:, :, 0:n], hts[i % 2][:, :, 0:n], AF.Square)

        def norm_b(i):
            n = tiles[i][1]
            b = self.bank()
            nb_[i] = b
            for c in range(8):
                K.mm(b[:, 0:n], self.CB["ones"], sqs[i % 2][:, c, 0:n], start=(c == 0), stop=(c == 7))

        def norm_c(i):
            n = tiles[i][1]
            rp = rps[i % 2][:, 0:n]
            K.actf(rp, nb_[i][:, 0:n], AF.Ln, bias=self.epst, scale=1.0 / D)
            K.actf(rp, rp, AF.Exp, scale=-0.5)
            for c in range(8):
                K.stt(hns[i % 2][:, c, 0:n], hts[i % 2][:, c, 0:n], gain[:, c:c + 1], rp, ALU.mult, ALU.mult)

        K.dma(K.sp, hts[0][:, :, 0:tiles[0][1]], self.hview(src, tiles[0][0], tiles[0][1]))
        norm_a(0)
        norm_b(0)
        norm_c(0)
        for i, (t0, n) in enumerate(tiles):
            h = hts[i % 2]
            hn = hns[i % 2]
            nxt = i + 1 < len(tiles)
            if nxt:
                t1, n1 = tiles[i + 1]
                K.dma(K.sp, hts[(i + 1) % 2][:, :, 0:n1], self.hview(src, t1, n1))
            for mp in range(16):
                b = self.bank()
                for j in range(2):
                    m = mp * 2 + j
                    for k in range(8):
                        K.mm(b[:, j * T:j * T + n], w1[:, k, m * 128:(m + 1) * 128], hn[:, k, 0:n],
                             start=(k == 0), stop=(k == 7))
                tm = tmp[mp % 2]
                K.actf(tm[:, :, 0:n], b.a.re("p (j t) -> p j t", j=2)[:, :, 0:n], AF.Relu)
                K.tt(act[:, mp * 2:mp * 2 + 2, 0:n], tm[:, :, 0:n], tm[:, :, 0:n], ALU.mult, E=K.pool)
            if nxt:
                norm_a(i + 1)
            for npair in range(4):
                b = self.bank()
                for j in range(2):
                    nn = npair * 2 + j
                    for m in range(32):
                        K.mm(b[:, j * T:j * T + n], w2[:, m, nn * 128:(nn + 1) * 128], act[:, m, 0:n],
                             start=(m == 0), stop=(m == 31))
                K.tt(h[:, npair * 2:npair * 2 + 2, 0:n], h[:, npair * 2:npair * 2 + 2, 0:n],
                     b.a.re("p (j t) -> p j t", j=2)[:, :, 0:n], ALU.add)
                if nxt and npair == 1:
                    norm_b(i + 1)
                if nxt and npair == 2:
                    norm_c(i + 1)
            K.dma(K.act, self.hview(dst, t0, n), h[:, :, 0:n])
        K.barrier()
        es.close()

    def final_phase(self, src):
        K = self.K
        es = contextlib.ExitStack()
        hts = [K.sb([128, 8, T], F32, "h", es) for _ in range(2)]
        hn = K.sb([128, 8, T], F32, "hn", es)
        sq = K.sb([128, 8, T], BF16, "sq", es)
        rp = K.sb([128, T], F32, "rp", es)
        ot = [K.sb([128, 2, D], F32, "ot", es) for _ in range(2)]
        gain = self.vec[:, 8 * (2 * self.L):8 * (2 * self.L + 1)]
        ntile = self.NT // T
        K.dma(K.sp, hts[0], self.hview(src, 0, T))
        for i in range(ntile):
            h = hts[i % 2]
            if i + 1 < ntile:
                K.dma(K.sp, hts[(i + 1) % 2], self.hview(src, (i + 1) * T, T))
            self.rmsnorm_fm(h, gain, hn, sq, rp)
            o = ot[i % 2]
            for s in range(2):
                for half in range(2):
                    b = self.bank()
                    for j in range(4):
                        c = half * 4 + j
                        K.mm(b[:, j * 128:(j + 1) * 128], hn[:, c, s * 128:(s + 1) * 128], self.C["ident"])
                    K.cp(o[:, s, half * 512:(half + 1) * 512], b, E=(K.dve if half == 0 else K.act))
            K.dma(K.act, self.out.a[i * T:(i + 1) * T, :].re("(s p) f -> p s f", p=128), o)
        K.barrier()
        es.close()


def host_vecs(inp, L):
    cols = []
    for l in range(L):
        cols.append(np.asarray(inp["norm_mix_w"][l], np.float32).reshape(8, 128).T)
    for l in range(L):
        cols.append(np.asarray(inp["norm_mlp_w"][l], np.float32).reshape(8, 128).T)
    cols.append(np.asarray(inp["final_norm_w"], np.float32).reshape(8, 128).T)
    return np.ascontiguousarray(np.concatenate(cols, axis=1))


def make_in_maps(inp, M, x, ncores):
    L = M.L
    consts = host_consts()
    cst = np.ascontiguousarray(np.concatenate([consts[n] for n in CONST_NAMES], axis=1))
    vecs = host_vecs(inp, L)
    shared = {"consts": cst, "vecs": vecs,
              "mlp_w1": np.ascontiguousarray(inp["mlp_w1"][:L]), "mlp_w2": np.ascontiguousarray(inp["mlp_w2"][:L])}
    if M.mix:
        shared.update(host_mixer_inputs(inp, L))
    maps = []
    for c in range(ncores):
        d = dict(shared)
        d["x"] = np.ascontiguousarray(x[2 * c:2 * c + 2]).reshape(M.NT, D)
        maps.append(d)
    return maps


EV_CONV = 0
EV_ALOG = 48
EV_DTB = 52
EV_GDNW = 56
EV_GLAW = 568
EV_W2B = 1080
EV_N = 1336


def host_even_vec(inp, i):
    v = np.zeros((128, EV_N), np.float32)
    cw = np.asarray(inp["gdn_conv_w"][i], np.float32)
    v[:, EV_CONV:EV_CONV + 48] = cw.T.reshape(12, 128, 4).transpose(1, 0, 2).reshape(128, 48)
    v[:, EV_ALOG:EV_ALOG + 4] = np.asarray(inp["gdn_a_log"][i])[None, :]
    v[:, EV_DTB:EV_DTB + 4] = np.asarray(inp["gdn_dt_bias"][i])[None, :]
    v[:, EV_GDNW:EV_GDNW + 512] = np.tile(np.asarray(inp["gdn_norm_w"][i]), 4)[None, :]
    v[:, EV_GLAW:EV_GLAW + 512] = np.tile(np.asarray(inp["gla_norm_w"][i]), 4)[None, :]
    v[0:16, EV_W2B:EV_W2B + 256] = np.asarray(inp["gla_gate_w2"][i])
    v[16, EV_W2B:EV_W2B + 256] = np.asarray(inp["gla_gate_b"][i])
    return v


def _even_phase(self, l, src, dst):
    K = self.K
    nc = self.nc
    i_l = l // 2
    esP = contextlib.ExitStack()
    C, CB = self.C, self.CB
    P = 128
    tps = self.seq // P
    WIN = self.even_w_in.a[i_l]
    WOUT = self.even_w_out.a[i_l]

    ev = K.sb([128, EV_N], F32, "ev", esP)
    K.dma(K.sp, ev, self.evec.a[i_l])
    nexpa = K.sb([128, 4], F32, "nexpa", esP)
    K.actf(nexpa, ev[:, EV_ALOG:EV_ALOG + 4], AF.Exp)
    K.ts(nexpa, nexpa, -1.0, ALU.mult)
    convw = ev[:, EV_CONV:EV_CONV + 48]
    gain = self.vec[:, 8 * l:8 * (l + 1)]
    SAs = [K.sb([128, 4, 128], F32, "SA", esP) for _ in range(NSEQ)]
    SBs = [K.sb([128, 2, 256], F32, "SB", esP) for _ in range(NSEQ)]
    hist = [K.sb([128, 12, 3], F32, "hist", esP) for _ in range(NSEQ)]
    for s_ in range(NSEQ):
        K.memset(SAs[s_], 0.0)
        K.memset(SBs[s_], 0.0)
        K.memset(hist[s_], 0.0)

    def bc4(v):
        return v.re("p (o t) -> p o t", o=1).bc([128, 4, 128])

    def bcl(v):
        return _v(v).re("p (h o) -> p h o", o=1).bc([128, 4, 128])

    def run_variant(hp, tiles):
        if not tiles:
            return
        es = contextlib.ExitStack()
        AD = F32 if hp else BF16
        ID = C["ident"] if hp else CB["ident"]

        def t512(name, dt=F32):
            return K.sb([128, 4, 128], dt, name, es)

        if hp:
            stg = [K.sb([128, 8, 512], F32, "stg", es) for _ in range(2)]
            cnt = [0]

            def _stream(dram2d, col0, n):
                t = stg[cnt[0] % 2]
                cnt[0] += 1
                src_ = dram2d[:, col0:col0 + n].re("(k p) n -> p k n", p=128)
                K.dma(K.sp, t[:, :, 0:n], src_)
                return lambda k: t[:, k, 0:n]

            wblock = lambda col0, n: _stream(WIN, col0, n)
            wblock32 = wblock
            woblock = lambda n0: _stream(WOUT, n0 * 128, 128)
        else:
            w_in = K.sb([128, 8, EVEN_IN], BF16, "w_in", es)
            for k in range(8):
                K.dma(K.pool, w_in[:, k, :], WIN[k * 128:(k + 1) * 128, :])
            w_out = K.sb([128, 8, D], BF16, "w_out", es)
            K.dma(K.pool, w_out, WOUT.re("(c p) n -> p c n", p=128))
            wg = K.sb([128, 8, 24], F32, "wg", es)
            for k in range(8):
                K.dma(K.sp, wg[:, k, 0:8], WIN[k * 128:(k + 1) * 128, 2048:2056])
                K.dma(K.sp, wg[:, k, 8:24], WIN[k * 128:(k + 1) * 128, 3592:3608])
            wblock = lambda col0, n: (lambda k: w_in[:, k, col0:col0 + n])
            wblock32 = lambda col0, n: (lambda k: wg[:, k, 0:8]) if col0 == 2048 else (lambda k: wg[:, k, 8:24])
            woblock = lambda n0: (lambda c: w_out[:, c, n0 * 128:(n0 + 1) * 128])

        h = K.sb([128, 8, P], F32, "h", es)
        hn32 = K.sb([128, 8, P], F32, "hn32", es)
        hn = hn32 if hp else K.sb([128, 8, P], BF16, "hn", es)
        sq = K.sb([128, 8, P], BF16, "sq", es)
        rp = K.sb([128, P], F32, "rp", es)
        pre = K.sb([128, 12, P + 3], F32, "pre", es)
        cacc = K.sb([128, 12, P], F32, "cacc", es)
        qkv = K.sb([128, 12, P], F32, "qkv", es)
        sq2 = K.sb([128, 8, P], BF16, "sq2", es)
        rinv = K.sb([128, 8, P], F32, "rinv", es)
        oFM = K.sb([128, 8, P], AD, "oFM", es)
        g1 = K.sb([32, P], F32, "g1", es)
        K.memset(g1, 1.0)
        loga = K.sb([128, 256], F32, "loga", es)
        gkT = K.sb([128, 256], F32, "gkT", es)
        qkF = K.sb([128, 4, 128], F32, "qkF", es)
        gcf = K.sb([128, 2, 128], F32, "gcf", es)
        nmid = K.sb([128, 2], F32, "nmid", es)
        eq = K.sb([128, 2, 128], F32, "eq", es)
        ek = K.sb([128, 2, 128], F32, "ek", es)
        eq0 = K.sb([128, 2, 128], F32, "eq0", es)
        qgm = K.sb([128, 2, 128], F32, "qgm", es)
        kgm = K.sb([128, 2, 128], F32, "kgm", es)
        qg0 = K.sb([128, 2, 128], AD, "qg0", es)
        edlB = K.sb([128, 256], F32, "edlB", es)
        kdB = K.sb([128, 256], AD, "kdB", es)
        gv = K.sb([128, 512], AD, "gv", es)
        grs = K.sb([128, 512], F32, "grs", es)
        zs = K.sb([128, 512], F32, "zs", es)
        attm = t512("attm", AD)
        SBb = None if hp else K.sb([128, 2, 256], BF16, "SBb", es)
        junk = K.sb([128, 128], F32, "junk", es)
        ssB = K.sb([128, 4], F32, "ssB", es)
        Gt = K.sb([128, 512], F32, "Gt", es)
        obT = K.sb([128, 512], AD, "obT", es)
        gr8 = K.sb([128, 8], F32, "gr8", es)
        beta = K.sb([128, 4], F32, "beta", es)
        nbeta = K.sb([128, 4], F32, "nbeta", es)
        gA = K.sb([128, 4], F32, "gA", es)
        gcc = K.sb([128, 4], F32, "gcc", es)
        eg = K.sb([128, 4], F32, "eg", es)
        egl = K.sb([128, 4], F32, "egl", es)
        edl = K.sb([128, 4], F32, "edl", es)
        beg = K.sb([128, 4], F32, "beg", es)
        Gtri = t512("Gtri")
        Dm = t512("Dm")
        DTm = t512("DTm")
        A = [t512("A0"), t512("A1")]
        AT = [t512("AT0"), t512("AT1")]
        PT = t512("PT")
        aqkT = t512("aqkT")
        vb = t512("vb")
        kbg = t512("kbg")
        kdA = t512("kdA")
        wneg = t512("wneg")
        vn = t512("vn")
        qg = t512("qg")
        ssA = K.sb([128, 4], F32, "ssA", es)
        oaT = K.sb([128, 512], AD, "oaT", es)

        for (s_, it) in tiles:
            SA, SB_ = SAs[s_], SBs[s_]
            K.dma(K.sp, h, self.hview(src, it * P, P))
            K.cp(pre[:, :, 0:3], hist[s_], E=K.pool)
            if not hp:
                K.cp(SBb, SB_, E=K.pool)
            Sst = SB_ if hp else SBb
            self.rmsnorm_fm(h, gain, [hn32] if hp else [hn, hn32], sq, rp, n=P)
            for c4 in range(3):
                wb = wblock(c4 * 512, 512)
                b = self.bank()
                for j in range(4):
                    for k in range(8):
                        K.mm(b[:, j * P:(j + 1) * P], wb(k)[:, j * 128:(j + 1) * 128], hn[:, k, :],
                             start=(k == 0), stop=(k == 7))
                K.cp(pre[:, c4 * 4:(c4 + 1) * 4, 3:3 + P], b.a.re("p (j t) -> p j t", j=4), E=K.act)
            shared = {}
            wb = wblock32(2048, 8)
            bg = self.bank()
            for k in range(8):
                K.mm(bg[:, 0:8], hn32[:, k, :], wb(k), start=(k == 0), stop=(k == 7))
            K.cp(gr8, bg[:, 0:8])
            for c in range(12):
                K.actf(cacc[:, c, :], pre[:, c, 0:P], AF.Copy, scale=convw[:, c * 4:c * 4 + 1])
                for j in range(1, 4):
                    K.stt(cacc[:, c, :], pre[:, c, j:j + P], convw[:, c * 4 + j:c * 4 + j + 1], cacc[:, c, :],
                          ALU.mult, ALU.add)
            K.cp(hist[s_], pre[:, :, P:P + 3], E=K.pool)
            K.actf(qkv, cacc, AF.Silu)
            K.tt(sq2, qkv[:, 0:8, :], qkv[:, 0:8, :], ALU.mult, E=K.pool)
            for half in range(2):
                b = self.bank()
                K.mm(b, CB["ones"], sq2[:, half * 4:(half + 1) * 4, :])
                K.actf(rinv[:, half * 4:(half + 1) * 4, :], b.a.re("p (j t) -> p j t", j=4), AF.Ln, bias=self.epst)
            K.actf(rinv, rinv, AF.Exp, scale=-0.5)
            K.stt(qkv[:, 0:4, :], qkv[:, 0:4, :], float(128 ** -0.5), rinv[:, 0:4, :], ALU.mult, ALU.mult)
            K.tt(qkv[:, 4:8, :], qkv[:, 4:8, :], rinv[:, 4:8, :], ALU.mult, E=K.pool)

            def seg_tm1():
                wb = wblock(1536, 512)
                bz = self.bank()
                for k in range(8):
                    K.mm(bz, hn[:, k, :], wb(k), start=(k == 0), stop=(k == 7))
                K.actf(zs, bz, AF.Silu)
                wb = wblock(2312, 256)
                bgk = self.bank()
                for k in range(8):
                    K.mm(bgk[:, 0:256], hn[:, k, :], wb(k), start=(k == 0), stop=(k == 7))
                K.cp(gkT, bgk[:, 0:256], E=K.act)
            def seg_tm2():
                wb = wblock(2568, 512)
                bgv = self.bank()
                for k in range(8):
                    K.mm(bgv, hn[:, k, :], wb(k), start=(k == 0), stop=(k == 7))
                K.cp(gv, bgv, E=K.act)
                wb = wblock(3080, 512)
                bgr = self.bank()
                for k in range(8):
                    K.mm(bgr, hn[:, k, :], wb(k), start=(k == 0), stop=(k == 7))
                K.actf(grs, bgr, AF.Silu)
            def seg_tm3():
                wb = wblock32(3592, 16)
                bl = self.bank()
                for k in range(8):
                    K.mm(bl[0:16, 0:P], wb(k), hn32[:, k, :], start=(k == 0), stop=(k == 7))
                K.cp(g1[0:16, :], bl[0:16, 0:P])
                wb = wblock(2056, 512)
                bqk = self.bank()
                for j in range(4):
                    for k in range(8):
                        K.mm(bqk[:, j * P:(j + 1) * P], wb(k)[:, j * 128:(j + 1) * 128], hn[:, k, :],
                             start=(k == 0), stop=(k == 7))
                K.cp(qkF, bqk.a.re("p (j t) -> p j t", j=4))
            def seg_gla1():
                bx = self.bank()
                K.mm(bx[:, 0:256], g1, ev[0:32, EV_W2B:EV_W2B + 256])
                K.ts(loga, bx[:, 0:256], -20.0, ALU.max)
                K.actf(loga, loga, AF.Exp, scale=-1.0)
                K.actf(loga, loga, AF.Ln, bias=1.0)
                K.ts(loga, loga, -1.0 / 16.0, ALU.mult, -1.0, ALU.max)
                bgc = self.bank()
                for cc in range(2):
                    K.mm(bgc[:, cc * 128:(cc + 1) * 128], loga[:, cc * 128:(cc + 1) * 128], C["tri"])
                K.mm(bgc[:, 256:512], C["ustr"], loga)
                K.cp(gcf, bgc[:, 0:256].re("p (c t) -> p c t", c=2))
                K.actf(edlB, bgc[:, 256:512], AF.Exp)
                K.tt(kdB, gkT, edlB, ALU.mult)
                K.ts(nmid, gcf[:, :, 63], -1.0, ALU.mult)
                for cc in range(2):
                    K.actf(eq[:, cc, :], gcf[:, cc, :], AF.Exp, bias=nmid[:, cc:cc + 1])
                    K.actf(ek[:, cc, :], gcf[:, cc, :], AF.Exp, bias=gcf[:, cc, 63:64], scale=-1.0)
                K.actf(eq0, gcf, AF.Exp)
                K.stt(qgm, qkF[:, 0:2, :], 0.125, eq, ALU.mult, ALU.mult)
                K.tt(kgm, qkF[:, 2:4, :], ek, ALU.mult)
                K.stt(qg0, qkF[:, 0:2, :], 0.125, eq0, ALU.mult, ALU.mult)
            def seg_gla2():
                bat = self.bank()
                for hh in range(4):
                    pr = slice((hh % 2) * 64, (hh % 2) * 64 + 64)
                    K.mm(bat[:, hh * 128:(hh + 1) * 128], kgm[pr, hh // 2, :], qgm[pr, hh // 2, :], ser=True)
                K.tt(attm, bat.a.re("p (h t) -> p h t", h=4), bc4(C["tri"]), ALU.mult)
                bo = self.bank()
                shared['bo'] = bo
                for hh in range(4):
                    pr = slice((hh % 2) * 64, (hh % 2) * 64 + 64)
                    K.mm(bo[:, hh * 128:(hh + 1) * 128], qg0[pr, hh // 2, :],
                         Sst[pr, hh // 2, (hh % 2) * 128:(hh % 2) * 128 + 128], start=True, stop=False, ser=True)
                    K.mm(bo[:, hh * 128:(hh + 1) * 128], attm[:, hh, :], gv[:, hh * 128:(hh + 1) * 128],
                         start=False, stop=True, ser=True)
                bs = self.bank()
                for pp in range(2):
                    K.mm(bs[:, pp * 256:(pp + 1) * 256], kdB[:, pp * 128:(pp + 1) * 128], gv[:, pp * 256:(pp + 1) * 256])
                for pp in range(2):
                    K.stt(SB_[:, pp, :], SB_[:, pp, :], eq0[:, pp, 127:128], bs[:, pp * 256:(pp + 1) * 256],
                          ALU.mult, ALU.add)
            def seg_gla3():
                bo = shared['bo']
                for hh in range(4):
                    K.actf(junk, bo[:, hh * 128:(hh + 1) * 128], AF.Square, accum=ssB[:, hh:hh + 1])
                K.actf(ssB, ssB, AF.Sqrt, bias=self.epst, scale=1.0 / 128.0)
                K.op(K.dve, lambda: nc.vector.reciprocal(ssB.a.ap, ssB.a.ap), [ssB], [ssB])
                K.tt(Gt, grs, ev[:, EV_GLAW:EV_GLAW + 512], ALU.mult, E=K.pool)
                for hh in range(4):
                    sl = slice(hh * 128, (hh + 1) * 128)
                    K.stt(obT[:, sl], bo[:, sl], ssB[:, hh:hh + 1], Gt[:, sl], ALU.mult, ALU.mult)
                btr = self.bank()
                for hh in range(4):
                    K.mm(btr[:, hh * 128:(hh + 1) * 128], obT[:, hh * 128:(hh + 1) * 128], ID)
                K.cp(oFM[:, 4:8, :], btr.a.re("p (h t) -> p h t", h=4), E=K.act)
            segs = [seg_tm1, seg_tm2, seg_tm3, seg_gla1, seg_gla2, seg_gla3]
            K.actf(beta, gr8[:, 0:4], AF.Sigmoid)
            K.ts(nbeta, beta, -1.0, ALU.mult)
            K.tt(gA, gr8[:, 4:8], ev[:, EV_DTB:EV_DTB + 4], ALU.add)
            K.actf(gA, gA, AF.Exp)
            K.actf(gA, gA, AF.Ln, bias=1.0)
            K.tt(gA, gA, nexpa, ALU.mult)
            K.tt(Gtri, bc4(C["tri"]), bcl(gA), ALU.mult)
            bsm = self.bank()
            K.mm(bsm[:, 0:4], C["tri"], gA)
            K.mm(bsm[:, 8:12], C["ones"], gA)
            br = self.bank()
            K.mm(br, C["ones"], Gtri)
            K.cp(gcc, bsm[:, 0:4])
            K.actf(eg, gcc, AF.Exp)
            K.actf(egl, bsm[:, 8:12], AF.Exp)
            K.tt(edl, bsm[:, 8:12], gcc, ALU.subtract)
            K.actf(edl, edl, AF.Exp)
            K.tt(beg, beta, eg, ALU.mult)
            for hh in range(4):
                sl = slice(hh * 128, (hh + 1) * 128)
                K.ts(Dm[:, hh, :], br[:, sl], gcc[:, hh:hh + 1], ALU.subtract, 0.0, ALU.max)
                K.ts(DTm[:, hh, :], br[:, sl], gcc[:, hh:hh + 1], ALU.subtract, 0.0, ALU.min)
            K.actf(Dm, Dm, AF.Exp, scale=-1.0)
            K.actf(DTm, DTm, AF.Exp)
            K.actf(qg, br.a.re("p (h t) -> p h t", h=4), AF.Exp)
            K.tt(Dm, Dm, bc4(C["ustr"]), ALU.mult, E=K.pool)
            K.tt(DTm, DTm, bc4(C["tri"]), ALU.mult, E=K.pool)
            K.tt(qg, qg, qkv[:, 0:4, :], ALU.mult, E=K.pool)
            bkk = self.bank()
            bqt = self.bank()
            for hh in range(4):
                sl = slice(hh * 128, (hh + 1) * 128)
                K.mm(bkk[:, sl], qkv[:, 4 + hh, :], qkv[:, 4 + hh, :])
                K.mm(bqt[:, sl], qkv[:, 4 + hh, :], qkv[:, hh, :])
            for hh in range(4):
                K.stt(A[0][:, hh, :], bkk[:, hh * 128:(hh + 1) * 128], nbeta[:, hh:hh + 1], Dm[:, hh, :],
                      ALU.mult, ALU.mult)
            K.tt(aqkT, bqt.a.re("p (h t) -> p h t", h=4), DTm, ALU.mult)
            bnt = self.bank()
            for hh in range(4):
                K.mm(bnt[:, hh * 128:(hh + 1) * 128], A[0][:, hh, :], C["ident"])
            K.cp(AT[0], bnt.a.re("p (h t) -> p h t", h=4), E=K.act)
            K.tt(PT, bnt.a.re("p (h t) -> p h t", h=4), bc4(C["ident"]), ALU.add)
            bkt = self.bank()
            bvt = self.bank()
            for hh in range(4):
                sl = slice(hh * 128, (hh + 1) * 128)
                K.mm(bkt[:, sl], qkv[:, 4 + hh, :], C["ident"])
                K.mm(bvt[:, sl], qkv[:, 8 + hh, :], C["ident"])
            K.tt(vb, bvt.a.re("p (h t) -> p h t", h=4), bcl(beta), ALU.mult)
            K.tt(kbg, bkt.a.re("p (h t) -> p h t", h=4), bcl(beg), ALU.mult)
            K.tt(kdA, bkt.a.re("p (h t) -> p h t", h=4), bcl(edl), ALU.mult)
            cur = 0
            for itr in range(6):
                nxt = 1 - cur
                b1 = self.bank()
                b2 = self.bank()
                for hh in range(4):
                    sl = slice(hh * 128, (hh + 1) * 128)
                    K.mm(b1[:, sl], AT[cur][:, hh, :], A[cur][:, hh, :])
                    if itr < 5:
                        K.mm(b2[:, sl], A[cur][:, hh, :], AT[cur][:, hh, :])
                K.cp(A[nxt], b1.a.re("p (h t) -> p h t", h=4), E=K.act)
                if itr < 5:
                    K.cp(AT[nxt], b2.a.re("p (h t) -> p h t", h=4), E=K.dve)
                b3 = self.bank()
                for hh in range(4):
                    K.mm(b3[:, hh * 128:(hh + 1) * 128], A[nxt][:, hh, :], PT[:, hh, :])
                K.tt(PT, PT, b3.a.re("p (h t) -> p h t", h=4), ALU.add)
                cur = nxt
                segs[itr]()
            bw = self.bank()
            for hh in range(4):
                K.mm(bw[:, hh * 128:(hh + 1) * 128], kbg[:, hh, :], PT[:, hh, :])
            K.actf(wneg, bw.a.re("p (h t) -> p h t", h=4), AF.Copy, scale=-1.0)
            bvn = self.bank()
            for hh in range(4):
                sl = slice(hh * 128, (hh + 1) * 128)
                K.mm(bvn[:, sl], PT[:, hh, :], vb[:, hh, :], start=True, stop=False)
                K.mm(bvn[:, sl], wneg[:, hh, :], SA[:, hh, :], start=False, stop=True)
            K.cp(vn, bvn.a.re("p (h t) -> p h t", h=4))
            boa = self.bank()
            for hh in range(4):
                sl = slice(hh * 128, (hh + 1) * 128)
                K.mm(boa[:, sl], qg[:, hh, :], SA[:, hh, :], start=True, stop=False)
                K.mm(boa[:, sl], aqkT[:, hh, :], vn[:, hh, :], start=False, stop=True)
            bsa = self.bank()
            for hh in range(4):
                K.mm(bsa[:, hh * 128:(hh + 1) * 128], kdA[:, hh, :], vn[:, hh, :])
            for hh in range(4):
                K.stt(SA[:, hh, :], SA[:, hh, :], egl[:, hh:hh + 1], bsa[:, hh * 128:(hh + 1) * 128],
                      ALU.mult, ALU.add)
            for hh in range(4):
                K.actf(junk, boa[:, hh * 128:(hh + 1) * 128], AF.Square, accum=ssA[:, hh:hh + 1])
            K.actf(ssA, ssA, AF.Sqrt, bias=self.epst, scale=1.0 / 128.0)
            K.op(K.dve, lambda: nc.vector.reciprocal(ssA.a.ap, ssA.a.ap), [ssA], [ssA])
            K.tt(Gt, zs, ev[:, EV_GDNW:EV_GDNW + 512], ALU.mult, E=K.pool)
            for hh in range(4):
                sl = slice(hh * 128, (hh + 1) * 128)
                K.stt(oaT[:, sl], boa[:, sl], ssA[:, hh:hh + 1], Gt[:, sl], ALU.mult, ALU.mult)
            btr = self.bank()
            for hh in range(4):
                K.mm(btr[:, hh * 128:(hh + 1) * 128], oaT[:, hh * 128:(hh + 1) * 128], ID)
            K.cp(oFM[:, 0:4, :], btr.a.re("p (h t) -> p h t", h=4), E=K.act)

            for half in range(2):
                b = self.bank()
                for j in range(4):
                    wo = woblock(half * 4 + j)
                    for c in range(8):
                        K.mm(b[:, j * P:(j + 1) * P], wo(c), oFM[:, c, :], start=(c == 0), stop=(c == 7))
                K.tt(h[:, half * 4:(half + 1) * 4, :], h[:, half * 4:(half + 1) * 4, :],
                     b.a.re("p (j t) -> p j t", j=4), ALU.add)
            K.dma(K.act, self.hview(dst, it * P, P), h)
        K.barrier()
        es.close()

    run_variant(True, [(s_, s_ * tps) for s_ in range(NSEQ)])
    run_variant(False, [(s_, s_ * tps + j) for s_ in range(NSEQ) for j in range(1, tps)])
    K.barrier()
    esP.close()


Model.even_phase = _even_phase


def _declare_mixer_inputs(self, ne, no):
    K = self.K
    self.even_w_in = K.dram("even_w_in", [ne, D, EVEN_IN], F32, kind="ExternalInput")
    self.even_w_out = K.dram("even_w_out", [ne, D, D], F32, kind="ExternalInput")
    self.evec = K.dram("evec", [ne, 128, EV_N], F32, kind="ExternalInput")
    if no > 0:
        self.declare_odd_inputs(no)


Model.declare_mixer_inputs = _declare_mixer_inputs


def host_mixer_inputs(inp, L):
    ne = (L + 1) // 2
    no = L // 2
    d = {"even_w_in": np.ascontiguousarray(inp["even_w_in"][:ne]),
         "even_w_out": np.ascontiguousarray(inp["even_w_out"][:ne]),
         "evec": np.stack([host_even_vec(inp, i) for i in range(ne)])}
    if no > 0:
        d.update(host_odd_inputs(inp, no))
    return d


OV_CONV = 0
OV_CONVB = 32
OV_DTB = 40
OV_ALOG = 48
OV_D = 56
OV_SNW = 568
OV_MU = 1080
OV_W0 = 1094
OV_A0 = 1606
OV_KK = 1610
OV_KA = 1614
OV_RK = 1618
OV_GNW = 1622
OV_GNB = 2134
OV_SEL = 2646
OV_N = 2678


def host_odd_vec(inp, i):
    v = np.zeros((128, OV_N), np.float32)
    f = lambda a, n: np.asarray(a, np.float32).reshape(n, 128).T
    cw = np.asarray(inp["ssd_conv_w"][i], np.float32)
    v[:, OV_CONV:OV_CONV + 32] = cw.T.reshape(8, 128, 4).transpose(1, 0, 2).reshape(128, 32)
    v[:, OV_CONVB:OV_CONVB + 8] = f(inp["ssd_conv_b"][i], 8)
    v[:, OV_DTB:OV_DTB + 8] = np.asarray(inp["ssd_dt_bias"][i])[None, :]
    v[:, OV_ALOG:OV_ALOG + 8] = np.asarray(inp["ssd_a_log"][i])[None, :]
    v[:, OV_D:OV_D + 512] = np.repeat(np.asarray(inp["ssd_d"][i]), 64)[None, :]
    v[:, OV_SNW:OV_SNW + 512] = np.asarray(inp["ssd_norm_w"][i])[None, :]
    v[:, OV_MU:OV_MU + 14] = f(inp["rwkv_mu"][i], 14)
    v[:, OV_W0:OV_W0 + 512] = np.asarray(inp["rwkv_w0"][i])[None, :]
    v[:, OV_A0:OV_A0 + 4] = f(inp["rwkv_a0"][i], 4)
    v[:, OV_KK:OV_KK + 4] = f(inp["rwkv_k_k"][i], 4)
    v[:, OV_KA:OV_KA + 4] = f(inp["rwkv_k_a"][i], 4)
    v[:, OV_RK:OV_RK + 4] = f(np.asarray(inp["rwkv_r_k"][i]).reshape(-1), 4)
    v[:, OV_GNW:OV_GNW + 512] = np.asarray(inp["rwkv_gn_w"][i])[None, :]
    v[:, OV_GNB:OV_GNB + 512] = np.asarray(inp["rwkv_gn_b"][i])[None, :]
    p = np.arange(128)
    sel = np.zeros((128, 4, 8), np.float32)
    for c in range(4):
        sel[p, c, 2 * c + p // 64] = 1.0
    v[:, OV_SEL:OV_SEL + 32] = sel.reshape(128, 32)
    return v


def host_odd_inputs(inp, no):
    a2 = np.zeros((no, 128, 512), np.float32)
    a2[:, 64:128, :] = np.asarray(inp["rwkv_a2"][:no])
    w2 = np.zeros((no, 128, 512), np.float32)
    w2[:, 0:64, :] = np.asarray(inp["rwkv_w2"][:no])
    return {"odd_w_in": np.ascontiguousarray(inp["odd_w_in"][:no]),
            "odd_w_out": np.ascontiguousarray(inp["odd_w_out"][:no]),
            "ovec": np.stack([host_odd_vec(inp, i) for i in range(no)]),
            "rw2": w2, "ra2": a2, "rg2": np.ascontiguousarray(inp["rwkv_g2"][:no])}


def _declare_odd_inputs(self, no):
    K = self.K
    self.odd_w_in = K.dram("odd_w_in", [no, D, ODD_IN], F32, kind="ExternalInput")
    self.odd_w_out = K.dram("odd_w_out", [no, D, D], F32, kind="ExternalInput")
    self.ovec = K.dram("ovec", [no, 128, OV_N], F32, kind="ExternalInput")
    self.rw2 = K.dram("rw2", [no, 128, 512], F32, kind="ExternalInput")
    self.ra2 = K.dram("ra2", [no, 128, 512], F32, kind="ExternalInput")
    self.rg2 = K.dram("rg2", [no, 128, 512], F32, kind="ExternalInput")


Model.declare_odd_inputs = _declare_odd_inputs


def _odd_phase(self, l, src, dst):
    K = self.K
    nc = self.nc
    i_l = l // 2
    esP = contextlib.ExitStack()
    C, CB = self.C, self.CB
    P = 128
    tps = self.seq // P
    WIN = self.odd_w_in.a[i_l]
    WOUT = self.odd_w_out.a[i_l]

    def bc4(v, n=4):
        return v.re("p (o t) -> p o t", o=1).bc([128, n, 128])

    def bcl(v, n, m):
        return _v(v).re("p (h o) -> p h o", o=1).bc([128, n, m])

    def recip(t):
        K.op(K.dve, lambda: nc.vector.reciprocal(_v(t).ap, _v(t).ap), [t], [t])

    ov = K.sb([128, OV_N], F32, "ov", esP)
    K.dma(K.sp, ov, self.ovec.a[i_l])
    w2t = K.sb([128, 512], F32, "w2t", esP)
    a2t = K.sb([128, 512], F32, "a2t", esP)
    g2t = K.sb([128, 512], F32, "g2t", esP)
    K.dma(K.sp, w2t, self.rw2.a[i_l])
    K.dma(K.sp, a2t, self.ra2.a[i_l])
    K.dma(K.sp, g2t, self.rg2.a[i_l])
    nexpa = K.sb([128, 8], F32, "nexpa", esP)
    K.actf(nexpa, ov[:, OV_ALOG:OV_ALOG + 8], AF.Exp)
    K.ts(nexpa, nexpa, -1.0, ALU.mult)
    gneps = K.sb([128, 1], F32, "gneps", esP)
    K.memset(gneps, 64e-5)
    convw = ov[:, OV_CONV:OV_CONV + 32]
    gain = self.vec[:, 8 * l:8 * (l + 1)]
    STs = [K.sb([128, 512], F32, "ST", esP) for _ in range(NSEQ)]
    Hss = [K.sb([128, 4, 128], F32, "Hs", esP) for _ in range(NSEQ)]
    histC = [K.sb([128, 8, 3], F32, "histC", esP) for _ in range(NSEQ)]
    histP = [K.sb([128, 14, 1], F32, "histP", esP) for _ in range(NSEQ)]
    for s_ in range(NSEQ):
        K.memset(STs[s_], 0.0)
        K.memset(Hss[s_], 0.0)
        K.memset(histC[s_], 0.0)
        K.memset(histP[s_], 0.0)

    def run_variant(hp, tiles):
        if not tiles:
            return
        es = contextlib.ExitStack()
        AD = F32 if hp else BF16
        ID = C["ident"] if hp else CB["ident"]
        if hp:
            stg = [K.sb([128, 8, 512], F32, "stg", es) for _ in range(2)]
            cnt = [0]

            def _stream(dram2d, col0, n):
                t = stg[cnt[0] % 2]
                cnt[0] += 1
                src_ = dram2d[:, col0:col0 + n].re("(k p) n -> p k n", p=128)
                K.dma(K.sp, t[:, :, 0:n], src_)
                return lambda k: t[:, k, 0:n]

            wblock = lambda col0, n: _stream(WIN, col0, n)
            wblock32 = wblock
            woblock = lambda n0: _stream(WOUT, n0 * 128, 128)
        else:
            w_in = K.sb([128, 8, ODD_IN], BF16, "w_in", es)
            for k in range(8):
                K.dma(K.pool, w_in[:, k, :], WIN[k * 128:(k + 1) * 128, :])
            w_out = K.sb([128, 8, D], BF16, "w_out", es)
            K.dma(K.pool, w_out, WOUT.re("(c p) n -> p c n", p=128))
            wg = K.sb([128, 8, 136], F32, "wg", es)
            for k in range(8):
                K.dma(K.sp, wg[:, k, 0:8], WIN[k * 128:(k + 1) * 128, 1536:1544])
                K.dma(K.sp, wg[:, k, 8:136], WIN[k * 128:(k + 1) * 128, 3080:3208])
            wblock = lambda col0, n: (lambda k: w_in[:, k, col0:col0 + n])
            wblock32 = lambda col0, n: (lambda k: wg[:, k, 0:8])
            woblock = lambda n0: (lambda c: w_out[:, c, n0 * 128:(n0 + 1) * 128])

        def sb(shape, name, dt=F32):
            return K.sb(shape, dt, name, es)

        h = sb([128, 8, P], "h"); hn32 = sb([128, 8, P], "hn32"); hn = hn32 if hp else sb([128, 8, P], "hn", BF16)
        sq = sb([128, 8, P], "sq", BF16); rp = sb([128, P], "rp")
        preC = sb([128, 8, P + 3], "preC"); cacc = sb([128, 8, P], "cacc"); xbc = sb([128, 8, P], "xbc")
        pdp = sb([128, 14, P + 1], "pdp"); pdm = sb([128, 14, P], "pdm"); pdd = pdm
        zs = sb([128, 512], "zs"); oFM = sb([128, 8, P], "oFM", AD)
        dt8 = sb([128, 8], "dt8"); a8 = sb([128, 8], "a8"); acc8 = sb([128, 8], "acc8")
        eac8 = sb([128, 8], "eac8"); edl8 = sb([128, 8], "edl8"); cd8 = sb([128, 8], "cd8"); dte8 = sb([128, 8], "dte8")
        segT = sb([128, 8, 128], "segT"); MT = sb([128, 8, 128], "MT", AD)
        xTs = sb([128, 512], "xTs"); xdt = sb([128, 8, 64], "xdt", AD); xdte = sb([128, 8, 64], "xdte", AD)
        Bb = sb([128, 256], "Bb", AD); Cb = sb([128, 2, 128], "Cb", AD)
        STb = None if hp else sb([128, 512], "STb", BF16)
        y1 = sb([128, 512], "y1"); y2 = sb([128, 512], "y2"); ss2 = sb([128, 2], "ss2"); junk = sb([128, 256], "junk")
        ycT = sb([128, 512], "ycT", AD)
        txw = sb([64, P], "txw"); ldT = sb([128, 512], "ldT"); aF = sb([128, 4, P], "aF"); sg = sb([128, P], "sg")
        kp = sb([128, 4, P], "kp"); kq = sb([128, 4, P], "kq"); kk = kq; bm = sb([128, 4, P], "bm"); gl4 = sb([128, 4], "gl4")
        sqk = sb([128, 4, P], "sqk"); lg = sb([128, 4, P], "lg"); lgx = sb([128, 4, P], "lgx"); nmid = sb([128, 4], "nmid")
        e_r = sb([128, 4, P], "e_r"); e_a = sb([128, 4, P], "e_a"); e_b = sb([128, 4, P], "e_b")
        e_r0 = sb([128, 4, P], "e_r0"); e_a0 = sb([128, 4, P], "e_a0")
        rt_ = e_r; r0_ = e_r0; at_ = e_a; a0_ = e_a0; bt_ = e_b; kt_ = xbc[:, 4:8, :]
        edlT = sb([128, 512], "edlT"); kdT = sb([128, 512], "kdT"); bdT = sb([128, 512], "bdT"); vT = sb([128, 512], "vT")
        A = [segT[:, 0:4, :], cacc[:, 0:4, :]]
        AT = [segT[:, 4:8, :], cacc[:, 4:8, :]]
        PT = xbc[:, 0:4, :]
        AakT = y1.a.re("p (h t) -> p h t", h=4); RbT = y2.a.re("p (h t) -> p h t", h=4); RkT = xTs.a.re("p (h t) -> p h t", h=4)
        Xs = sb([128, 4, 64], "Xs"); Us = sb([128, 8, 64], "Us")
        yd = lg.a.re("p c t -> p (c t)").re("p (h q) -> p h q", h=8); ym = lgx.a.re("p c t -> p (c t)").re("p (h q) -> p h q", h=8); mean8 = sb([128, 8], "mean8"); var8 = sb([128, 8], "var8")
        rkr = sb([128, 4, P], "rkr"); rks = sb([128, 8], "rks"); gT = sb([128, 512], "gT"); ydT = sb([128, 512], "ydT", AD)

        for (s_, it) in tiles:
            ST, Hs = STs[s_], Hss[s_]
            K.dma(K.sp, h, self.hview(src, it * P, P))
            K.cp(preC[:, :, 0:3], histC[s_], E=K.pool)
            K.cp(pdp[:, :, 0:1], histP[s_], E=K.pool)
            if not hp:
                K.cp(STb, ST, E=K.pool)
            Sst = ST if hp else STb
            self.rmsnorm_fm(h, gain, [hn32] if hp else [hn, hn32], sq, rp, n=P)
            for c4 in range(2):
                wb = wblock(512 + c4 * 512, 512)
                b = self.bank()
                for j in range(4):
                    for k in range(8):
                        K.mm(b[:, j * P:(j + 1) * P], wb(k)[:, j * 128:(j + 1) * 128], hn[:, k, :],
                             start=(k == 0), stop=(k == 7))
                K.cp(preC[:, c4 * 4:(c4 + 1) * 4, 3:3 + P], b.a.re("p (j t) -> p j t", j=4), E=K.act)
            for c4 in range(4):
                b = self.bank()
                nj = 4 if c4 < 3 else 2
                wb = wblock(1544 + c4 * 512, nj * 128)
                for j in range(nj):
                    c = c4 * 4 + j
                    for k in range(8):
                        if c == 12 and not hp:
                            K.mm(b[:, j * P:(j + 1) * P], wg[:, k, 8:136], hn32[:, k, :], start=(k == 0), stop=(k == 7))
                        else:
                            K.mm(b[:, j * P:(j + 1) * P], wb(k)[:, j * 128:(j + 1) * 128], hn[:, k, :],
                                 start=(k == 0), stop=(k == 7))
                K.cp(pdp[:, c4 * 4:c4 * 4 + nj, 1:1 + P], b[:, 0:nj * P].re("p (j t) -> p j t", j=nj), E=K.act)
            wb = wblock(0, 512)
            bz = self.bank()
            for k in range(8):
                K.mm(bz, hn[:, k, :], wb(k), start=(k == 0), stop=(k == 7))
            K.actf(zs, bz, AF.Silu)
            wb = wblock32(1536, 8)
            bg = self.bank()
            for k in range(8):
                K.mm(bg[:, 0:8], hn32[:, k, :], wb(k), start=(k == 0), stop=(k == 7))
            K.tt(dt8, bg[:, 0:8], ov[:, OV_DTB:OV_DTB + 8], ALU.add)
            K.ts(dt8, dt8, 60.0, ALU.min)
            K.actf(dt8, dt8, AF.Exp)
            K.actf(dt8, dt8, AF.Ln, bias=1.0)
            K.tt(a8, dt8, nexpa, ALU.mult)
            for c in range(8):
                K.actf(cacc[:, c, :], preC[:, c, 0:P], AF.Copy, scale=convw[:, c * 4:c * 4 + 1])
                for j in range(1, 4):
                    K.stt(cacc[:, c, :], preC[:, c, j:j + P], convw[:, c * 4 + j:c * 4 + j + 1], cacc[:, c, :],
                          ALU.mult, ALU.add)
                K.actf(xbc[:, c, :], cacc[:, c, :], AF.Silu, bias=ov[:, OV_CONVB + c:OV_CONVB + c + 1])
            K.cp(histC[s_], preC[:, :, P:P + 3], E=K.pool)
            Atri = cacc
            K.tt(Atri, bc4(C["tri"], 8), bcl(a8, 8, 128), ALU.mult)
            bsm = self.bank()
            K.mm(bsm[:, 0:8], C["tri"], a8)
            K.mm(bsm[:, 8:16], C["ones"], a8)
            K.cp(acc8, bsm[:, 0:8])
            K.actf(eac8, acc8, AF.Exp)
            K.actf(cd8, bsm[:, 8:16], AF.Exp)
            K.tt(edl8, bsm[:, 8:16], acc8, ALU.subtract)
            K.actf(edl8, edl8, AF.Exp)
            K.tt(dte8, dt8, edl8, ALU.mult)
            for hg in range(2):
                br = self.bank()
                K.mm(br, C["ones"], Atri[:, hg * 4:(hg + 1) * 4, :])
                for hh in range(4):
                    hd = hg * 4 + hh
                    K.ts(segT[:, hd, :], br[:, hh * 128:(hh + 1) * 128], acc8[:, hd:hd + 1], ALU.subtract, 0.0, ALU.min)
            K.actf(segT, segT, AF.Exp)
            K.tt(segT, segT, bc4(C["tri"], 8), ALU.mult, E=K.pool)
            bcb = self.bank()
            for g in range(2):
                K.mm(bcb[:, g * 128:(g + 1) * 128], xbc[:, 4 + g, :], xbc[:, 6 + g, :])
            for g in range(2):
                K.tt(MT[:, g * 4:(g + 1) * 4, :], segT[:, g * 4:(g + 1) * 4, :],
                     bcb[:, g * 128:(g + 1) * 128].re("p (o t) -> p o t", o=1).bc([128, 4, 128]), ALU.mult)
            K.cp(Cb, xbc[:, 6:8, :], E=K.pool)
            bxt = self.bank()
            for c in range(4):
                K.mm(bxt[:, c * 128:(c + 1) * 128], xbc[:, c, :], C["ident"])
            bbt = self.bank()
            for g in range(2):
                K.mm(bbt[:, g * 128:(g + 1) * 128], xbc[:, 4 + g, :], C["ident"])
            K.cp(xTs, bxt, E=K.act)
            K.cp(Bb, bbt[:, 0:256], E=K.act)
            K.tt(xdt, xTs.a.re("p (h q) -> p h q", h=8), bcl(dt8, 8, 64), ALU.mult)
            K.tt(xdte, xTs.a.re("p (h q) -> p h q", h=8), bcl(dte8, 8, 64), ALU.mult)
            by = self.bank()
            for hd in range(8):
                K.mm(by[:, hd * 64:(hd + 1) * 64], MT[:, hd, :], xdt[:, hd, :])
            bi = self.bank()
            for g in range(2):
                K.mm(bi[:, g * 256:(g + 1) * 256], Cb[:, g, :], Sst[:, g * 256:(g + 1) * 256])
            bst = self.bank()
            for g in range(2):
                K.mm(bst[:, g * 256:(g + 1) * 256], Bb[:, g * 128:(g + 1) * 128],
                     xdte[:, g * 4:(g + 1) * 4, :].re("p h q -> p (h q)"))
            K.tt(y1.a.re("p (h q) -> p h q", h=8), bi.a.re("p (h q) -> p h q", h=8), bcl(eac8, 8, 64), ALU.mult)
            K.tt(y1, y1, by, ALU.add)
            K.tt(y2, xTs, ov[:, OV_D:OV_D + 512], ALU.mult, E=K.pool)
            K.tt(y1, y1, y2, ALU.add)
            K.tt(y1, y1, zs, ALU.mult)
            K.tt(ST.a.re("p (h q) -> p h q", h=8), ST.a.re("p (h q) -> p h q", h=8), bcl(cd8, 8, 64), ALU.mult)
            K.tt(ST, ST, bst, ALU.add)
            for g in range(2):
                K.actf(junk, y1[:, g * 256:(g + 1) * 256], AF.Square, accum=ss2[:, g:g + 1])
            K.actf(ss2, ss2, AF.Sqrt, bias=self.epst, scale=1.0 / 256.0)
            recip(ss2)
            K.tt(y2, y1, ov[:, OV_SNW:OV_SNW + 512], ALU.mult, E=K.pool)
            for g in range(2):
                K.ts(ycT[:, g * 256:(g + 1) * 256], y2[:, g * 256:(g + 1) * 256], ss2[:, g:g + 1], ALU.mult)
            btr = self.bank()
            for hh in range(4):
                K.mm(btr[:, hh * 128:(hh + 1) * 128], ycT[:, hh * 128:(hh + 1) * 128], ID)
            K.cp(oFM[:, 0:4, :], btr.a.re("p (h t) -> p h t", h=4), E=K.act)

            K.tt(pdd, pdp[:, :, 0:P], pdp[:, :, 1:P + 1], ALU.subtract)
            K.tt(pdd, pdd, bcl(ov[:, OV_MU:OV_MU + 14], 14, P), ALU.mult, E=K.pool)
            K.tt(pdm, pdd, pdp[:, :, 1:P + 1], ALU.add)
            K.cp(histP[s_], pdp[:, :, P:P + 1], E=K.pool)
            R_, K_, V_ = pdm[:, 0:4, :], pdm[:, 4:8, :], pdm[:, 8:12, :]
            K.actf(txw, pdm[0:64, 12, :], AF.Tanh)
            K.actf(sg, pdm[:, 13, :], AF.Sigmoid)
            bw = self.bank()
            K.mm(bw, txw, w2t[0:64, :])
            K.tt(ldT, bw, ov[:, OV_W0:OV_W0 + 512], ALU.add)
            K.actf(ldT, ldT, AF.Sigmoid)
            K.ts(ldT, ldT, -float(np.exp(-0.5)), ALU.mult)
            ba = self.bank()
            for c in range(4):
                K.mm(ba[:, c * P:(c + 1) * P], a2t[64:128, c * 128:(c + 1) * 128], pdm[64:128, 12, :])
            for c in range(4):
                K.actf(aF[:, c, :], ba[:, c * P:(c + 1) * P], AF.Sigmoid, bias=ov[:, OV_A0 + c:OV_A0 + c + 1])
            bgm = self.bank()
            K.mm(bgm, sg, g2t)
            K.cp(gT, bgm, E=K.act)
            for c in range(4):
                K.ts(kp[:, c, :], aF[:, c, :], 1.0, ALU.subtract, ov[:, OV_KA + c:OV_KA + c + 1], ALU.mult)
                K.stt(kp[:, c, :], kp[:, c, :], 1.0, pdm[:, 4 + c, :], ALU.add, ALU.mult)
                K.ts(kq[:, c, :], pdm[:, 4 + c, :], ov[:, OV_KK + c:OV_KK + c + 1], ALU.mult, E=K.pool)
                K.stt(rkr[:, c, :], pdm[:, c, :], ov[:, OV_RK + c:OV_RK + c + 1], kp[:, c, :], ALU.mult, ALU.mult)
            K.tt(sqk, kq, kq, ALU.mult, E=K.pool)
            bss = self.bank()
            K.mm(bss, C["bones"], sqk)
            K.actf(sqk, bss.a.re("p (c t) -> p c t", c=4), AF.Ln, bias=self.epst)
            K.actf(sqk, sqk, AF.Exp, scale=-0.5)
            K.tt(kk, kq, sqk, ALU.mult)
            K.tt(bm, kk, aF, ALU.mult, E=K.pool)
            blg = self.bank()
            blx = self.bank()
            for c in range(4):
                K.mm(blg[:, c * P:(c + 1) * P], ldT[:, c * 128:(c + 1) * 128], C["tri"])
                K.mm(blx[:, c * P:(c + 1) * P], ldT[:, c * 128:(c + 1) * 128], C["tris"])
            K.cp(lg, blg.a.re("p (c t) -> p c t", c=4))
            K.cp(lgx, blx.a.re("p (c t) -> p c t", c=4), E=K.act)
            K.ts(nmid, lg[:, :, 63], -1.0, ALU.mult)
            for c in range(4):
                K.actf(e_r[:, c, :], lg[:, c, :], AF.Exp, bias=nmid[:, c:c + 1])
                K.actf(e_a[:, c, :], lgx[:, c, :], AF.Exp, bias=nmid[:, c:c + 1])
                K.actf(e_b[:, c, :], lg[:, c, :], AF.Exp, bias=lg[:, c, 63:64], scale=-1.0)
            K.actf(e_r0, lg, AF.Exp)
            K.actf(e_a0, lgx, AF.Exp)
            K.cp(gl4, e_r0[:, :, 127])
            K.tt(rt_, R_, e_r, ALU.mult)
            K.tt(r0_, R_, e_r0, ALU.mult, E=K.pool)
            K.stt(at_, kk, -1.0, e_a, ALU.mult, ALU.mult)
            K.stt(a0_, kk, -1.0, e_a0, ALU.mult, ALU.mult)
            K.tt(kt_, kp, e_b, ALU.mult)
            K.tt(bt_, bm, e_b, ALU.mult, E=K.pool)
            bed = self.bank()
            K.mm(bed, C["ustr"], ldT)
            K.actf(edlT, bed, AF.Exp)
            for (srcT, dstT, scl) in ((kp, kdT, True), (bm, bdT, True), (V_, vT, False)):
                b = self.bank()
                for c in range(4):
                    K.mm(b[:, c * 128:(c + 1) * 128], srcT[:, c, :], C["ident"])
                if scl:
                    K.tt(dstT, b, edlT, ALU.mult)
                else:
                    K.cp(dstT, b, E=K.act)
            brk = self.bank()
            for c in range(4):
                K.mm(brk[:, 0:8], rkr[:, c, :], ov[:, OV_SEL + c * 8:OV_SEL + (c + 1) * 8], start=(c == 0), stop=(c == 3))
            K.cp(rks, brk[:, 0:8])
            byd = self.ps[7]
            for hg in range(2):
                b1, b2, b3, b4, b5 = [self.bank() for _ in range(5)]
                for hh in range(4):
                    hd = hg * 4 + hh
                    hq, pr = hd // 2, slice((hd % 2) * 64, (hd % 2) * 64 + 64)
                    sl = slice(hh * 128, (hh + 1) * 128)
                    K.mm(b1[:, sl], bt_[pr, hq, :], at_[pr, hq, :])
                    K.mm(b2[:, sl], at_[pr, hq, :], bt_[pr, hq, :])
                    K.mm(b3[:, sl], kt_[pr, hq, :], at_[pr, hq, :])
                    K.mm(b4[:, sl], bt_[pr, hq, :], rt_[pr, hq, :])
                    K.mm(b5[:, sl], kt_[pr, hq, :], rt_[pr, hq, :])
                K.tt(AT[0], b1.a.re("p (h t) -> p h t", h=4), bc4(C["tris"]), ALU.mult)
                K.tt(A[0], b2.a.re("p (h t) -> p h t", h=4), bc4(C["ustr"]), ALU.mult)
                K.tt(AakT, b3.a.re("p (h t) -> p h t", h=4), bc4(C["tris"]), ALU.mult)
                K.tt(RbT, b4.a.re("p (h t) -> p h t", h=4), bc4(C["tri"]), ALU.mult)
                K.tt(RkT, b5.a.re("p (h t) -> p h t", h=4), bc4(C["tri"]), ALU.mult)
                K.tt(PT, AT[0], bc4(C["ident"]), ALU.add, E=K.pool)
                cur = 0
                for itr in range(6):
                    nxt = 1 - cur
                    c1 = self.bank()
                    c2 = self.bank()
                    for hh in range(4):
                        sl = slice(hh * 128, (hh + 1) * 128)
                        K.mm(c1[:, sl], AT[cur][:, hh, :], A[cur][:, hh, :])
                        if itr < 5:
                            K.mm(c2[:, sl], A[cur][:, hh, :], AT[cur][:, hh, :])
                    K.cp(A[nxt], c1.a.re("p (h t) -> p h t", h=4), E=K.act)
                    if itr < 5:
                        K.cp(AT[nxt], c2.a.re("p (h t) -> p h t", h=4), E=K.dve)
                    c3 = self.bank()
                    for hh in range(4):
                        K.mm(c3[:, hh * 128:(hh + 1) * 128], A[nxt][:, hh, :], PT[:, hh, :])
                    K.tt(PT, PT, c3.a.re("p (h t) -> p h t", h=4), ALU.add)
                    cur = nxt
                bx = self.bank()
                for hh in range(4):
                    hd = hg * 4 + hh
                    hq, pr = hd // 2, slice((hd % 2) * 64, (hd % 2) * 64 + 64)
                    vs = slice((hd % 2) * 64, (hd % 2) * 64 + 64)
                    K.mm(bx[:, hh * 64:(hh + 1) * 64], a0_[pr, hq, :], Hs[pr, hq, vs], start=True, stop=False)
                    K.mm(bx[:, hh * 64:(hh + 1) * 64], AakT[:, hh, :], vT[:, hd * 64:(hd + 1) * 64], start=False, stop=True)
                K.cp(Xs, bx[:, 0:256].re("p (h q) -> p h q", h=4))
                bu = self.bank()
                for hh in range(4):
                    K.mm(bu[:, hh * 64:(hh + 1) * 64], PT[:, hh, :], Xs[:, hh, :])
                K.cp(Us[:, hg * 4:(hg + 1) * 4, :], bu[:, 0:256].re("p (h q) -> p h q", h=4))
                for hh in range(4):
                    hd = hg * 4 + hh
                    hq, pr = hd // 2, slice((hd % 2) * 64, (hd % 2) * 64 + 64)
                    vs = slice((hd % 2) * 64, (hd % 2) * 64 + 64)
                    o = byd[:, hd * 64:(hd + 1) * 64]
                    K.mm(o, r0_[pr, hq, :], Hs[pr, hq, vs], start=True, stop=False)
                    K.mm(o, RbT[:, hh, :], Us[:, hd, :], start=False, stop=False)
                    K.mm(o, RkT[:, hh, :], vT[:, hd * 64:(hd + 1) * 64], start=False, stop=True)
            bh = self.bank()
            for hq in range(4):
                o = bh[:, hq * 128:(hq + 1) * 128]
                K.mm(o, bdT[:, hq * 128:(hq + 1) * 128], Us[:, 2 * hq:2 * hq + 2, :].re("p h q -> p (h q)"), start=True, stop=False)
                K.mm(o, kdT[:, hq * 128:(hq + 1) * 128], vT[:, hq * 128:(hq + 1) * 128], start=False, stop=True)
            for hq in range(4):
                K.stt(Hs[:, hq, :], Hs[:, hq, :], gl4[:, hq:hq + 1], bh[:, hq * 128:(hq + 1) * 128], ALU.mult, ALU.add)
            K.cp(yd, byd.a.re("p (h q) -> p h q", h=8), E=K.act)
            K.red(mean8, yd)
            K.ts(mean8, mean8, 1.0 / 64.0, ALU.mult)
            K.tt(ym, yd, bcl(mean8, 8, 64), ALU.subtract)
            K.tt(yd, ym, ym, ALU.mult, E=K.pool)
            K.red(var8, yd)
            K.actf(var8, var8, AF.Sqrt, bias=gneps, scale=1.0 / 64.0)
            recip(var8)
            K.tt(ym, ym, bcl(var8, 8, 64), ALU.mult)
            ymf = _v(ym).re("p h q -> p (h q)")
            K.tt(ymf, ymf, ov[:, OV_GNW:OV_GNW + 512], ALU.mult, E=K.pool)
            K.tt(ymf, ymf, ov[:, OV_GNB:OV_GNB + 512], ALU.add)
            K.tt(yd, vT.a.re("p (h q) -> p h q", h=8), bcl(rks, 8, 64), ALU.mult)
            K.tt(ym, ym, yd, ALU.add)
            K.tt(ydT, ymf, gT, ALU.mult)
            btr = self.bank()
            for hh in range(4):
                K.mm(btr[:, hh * 128:(hh + 1) * 128], ydT[:, hh * 128:(hh + 1) * 128], ID)
            K.cp(oFM[:, 4:8, :], btr.a.re("p (h t) -> p h t", h=4), E=K.act)
            for half in range(2):
                b = self.bank()
                for j in range(4):
                    n = half * 4 + j
                    wo = woblock(n)
                    for c in range(8):
                        K.mm(b[:, j * P:(j + 1) * P], wo(c), oFM[:, c, :], start=(c == 0), stop=(c == 7))
                K.tt(h[:, half * 4:(half + 1) * 4, :], h[:, half * 4:(half + 1) * 4, :],
                     b.a.re("p (j t) -> p j t", j=4), ALU.add)
            K.dma(K.act, self.hview(dst, it * P, P), h)
        K.barrier()
        es.close()

    run_variant(True, [(s_, s_ * tps) for s_ in range(NSEQ)])
    run_variant(False, [(s_, s_ * tps + j) for s_ in range(NSEQ) for j in range(1, tps)])
    K.barrier()
    esP.close()


Model.odd_phase = _odd_phase


def kernel(**inputs):
    inp = {k: np.asarray(v) for k, v in inputs.items()}
    x = inp["x"]
    M = Model(n_layers=4, seq=x.shape[1], mix=True)
    nc = M.build()
    maps = make_in_maps(inp, M, x, 8)
    res = run_bass_kernel_spmd(nc, maps, core_ids=list(range(8)))
    out = np.concatenate([r["out"].reshape(NSEQ, x.shape[1], D) for r in res.results], axis=0)
    return out.astype(np.float32)
```

```python
import contextlib
import os
import numpy as np
import ml_dtypes
import concourse.bass as bass
import concourse.mybir as mybir
from concourse.bass_utils import run_bass_kernel_spmd

F32 = mybir.dt.float32
BF16 = mybir.dt.bfloat16
AF = mybir.ActivationFunctionType
ALU = mybir.AluOpType
AX = mybir.AxisListType

EPOCH = 30000


class Tl:
    def __init__(self, h, name):
        self.h = h
        self.name = name
        self.w = None
        self.r = {}
        self.dsem = None
        self.dcnt = 0
        self.psum = False

    def __getitem__(self, idx):
        return V(self, self.h[idx])

    @property
    def a(self):
        return V(self, self.h[:])


class V:
    def __init__(self, t, ap):
        self.t = t
        self.ap = ap

    def __getitem__(self, idx):
        return V(self.t, self.ap[idx])

    def bitcast(self, dt):
        return V(self.t, self.ap.bitcast(dt))

    def bc(self, shape):
        return V(self.t, self.ap.to_broadcast(list(shape)))

    def re(self, s, **kw):
        return V(self.t, self.ap.rearrange(s, **kw))


def _v(x):
    return x.a if isinstance(x, Tl) else x


class Eng:
    def __init__(self, K, name, eng):
        self.K = K
        self.name = name
        self.eng = eng
        self.epoch = -1
        self.cnt = EPOCH
        self.sem = None
        self.known = {}

    def tick(self, ins):
        if self.cnt >= EPOCH:
            self.epoch += 1
            self.cnt = 0
            self.sem = self.K.new_sem(f"s_{self.name}_{self.epoch}")
        self.cnt += 1
        ins.then_inc(self.sem, 1)
        return (id(self.sem), self.sem, self.cnt, self.name)


class KB:
    def __init__(self, nc):
        self.nc = nc
        self.es = contextlib.ExitStack()
        self.pe = Eng(self, "pe", nc.tensor)
        self.dve = Eng(self, "dve", nc.vector)
        self.act = Eng(self, "act", nc.scalar)
        self.pool = Eng(self, "pool", nc.gpsimd)
        self.sp = Eng(self, "sp", nc.sync)
        self.engs = [self.pe, self.dve, self.act, self.pool, self.sp]
        self.nsem = 0
        self.ntile = 0
        self.all_dma = {}
        self.ninst = 0

    def new_sem(self, name):
        self.nsem += 1
        return self.es.enter_context(self.nc.semaphore(f"{name}_{self.nsem}"))

    def sb(self, shape, dt=F32, name="t", es=None):
        self.ntile += 1
        nm = f"{name}_{self.ntile}"
        h = (es or self.es).enter_context(self.nc.sbuf_tensor(nm, list(shape), dt))
        return Tl(h, nm)

    def psum(self, shape, dt=F32, name="p", es=None):
        self.ntile += 1
        nm = f"{name}_{self.ntile}"
        h = (es or self.es).enter_context(self.nc.psum_tensor(nm, list(shape), dt))
        t = Tl(h, nm)
        t.psum = True
        return t

    def dram(self, name, shape, dt=F32, kind="Internal"):
        h = self.nc.dram_tensor(name, list(shape), dt, kind=kind)
        return Tl(h.ap(), name)

    def _wait(self, E, toks):
        best = {}
        for tk in toks:
            if tk is None:
                continue
            sid, sem, val, who = tk
            if who == E.name and E is self.pe:
                continue
            if E.known.get(sid, 0) >= val:
                continue
            if best.get(sid, (None, 0))[1] < val:
                best[sid] = (sem, val)
        for sid, (sem, val) in best.items():
            E.eng.wait_ge(sem, val)
            E.known[sid] = val
            self.ninst += 1

    skip = False

    def op(self, E, fn, reads, writes):
        if self.skip:
            return None
        toks = []
        rt = [(_v(x).t) for x in reads if x is not None and not isinstance(x, (int, float))]
        wt = [(_v(x).t) for x in writes if x is not None]
        for t in rt:
            toks.append(t.w)
            if t.psum:
                for k, tk in t.r.items():
                    if k != E.name:
                        toks.append(tk)
        for t in wt:
            toks.append(t.w)
            for k, tk in t.r.items():
                if k == E.name:
                    continue
                toks.append(tk)
        self._wait(E, toks)
        ins = fn()
        tok = E.tick(ins)
        self.ninst += 1
        for t in rt:
            t.r[E.name] = tok
        for t in wt:
            t.w = tok
            t.r = {}
        return tok

    def dma(self, Q, out, in_):
        out = _v(out)
        in_ = _v(in_)
        ot, it = out.t, in_.t
        toks = [it.w, ot.w] + list(ot.r.values())
        self._wait(Q, toks)
        if ot.dsem is None:
            ot.dsem = self.new_sem("d_" + ot.name)
        ins = Q.eng.dma_start(out=out.ap, in_=in_.ap)
        ins.then_inc(ot.dsem, 16)
        ot.dcnt += 16
        tok = (id(ot.dsem), ot.dsem, ot.dcnt, "dma")
        self.ninst += 1
        ot.w = tok
        ot.r = {}
        it.r["dma%d" % id(ot.dsem)] = tok
        self.all_dma[id(ot.dsem)] = tok
        return tok

    def barrier(self):
        toks = list(self.all_dma.values())
        for E in self.engs:
            if E.sem is not None and E.cnt > 0:
                toks.append((id(E.sem), E.sem, E.cnt, E.name + "_b"))
        for E in self.engs:
            self._wait(E, toks)

    def mm(self, out, lhsT, rhs, start=True, stop=True, ser=False):
        out, lhsT, rhs = _v(out), _v(lhsT), _v(rhs)
        if ser and not self.skip and self.pe.sem is not None and self.pe.cnt > 0:
            self.pe.eng.wait_ge(self.pe.sem, self.pe.cnt)
        return self.op(self.pe, lambda: self.nc.tensor.matmul(out.ap, lhsT.ap, rhs.ap, start=start, stop=stop),
                       [lhsT, rhs], [out])

    def actf(self, out, in_, func, bias=None, scale=None, accum=None, E=None):
        out, in_ = _v(out), _v(in_)
        kw = {}
        rd = [in_]
        if bias is not None:
            if isinstance(bias, (int, float)):
                kw["bias"] = float(bias)
            else:
                bias = _v(bias)
                kw["bias"] = bias.ap
                rd.append(bias)
        if scale is not None:
            if isinstance(scale, (int, float)):
                kw["scale"] = float(scale)
            else:
                scale = _v(scale)
                kw["scale"] = scale.ap
                rd.append(scale)
        wr = [out]
        if accum is not None:
            accum = _v(accum)
            kw["accum_out"] = accum.ap
            wr.append(accum)
        return self.op(self.act, lambda: self.nc.scalar.activation(out.ap, in_.ap, func, **kw), rd, wr)

    def _ve(self, E):
        return E or self.dve

    def _sc(self, s, rd):
        if isinstance(s, (int, float)):
            return float(s)
        s = _v(s)
        rd.append(s)
        return s.ap

    def ts(self, out, in0, s1, op0, s2=None, op1=None, E=None, accum=None):
        E = self._ve(E)
        out, in0 = _v(out), _v(in0)
        rd = [in0]
        a1 = self._sc(s1, rd)
        a2 = None if s2 is None else self._sc(s2, rd)
        wr = [out]
        kw = {}
        if accum is not None:
            accum = _v(accum)
            kw["accum_out"] = accum.ap
            wr.append(accum)
        if op1 is None:
            return self.op(E, lambda: E.eng.tensor_scalar(out.ap, in0.ap, a1, None, op0, **kw), rd, wr)
        return self.op(E, lambda: E.eng.tensor_scalar(out.ap, in0.ap, a1, a2, op0, op1, **kw), rd, wr)

    def tt(self, out, in0, in1, op, E=None):
        E = self._ve(E)
        out, in0, in1 = _v(out), _v(in0), _v(in1)
        return self.op(E, lambda: E.eng.tensor_tensor(out.ap, in0.ap, in1.ap, op), [in0, in1], [out])

    def stt(self, out, in0, s, in1, op0, op1, E=None):
        E = self.dve
        out, in0, in1 = _v(out), _v(in0), _v(in1)
        rd = [in0, in1]
        a = self._sc(s, rd)
        return self.op(E, lambda: E.eng.scalar_tensor_tensor(out.ap, in0.ap, a, in1.ap, op0, op1), rd, [out])

    def cp(self, out, in_, E=None):
        E = self._ve(E)
        out, in_ = _v(out), _v(in_)
        if E is self.act:
            return self.op(E, lambda: self.nc.scalar.copy(out.ap, in_.ap), [in_], [out])
        return self.op(E, lambda: E.eng.tensor_copy(out.ap, in_.ap), [in_], [out])

    def memset(self, out, val, E=None):
        E = self._ve(E)
        out = _v(out)
        return self.op(E, lambda: E.eng.memset(out.ap, val), [], [out])

    def red(self, out, in_, op=ALU.add, E=None):
        E = self._ve(E)
        out, in_ = _v(out), _v(in_)
        return self.op(E, lambda: E.eng.tensor_reduce(out.ap, in_.ap, AX.X, op), [in_], [out])


D = 1024
NSEQ = 2
T = 256
CH = 128
EVEN_IN = 3608
ODD_IN = 3336
NEPS = 1e-6


def host_consts():
    i = np.arange(128)
    c = {}
    c["ident"] = np.eye(128, dtype=np.float32)
    c["tri"] = (i[:, None] <= i[None, :]).astype(np.float32)
    c["tris"] = (i[:, None] < i[None, :]).astype(np.float32)
    c["ustr"] = (i[:, None] > i[None, :]).astype(np.float32)
    c["ones"] = np.ones((128, 128), np.float32)
    c["bones"] = ((i[:, None] // 64) == (i[None, :] // 64)).astype(np.float32)
    return c


CONST_NAMES = ["ident", "tri", "tris", "ustr", "ones", "bones"]


class Model:
    def __init__(self, n_layers=4, seq=2048, mix=True, dbg=False):
        self.L = n_layers
        self.seq = seq
        self.NT = NSEQ * seq
        self.mix = mix
        self.dbg = dbg

    def build(self):
        nc = bass.Bass("TRN2", target_bir_lowering=False)
        self.nc = nc
        K = KB(nc)
        self.K = K
        L, NT = self.L, self.NT
        ne = (L + 1) // 2
        no = L // 2
        self.x = K.dram("x", [NT, D], F32, kind="ExternalInput")
        self.out = K.dram("out", [NT, D], F32, kind="ExternalOutput")
        self.hA = K.dram("hA", [D, NT], F32)
        self.hB = K.dram("hB", [D, NT], F32)
        self.cst_d = K.dram("consts", [128, len(CONST_NAMES) * 128], F32, kind="ExternalInput")
        self.w1 = K.dram("mlp_w1", [L, D, 4 * D], F32, kind="ExternalInput")
        self.w2 = K.dram("mlp_w2", [L, 4 * D, D], F32, kind="ExternalInput")
        self.nvec = 3 * L + 1
        self.vec_d = K.dram("vecs", [128, 8 * (2 * L + 1)], F32, kind="ExternalInput")
        if self.mix:
            self.declare_mixer_inputs(ne, no)
        self.cst = K.sb([128, len(CONST_NAMES) * 128], F32, "cst")
        K.dma(K.sp, self.cst, self.cst_d)
        self.C = {n: self.cst[:, i * 128:(i + 1) * 128] for i, n in enumerate(CONST_NAMES)}
        self.cstb = K.sb([128, len(CONST_NAMES) * 128], BF16, "cstb")
        K.cp(self.cstb, self.cst)
        self.CB = {n: self.cstb[:, i * 128:(i + 1) * 128] for i, n in enumerate(CONST_NAMES)}
        self.vec = K.sb([128, 8 * (2 * L + 1)], F32, "vec")
        K.dma(K.sp, self.vec, self.vec_d)
        self.epst = K.sb([128, 1], F32, "eps")
        K.memset(self.epst, NEPS)
        self.ps = [K.psum([128, 512], F32, "bank") for _ in range(8)]
        self.psi = 0

        self.phase0()
        src, dst = self.hA, self.hB
        for l in range(L):
            if self.mix:
                K.barrier()
                if l % 2 == 0:
                    self.even_phase(l, src, dst)
                elif os.environ.get("NOODD"):
                    src, dst = dst, src
                else:
                    self.odd_phase(l, src, dst)
                src, dst = dst, src
            K.barrier()
            self.mlp_phase(l, src, dst)
            src, dst = dst, src
        K.barrier()
        self.final_phase(src)
        K.barrier()
        return nc

    def bank(self):
        b = self.ps[self.psi % 7]
        self.psi += 1
        return b

    def hview(self, hd, t0, n):
        return hd.a.re("(c p) t -> p c t", p=128)[:, :, t0:t0 + n]

    def phase0(self):
        K = self.K
        es = contextlib.ExitStack()
        xt = [K.sb([128, 2, D], F32, "xt", es) for _ in range(2)]
        ht = [K.sb([128, 8, T], F32, "ht", es) for _ in range(2)]
        for i in range(self.NT // T):
            xs, hs = xt[i % 2], ht[i % 2]
            K.dma(K.sp, xs, self.x.a[i * T:(i + 1) * T, :].re("(s p) f -> p s f", p=128))
            for s in range(2):
                for half in range(2):
                    b = self.bank()
                    for j in range(4):
                        c = half * 4 + j
                        K.mm(b[:, j * 128:(j + 1) * 128], xs[:, s, c * 128:(c + 1) * 128], self.C["ident"])
                    dst = hs[:, half * 4:(half + 1) * 4, s * 128:(s + 1) * 128]
                    K.cp(dst, b.a.re("p (j t) -> p j t", j=4), E=(K.dve if half == 0 else K.act))
            K.dma(K.act, self.hview(self.hA, i * T, T), hs)
        K.barrier()
        es.close()

    def rmsnorm_fm(self, h, gain, hn, sq, rp, n=T):
        K = self.K
        K.actf(sq, h, AF.Square)
        b = self.bank()
        for c in range(8):
            K.mm(b[:, 0:n], self.CB["ones"], _v(sq)[:, c, :], start=(c == 0), stop=(c == 7))
        K.actf(rp, b[:, 0:n], AF.Sqrt, bias=self.epst, scale=1.0 / D)
        K.op(K.dve, lambda: self.nc.vector.reciprocal(_v(rp).ap, _v(rp).ap), [rp], [rp])
        for c in range(8):
            for o in (hn if isinstance(hn, (list, tuple)) else [hn]):
                K.stt(_v(o)[:, c, :], _v(h)[:, c, :], gain[:, c:c + 1], rp, ALU.mult, ALU.mult,
                      E=(K.dve if c % 2 == 0 else K.pool))

    def mlp_phase(self, l, src, dst):
        K = self.K
        P = 128
        gain = self.vec[:, 8 * (self.L + l):8 * (self.L + l + 1)]
        W1 = self.w1.a[l]
        W2 = self.w2.a[l]
        tiles = []
        for s_ in range(NSEQ):
            if self.seq > P:
                tiles.append((s_ * self.seq + P, min(T, self.seq) - P))
            for j in range(1, self.seq // T):
                tiles.append((s_ * self.seq + j * T, T))
        esW = contextlib.ExitStack()
        if tiles:
            w1 = K.sb([128, 8, 4 * D], BF16, "w1", esW)
            w2 = K.sb([128, 32, D], BF16, "w2", esW)
            for k in range(8):
                K.dma(K.pool, w1[:, k, :], W1[k * 128:(k + 1) * 128, :])
            for g in range(4):
                K.dma(K.pool, w2[:, g * 8:(g + 1) * 8, :],
                      W2[g * 1024:(g + 1) * 1024, :].re("(m p) n -> p m n", p=128))
        es = contextlib.ExitStack()
        NB = NSEQ * P
        h = K.sb([128, 8, NB], F32, "h", es)
        hn32 = K.sb([128, 8, NB], F32, "hn32", es)
        sq = K.sb([128, 8, NB], BF16, "sq", es)
        rp = K.sb([128, NB], F32, "rp", es)
        act32 = K.sb([128, 32, NB], F32, "act32", es)
        tmp32 = K.sb([128, 2, NB], F32, "tmp32", es)
        stg = [K.sb([128, 2048], F32, "stg", es) for _ in range(2)]
        ns = 0
        for s_ in range(NSEQ):
            K.dma(K.sp, h[:, :, s_ * P:(s_ + 1) * P], self.hview(src, s_ * self.seq, P))
        self.rmsnorm_fm(h, gain, hn32, sq, rp, n=NB)
        for mb in range(16):
            st = stg[ns % 2].a.re("p (k n) -> p k n", k=8)
            ns += 1
            srcw = W1[:, mb * 256:(mb + 1) * 256].re("(k p) n -> p k n", p=128)
            K.dma(K.sp, st, srcw)
            b = self.bank()
            for j in range(2):
                for k in range(8):
                    K.mm(b[:, j * NB:(j + 1) * NB], st[:, k, j * 128:(j + 1) * 128], hn32[:, k, :],
                         start=(k == 0), stop=(k == 7))
            K.actf(tmp32, b.a.re("p (j t) -> p j t", j=2), AF.Relu)
            K.tt(act32[:, mb * 2:mb * 2 + 2, :], tmp32, tmp32, ALU.mult, E=K.pool)
        for n in range(8):
            b = self.bank()
            for hf in range(2):
                st = stg[ns % 2].a.re("p (m n) -> p m n", m=16)
                ns += 1
                srcw = W2[hf * 2048:(hf + 1) * 2048, n * 128:(n + 1) * 128].re("(m p) n -> p m n", p=128)
                K.dma(K.sp, st, srcw)
                for m in range(16):
                    mm_ = hf * 16 + m
                    K.mm(b[:, 0:NB], st[:, m, :], act32[:, mm_, :], start=(mm_ == 0), stop=(mm_ == 31))
            K.tt(h[:, n, :], h[:, n, :], b[:, 0:NB], ALU.add)
        for s_ in range(NSEQ):
            K.dma(K.act, self.hview(dst, s_ * self.seq, P), h[:, :, s_ * P:(s_ + 1) * P])
        K.barrier()
        es.close()
        if not tiles:
            esW.close()
            return
        es = esW
        hts = [K.sb([128, 8, T], F32, "h", es) for _ in range(2)]
        hns = [K.sb([128, 8, T], BF16, "hn", es) for _ in range(2)]
        sqs = [K.sb([128, 8, T], BF16, "sq", es) for _ in range(2)]
        rps = [K.sb([128, T], F32, "rp", es) for _ in range(2)]
        act = K.sb([128, 32, T], BF16, "act", es)
        tmp = [K.sb([128, 2, T], BF16, "tmp", es) for _ in range(2)]
        nb_ = {}

        def norm_a(i):
            n = tiles[i][1]
            K.actf(sqs[i % 2][:, :, 0:n], hts[i % 2][:, :, 0:n], AF.Square)

        def norm_b(i):
            n = tiles[i][1]
            b = self.bank()
            nb_[i] = b
            for c in range(8):
                K.mm(b[:, 0:n], self.CB["ones"], sqs[i % 2][:, c, 0:n], start=(c == 0), stop=(c == 7))

        def norm_c(i):
            n = tiles[i][1]
            rp = rps[i % 2][:, 0:n]
            K.actf(rp, nb_[i][:, 0:n], AF.Sqrt, bias=self.epst, scale=1.0 / D)
            K.op(K.dve, lambda: self.nc.vector.reciprocal(rp.ap, rp.ap), [rp], [rp])
            for c in range(8):
                K.stt(hns[i % 2][:, c, 0:n], hts[i % 2][:, c, 0:n], gain[:, c:c + 1], rp, ALU.mult, ALU.mult)

        K.dma(K.sp, hts[0][:, :, 0:tiles[0][1]], self.hview(src, tiles[0][0], tiles[0][1]))
        norm_a(0)
        norm_b(0)
        norm_c(0)
        for i, (t0, n) in enumerate(tiles):
            h = hts[i % 2]
            hn = hns[i % 2]
            nxt = i + 1 < len(tiles)
            if nxt:
                t1, n1 = tiles[i + 1]
                K.dma(K.sp, hts[(i + 1) % 2][:, :, 0:n1], self.hview(src, t1, n1))
            for mp in range(16):
                b = self.bank()
                for j in range(2):
                    m = mp * 2 + j
                    for k in range(8):
                        K.mm(b[:, j * T:j * T + n], w1[:, k, m * 128:(m + 1) * 128], hn[:, k, 0:n],
                             start=(k == 0), stop=(k == 7))
                tm = tmp[mp % 2]
                K.actf(tm[:, :, 0:n], b.a.re("p (j t) -> p j t", j=2)[:, :, 0:n], AF.Relu)
                K.tt(act[:, mp * 2:mp * 2 + 2, 0:n], tm[:, :, 0:n], tm[:, :, 0:n], ALU.mult, E=K.pool)
            if nxt:
                norm_a(i + 1)
            for npair in range(4):
                b = self.bank()
                for j in range(2):
                    nn = npair * 2 + j
                    for m in range(32):
                        K.mm(b[:, j * T:j * T + n], w2[:, m, nn * 128:(nn + 1) * 128], act[:, m, 0:n],
                             start=(m == 0), stop=(m == 31))
                K.tt(h[:, npair * 2:npair * 2 + 2, 0:n], h[:, npair * 2:npair * 2 + 2, 0:n],
                     b.a.re("p (j t) -> p j t", j=2)[:, :, 0:n], ALU.add)
                if nxt and npair == 1:
                    norm_b(i + 1)
                if nxt and npair == 2:
                    norm_c(i + 1)
            K.dma(K.act, self.hview(dst, t0, n), h[:, :, 0:n])
        K.barrier()
        es.close()

    def final_phase(self, src):
        K = self.K
        es = contextlib.ExitStack()
        hts = [K.sb([128, 8, T], F32, "h", es) for _ in range(2)]
        hn = K.sb([128, 8, T], F32, "hn", es)
        sq = K.sb([128, 8, T], BF16, "sq", es)
        rp = K.sb([128, T], F32, "rp", es)
        ot = [K.sb([128, 2, D], F32, "ot", es) for _ in range(2)]
        gain = self.vec[:, 8 * (2 * self.L):8 * (2 * self.L + 1)]
        ntile = self.NT // T
        K.dma(K.sp, hts[0], self.hview(src, 0, T))
        for i in range(ntile):
            h = hts[i % 2]
            if i + 1 < ntile:
                K.dma(K.sp, hts[(i + 1) % 2], self.hview(src, (i + 1) * T, T))
            self.rmsnorm_fm(h, gain, hn, sq, rp)
            o = ot[i % 2]
            for s in range(2):
                for half in range(2):
                    b = self.bank()
                    for j in range(4):
                        c = half * 4 + j
                        K.mm(b[:, j * 128:(j + 1) * 128], hn[:, c, s * 128:(s + 1) * 128], self.C["ident"])
                    K.cp(o[:, s, half * 512:(half + 1) * 512], b, E=(K.dve if half == 0 else K.act))
            K.dma(K.act, self.out.a[i * T:(i + 1) * T, :].re("(s p) f -> p s f", p=128), o)
        K.barrier()
        es.close()


def host_vecs(inp, L):
    cols = []
    for l in range(L):
        cols.append(np.asarray(inp["norm_mix_w"][l], np.float32).reshape(8, 128).T)
    for l in range(L):
        cols.append(np.asarray(inp["norm_mlp_w"][l], np.float32).reshape(8, 128).T)
    cols.append(np.asarray(inp["final_norm_w"], np.float32).reshape(8, 128).T)
    return np.ascontiguousarray(np.concatenate(cols, axis=1))


def make_in_maps(inp, M, x, ncores):
    L = M.L
    consts = host_consts()
    cst = np.ascontiguousarray(np.concatenate([consts[n] for n in CONST_NAMES], axis=1))
    vecs = host_vecs(inp, L)
    shared = {"consts": cst, "vecs": vecs,
              "mlp_w1": np.ascontiguousarray(inp["mlp_w1"][:L]), "mlp_w2": np.ascontiguousarray(inp["mlp_w2"][:L])}
    if M.mix:
        shared.update(host_mixer_inputs(inp, L))
    maps = []
    for c in range(ncores):
        d = dict(shared)
        d["x"] = np.ascontiguousarray(x[2 * c:2 * c + 2]).reshape(M.NT, D)
        maps.append(d)
    return maps


EV_CONV = 0
EV_ALOG = 48
EV_DTB = 52
EV_GDNW = 56
EV_GLAW = 568
EV_W2B = 1080
EV_N = 1336


def host_even_vec(inp, i):
    v = np.zeros((128, EV_N), np.float32)
    cw = np.asarray(inp["gdn_conv_w"][i], np.float32)
    v[:, EV_CONV:EV_CONV + 48] = cw.T.reshape(12, 128, 4).transpose(1, 0, 2).reshape(128, 48)
    v[:, EV_ALOG:EV_ALOG + 4] = np.asarray(inp["gdn_a_log"][i])[None, :]
    v[:, EV_DTB:EV_DTB + 4] = np.asarray(inp["gdn_dt_bias"][i])[None, :]
    v[:, EV_GDNW:EV_GDNW + 512] = np.tile(np.asarray(inp["gdn_norm_w"][i]), 4)[None, :]
    v[:, EV_GLAW:EV_GLAW + 512] = np.tile(np.asarray(inp["gla_norm_w"][i]), 4)[None, :]
    v[0:16, EV_W2B:EV_W2B + 256] = np.asarray(inp["gla_gate_w2"][i])
    v[16, EV_W2B:EV_W2B + 256] = np.asarray(inp["gla_gate_b"][i])
    return v


def _even_phase(self, l, src, dst):
    K = self.K
    nc = self.nc
    i_l = l // 2
    esP = contextlib.ExitStack()
    C, CB = self.C, self.CB
    P = 128
    tps = self.seq // P
    WIN = self.even_w_in.a[i_l]
    WOUT = self.even_w_out.a[i_l]

    ev = K.sb([128, EV_N], F32, "ev", esP)
    K.dma(K.sp, ev, self.evec.a[i_l])
    nexpa = K.sb([128, 4], F32, "nexpa", esP)
    K.actf(nexpa, ev[:, EV_ALOG:EV_ALOG + 4], AF.Exp)
    K.ts(nexpa, nexpa, -1.0, ALU.mult)
    convw = ev[:, EV_CONV:EV_CONV + 48]
    gain = self.vec[:, 8 * l:8 * (l + 1)]
    SAs = [K.sb([128, 4, 128], F32, "SA", esP) for _ in range(NSEQ)]
    SBs = [K.sb([128, 2, 256], F32, "SB", esP) for _ in range(NSEQ)]
    hist = [K.sb([128, 12, 3], F32, "hist", esP) for _ in range(NSEQ)]
    for s_ in range(NSEQ):
        K.memset(SAs[s_], 0.0)
        K.memset(SBs[s_], 0.0)
        K.memset(hist[s_], 0.0)

    def bc4(v):
        return v.re("p (o t) -> p o t", o=1).bc([128, 4, 128])

    def bcl(v):
        return _v(v).re("p (h o) -> p h o", o=1).bc([128, 4, 128])

    def run_variant(hp, tiles):
        if not tiles:
            return
        es = contextlib.ExitStack()
        AD = F32 if hp else BF16
        ID = C["ident"] if hp else CB["ident"]

        def t512(name, dt=F32):
            return K.sb([128, 4, 128], dt, name, es)

        if hp:
            stg = [K.sb([128, 8, 512], F32, "stg", es) for _ in range(2)]
            cnt = [0]

            def _stream(dram2d, col0, n):
                t = stg[cnt[0] % 2]
                cnt[0] += 1
                src_ = dram2d[:, col0:col0 + n].re("(k p) n -> p k n", p=128)
                K.dma(K.sp, t[:, :, 0:n], src_)
                return lambda k: t[:, k, 0:n]

            wblock = lambda col0, n: _stream(WIN, col0, n)
            wblock32 = wblock
            woblock = lambda n0: _stream(WOUT, n0 * 128, 128)
        else:
            w_in = K.sb([128, 8, EVEN_IN], BF16, "w_in", es)
            for k in range(8):
                K.dma(K.pool, w_in[:, k, :], WIN[k * 128:(k + 1) * 128, :])
            w_out = K.sb([128, 8, D], BF16, "w_out", es)
            K.dma(K.pool, w_out, WOUT.re("(c p) n -> p c n", p=128))
            wg = K.sb([128, 8, 24], F32, "wg", es)
            for k in range(8):
                K.dma(K.sp, wg[:, k, 0:8], WIN[k * 128:(k + 1) * 128, 2048:2056])
                K.dma(K.sp, wg[:, k, 8:24], WIN[k * 128:(k + 1) * 128, 3592:3608])
            wblock = lambda col0, n: (lambda k: w_in[:, k, col0:col0 + n])
            wblock32 = lambda col0, n: (lambda k: wg[:, k, 0:8]) if col0 == 2048 else (lambda k: wg[:, k, 8:24])
            woblock = lambda n0: (lambda c: w_out[:, c, n0 * 128:(n0 + 1) * 128])

        hts = [K.sb([128, 8, P], F32, "h", es) for _ in range(2)]
        hn32 = K.sb([128, 8, P], F32, "hn32", es)
        hn = hn32 if hp else K.sb([128, 8, P], BF16, "hn", es)
        sq = K.sb([128, 8, P], BF16, "sq", es)
        rp = K.sb([128, P], F32, "rp", es)
        pre = K.sb([128, 12, P + 3], F32, "pre", es)
        cacc = K.sb([128, 12, P], F32, "cacc", es)
        qkv = K.sb([128, 12, P], F32, "qkv", es)
        sq2 = K.sb([128, 8, P], BF16, "sq2", es)
        rinv = K.sb([128, 8, P], F32, "rinv", es)
        oFM = K.sb([128, 8, P], AD, "oFM", es)
        g1 = K.sb([32, P], F32, "g1", es)
        K.memset(g1, 1.0)
        loga = K.sb([128, 256], F32, "loga", es)
        gkT = K.sb([128, 256], F32, "gkT", es)
        qkF = K.sb([128, 4, 128], F32, "qkF", es)
        gcf = K.sb([128, 2, 128], F32, "gcf", es)
        nmid = K.sb([128, 2], F32, "nmid", es)
        eq = K.sb([128, 2, 128], F32, "eq", es)
        ek = K.sb([128, 2, 128], F32, "ek", es)
        eq0 = K.sb([128, 2, 128], F32, "eq0", es)
        qgm = K.sb([128, 2, 128], F32, "qgm", es)
        kgm = K.sb([128, 2, 128], F32, "kgm", es)
        qg0 = K.sb([128, 2, 128], AD, "qg0", es)
        edlB = K.sb([128, 256], F32, "edlB", es)
        kdB = K.sb([128, 256], AD, "kdB", es)
        gv = K.sb([128, 512], AD, "gv", es)
        grs = K.sb([128, 512], F32, "grs", es)
        zs = K.sb([128, 512], F32, "zs", es)
        attm = t512("attm", AD)
        SBb = None if hp else K.sb([128, 2, 256], BF16, "SBb", es)
        junk = K.sb([128, 128], F32, "junk", es)
        ssB = K.sb([128, 4], F32, "ssB", es)
        Gt = K.sb([128, 512], F32, "Gt", es)
        obT = K.sb([128, 512], AD, "obT", es)
        gr8 = K.sb([128, 8], F32, "gr8", es)
        beta = K.sb([128, 4], F32, "beta", es)
        nbeta = K.sb([128, 4], F32, "nbeta", es)
        gA = K.sb([128, 4], F32, "gA", es)
        gcc = K.sb([128, 4], F32, "gcc", es)
        eg = K.sb([128, 4], F32, "eg", es)
        egl = K.sb([128, 4], F32, "egl", es)
        edl = K.sb([128, 4], F32, "edl", es)
        beg = K.sb([128, 4], F32, "beg", es)
        Gtri = t512("Gtri")
        Dm = t512("Dm")
        DTm = t512("DTm")
        A = [t512("A0"), t512("A1")]
        AT = [t512("AT0"), t512("AT1")]
        PT = t512("PT")
        aqkT = t512("aqkT")
        vb = t512("vb")
        kbg = t512("kbg")
        kdA = t512("kdA")
        wneg = t512("wneg")
        vn = t512("vn")
        qg = t512("qg")
        ssA = K.sb([128, 4], F32, "ssA", es)
        oaT = K.sb([128, 512], AD, "oaT", es)

        for ti, (s_, it) in enumerate(tiles):
            SA, SB_ = SAs[s_], SBs[s_]
            h = hts[ti % 2]
            if ti == 0:
                K.dma(K.sp, h, self.hview(src, it * P, P))
            if ti + 1 < len(tiles):
                K.dma(K.sp, hts[(ti + 1) % 2], self.hview(src, tiles[ti + 1][1] * P, P))
            K.cp(pre[:, :, 0:3], hist[s_], E=K.pool)
            if not hp:
                K.cp(SBb, SB_, E=K.pool)
            Sst = SB_ if hp else SBb
            self.rmsnorm_fm(h, gain, [hn32] if hp else [hn, hn32], sq, rp, n=P)
            for c4 in range(3):
                wb = wblock(c4 * 512, 512)
                b = self.bank()
                for j in range(4):
                    for k in range(8):
                        K.mm(b[:, j * P:(j + 1) * P], wb(k)[:, j * 128:(j + 1) * 128], hn[:, k, :],
                             start=(k == 0), stop=(k == 7))
                K.cp(pre[:, c4 * 4:(c4 + 1) * 4, 3:3 + P], b.a.re("p (j t) -> p j t", j=4), E=K.act)
            shared = {}
            wb = wblock32(2048, 8)
            bg = self.bank()
            for k in range(8):
                K.mm(bg[:, 0:8], hn32[:, k, :], wb(k), start=(k == 0), stop=(k == 7))
            K.cp(gr8, bg[:, 0:8])
            for c in range(12):
                K.actf(cacc[:, c, :], pre[:, c, 0:P], AF.Copy, scale=convw[:, c * 4:c * 4 + 1])
                for j in range(1, 4):
                    K.stt(cacc[:, c, :], pre[:, c, j:j + P], convw[:, c * 4 + j:c * 4 + j + 1], cacc[:, c, :],
                          ALU.mult, ALU.add)
            K.cp(hist[s_], pre[:, :, P:P + 3], E=K.pool)
            K.actf(qkv, cacc, AF.Silu)
            K.tt(sq2, qkv[:, 0:8, :], qkv[:, 0:8, :], ALU.mult, E=K.pool)
            for half in range(2):
                b = self.bank()
                K.mm(b, CB["ones"], sq2[:, half * 4:(half + 1) * 4, :])
                K.actf(rinv[:, half * 4:(half + 1) * 4, :], b.a.re("p (j t) -> p j t", j=4), AF.Ln, bias=self.epst)
            K.actf(rinv, rinv, AF.Exp, scale=-0.5)
            K.stt(qkv[:, 0:4, :], qkv[:, 0:4, :], float(128 ** -0.5), rinv[:, 0:4, :], ALU.mult, ALU.mult)
            K.tt(qkv[:, 4:8, :], qkv[:, 4:8, :], rinv[:, 4:8, :], ALU.mult, E=K.pool)

            def seg_tm1():
                wb = wblock(1536, 512)
                bz = self.bank()
                for k in range(8):
                    K.mm(bz, hn[:, k, :], wb(k), start=(k == 0), stop=(k == 7))
                K.actf(zs, bz, AF.Silu)
                wb = wblock(2312, 256)
                bgk = self.bank()
                for k in range(8):
                    K.mm(bgk[:, 0:256], hn[:, k, :], wb(k), start=(k == 0), stop=(k == 7))
                K.cp(gkT, bgk[:, 0:256], E=K.act)
            def seg_tm2():
                wb = wblock(2568, 512)
                bgv = self.bank()
                for k in range(8):
                    K.mm(bgv, hn[:, k, :], wb(k), start=(k == 0), stop=(k == 7))
                K.cp(gv, bgv, E=K.act)
                wb = wblock(3080, 512)
                bgr = self.bank()
                for k in range(8):
                    K.mm(bgr, hn[:, k, :], wb(k), start=(k == 0), stop=(k == 7))
                K.actf(grs, bgr, AF.Silu)
            def seg_tm3():
                wb = wblock32(3592, 16)
                bl = self.bank()
                for k in range(8):
                    K.mm(bl[0:16, 0:P], wb(k), hn32[:, k, :], start=(k == 0), stop=(k == 7))
                K.cp(g1[0:16, :], bl[0:16, 0:P])
                wb = wblock(2056, 512)
                bqk = self.bank()
                for j in range(4):
                    for k in range(8):
                        K.mm(bqk[:, j * P:(j + 1) * P], wb(k)[:, j * 128:(j + 1) * 128], hn[:, k, :],
                             start=(k == 0), stop=(k == 7))
                K.cp(qkF, bqk.a.re("p (j t) -> p j t", j=4))
            def seg_gla1():
                bx = self.bank()
                K.mm(bx[:, 0:256], g1, ev[0:32, EV_W2B:EV_W2B + 256])
                K.ts(loga, bx[:, 0:256], -20.0, ALU.max)
                K.actf(loga, loga, AF.Exp, scale=-1.0)
                K.actf(loga, loga, AF.Ln, bias=1.0)
                K.ts(loga, loga, -1.0 / 16.0, ALU.mult, -1.0, ALU.max)
                bgc = self.bank()
                for cc in range(2):
                    K.mm(bgc[:, cc * 128:(cc + 1) * 128], loga[:, cc * 128:(cc + 1) * 128], C["tri"])
                K.mm(bgc[:, 256:512], C["ustr"], loga)
                K.cp(gcf, bgc[:, 0:256].re("p (c t) -> p c t", c=2))
                K.actf(edlB, bgc[:, 256:512], AF.Exp)
                K.tt(kdB, gkT, edlB, ALU.mult)
                K.ts(nmid, gcf[:, :, 63], -1.0, ALU.mult)
                for cc in range(2):
                    K.actf(eq[:, cc, :], gcf[:, cc, :], AF.Exp, bias=nmid[:, cc:cc + 1])
                    K.actf(ek[:, cc, :], gcf[:, cc, :], AF.Exp, bias=gcf[:, cc, 63:64], scale=-1.0)
                K.actf(eq0, gcf, AF.Exp)
                K.stt(qgm, qkF[:, 0:2, :], 0.125, eq, ALU.mult, ALU.mult)
                K.tt(kgm, qkF[:, 2:4, :], ek, ALU.mult)
                K.stt(qg0, qkF[:, 0:2, :], 0.125, eq0, ALU.mult, ALU.mult)
            def seg_gla2():
                bat = self.bank()
                for hh in range(4):
                    pr = slice((hh % 2) * 64, (hh % 2) * 64 + 64)
                    K.mm(bat[:, hh * 128:(hh + 1) * 128], kgm[pr, hh // 2, :], qgm[pr, hh // 2, :], ser=True)
                K.tt(attm, bat.a.re("p (h t) -> p h t", h=4), bc4(C["tri"]), ALU.mult)
                bo = self.bank()
                shared['bo'] = bo
                for hh in range(4):
                    pr = slice((hh % 2) * 64, (hh % 2) * 64 + 64)
                    K.mm(bo[:, hh * 128:(hh + 1) * 128], qg0[pr, hh // 2, :],
                         Sst[pr, hh // 2, (hh % 2) * 128:(hh % 2) * 128 + 128], start=True, stop=False, ser=True)
                    K.mm(bo[:, hh * 128:(hh + 1) * 128], attm[:, hh, :], gv[:, hh * 128:(hh + 1) * 128],
                         start=False, stop=True, ser=True)
                bs = self.bank()
                for pp in range(2):
                    K.mm(bs[:, pp * 256:(pp + 1) * 256], kdB[:, pp * 128:(pp + 1) * 128], gv[:, pp * 256:(pp + 1) * 256])
                for pp in range(2):
                    K.stt(SB_[:, pp, :], SB_[:, pp, :], eq0[:, pp, 127:128], bs[:, pp * 256:(pp + 1) * 256],
                          ALU.mult, ALU.add)
            def seg_gla3():
                bo = shared['bo']
                for hh in range(4):
                    K.actf(junk, bo[:, hh * 128:(hh + 1) * 128], AF.Square, accum=ssB[:, hh:hh + 1])
                K.actf(ssB, ssB, AF.Sqrt, bias=self.epst, scale=1.0 / 128.0)
                K.op(K.dve, lambda: nc.vector.reciprocal(ssB.a.ap, ssB.a.ap), [ssB], [ssB])
                K.tt(Gt, grs, ev[:, EV_GLAW:EV_GLAW + 512], ALU.mult, E=K.pool)
                for hh in range(4):
                    sl = slice(hh * 128, (hh + 1) * 128)
                    K.stt(obT[:, sl], bo[:, sl], ssB[:, hh:hh + 1], Gt[:, sl], ALU.mult, ALU.mult)
                btr = self.bank()
                for hh in range(4):
                    K.mm(btr[:, hh * 128:(hh + 1) * 128], obT[:, hh * 128:(hh + 1) * 128], ID)
                K.cp(oFM[:, 4:8, :], btr.a.re("p (h t) -> p h t", h=4), E=K.act)
            segs = [seg_tm1, seg_tm2, seg_tm3, seg_gla1, seg_gla2, seg_gla3]
            K.actf(beta, gr8[:, 0:4], AF.Sigmoid)
            K.ts(nbeta, beta, -1.0, ALU.mult)
            K.tt(gA, gr8[:, 4:8], ev[:, EV_DTB:EV_DTB + 4], ALU.add)
            K.actf(gA, gA, AF.Exp)
            K.actf(gA, gA, AF.Ln, bias=1.0)
            K.tt(gA, gA, nexpa, ALU.mult)
            K.tt(Gtri, bc4(C["tri"]), bcl(gA), ALU.mult)
            bsm = self.bank()
            K.mm(bsm[:, 0:4], C["tri"], gA)
            K.mm(bsm[:, 8:12], C["ones"], gA)
            br = self.bank()
            K.mm(br, C["ones"], Gtri)
            K.cp(gcc, bsm[:, 0:4])
            K.actf(eg, gcc, AF.Exp)
            K.actf(egl, bsm[:, 8:12], AF.Exp)
            K.tt(edl, bsm[:, 8:12], gcc, ALU.subtract)
            K.actf(edl, edl, AF.Exp)
            K.tt(beg, beta, eg, ALU.mult)
            for hh in range(4):
                sl = slice(hh * 128, (hh + 1) * 128)
                K.ts(Dm[:, hh, :], br[:, sl], gcc[:, hh:hh + 1], ALU.subtract, 0.0, ALU.max)
                K.ts(DTm[:, hh, :], br[:, sl], gcc[:, hh:hh + 1], ALU.subtract, 0.0, ALU.min)
            K.actf(Dm, Dm, AF.Exp, scale=-1.0)
            K.actf(DTm, DTm, AF.Exp)
            K.actf(qg, br.a.re("p (h t) -> p h t", h=4), AF.Exp)
            K.tt(Dm, Dm, bc4(C["ustr"]), ALU.mult, E=K.pool)
            K.tt(DTm, DTm, bc4(C["tri"]), ALU.mult, E=K.pool)
            K.tt(qg, qg, qkv[:, 0:4, :], ALU.mult, E=K.pool)
            bkk = self.bank()
            bqt = self.bank()
            for hh in range(4):
                sl = slice(hh * 128, (hh + 1) * 128)
                K.mm(bkk[:, sl], qkv[:, 4 + hh, :], qkv[:, 4 + hh, :])
                K.mm(bqt[:, sl], qkv[:, 4 + hh, :], qkv[:, hh, :])
            for hh in range(4):
                K.stt(A[0][:, hh, :], bkk[:, hh * 128:(hh + 1) * 128], nbeta[:, hh:hh + 1], Dm[:, hh, :],
                      ALU.mult, ALU.mult)
            K.tt(aqkT, bqt.a.re("p (h t) -> p h t", h=4), DTm, ALU.mult)
            bnt = self.bank()
            for hh in range(4):
                K.mm(bnt[:, hh * 128:(hh + 1) * 128], A[0][:, hh, :], C["ident"])
            K.cp(AT[0], bnt.a.re("p (h t) -> p h t", h=4), E=K.act)
            K.tt(PT, bnt.a.re("p (h t) -> p h t", h=4), bc4(C["ident"]), ALU.add)
            bkt = self.bank()
            bvt = self.bank()
            for hh in range(4):
                sl = slice(hh * 128, (hh + 1) * 128)
                K.mm(bkt[:, sl], qkv[:, 4 + hh, :], C["ident"])
                K.mm(bvt[:, sl], qkv[:, 8 + hh, :], C["ident"])
            K.tt(vb, bvt.a.re("p (h t) -> p h t", h=4), bcl(beta), ALU.mult)
            K.tt(kbg, bkt.a.re("p (h t) -> p h t", h=4), bcl(beg), ALU.mult)
            K.tt(kdA, bkt.a.re("p (h t) -> p h t", h=4), bcl(edl), ALU.mult)
            cur = 0
            for itr in range(6):
                nxt = 1 - cur
                b1 = self.bank()
                b2 = self.bank()
                for hh in range(4):
                    sl = slice(hh * 128, (hh + 1) * 128)
                    K.mm(b1[:, sl], AT[cur][:, hh, :], A[cur][:, hh, :])
                    if itr < 5:
                        K.mm(b2[:, sl], A[cur][:, hh, :], AT[cur][:, hh, :])
                K.cp(A[nxt], b1.a.re("p (h t) -> p h t", h=4), E=K.act)
                if itr < 5:
                    K.cp(AT[nxt], b2.a.re("p (h t) -> p h t", h=4), E=K.dve)
                b3 = self.bank()
                for hh in range(4):
                    K.mm(b3[:, hh * 128:(hh + 1) * 128], A[nxt][:, hh, :], PT[:, hh, :])
                K.tt(PT, PT, b3.a.re("p (h t) -> p h t", h=4), ALU.add)
                cur = nxt
                segs[itr]()
            bw = self.bank()
            for hh in range(4):
                K.mm(bw[:, hh * 128:(hh + 1) * 128], kbg[:, hh, :], PT[:, hh, :])
            K.actf(wneg, bw.a.re("p (h t) -> p h t", h=4), AF.Copy, scale=-1.0)
            bvn = self.bank()
            for hh in range(4):
                sl = slice(hh * 128, (hh + 1) * 128)
                K.mm(bvn[:, sl], PT[:, hh, :], vb[:, hh, :], start=True, stop=False)
                K.mm(bvn[:, sl], wneg[:, hh, :], SA[:, hh, :], start=False, stop=True)
            K.cp(vn, bvn.a.re("p (h t) -> p h t", h=4))
            boa = self.bank()
            for hh in range(4):
                sl = slice(hh * 128, (hh + 1) * 128)
                K.mm(boa[:, sl], qg[:, hh, :], SA[:, hh, :], start=True, stop=False)
                K.mm(boa[:, sl], aqkT[:, hh, :], vn[:, hh, :], start=False, stop=True)
            bsa = self.bank()
            for hh in range(4):
                K.mm(bsa[:, hh * 128:(hh + 1) * 128], kdA[:, hh, :], vn[:, hh, :])
            for hh in range(4):
                K.stt(SA[:, hh, :], SA[:, hh, :], egl[:, hh:hh + 1], bsa[:, hh * 128:(hh + 1) * 128],
                      ALU.mult, ALU.add)
            for hh in range(4):
                K.actf(junk, boa[:, hh * 128:(hh + 1) * 128], AF.Square, accum=ssA[:, hh:hh + 1])
            K.actf(ssA, ssA, AF.Sqrt, bias=self.epst, scale=1.0 / 128.0)
            K.op(K.dve, lambda: nc.vector.reciprocal(ssA.a.ap, ssA.a.ap), [ssA], [ssA])
            K.tt(Gt, zs, ev[:, EV_GDNW:EV_GDNW + 512], ALU.mult, E=K.pool)
            for hh in range(4):
                sl = slice(hh * 128, (hh + 1) * 128)
                K.stt(oaT[:, sl], boa[:, sl], ssA[:, hh:hh + 1], Gt[:, sl], ALU.mult, ALU.mult)
            btr = self.bank()
            for hh in range(4):
                K.mm(btr[:, hh * 128:(hh + 1) * 128], oaT[:, hh * 128:(hh + 1) * 128], ID)
            K.cp(oFM[:, 0:4, :], btr.a.re("p (h t) -> p h t", h=4), E=K.act)

            for half in range(2):
                b = self.bank()
                for j in range(4):
                    wo = woblock(half * 4 + j)
                    for c in range(8):
                        K.mm(b[:, j * P:(j + 1) * P], wo(c), oFM[:, c, :], start=(c == 0), stop=(c == 7))
                K.tt(h[:, half * 4:(half + 1) * 4, :], h[:, half * 4:(half + 1) * 4, :],
                     b.a.re("p (j t) -> p j t", j=4), ALU.add)
            K.dma(K.act, self.hview(dst, it * P, P), h)
        K.barrier()
        es.close()

    run_variant(True, [(s_, s_ * tps) for s_ in range(NSEQ)])
    run_variant(False, [(s_, s_ * tps + j) for s_ in range(NSEQ) for j in range(1, tps)])
    K.barrier()
    esP.close()


Model.even_phase = _even_phase


def _declare_mixer_inputs(self, ne, no):
    K = self.K
    self.even_w_in = K.dram("even_w_in", [ne, D, EVEN_IN], F32, kind="ExternalInput")
    self.even_w_out = K.dram("even_w_out", [ne, D, D], F32, kind="ExternalInput")
    self.evec = K.dram("evec", [ne, 128, EV_N], F32, kind="ExternalInput")
    if no > 0:
        self.declare_odd_inputs(no)


Model.declare_mixer_inputs = _declare_mixer_inputs


def host_mixer_inputs(inp, L):
    ne = (L + 1) // 2
    no = L // 2
    d = {"even_w_in": np.ascontiguousarray(inp["even_w_in"][:ne]),
         "even_w_out": np.ascontiguousarray(inp["even_w_out"][:ne]),
         "evec": np.stack([host_even_vec(inp, i) for i in range(ne)])}
    if no > 0:
        d.update(host_odd_inputs(inp, no))
    return d


OV_CONV = 0
OV_CONVB = 32
OV_DTB = 40
OV_ALOG = 48
OV_D = 56
OV_SNW = 568
OV_MU = 1080
OV_W0 = 1094
OV_A0 = 1606
OV_KK = 1610
OV_KA = 1614
OV_RK = 1618
OV_GNW = 1622
OV_GNB = 2134
OV_SEL = 2646
OV_N = 2678


def host_odd_vec(inp, i):
    v = np.zeros((128, OV_N), np.float32)
    f = lambda a, n: np.asarray(a, np.float32).reshape(n, 128).T
    cw = np.asarray(inp["ssd_conv_w"][i], np.float32)
    v[:, OV_CONV:OV_CONV + 32] = cw.T.reshape(8, 128, 4).transpose(1, 0, 2).reshape(128, 32)
    v[:, OV_CONVB:OV_CONVB + 8] = f(inp["ssd_conv_b"][i], 8)
    v[:, OV_DTB:OV_DTB + 8] = np.asarray(inp["ssd_dt_bias"][i])[None, :]
    v[:, OV_ALOG:OV_ALOG + 8] = np.asarray(inp["ssd_a_log"][i])[None, :]
    v[:, OV_D:OV_D + 512] = np.repeat(np.asarray(inp["ssd_d"][i]), 64)[None, :]
    v[:, OV_SNW:OV_SNW + 512] = np.asarray(inp["ssd_norm_w"][i])[None, :]
    v[:, OV_MU:OV_MU + 14] = f(inp["rwkv_mu"][i], 14)
    v[:, OV_W0:OV_W0 + 512] = np.asarray(inp["rwkv_w0"][i])[None, :]
    v[:, OV_A0:OV_A0 + 4] = f(inp["rwkv_a0"][i], 4)
    v[:, OV_KK:OV_KK + 4] = f(inp["rwkv_k_k"][i], 4)
    v[:, OV_KA:OV_KA + 4] = f(inp["rwkv_k_a"][i], 4)
    v[:, OV_RK:OV_RK + 4] = f(np.asarray(inp["rwkv_r_k"][i]).reshape(-1), 4)
    v[:, OV_GNW:OV_GNW + 512] = np.asarray(inp["rwkv_gn_w"][i])[None, :]
    v[:, OV_GNB:OV_GNB + 512] = np.asarray(inp["rwkv_gn_b"][i])[None, :]
    p = np.arange(128)
    sel = np.zeros((128, 4, 8), np.float32)
    for c in range(4):
        sel[p, c, 2 * c + p // 64] = 1.0
    v[:, OV_SEL:OV_SEL + 32] = sel.reshape(128, 32)
    return v


def host_odd_inputs(inp, no):
    a2 = np.zeros((no, 128, 512), np.float32)
    a2[:, 64:128, :] = np.asarray(inp["rwkv_a2"][:no])
    w2 = np.zeros((no, 128, 512), np.float32)
    w2[:, 0:64, :] = np.asarray(inp["rwkv_w2"][:no])
    return {"odd_w_in": np.ascontiguousarray(inp["odd_w_in"][:no]),
            "odd_w_out": np.ascontiguousarray(inp["odd_w_out"][:no]),
            "ovec": np.stack([host_odd_vec(inp, i) for i in range(no)]),
            "rw2": w2, "ra2": a2, "rg2": np.ascontiguousarray(inp["rwkv_g2"][:no])}


def _declare_odd_inputs(self, no):
    K = self.K
    self.odd_w_in = K.dram("odd_w_in", [no, D, ODD_IN], F32, kind="ExternalInput")
    self.odd_w_out = K.dram("odd_w_out", [no, D, D], F32, kind="ExternalInput")
    self.ovec = K.dram("ovec", [no, 128, OV_N], F32, kind="ExternalInput")
    self.rw2 = K.dram("rw2", [no, 128, 512], F32, kind="ExternalInput")
    self.ra2 = K.dram("ra2", [no, 128, 512], F32, kind="ExternalInput")
    self.rg2 = K.dram("rg2", [no, 128, 512], F32, kind="ExternalInput")


Model.declare_odd_inputs = _declare_odd_inputs


def _odd_phase(self, l, src, dst):
    K = self.K
    nc = self.nc
    i_l = l // 2
    esP = contextlib.ExitStack()
    C, CB = self.C, self.CB
    P = 128
    tps = self.seq // P
    WIN = self.odd_w_in.a[i_l]
    WOUT = self.odd_w_out.a[i_l]

    def bc4(v, n=4):
        return v.re("p (o t) -> p o t", o=1).bc([128, n, 128])

    def bcl(v, n, m):
        return _v(v).re("p (h o) -> p h o", o=1).bc([128, n, m])

    def recip(t):
        K.op(K.dve, lambda: nc.vector.reciprocal(_v(t).ap, _v(t).ap), [t], [t])

    ov = K.sb([128, OV_N], F32, "ov", esP)
    K.dma(K.sp, ov, self.ovec.a[i_l])
    w2t = K.sb([128, 512], F32, "w2t", esP)
    a2t = K.sb([128, 512], F32, "a2t", esP)
    g2t = K.sb([128, 512], F32, "g2t", esP)
    K.dma(K.sp, w2t, self.rw2.a[i_l])
    K.dma(K.sp, a2t, self.ra2.a[i_l])
    K.dma(K.sp, g2t, self.rg2.a[i_l])
    nexpa = K.sb([128, 8], F32, "nexpa", esP)
    K.actf(nexpa, ov[:, OV_ALOG:OV_ALOG + 8], AF.Exp)
    K.ts(nexpa, nexpa, -1.0, ALU.mult)
    gneps = K.sb([128, 1], F32, "gneps", esP)
    K.memset(gneps, 64e-5)
    convw = ov[:, OV_CONV:OV_CONV + 32]
    gain = self.vec[:, 8 * l:8 * (l + 1)]
    STs = [K.sb([128, 512], F32, "ST", esP) for _ in range(NSEQ)]
    Hss = [K.sb([128, 4, 128], F32, "Hs", esP) for _ in range(NSEQ)]
    histC = [K.sb([128, 8, 3], F32, "histC", esP) for _ in range(NSEQ)]
    histP = [K.sb([128, 14, 1], F32, "histP", esP) for _ in range(NSEQ)]
    for s_ in range(NSEQ):
        K.memset(STs[s_], 0.0)
        K.memset(Hss[s_], 0.0)
        K.memset(histC[s_], 0.0)
        K.memset(histP[s_], 0.0)

    def run_variant(hp, tiles):
        if not tiles:
            return
        es = contextlib.ExitStack()
        AD = F32 if hp else BF16
        ID = C["ident"] if hp else CB["ident"]
        if hp:
            stg = [K.sb([128, 8, 512], F32, "stg", es) for _ in range(2)]
            cnt = [0]

            def _stream(dram2d, col0, n):
                t = stg[cnt[0] % 2]
                cnt[0] += 1
                src_ = dram2d[:, col0:col0 + n].re("(k p) n -> p k n", p=128)
                K.dma(K.sp, t[:, :, 0:n], src_)
                return lambda k: t[:, k, 0:n]

            wblock = lambda col0, n: _stream(WIN, col0, n)
            wblock32 = wblock
            woblock = lambda n0: _stream(WOUT, n0 * 128, 128)
        else:
            w_in = K.sb([128, 8, ODD_IN], BF16, "w_in", es)
            for k in range(8):
                K.dma(K.pool, w_in[:, k, :], WIN[k * 128:(k + 1) * 128, :])
            w_out = K.sb([128, 8, D], BF16, "w_out", es)
            K.dma(K.pool, w_out, WOUT.re("(c p) n -> p c n", p=128))
            wg = K.sb([128, 8, 136], F32, "wg", es)
            for k in range(8):
                K.dma(K.sp, wg[:, k, 0:8], WIN[k * 128:(k + 1) * 128, 1536:1544])
                K.dma(K.sp, wg[:, k, 8:136], WIN[k * 128:(k + 1) * 128, 3080:3208])
            wblock = lambda col0, n: (lambda k: w_in[:, k, col0:col0 + n])
            wblock32 = lambda col0, n: (lambda k: wg[:, k, 0:8])
            woblock = lambda n0: (lambda c: w_out[:, c, n0 * 128:(n0 + 1) * 128])

        def sb(shape, name, dt=F32):
            return K.sb(shape, dt, name, es)

        h = sb([128, 8, P], "h"); hn32 = sb([128, 8, P], "hn32"); hn = hn32 if hp else sb([128, 8, P], "hn", BF16)
        sq = sb([128, 8, P], "sq", BF16); rp = sb([128, P], "rp")
        preC = sb([128, 8, P + 3], "preC"); cacc = sb([128, 8, P], "cacc"); xbc = sb([128, 8, P], "xbc")
        pdp = sb([128, 14, P + 1], "pdp"); pdm = sb([128, 14, P], "pdm"); pdd = pdm
        zs = sb([128, 512], "zs"); oFM = sb([128, 8, P], "oFM", AD)
        dt8 = sb([128, 8], "dt8"); a8 = sb([128, 8], "a8"); acc8 = sb([128, 8], "acc8")
        eac8 = sb([128, 8], "eac8"); edl8 = sb([128, 8], "edl8"); cd8 = sb([128, 8], "cd8"); dte8 = sb([128, 8], "dte8")
        segT = sb([128, 8, 128], "segT"); MT = sb([128, 8, 128], "MT", AD)
        xTs = sb([128, 512], "xTs"); xdt = sb([128, 8, 64], "xdt", AD); xdte = sb([128, 8, 64], "xdte", AD)
        Bb = sb([128, 256], "Bb", AD); Cb = sb([128, 2, 128], "Cb", AD)
        STb = None if hp else sb([128, 512], "STb", BF16)
        y1 = sb([128, 512], "y1"); y2 = sb([128, 512], "y2"); ss2 = sb([128, 2], "ss2"); junk = sb([128, 256], "junk")
        ycT = sb([128, 512], "ycT", AD)
        txw = sb([64, P], "txw"); ldT = sb([128, 512], "ldT"); aF = sb([128, 4, P], "aF"); sg = sb([128, P], "sg")
        kp = sb([128, 4, P], "kp"); kq = sb([128, 4, P], "kq"); kk = kq; bm = sb([128, 4, P], "bm"); gl4 = sb([128, 4], "gl4")
        sqk = sb([128, 4, P], "sqk"); lg = sb([128, 4, P], "lg"); lgx = sb([128, 4, P], "lgx"); nmid = sb([128, 4], "nmid")
        e_r = sb([128, 4, P], "e_r"); e_a = sb([128, 4, P], "e_a"); e_b = sb([128, 4, P], "e_b")
        e_r0 = sb([128, 4, P], "e_r0"); e_a0 = sb([128, 4, P], "e_a0")
        rt_ = e_r; r0_ = e_r0; at_ = e_a; a0_ = e_a0; bt_ = e_b; kt_ = xbc[:, 4:8, :]
        edlT = sb([128, 512], "edlT"); kdT = sb([128, 512], "kdT"); bdT = sb([128, 512], "bdT"); vT = sb([128, 512], "vT")
        A = [segT[:, 0:4, :], cacc[:, 0:4, :]]
        AT = [segT[:, 4:8, :], cacc[:, 4:8, :]]
        PT = xbc[:, 0:4, :]
        AakT = y1.a.re("p (h t) -> p h t", h=4); RbT = y2.a.re("p (h t) -> p h t", h=4); RkT = xTs.a.re("p (h t) -> p h t", h=4)
        Xs = sb([128, 4, 64], "Xs"); Us = sb([128, 8, 64], "Us")
        yd = lg.a.re("p c t -> p (c t)").re("p (h q) -> p h q", h=8); ym = lgx.a.re("p c t -> p (c t)").re("p (h q) -> p h q", h=8); mean8 = sb([128, 8], "mean8"); var8 = sb([128, 8], "var8")
        rkr = sb([128, 4, P], "rkr"); rks = sb([128, 8], "rks"); gT = sb([128, 512], "gT"); ydT = sb([128, 512], "ydT", AD)

        for (s_, it) in tiles:
            ST, Hs = STs[s_], Hss[s_]
            K.dma(K.sp, h, self.hview(src, it * P, P))
            K.cp(preC[:, :, 0:3], histC[s_], E=K.pool)
            K.cp(pdp[:, :, 0:1], histP[s_], E=K.pool)
            if not hp:
                K.cp(STb, ST, E=K.pool)
            Sst = ST if hp else STb
            self.rmsnorm_fm(h, gain, [hn32] if hp else [hn, hn32], sq, rp, n=P)
            for c4 in range(2):
                wb = wblock(512 + c4 * 512, 512)
                b = self.bank()
                for j in range(4):
                    for k in range(8):
                        K.mm(b[:, j * P:(j + 1) * P], wb(k)[:, j * 128:(j + 1) * 128], hn[:, k, :],
                             start=(k == 0), stop=(k == 7))
                K.cp(preC[:, c4 * 4:(c4 + 1) * 4, 3:3 + P], b.a.re("p (j t) -> p j t", j=4), E=K.act)
            for c4 in range(4):
                b = self.bank()
                nj = 4 if c4 < 3 else 2
                wb = wblock(1544 + c4 * 512, nj * 128)
                for j in range(nj):
                    c = c4 * 4 + j
                    for k in range(8):
                        if c == 12 and not hp:
                            K.mm(b[:, j * P:(j + 1) * P], wg[:, k, 8:136], hn32[:, k, :], start=(k == 0), stop=(k == 7))
                        else:
                            K.mm(b[:, j * P:(j + 1) * P], wb(k)[:, j * 128:(j + 1) * 128], hn[:, k, :],
                                 start=(k == 0), stop=(k == 7))
                K.cp(pdp[:, c4 * 4:c4 * 4 + nj, 1:1 + P], b[:, 0:nj * P].re("p (j t) -> p j t", j=nj), E=K.act)
            wb = wblock(0, 512)
            bz = self.bank()
            for k in range(8):
                K.mm(bz, hn[:, k, :], wb(k), start=(k == 0), stop=(k == 7))
            K.actf(zs, bz, AF.Silu)
            wb = wblock32(1536, 8)
            bg = self.bank()
            for k in range(8):
                K.mm(bg[:, 0:8], hn32[:, k, :], wb(k), start=(k == 0), stop=(k == 7))
            K.tt(dt8, bg[:, 0:8], ov[:, OV_DTB:OV_DTB + 8], ALU.add)
            K.ts(dt8, dt8, 60.0, ALU.min)
            K.actf(dt8, dt8, AF.Exp)
            K.actf(dt8, dt8, AF.Ln, bias=1.0)
            K.tt(a8, dt8, nexpa, ALU.mult)
            for c in range(8):
                K.actf(cacc[:, c, :], preC[:, c, 0:P], AF.Copy, scale=convw[:, c * 4:c * 4 + 1])
                for j in range(1, 4):
                    K.stt(cacc[:, c, :], preC[:, c, j:j + P], convw[:, c * 4 + j:c * 4 + j + 1], cacc[:, c, :],
                          ALU.mult, ALU.add)
                K.actf(xbc[:, c, :], cacc[:, c, :], AF.Silu, bias=ov[:, OV_CONVB + c:OV_CONVB + c + 1])
            K.cp(histC[s_], preC[:, :, P:P + 3], E=K.pool)
            Atri = cacc
            K.tt(Atri, bc4(C["tri"], 8), bcl(a8, 8, 128), ALU.mult)
            bsm = self.bank()
            K.mm(bsm[:, 0:8], C["tri"], a8)
            K.mm(bsm[:, 8:16], C["ones"], a8)
            K.cp(acc8, bsm[:, 0:8])
            K.actf(eac8, acc8, AF.Exp)
            K.actf(cd8, bsm[:, 8:16], AF.Exp)
            K.tt(edl8, bsm[:, 8:16], acc8, ALU.subtract)
            K.actf(edl8, edl8, AF.Exp)
            K.tt(dte8, dt8, edl8, ALU.mult)
            for hg in range(2):
                br = self.bank()
                K.mm(br, C["ones"], Atri[:, hg * 4:(hg + 1) * 4, :])
                for hh in range(4):
                    hd = hg * 4 + hh
                    K.ts(segT[:, hd, :], br[:, hh * 128:(hh + 1) * 128], acc8[:, hd:hd + 1], ALU.subtract, 0.0, ALU.min)
            K.actf(segT, segT, AF.Exp)
            K.tt(segT, segT, bc4(C["tri"], 8), ALU.mult, E=K.pool)
            bcb = self.bank()
            for g in range(2):
                K.mm(bcb[:, g * 128:(g + 1) * 128], xbc[:, 4 + g, :], xbc[:, 6 + g, :])
            for g in range(2):
                K.tt(MT[:, g * 4:(g + 1) * 4, :], segT[:, g * 4:(g + 1) * 4, :],
                     bcb[:, g * 128:(g + 1) * 128].re("p (o t) -> p o t", o=1).bc([128, 4, 128]), ALU.mult)
            K.cp(Cb, xbc[:, 6:8, :], E=K.pool)
            bxt = self.bank()
            for c in range(4):
                K.mm(bxt[:, c * 128:(c + 1) * 128], xbc[:, c, :], C["ident"])
            bbt = self.bank()
            for g in range(2):
                K.mm(bbt[:, g * 128:(g + 1) * 128], xbc[:, 4 + g, :], C["ident"])
            K.cp(xTs, bxt, E=K.act)
            K.cp(Bb, bbt[:, 0:256], E=K.act)
            K.tt(xdt, xTs.a.re("p (h q) -> p h q", h=8), bcl(dt8, 8, 64), ALU.mult)
            K.tt(xdte, xTs.a.re("p (h q) -> p h q", h=8), bcl(dte8, 8, 64), ALU.mult)
            by = self.bank()
            for hd in range(8):
                K.mm(by[:, hd * 64:(hd + 1) * 64], MT[:, hd, :], xdt[:, hd, :])
            bi = self.bank()
            for g in range(2):
                K.mm(bi[:, g * 256:(g + 1) * 256], Cb[:, g, :], Sst[:, g * 256:(g + 1) * 256])
            bst = self.bank()
            for g in range(2):
                K.mm(bst[:, g * 256:(g + 1) * 256], Bb[:, g * 128:(g + 1) * 128],
                     xdte[:, g * 4:(g + 1) * 4, :].re("p h q -> p (h q)"))
            K.tt(y1.a.re("p (h q) -> p h q", h=8), bi.a.re("p (h q) -> p h q", h=8), bcl(eac8, 8, 64), ALU.mult)
            K.tt(y1, y1, by, ALU.add)
            K.tt(y2, xTs, ov[:, OV_D:OV_D + 512], ALU.mult, E=K.pool)
            K.tt(y1, y1, y2, ALU.add)
            K.tt(y1, y1, zs, ALU.mult)
            K.tt(ST.a.re("p (h q) -> p h q", h=8), ST.a.re("p (h q) -> p h q", h=8), bcl(cd8, 8, 64), ALU.mult)
            K.tt(ST, ST, bst, ALU.add)
            for g in range(2):
                K.actf(junk, y1[:, g * 256:(g + 1) * 256], AF.Square, accum=ss2[:, g:g + 1])
            K.actf(ss2, ss2, AF.Sqrt, bias=self.epst, scale=1.0 / 256.0)
            recip(ss2)
            K.tt(y2, y1, ov[:, OV_SNW:OV_SNW + 512], ALU.mult, E=K.pool)
            for g in range(2):
                K.ts(ycT[:, g * 256:(g + 1) * 256], y2[:, g * 256:(g + 1) * 256], ss2[:, g:g + 1], ALU.mult)
            btr = self.bank()
            for hh in range(4):
                K.mm(btr[:, hh * 128:(hh + 1) * 128], ycT[:, hh * 128:(hh + 1) * 128], ID)
            K.cp(oFM[:, 0:4, :], btr.a.re("p (h t) -> p h t", h=4), E=K.act)

            K.tt(pdd, pdp[:, :, 0:P], pdp[:, :, 1:P + 1], ALU.subtract)
            K.tt(pdd, pdd, bcl(ov[:, OV_MU:OV_MU + 14], 14, P), ALU.mult, E=K.pool)
            K.tt(pdm, pdd, pdp[:, :, 1:P + 1], ALU.add)
            K.cp(histP[s_], pdp[:, :, P:P + 1], E=K.pool)
            R_, K_, V_ = pdm[:, 0:4, :], pdm[:, 4:8, :], pdm[:, 8:12, :]
            K.actf(txw, pdm[0:64, 12, :], AF.Tanh)
            K.actf(sg, pdm[:, 13, :], AF.Sigmoid)
            bw = self.bank()
            K.mm(bw, txw, w2t[0:64, :])
            K.tt(ldT, bw, ov[:, OV_W0:OV_W0 + 512], ALU.add)
            K.actf(ldT, ldT, AF.Sigmoid)
            K.ts(ldT, ldT, -float(np.exp(-0.5)), ALU.mult)
            ba = self.bank()
            for c in range(4):
                K.mm(ba[:, c * P:(c + 1) * P], a2t[64:128, c * 128:(c + 1) * 128], pdm[64:128, 12, :])
            for c in range(4):
                K.actf(aF[:, c, :], ba[:, c * P:(c + 1) * P], AF.Sigmoid, bias=ov[:, OV_A0 + c:OV_A0 + c + 1])
            bgm = self.bank()
            K.mm(bgm, sg, g2t)
            K.cp(gT, bgm, E=K.act)
            for c in range(4):
                K.ts(kp[:, c, :], aF[:, c, :], 1.0, ALU.subtract, ov[:, OV_KA + c:OV_KA + c + 1], ALU.mult)
                K.stt(kp[:, c, :], kp[:, c, :], 1.0, pdm[:, 4 + c, :], ALU.add, ALU.mult)
                K.ts(kq[:, c, :], pdm[:, 4 + c, :], ov[:, OV_KK + c:OV_KK + c + 1], ALU.mult, E=K.pool)
                K.stt(rkr[:, c, :], pdm[:, c, :], ov[:, OV_RK + c:OV_RK + c + 1], kp[:, c, :], ALU.mult, ALU.mult)
            K.tt(sqk, kq, kq, ALU.mult, E=K.pool)
            bss = self.bank()
            K.mm(bss, C["bones"], sqk)
            K.actf(sqk, bss.a.re("p (c t) -> p c t", c=4), AF.Ln, bias=self.epst)
            K.actf(sqk, sqk, AF.Exp, scale=-0.5)
            K.tt(kk, kq, sqk, ALU.mult)
            K.tt(bm, kk, aF, ALU.mult, E=K.pool)
            blg = self.bank()
            blx = self.bank()
            for c in range(4):
                K.mm(blg[:, c * P:(c + 1) * P], ldT[:, c * 128:(c + 1) * 128], C["tri"])
                K.mm(blx[:, c * P:(c + 1) * P], ldT[:, c * 128:(c + 1) * 128], C["tris"])
            K.cp(lg, blg.a.re("p (c t) -> p c t", c=4))
            K.cp(lgx, blx.a.re("p (c t) -> p c t", c=4), E=K.act)
            K.ts(nmid, lg[:, :, 63], -1.0, ALU.mult)
            for c in range(4):
                K.actf(e_r[:, c, :], lg[:, c, :], AF.Exp, bias=nmid[:, c:c + 1])
                K.actf(e_a[:, c, :], lgx[:, c, :], AF.Exp, bias=nmid[:, c:c + 1])
                K.actf(e_b[:, c, :], lg[:, c, :], AF.Exp, bias=lg[:, c, 63:64], scale=-1.0)
            K.actf(e_r0, lg, AF.Exp)
            K.actf(e_a0, lgx, AF.Exp)
            K.cp(gl4, e_r0[:, :, 127])
            K.tt(rt_, R_, e_r, ALU.mult)
            K.tt(r0_, R_, e_r0, ALU.mult, E=K.pool)
            K.stt(at_, kk, -1.0, e_a, ALU.mult, ALU.mult)
            K.stt(a0_, kk, -1.0, e_a0, ALU.mult, ALU.mult)
            K.tt(kt_, kp, e_b, ALU.mult)
            K.tt(bt_, bm, e_b, ALU.mult, E=K.pool)
            bed = self.bank()
            K.mm(bed, C["ustr"], ldT)
            K.actf(edlT, bed, AF.Exp)
            for (srcT, dstT, scl) in ((kp, kdT, True), (bm, bdT, True), (V_, vT, False)):
                b = self.bank()
                for c in range(4):
                    K.mm(b[:, c * 128:(c + 1) * 128], srcT[:, c, :], C["ident"])
                if scl:
                    K.tt(dstT, b, edlT, ALU.mult)
                else:
                    K.cp(dstT, b, E=K.act)
            brk = self.bank()
            for c in range(4):
                K.mm(brk[:, 0:8], rkr[:, c, :], ov[:, OV_SEL + c * 8:OV_SEL + (c + 1) * 8], start=(c == 0), stop=(c == 3))
            K.cp(rks, brk[:, 0:8])
            byd = self.ps[7]
            for hg in range(2):
                b1, b2, b3, b4, b5 = [self.bank() for _ in range(5)]
                for hh in range(4):
                    hd = hg * 4 + hh
                    hq, pr = hd // 2, slice((hd % 2) * 64, (hd % 2) * 64 + 64)
                    sl = slice(hh * 128, (hh + 1) * 128)
                    K.mm(b1[:, sl], bt_[pr, hq, :], at_[pr, hq, :])
                    K.mm(b2[:, sl], at_[pr, hq, :], bt_[pr, hq, :])
                    K.mm(b3[:, sl], kt_[pr, hq, :], at_[pr, hq, :])
                    K.mm(b4[:, sl], bt_[pr, hq, :], rt_[pr, hq, :])
                    K.mm(b5[:, sl], kt_[pr, hq, :], rt_[pr, hq, :])
                K.tt(AT[0], b1.a.re("p (h t) -> p h t", h=4), bc4(C["tris"]), ALU.mult)
                K.tt(A[0], b2.a.re("p (h t) -> p h t", h=4), bc4(C["ustr"]), ALU.mult)
                K.tt(AakT, b3.a.re("p (h t) -> p h t", h=4), bc4(C["tris"]), ALU.mult)
                K.tt(RbT, b4.a.re("p (h t) -> p h t", h=4), bc4(C["tri"]), ALU.mult)
                K.tt(RkT, b5.a.re("p (h t) -> p h t", h=4), bc4(C["tri"]), ALU.mult)
                K.tt(PT, AT[0], bc4(C["ident"]), ALU.add, E=K.pool)
                cur = 0
                for itr in range(6):
                    nxt = 1 - cur
                    c1 = self.bank()
                    c2 = self.bank()
                    for hh in range(4):
                        sl = slice(hh * 128, (hh + 1) * 128)
                        K.mm(c1[:, sl], AT[cur][:, hh, :], A[cur][:, hh, :])
                        if itr < 5:
                            K.mm(c2[:, sl], A[cur][:, hh, :], AT[cur][:, hh, :])
                    K.cp(A[nxt], c1.a.re("p (h t) -> p h t", h=4), E=K.act)
                    if itr < 5:
                        K.cp(AT[nxt], c2.a.re("p (h t) -> p h t", h=4), E=K.dve)
                    c3 = self.bank()
                    for hh in range(4):
                        K.mm(c3[:, hh * 128:(hh + 1) * 128], A[nxt][:, hh, :], PT[:, hh, :])
                    K.tt(PT, PT, c3.a.re("p (h t) -> p h t", h=4), ALU.add)
                    cur = nxt
                bx = self.bank()
                for hh in range(4):
                    hd = hg * 4 + hh
                    hq, pr = hd // 2, slice((hd % 2) * 64, (hd % 2) * 64 + 64)
                    vs = slice((hd % 2) * 64, (hd % 2) * 64 + 64)
                    K.mm(bx[:, hh * 64:(hh + 1) * 64], a0_[pr, hq, :], Hs[pr, hq, vs], start=True, stop=False)
                    K.mm(bx[:, hh * 64:(hh + 1) * 64], AakT[:, hh, :], vT[:, hd * 64:(hd + 1) * 64], start=False, stop=True)
                K.cp(Xs, bx[:, 0:256].re("p (h q) -> p h q", h=4))
                bu = self.bank()
                for hh in range(4):
                    K.mm(bu[:, hh * 64:(hh + 1) * 64], PT[:, hh, :], Xs[:, hh, :])
                K.cp(Us[:, hg * 4:(hg + 1) * 4, :], bu[:, 0:256].re("p (h q) -> p h q", h=4))
                for hh in range(4):
                    hd = hg * 4 + hh
                    hq, pr = hd // 2, slice((hd % 2) * 64, (hd % 2) * 64 + 64)
                    vs = slice((hd % 2) * 64, (hd % 2) * 64 + 64)
                    o = byd[:, hd * 64:(hd + 1) * 64]
                    K.mm(o, r0_[pr, hq, :], Hs[pr, hq, vs], start=True, stop=False)
                    K.mm(o, RbT[:, hh, :], Us[:, hd, :], start=False, stop=False)
                    K.mm(o, RkT[:, hh, :], vT[:, hd * 64:(hd + 1) * 64], start=False, stop=True)
            bh = self.bank()
            for hq in range(4):
                o = bh[:, hq * 128:(hq + 1) * 128]
                K.mm(o, bdT[:, hq * 128:(hq + 1) * 128], Us[:, 2 * hq:2 * hq + 2, :].re("p h q -> p (h q)"), start=True, stop=False)
                K.mm(o, kdT[:, hq * 128:(hq + 1) * 128], vT[:, hq * 128:(hq + 1) * 128], start=False, stop=True)
            for hq in range(4):
                K.stt(Hs[:, hq, :], Hs[:, hq, :], gl4[:, hq:hq + 1], bh[:, hq * 128:(hq + 1) * 128], ALU.mult, ALU.add)
            K.cp(yd, byd.a.re("p (h q) -> p h q", h=8), E=K.act)
            K.red(mean8, yd)
            K.ts(mean8, mean8, 1.0 / 64.0, ALU.mult)
            K.tt(ym, yd, bcl(mean8, 8, 64), ALU.subtract)
            K.tt(yd, ym, ym, ALU.mult, E=K.pool)
            K.red(var8, yd)
            K.actf(var8, var8, AF.Sqrt, bias=gneps, scale=1.0 / 64.0)
            recip(var8)
            K.tt(ym, ym, bcl(var8, 8, 64), ALU.mult)
            ymf = _v(ym).re("p h q -> p (h q)")
            K.tt(ymf, ymf, ov[:, OV_GNW:OV_GNW + 512], ALU.mult, E=K.pool)
            K.tt(ymf, ymf, ov[:, OV_GNB:OV_GNB + 512], ALU.add)
            K.tt(yd, vT.a.re("p (h q) -> p h q", h=8), bcl(rks, 8, 64), ALU.mult)
            K.tt(ym, ym, yd, ALU.add)
            K.tt(ydT, ymf, gT, ALU.mult)
            btr = self.bank()
            for hh in range(4):
                K.mm(btr[:, hh * 128:(hh + 1) * 128], ydT[:, hh * 128:(hh + 1) * 128], ID)
            K.cp(oFM[:, 4:8, :], btr.a.re("p (h t) -> p h t", h=4), E=K.act)
            for half in range(2):
                b = self.bank()
                for j in range(4):
                    n = half * 4 + j
                    wo = woblock(n)
                    for c in range(8):
                        K.mm(b[:, j * P:(j + 1) * P], wo(c), oFM[:, c, :], start=(c == 0), stop=(c == 7))
                K.tt(h[:, half * 4:(half + 1) * 4, :], h[:, half * 4:(half + 1) * 4, :],
                     b.a.re("p (j t) -> p j t", j=4), ALU.add)
            K.dma(K.act, self.hview(dst, it * P, P), h)
        K.barrier()
        es.close()

    run_variant(True, [(s_, s_ * tps) for s_ in range(NSEQ)])
    run_variant(False, [(s_, s_ * tps + j) for s_ in range(NSEQ) for j in range(1, tps)])
    K.barrier()
    esP.close()


Model.odd_phase = _odd_phase


def kernel(**inputs):
    inp = {k: np.asarray(v) for k, v in inputs.items()}
    x = inp["x"]
    M = Model(n_layers=4, seq=x.shape[1], mix=True)
    nc = M.build()
    maps = make_in_maps(inp, M, x, 8)
    res = run_bass_kernel_spmd(nc, maps, core_ids=list(range(8)))
    out = np.concatenate([r["out"].reshape(NSEQ, x.shape[1], D) for r in res.results], axis=0)
    return out.astype(np.float32)
```
